# Optimizing a Trainium2 kernel written in Bass

```python
import jax, jax.numpy as jnp
from jax import lax
import numpy as np

D_MODEL = 1024
BATCH = 32
SEQ = 2048
DEPTH = 1

CHUNK = 64
N_MEM = 256

A_HEADS = 8
A_HEAD_DIM = 64
A_WIDTH = A_HEADS * A_HEAD_DIM
DECAY_LORA = 64
AAA_LORA = 64
GATE_LORA = 128
LNX_EPS = 64e-5

B_BLOCKS = 8
B_WIDTH = 512
B_BLOCK_DIM = B_WIDTH // B_BLOCKS
B_CONV = 4
LRU_C = 8.0

X_HEADS = 4
X_HEAD_DIM = D_MODEL // X_HEADS

D_FF = 2816
FFN_CONV = 3

NORM_EPS = 1e-6

RWKV_COLS = 3 * A_WIDTH + DECAY_LORA + AAA_LORA + GATE_LORA
P_IN = RWKV_COLS + 2 * B_WIDTH + 2 * D_MODEL

kernel_name = "hybrid_rwkv7_rglru_gated_block"


def rms_norm(x, g):
    xf = x.astype(jnp.float32)
    y = xf * lax.rsqrt(jnp.mean(xf * xf, axis=-1, keepdims=True) + NORM_EPS)
    return (y * g.astype(jnp.float32)).astype(x.dtype)


def causal_dwconv(x, w, b):
    k_width = w.shape[0]
    s = x.shape[1]
    xp = jnp.pad(x, ((0, 0), (k_width - 1, 0), (0, 0)))
    y = b
    for j in range(k_width):
        y = y + xp[:, j:j + s] * w[j]
    return y


def token_shift(p):
    return jnp.pad(p, ((0, 0), (1, 0), (0, 0)))[:, :-1]


def rwkv7_recurrence(r, decay, k, v, a, b):
    def step(state, inp):
        r_t, w_t, k_t, v_t, a_t, b_t = inp
        sa = jnp.einsum('bhij,bhj->bhi', state, a_t)
        state = (state * w_t[:, :, None, :]
                 + sa[..., None] * b_t[:, :, None, :]
                 + v_t[..., None] * k_t[:, :, None, :])
        y_t = jnp.einsum('bhij,bhj->bhi', state, r_t)
        return state, y_t
    bsz, _, nh, nd = r.shape
    xs = tuple(jnp.swapaxes(t, 0, 1) for t in (r, decay, k, v, a, b))
    s0 = jnp.zeros((bsz, nh, nd, nd), jnp.float32)
    _, ys = lax.scan(step, s0, xs)
    return jnp.swapaxes(ys, 0, 1)


def rwkv7_branch(p, mu, w0, w_up, a0, a_up, g_up, k_k, k_a, r_k, lnx_g, lnx_b):
    bsz, s, _ = p.shape
    p = p.astype(jnp.float32)
    p = p + (token_shift(p) - p) * mu
    c0, c1, c2 = A_WIDTH, 2 * A_WIDTH, 3 * A_WIDTH
    c3, c4 = c2 + DECAY_LORA, c2 + DECAY_LORA + AAA_LORA
    r, k, v = p[..., :c0], p[..., c0:c1], p[..., c1:c2]
    pw, pa, pg = p[..., c2:c3], p[..., c3:c4], p[..., c4:]
    w_log = -jax.nn.softplus(-(w0 + jnp.tanh(pw) @ w_up)) - 0.5
    decay = jnp.exp(-jnp.exp(w_log))
    a = jax.nn.sigmoid(a0 + pa @ a_up)
    g = jax.nn.sigmoid(pg) @ g_up
    kk = k * k_k
    k = k * (1.0 + (a - 1.0) * k_a)
    hs = lambda t: t.reshape(bsz, s, A_HEADS, A_HEAD_DIM)
    r, k, v, kk, a, decay = hs(r), hs(k), hs(v), hs(kk), hs(a), hs(decay)
    kk = kk / jnp.maximum(jnp.sqrt(jnp.sum(kk * kk, axis=-1, keepdims=True)), 1e-12)
    y = rwkv7_recurrence(r, decay, k, v, -kk, kk * a)
    mean = jnp.mean(y, axis=-1, keepdims=True)
    var = jnp.mean(jnp.square(y - mean), axis=-1, keepdims=True)
    y = ((y - mean) * lax.rsqrt(var + LNX_EPS)).reshape(bsz, s, A_WIDTH) * lnx_g + lnx_b
    bonus = jnp.sum(r * k * r_k, axis=-1, keepdims=True) * v
    return (y + bonus.reshape(bsz, s, A_WIDTH)) * g


def _lin_combine(c1, c2):
    a1, b1 = c1
    a2, b2 = c2
    return a1 * a2, a2 * b1 + b2


def rglru_branch(px, py, conv_w, conv_b, w_ga, b_ga, w_gx, b_gx, lam):
    bsz, s, _ = px.shape
    xb = causal_dwconv(px.astype(jnp.float32), conv_w, conv_b)
    xh = xb.reshape(bsz, s, B_BLOCKS, B_BLOCK_DIM)
    gate_a = jax.nn.sigmoid(jnp.einsum('bshi,hij->bshj', xh, w_ga).reshape(bsz, s, B_WIDTH) + b_ga)
    gate_x = jax.nn.sigmoid(jnp.einsum('bshi,hij->bshj', xh, w_gx).reshape(bsz, s, B_WIDTH) + b_gx)
    log_a = LRU_C * gate_a * jax.nn.log_sigmoid(lam)
    a = jnp.exp(log_a)
    mult = jnp.sqrt(-jnp.expm1(2.0 * log_a))
    pos = jnp.arange(s)[None, :, None]
    mult = jnp.where(pos == 0, 1.0, mult)
    u = xb * gate_x * mult
    _, h = lax.associative_scan(_lin_combine, (a, u), axis=1)
    return h * jax.nn.gelu(py.astype(jnp.float32))


def cross_attention(h, m, w_q, w_kv, w_o):
    bsz, s, _ = h.shape
    q = (h @ w_q).reshape(bsz, s, X_HEADS, X_HEAD_DIM)
    kv = m @ w_kv
    k = kv[..., :D_MODEL].reshape(bsz, -1, X_HEADS, X_HEAD_DIM)
    v = kv[..., D_MODEL:].reshape(bsz, -1, X_HEADS, X_HEAD_DIM)
    sc = jnp.einsum('bqhd,bkhd->bhqk', q, k).astype(jnp.float32) * (X_HEAD_DIM ** -0.5)
    pr = jax.nn.softmax(sc, axis=-1).astype(v.dtype)
    o = jnp.einsum('bhqk,bkhd->bqhd', pr, v).reshape(bsz, s, D_MODEL)
    return o @ w_o


def conv_ffn(h, w_in, conv_w, conv_b, w_out):
    u = h @ w_in
    gate = causal_dwconv(u[..., :D_FF], conv_w, conv_b)
    return (jax.nn.gelu(gate) * u[..., D_FF:]) @ w_out


def setup_inputs(seed: int = 0) -> dict:
    key = jax.random.key(seed)
    ks = iter(jax.random.split(key, 48))
    L = DEPTH
    f32 = jnp.float32

    def nrm(shape, scale):
        return jax.random.normal(next(ks), shape, f32) * scale

    u_lam = jax.random.uniform(next(ks), (L, B_WIDTH), f32, minval=0.9, maxval=0.999)
    return {
        "x": nrm((BATCH, SEQ, D_MODEL), 1.0),
        "mem": nrm((BATCH, N_MEM, D_MODEL), 1.0),
        "norm_mix_g": 1.0 + nrm((L, D_MODEL), 0.05),
        "w_in": nrm((L, D_MODEL, P_IN), D_MODEL ** -0.5),
        "b_in": nrm((L, P_IN), 0.02),
        "mu_shift": jax.random.uniform(next(ks), (L, RWKV_COLS), f32),
        "w0": nrm((L, A_WIDTH), 0.5) - 0.5,
        "w_lora_up": nrm((L, DECAY_LORA, A_WIDTH), 0.1),
        "a0": nrm((L, A_WIDTH), 0.5),
        "a_lora_up": nrm((L, AAA_LORA, A_WIDTH), AAA_LORA ** -0.5),
        "g_lora_up": nrm((L, GATE_LORA, A_WIDTH), GATE_LORA ** -0.5),
        "k_k": 0.85 + nrm((L, A_WIDTH), 0.05),
        "k_a": 1.0 + nrm((L, A_WIDTH), 0.05),
        "r_k": nrm((L, A_HEADS, A_HEAD_DIM), 0.1),
        "lnx_g": 1.0 + nrm((L, A_WIDTH), 0.05),
        "lnx_b": nrm((L, A_WIDTH), 0.02),
        "w_branch_a": nrm((L, A_WIDTH, D_MODEL), A_WIDTH ** -0.5),
        "conv_b_w": nrm((L, B_CONV, B_WIDTH), B_CONV ** -0.5),
        "conv_b_b": nrm((L, B_WIDTH), 0.02),
        "w_rg_a": nrm((L, B_BLOCKS, B_BLOCK_DIM, B_BLOCK_DIM), B_BLOCK_DIM ** -0.5),
        "b_rg_a": nrm((L, B_WIDTH), 0.02),
        "w_rg_x": nrm((L, B_BLOCKS, B_BLOCK_DIM, B_BLOCK_DIM), B_BLOCK_DIM ** -0.5),
        "b_rg_x": nrm((L, B_WIDTH), 0.02),
        "lru_lambda": jnp.log(u_lam) - jnp.log1p(-u_lam),
        "w_branch_b": nrm((L, B_WIDTH, D_MODEL), B_WIDTH ** -0.5),
        "w_mix_out": nrm((L, D_MODEL, D_MODEL), D_MODEL ** -0.5),
        "norm_x_g": 1.0 + nrm((L, D_MODEL), 0.05),
        "norm_mem_g": 1.0 + nrm((L, D_MODEL), 0.05),
        "w_cq": nrm((L, D_MODEL, D_MODEL), D_MODEL ** -0.5),
        "w_ckv": nrm((L, D_MODEL, 2 * D_MODEL), D_MODEL ** -0.5),
        "w_co": nrm((L, D_MODEL, D_MODEL), D_MODEL ** -0.5),
        "norm_ffn_g": 1.0 + nrm((L, D_MODEL), 0.05),
        "w_ffn_in": nrm((L, D_MODEL, 2 * D_FF), D_MODEL ** -0.5),
        "ffn_conv_w": nrm((L, FFN_CONV, D_FF), FFN_CONV ** -0.5),
        "ffn_conv_b": nrm((L, D_FF), 0.02),
        "w_ffn_out": nrm((L, D_FF, D_MODEL), D_FF ** -0.5),
        "norm_final_g": 1.0 + nrm((D_MODEL,), 0.05),
    }


def reference(x, mem, norm_mix_g, w_in, b_in, mu_shift, w0, w_lora_up, a0, a_lora_up,
              g_lora_up, k_k, k_a, r_k, lnx_g, lnx_b, w_branch_a, conv_b_w, conv_b_b,
              w_rg_a, b_rg_a, w_rg_x, b_rg_x, lru_lambda, w_branch_b, w_mix_out,
              norm_x_g, norm_mem_g, w_cq, w_ckv, w_co, norm_ffn_g, w_ffn_in,
              ffn_conv_w, ffn_conv_b, w_ffn_out, norm_final_g):
    dt = x.dtype
    cb0 = RWKV_COLS
    cb1 = cb0 + B_WIDTH
    cb2 = cb1 + B_WIDTH
    cg1 = cb2 + D_MODEL
    for l in range(DEPTH):
        h = rms_norm(x, norm_mix_g[l])
        proj = h @ w_in[l] + b_in[l]
        y_a = rwkv7_branch(proj[..., :cb0], mu_shift[l], w0[l], w_lora_up[l], a0[l],
                           a_lora_up[l], g_lora_up[l], k_k[l], k_a[l], r_k[l],
                           lnx_g[l], lnx_b[l]).astype(dt) @ w_branch_a[l]
        y_b = rglru_branch(proj[..., cb0:cb1], proj[..., cb1:cb2], conv_b_w[l], conv_b_b[l],
                           w_rg_a[l], b_rg_a[l], w_rg_x[l], b_rg_x[l],
                           lru_lambda[l]).astype(dt) @ w_branch_b[l]
        merged = (jax.nn.sigmoid(proj[..., cb2:cg1]) * y_a
                  + jax.nn.sigmoid(proj[..., cg1:]) * y_b)
        x = x + merged @ w_mix_out[l]
        x = x + cross_attention(rms_norm(x, norm_x_g[l]), rms_norm(mem, norm_mem_g[l]),
                                w_cq[l], w_ckv[l], w_co[l])
        x = x + conv_ffn(rms_norm(x, norm_ffn_g[l]), w_ffn_in[l], ffn_conv_w[l],
                         ffn_conv_b[l], w_ffn_out[l])
    return rms_norm(x, norm_final_g)
```

```python
import numpy as np
from contextlib import ExitStack
import concourse.bass as bass
import concourse.mybir as mybir
from concourse.bass_utils import run_bass_kernel_spmd

F32 = mybir.dt.float32
BF16 = mybir.dt.bfloat16
AF = mybir.ActivationFunctionType
ALU = mybir.AluOpType
AX = mybir.AxisListType

D = 1024
NMEM = 256
AW = 512
RWKV_COLS = 1792
P_IN = 4864
DFF = 2816
NORM_EPS = 1e-6
LNX_EPS = 64e-5
SAME_ENGINE_SYNC = True
import os as _os
RSTOP = int(_os.environ.get("RSTOP", "0"))


_ALL_TILES = []


class T:
    def __init__(self, ap, name=""):
        self.ap = ap if isinstance(ap, bass.AP) else ap[:]
        self.w = None
        self.r = []
        self.name = name
        self.dsem = None
        _ALL_TILES.append(self)

    def __getitem__(self, k):
        return V([self], self.ap[k])

    @property
    def v(self):
        return V([self], self.ap)


class V:
    def __init__(self, ts, ap):
        self.ts = ts
        self.ap = ap

    def __getitem__(self, k):
        return V(self.ts, self.ap[k])

    def re(self, pat, **kw):
        return V(self.ts, self.ap.rearrange(pat, **kw))

    def bc(self, shape):
        return V(self.ts, self.ap.to_broadcast(shape))

    def un(self, axis):
        return V(self.ts, self.ap.unsqueeze(axis))

    def bitcast(self, dt):
        return V(self.ts, self.ap.bitcast(dt))


def _ap(x):
    return x.ap if isinstance(x, V) else x


class Sync:
    def __init__(self, nc, es, n_dma_sems=64):
        self.nc = nc
        self.es = es
        self.engs = {'pe': nc.tensor, 'act': nc.scalar, 'dve': nc.vector, 'pool': nc.gpsimd, 'sp': nc.sync}
        self.sem = {}
        self.cnt = {}
        self.seen = {}
        for e in self.engs:
            self.sem[e] = es.enter_context(nc.semaphore("c_" + e))
            self.cnt[e] = 0
            self.seen[e] = {}
        self.dpool = [{'sem': es.enter_context(nc.semaphore("d%d" % i)), 'cnt': 0} for i in range(n_dma_sems)]
        self.dfree = list(range(n_dma_sems))
        self.n_inst = 0
        self.bar1 = es.enter_context(nc.semaphore("bar1"))
        self.bar2 = es.enter_context(nc.semaphore("bar2"))
        self.bar_k = 0

    def _wait(self, e, tok):
        if tok is None:
            return
        sem, val, src = tok
        if src == e and (e == 'pe' or not SAME_ENGINE_SYNC):
            return
        key = sem.name
        if self.seen[e].get(key, 0) >= val:
            return
        self.seen[e][key] = val
        self.engs[e].wait_ge(sem, val)

    def deps(self, e, reads, writes):
        for v in reads:
            if not isinstance(v, V):
                continue
            for t in v.ts:
                self._wait(e, t.w)
        for v in writes:
            for t in v.ts:
                self._wait(e, t.w)
                for tok in t.r:
                    self._wait(e, tok)

    def done(self, tok, reads, writes):
        for v in reads:
            if not isinstance(v, V):
                continue
            for t in v.ts:
                t.r.append(tok)
                if len(t.r) > 24:
                    t.r = t.r[-24:] if False else t.r
        for v in writes:
            for t in v.ts:
                t.w = tok
                t.r = []

    def op(self, e, fn, reads, writes, signal=True):
        self.deps(e, reads, writes)
        inst = fn(self.engs[e])
        self.n_inst += 1
        if signal:
            self.cnt[e] += 1
            inst.then_inc(self.sem[e], 1)
            tok = (self.sem[e], self.cnt[e], e)
        else:
            tok = (self.sem[e], self.cnt[e] + 1, e)
        self.done(tok, reads, writes)
        return inst

    def _dsem(self, t):
        if t.dsem is None:
            if not self.dfree:
                raise RuntimeError("out of dma semaphores")
            t.dsem = self.dpool[self.dfree.pop(0)]
        return t.dsem

    def dma(self, q, out, in_, **kw):
        reads = [in_] if isinstance(in_, V) else []
        writes = [out] if isinstance(out, V) else []
        self.deps(q, reads, writes)
        sbv = out if isinstance(out, V) else in_
        ds = self._dsem(sbv.ts[0])
        inst = self.engs[q].dma_start(out=_ap(out), in_=_ap(in_), **kw)
        self.n_inst += 1
        ds['cnt'] += 16
        inst.then_inc(ds['sem'], 16)
        tok = (ds['sem'], ds['cnt'], 'dma')
        self.done(tok, reads, writes)
        return tok

    def barrier(self):
        toks = [(self.sem[f], self.cnt[f], f) for f in self.engs if self.cnt[f] > 0]
        toks += [(d['sem'], d['cnt'], 'dma') for d in self.dpool if d['cnt'] > 0]
        for e in self.engs:
            for tok in toks:
                if tok[2] == e:
                    continue
                self._wait(e, tok)

    def release_dma_sems(self):
        self.dfree = list(range(len(self.dpool)))
        for t in _ALL_TILES:
            t.dsem = None

    def need_reset(self, limit=2600):
        return max(self.cnt.values()) > limit or max(d['cnt'] for d in self.dpool) > limit

    def hard_barrier(self):
        self.barrier()
        self.bar_k += 1
        k = self.bar_k
        for e in self.engs:
            self.engs[e].sem_inc(self.bar1, 1)
        sp = self.engs['sp']
        sp.wait_ge(self.bar1, len(self.engs) * k)
        for e in self.engs:
            if self.cnt[e] > 0:
                sp.sem_clear(self.sem[e])
        for d in self.dpool:
            if d['cnt'] > 0:
                sp.sem_clear(d['sem'])
        sp.sem_inc(self.bar2, 1)
        for e in self.engs:
            if e != 'sp':
                self.engs[e].wait_ge(self.bar2, k)
        for e in self.engs:
            self.cnt[e] = 0
            self.seen[e] = {}
        for d in self.dpool:
            d['cnt'] = 0
        for t in _ALL_TILES:
            t.w = None
            t.r = []


class Prog:
    def __init__(self, nb=4, seq=2048, phases="LRMBC", dbg=False):
        self.nb, self.seq, self.phases, self.dbg = nb, seq, phases, dbg
        self.ntok = nb * seq
        nc = self.nc = bass.Bass("TRN2", target_bir_lowering=False)
        self.inp = {}
        del _ALL_TILES[:]

        def din(name, shape):
            self.inp[name] = nc.dram_tensor(name, list(shape), F32, kind="ExternalInput").ap()
            return self.inp[name]

        ntok = self.ntok
        din("x", [ntok, D])
        din("mem", [nb * NMEM, D])
        din("norm_mix_g", [D]); din("w_in", [D, P_IN]); din("b_in", [P_IN]); din("mu_shift", [RWKV_COLS])
        din("w0", [AW]); din("w_lora_up", [64, AW]); din("a0", [AW]); din("a_lora_up", [64, AW])
        din("g_lora_up", [128, AW]); din("k_k", [AW]); din("k_a", [AW]); din("r_k", [AW])
        din("lnx_g", [AW]); din("lnx_b", [AW]); din("w_branch_a", [AW, D])
        din("conv_b_w", [4, AW]); din("conv_b_b", [AW]); din("w_rg_a", [8, 64, 64]); din("b_rg_a", [AW])
        din("w_rg_x", [8, 64, 64]); din("b_rg_x", [AW]); din("lru_lambda", [AW]); din("w_branch_b", [AW, D])
        din("w_mix_out", [D, D]); din("norm_x_g", [D]); din("norm_mem_g", [D]); din("w_cq", [D, D])
        din("w_ckv", [D, 2 * D]); din("w_co", [D, D]); din("norm_ffn_g", [D]); din("w_ffn_in", [D, 2 * DFF])
        din("ffn_conv_w", [3, DFF]); din("ffn_conv_b", [DFF]); din("w_ffn_out", [DFF, D]); din("norm_final_g", [D])
        self.out = nc.dram_tensor("out", [ntok, D], F32, kind="ExternalOutput").ap()
        self.x1 = nc.dram_tensor("x1s", [ntok, D], F32).ap()
        self.x2 = nc.dram_tensor("x2s", [ntok, D], F32).ap()
        self.dbg_out = {}

        with ExitStack() as es:
            self.es = es
            self.S = Sync(nc, es)
            self.ps = [T(es.enter_context(nc.psum_tensor("ps%d" % i, [128, 512], F32)), "ps%d" % i) for i in range(8)]
            self.ps_i = 0
            self.ident = self.sb(es, "ident", [128, 128], BF16)
            self.identf = self.sb(es, "identf", [128, 128], F32)
            for idt in (self.ident, self.identf):
                self.S.op('pool', lambda e: e.memset(idt.ap[:], 0.0), [], [idt.v])
                self.S.op('pool', lambda e: e.affine_select(idt.ap[:], idt.ap[:], pattern=[[-1, 128]],
                                                           compare_op=ALU.not_equal, fill=1.0, base=0,
                                                           channel_multiplier=1), [idt.v], [idt.v])
            self.mhalf = self.sb(es, "mhalf", [128, 512], F32)
            self.memset('pool', self.mhalf.v, -0.5)
            self.phalf = self.sb(es, "phalf", [128, 512], F32)
            self.memset('pool', self.phalf.v, 0.5)
            self.yas = nc.dram_tensor("yas", [AW, ntok], BF16).ap()
            self.ybs = nc.dram_tensor("ybs", [AW, ntok], BF16).ap()
            src = {'L': self.inp["x"], 'R': self.inp["x"], 'M': self.inp["x"], 'B': self.x1, 'C': self.x2}
            dst = {'L': None, 'R': None, 'M': self.x1, 'B': self.x2, 'C': self.out}
            order = [p for p in "LRMBC" if p in phases]
            chain = [p for p in order if p in "MBC"]
            for i, p in enumerate(order):
                s_ap, d_ap = src[p], dst[p]
                if p in chain:
                    if chain.index(p) == 0:
                        s_ap = self.inp["x"]
                    if chain.index(p) == len(chain) - 1:
                        d_ap = self.out
                with ExitStack() as pes:
                    getattr(self, "phase" + p)(pes, s_ap, d_ap, final=(p == 'C'))
                    self.S.hard_barrier()
                self.S.release_dma_sems()
            self.S.barrier()

    def sb(self, es, name, shape, dt):
        return T(es.enter_context(self.nc.sbuf_tensor(name, list(shape), dt)), name)

    def next_ps(self):
        t = self.ps[self.ps_i]
        self.ps_i = (self.ps_i + 1) % 8
        return t

    def act(self, out, in_, func, bias=0.0, scale=1.0, accum=None):
        rd = [in_] + [a for a in (bias, scale) if isinstance(a, V)]
        wr = [out] + ([accum] if accum is not None else [])
        kw = {}
        if accum is not None:
            kw['accum_out'] = accum.ap
        return self.S.op('act', lambda e: e.activation(out=out.ap, in_=in_.ap, func=func, bias=_ap(bias),
                                                       scale=_ap(scale), **kw), rd, wr)

    def tt(self, eng, out, in0, in1, op):
        return self.S.op(eng, lambda e: e.tensor_tensor(out=out.ap, in0=in0.ap, in1=in1.ap, op=op), [in0, in1], [out])

    def ts(self, eng, out, in0, s1, s2, op0, op1=None):
        rd = [in0] + [a for a in (s1, s2) if isinstance(a, V)]
        if op1 is None:
            return self.S.op(eng, lambda e: e.tensor_scalar(out=out.ap, in0=in0.ap, scalar1=_ap(s1), scalar2=None,
                                                            op0=op0), rd, [out])
        return self.S.op(eng, lambda e: e.tensor_scalar(out=out.ap, in0=in0.ap, scalar1=_ap(s1), scalar2=_ap(s2),
                                                        op0=op0, op1=op1), rd, [out])

    def stt(self, eng, out, in0, scalar, in1, op0, op1):
        rd = [in0, in1] + ([scalar] if isinstance(scalar, V) else [])
        return self.S.op(eng, lambda e: e.scalar_tensor_tensor(out=out.ap, in0=in0.ap, scalar=_ap(scalar), in1=in1.ap,
                                                               op0=op0, op1=op1), rd, [out])

    def cp(self, eng, out, in_):
        if eng == 'act':
            return self.act(out, in_, AF.Copy)
        return self.S.op(eng, lambda e: e.tensor_copy(out=out.ap, in_=in_.ap), [in_], [out])

    def rsqrt(self, out, in_):
        shp = list(in_.ap.shape)
        mh = self.mhalf[0:shp[0], 0:shp[-1]]
        if len(shp) == 3:
            mh = mh.un(1).bc(shp)
        return self.tt('pool', out, in_, mh, ALU.pow)

    def memset(self, eng, out, val):
        return self.S.op(eng, lambda e: e.memset(out.ap, val), [], [out])

    def mm(self, out, lhsT, rhs, start, stop, signal=None):
        if signal is None:
            signal = stop
        return self.S.op('pe', lambda e: e.matmul(out.ap, lhsT=lhsT.ap, rhs=rhs.ap, start=start, stop=stop),
                         [lhsT, rhs], [out], signal=signal)

    def tr(self, out, in_, signal=True, f32=False):
        idt = self.identf if f32 else self.ident
        n = in_.ap.shape[0]
        return self.S.op('pe', lambda e: e.transpose(out.ap, in_.ap, idt.ap[0:n, 0:n]), [in_, idt.v], [out],
                         signal=signal)

    def dma(self, q, out, in_, **kw):
        return self.S.dma(q, out, in_, **kw)

    def load_weight(self, stg, dst, src_rows, ncols, scale=None, piece=2048):
        rows = src_rows.shape[0]
        c0 = 0
        while c0 < ncols:
            n = min(piece, ncols - c0)
            st = stg[self._stg_i % len(stg)]
            eng = ('dve', 'pool', 'act')[self._stg_i % 3]
            self._stg_i += 1
            self.dma('sp', st[0:rows, 0:n], src_rows[:, c0:c0 + n])
            o = dst[:, c0:c0 + n]
            i = st[0:rows, 0:n]
            if scale is None:
                self.cp(eng, o, i)
            elif eng == 'act':
                if isinstance(scale, V):
                    self.act(o, i, AF.Copy, scale=scale)
                else:
                    self.act(o, i, AF.Copy, scale=float(scale))
            else:
                self.ts(eng, o, i, scale, None, ALU.mult)
            c0 += n

    def load_cols(self, es, name, specs):
        cols = {}
        n = 0
        for k, v in specs:
            cols[k] = n
            n += (v.shape[0] + 127) // 128
        res = self.sb(es, name, [128, n], F32)
        with ExitStack() as tes:
            ngrp = (n + 127) // 128
            stage = [self.sb(tes, name + "_st%d" % g, [128, 128], F32) for g in range(ngrp)]
            for st in stage:
                self.memset('pool', st.v, 0.0)
            for k, v in specs:
                L = v.shape[0]
                m = (L + 127) // 128
                c = cols[k]
                r = 0
                while r < m:
                    g, rr = divmod(c + r, 128)
                    cnt = min(m - r, 128 - rr)
                    if L >= 128:
                        self.dma('sp', stage[g][rr:rr + cnt, :], v[r * 128:(r + cnt) * 128].rearrange("(m p) -> m p", p=128))
                    else:
                        self.dma('sp', stage[g][rr:rr + 1, 0:L], v.rearrange("(m p) -> m p", m=1))
                    r += cnt
            for g in range(ngrp):
                w = min(128, n - g * 128)
                ps = self.next_ps()
                self.tr(ps[:, 0:w], stage[g][0:w, :], f32=True)
                self.cp('dve', res[:, g * 128:g * 128 + w], ps[:, 0:w])
            self.S.barrier()
        return res, cols

    def rms_h(self, xt, nsub, ss, rstd, junk, h):
        for s in range(nsub):
            self.act(junk.v, xt[:, s, :], AF.Square, accum=ss[:, s:s + 1])
        self.ts('dve', rstd[:, 0:nsub], ss[:, 0:nsub], 1.0 / D, NORM_EPS, ALU.mult, ALU.add)
        self.rsqrt(rstd[:, 0:nsub], rstd[:, 0:nsub])
        for s in range(nsub):
            self.ts('pool', h[:, s, :], xt[:, s, :], rstd[:, s:s + 1], None, ALU.mult)

    def transpose_h(self, h, nsub, hT, evac=('act', 'dve')):
        tw = nsub * 128
        per_bank = 1024 // tw
        c = 0
        i = 0
        while c < 8:
            ps = self.next_ps()
            pb = ps.v.bitcast(BF16)
            nchunk = min(per_bank, 8 - c)
            for cc in range(nchunk):
                for s in range(nsub):
                    last = (cc == nchunk - 1 and s == nsub - 1)
                    self.tr(pb[:, cc * tw + s * 128: cc * tw + (s + 1) * 128], h[:, s, (c + cc) * 128:(c + cc + 1) * 128],
                            signal=last)
            self.cp(evac[i % len(evac)], hT[:, c:c + nchunk, :], pb[:, 0:nchunk * tw].re("p (c t) -> p c t", c=nchunk))
            c += nchunk
            i += 1

    def norm_from_dram(self, src, tok0, NS, xbufs, sst, rst, junk, h):
        for s in range(NS):
            xb = xbufs[self._xb_i % len(xbufs)]
            self._xb_i += 1
            self.dma('sp', xb.v, src[tok0 + s * 128: tok0 + (s + 1) * 128, :])
            self.act(junk.v, xb.v, AF.Square, accum=sst[s].v)
            self.ts('dve', rst[s].v, sst[s].v, 1.0 / D, NORM_EPS, ALU.mult, ALU.add)
            self.rsqrt(rst[s].v, rst[s].v)
            self.ts('pool', h[:, s, :], xb.v, rst[s].v, None, ALU.mult)

    def norm_bufs(self, es, pfx, NS):
        xbufs = [self.sb(es, pfx + "_xb%d" % i, [128, D], F32) for i in range(2)]
        sst = [self.sb(es, pfx + "_ss%d" % i, [128, 1], F32) for i in range(NS)]
        rst = [self.sb(es, pfx + "_rs%d" % i, [128, 1], F32) for i in range(NS)]
        junk = self.sb(es, pfx + "_junk", [128, D], BF16)
        self._xb_i = 0
        return xbufs, sst, rst, junk

    def phaseL(self, es, src, dst, final=False):
        S = self.S
        inp = self.inp
        TT = 512
        NS = 4
        self._stg_i = 0
        self._ev_i = 0
        cols, ci = self.load_cols(es, "l_cols", [
            ("g", inp["norm_mix_g"]), ("bin", inp["b_in"][1792:2816]),
            ("cw0", inp["conv_b_w"][0]), ("cw1", inp["conv_b_w"][1]), ("cw2", inp["conv_b_w"][2]), ("cw3", inp["conv_b_w"][3]),
            ("cbb", inp["conv_b_b"]), ("bra", inp["b_rg_a"]), ("brx", inp["b_rg_x"]), ("lam", inp["lru_lambda"])])
        hb = self.sb(es, "l_hb", [128, 8], F32)
        self.ts('dve', hb[:, 0:4], cols[:, ci["bra"]:ci["bra"] + 4], 0.5, None, ALU.mult)
        self.ts('dve', hb[:, 4:8], cols[:, ci["brx"]:ci["brx"] + 4], 0.5, None, ALU.mult)
        cA = self.sb(es, "l_cA", [128, 8], F32)
        lt = self.sb(es, "l_lt", [128, 4], F32)
        self.act(lt.v, cols[:, ci["lam"]:ci["lam"] + 4], AF.Exp, scale=-1.0)
        self.ts('dve', lt.v, lt.v, 1.0, None, ALU.add)
        self.act(lt.v, lt.v, AF.Ln)
        self.ts('dve', cA[:, 0:4], lt.v, -4.0, None, ALU.mult)
        self.ts('dve', cA[:, 4:8], lt.v, -8.0, None, ALU.mult)
        win = self.sb(es, "l_win", [128, 8, 1024], BF16)
        wg = self.sb(es, "l_wg", [128, 2, 4, 128], BF16)
        with ExitStack() as tes:
            stg = [self.sb(tes, "l_stg%d" % i, [128, 1024], F32) for i in range(3)]
            for k in range(8):
                self.load_weight(stg, win[:, k, :], inp["w_in"][k * 128:(k + 1) * 128, 1792:2816], 1024,
                                 scale=cols[:, ci["g"] + k: ci["g"] + k + 1], piece=1024)
            for gi, nm in enumerate(("w_rg_a", "w_rg_x")):
                st = stg[gi]
                self.memset('pool', st.v, 0.0)
                for blk in range(8):
                    c, hl = divmod(blk, 2)
                    self.dma('sp', st[hl * 64:(hl + 1) * 64, c * 128 + hl * 64: c * 128 + (hl + 1) * 64], inp[nm][blk])
                self.cp('dve', wg[:, gi, :, :], st[:, 0:512].re("p (c m) -> p c m", c=4))
            S.barrier()
        xbufs, sst, rst, junk = self.norm_bufs(es, "l", NS)
        h = self.sb(es, "l_h", [128, NS, D], BF16)
        hT = self.sb(es, "l_hT", [128, 8, TT], BF16)
        PX = [self.sb(es, "l_px%d" % c, [128, 3 + TT], F32) for c in range(4)]
        GY = [self.sb(es, "l_gy%d" % c, [128, TT], F32) for c in range(4)]
        YB = [self.sb(es, "l_yb%d" % c, [128, TT], BF16) for c in range(4)]
        carry = [self.sb(es, "l_cy%d" % c, [128, 1], F32) for c in range(4)]
        NW = 2
        def wk(nm, dt=F32):
            return [self.sb(es, "l_%s%d" % (nm, i), [128, TT], dt) for i in range(NW)]
        acc = wk("acc"); xbb = wk("xbb", BF16); ta = wk("ta"); tx = wk("tx"); av = wk("av"); a2 = wk("a2"); u = wk("u"); hl_ = wk("hl")
        ntile = self.ntok // TT
        tiles_per_b = self.seq // TT
        for i in range(ntile):
            if S.need_reset():
                S.hard_barrier()
            tok0 = i * TT
            first = (i % tiles_per_b == 0)
            if first:
                for c in range(4):
                    self.memset('pool', PX[c][:, 0:3], 0.0)
                    self.memset('pool', carry[c].v, 0.0)
            self.norm_from_dram(src, tok0, NS, xbufs, sst, rst, junk, h)
            self.transpose_h(h, NS, hT)
            for c in range(4):
                ps = self.next_ps()
                for k in range(8):
                    self.mm(ps.v, win[:, k, c * 128:(c + 1) * 128], hT[:, k, :], start=(k == 0), stop=(k == 7))
                self.act(PX[c][:, 3:3 + TT], ps.v, AF.Identity, bias=cols[:, ci["bin"] + c: ci["bin"] + c + 1])
            for c in range(4):
                ps = self.next_ps()
                for k in range(8):
                    self.mm(ps.v, win[:, k, 512 + c * 128: 512 + (c + 1) * 128], hT[:, k, :], start=(k == 0), stop=(k == 7))
                self.act(GY[c].v, ps.v, AF.Gelu_apprx_tanh, bias=cols[:, ci["bin"] + 4 + c: ci["bin"] + 5 + c])
            for c in range(4):
                w = i * 4 + c
                A = acc[w % NW]; XB = xbb[w % NW]; TA = ta[w % NW]; TX = tx[w % NW]; AV = av[w % NW]; A2 = a2[w % NW]
                U = u[w % NW]; HL = hl_[w % NW]
                cw = [cols[:, ci["cw%d" % j] + c: ci["cw%d" % j] + c + 1] for j in range(4)]
                self.ts('dve', A.v, PX[c][:, 0:TT], cw[0], cols[:, ci["cbb"] + c: ci["cbb"] + c + 1], ALU.mult, ALU.add)
                for j in range(1, 4):
                    self.stt('dve', A.v, PX[c][:, j:j + TT], cw[j], A.v, ALU.mult, ALU.add)
                self.cp('pool', PX[c][:, 0:3], PX[c][:, TT:TT + 3])
                self.cp('pool', XB.v, A.v)
                psa = self.next_ps()
                self.mm(psa.v, wg[:, 0, c, :], XB.v, start=True, stop=True)
                psx = self.next_ps()
                self.mm(psx.v, wg[:, 1, c, :], XB.v, start=True, stop=True)
                self.act(TA.v, psa.v, AF.Tanh, scale=0.5, bias=hb[:, c:c + 1])
                self.act(TX.v, psx.v, AF.Tanh, scale=0.5, bias=hb[:, 4 + c:5 + c])
                self.act(AV.v, TA.v, AF.Exp, scale=cA[:, c:c + 1], bias=cA[:, c:c + 1])
                self.act(A2.v, TA.v, AF.Exp, scale=cA[:, 4 + c:5 + c], bias=cA[:, 4 + c:5 + c])
                self.ts('pool', A2.v, A2.v, -1.0, 1.0, ALU.mult, ALU.add)
                self.ts('pool', A2.v, A2.v, 0.0, None, ALU.max)
                self.tt('pool', A2.v, A2.v, self.phalf[:, 0:TT], ALU.pow)
                if first:
                    self.memset('pool', A2[:, 0:1], 1.0)
                self.stt('dve', U.v, TX.v, 1.0, A.v, ALU.add, ALU.mult)
                self.tt('pool', U.v, U.v, A2.v, ALU.mult)
                self.S.op('dve', lambda e: e.tensor_tensor_scan(HL.ap, AV.ap, U.ap, carry[c].ap, ALU.mult, ALU.add),
                          [AV.v, U.v, carry[c].v], [HL.v])
                self.cp('pool', carry[c].v, HL[:, TT - 1:TT])
                self.tt('pool', YB[c].v, HL.v, GY[c].v, ALU.mult)
                self.dma('sp', self.ybs[c * 128:(c + 1) * 128, tok0:tok0 + TT], YB[c].v)

    def phaseR(self, es, src, dst, final=False):
        S = self.S
        inp = self.inp
        TT = 256
        NS = 2
        NQ = 4
        CH = 64
        self._stg_i = 0
        self._ev_i = 0
        cols, ci = self.load_cols(es, "r_cols", [
            ("g", inp["norm_mix_g"]), ("bin", inp["b_in"][0:RWKV_COLS]), ("mu", inp["mu_shift"]),
            ("w0", inp["w0"]), ("a0", inp["a0"]), ("kk", inp["k_k"]), ("ka", inp["k_a"]), ("rk", inp["r_k"])])

        def col(key, j):
            return cols[:, ci[key] + j: ci[key] + j + 1]

        der = self.sb(es, "r_der", [128, 32], F32)
        self.ts('dve', der[:, 0:14], cols[:, ci["mu"]:ci["mu"] + 14], -1.0, 1.0, ALU.mult, ALU.add)
        self.ts('dve', der[:, 14:18], cols[:, ci["w0"]:ci["w0"] + 4], 0.5, None, ALU.mult)
        self.ts('dve', der[:, 18:22], cols[:, ci["a0"]:ci["a0"] + 4], 0.5, None, ALU.mult)
        self.ts('dve', der[:, 22:26], cols[:, ci["ka"]:ci["ka"] + 4], 0.5, None, ALU.mult)
        self.ts('dve', der[:, 26:30], cols[:, ci["ka"]:ci["ka"] + 4], -0.5, 1.0, ALU.mult, ALU.add)
        rkb = self.sb(es, "r_rkb", [128, 4], BF16)
        self.cp('dve', rkb.v, cols[:, ci["rk"]:ci["rk"] + 4])
        m64 = [self.sb(es, "r_m64_%d" % i, [64, 64], BF16) for i in range(3)]
        masks = [self.sb(es, "r_mask%d" % i, [128, 128], BF16) for i in range(3)]
        specs = [([[1, 64]], -1, ALU.is_gt), ([[-1, 64]], 1, ALU.is_gt), ([[1, 64]], -1, ALU.is_ge)]
        for mi in range(3):
            pat, cm, op = specs[mi]
            self.memset('pool', m64[mi].v, 1.0)
            self.S.op('pool', lambda e: e.affine_select(m64[mi].ap, m64[mi].ap, pattern=pat, compare_op=op, fill=0.0,
                                                        base=0, channel_multiplier=cm), [m64[mi].v], [m64[mi].v])
            for a in range(2):
                for b_ in range(2):
                    self.dma('sp', masks[mi][a * 64:(a + 1) * 64, b_ * 64:(b_ + 1) * 64], m64[mi].v)
        mask_su, mask_sl, mask_u = masks
        bones = self.sb(es, "r_bones", [128, 128], BF16)
        self.memset('pool', bones.v, 0.0)
        self.memset('pool', bones[0:64, 0:64], 1.0)
        self.memset('pool', bones[64:128, 64:128], 1.0)
        m01 = self.sb(es, "r_m01", [128, TT], F32)
        self.memset('pool', m01.v, 1.0)
        self.memset('pool', m01.v.re("p (q s) -> p q s", s=CH)[:, :, 0:1], 0.0)
        lnxg = self.sb(es, "r_lnxg", [128, 4, 64], F32)
        lnxb = self.sb(es, "r_lnxb", [128, 4, 64], F32)
        for c in range(4):
            for hl in range(2):
                hh = 2 * c + hl
                self.dma('sp', lnxg[hl * 64:(hl + 1) * 64, c, :], inp["lnx_g"][hh * 64:(hh + 1) * 64].partition_broadcast(64))
                self.dma('sp', lnxb[hl * 64:(hl + 1) * 64, c, :], inp["lnx_b"][hh * 64:(hh + 1) * 64].partition_broadcast(64))
        win = self.sb(es, "r_win", [128, 8, RWKV_COLS], BF16)
        wlora = self.sb(es, "r_wlora", [128, 2, AW], BF16)
        gup = self.sb(es, "r_gup", [128, AW], BF16)
        with ExitStack() as tes:
            stg = [self.sb(tes, "r_stg%d" % i, [128, RWKV_COLS], F32) for i in range(3)]
            for k in range(8):
                self.load_weight(stg, win[:, k, :], inp["w_in"][k * 128:(k + 1) * 128, 0:RWKV_COLS], RWKV_COLS,
                                 scale=cols[:, ci["g"] + k: ci["g"] + k + 1], piece=RWKV_COLS)
            st = stg[0]
            self.memset('pool', st[:, 0:2 * AW], 0.0)
            self.dma('sp', st[0:64, 0:AW], inp["w_lora_up"])
            self.dma('sp', st[64:128, AW:2 * AW], inp["a_lora_up"])
            self.cp('dve', wlora.v, st[:, 0:2 * AW].re("p (a m) -> p a m", a=2))
            st = stg[1]
            self.dma('sp', st[:, 0:AW], inp["g_lora_up"])
            self.cp('dve', gup.v, st[:, 0:AW])
            S.barrier()
        xbufs, sst, rst, junk = self.norm_bufs(es, "r", NS)
        h = self.sb(es, "r_h", [128, NS, D], BF16)
        hT = self.sb(es, "r_hT", [128, 8, TT], BF16)
        PW = [self.sb(es, "r_pw%d" % i, [128, 1 + TT], F32) for i in range(3)]
        pcarry = [self.sb(es, "r_pc%d" % m, [128, 1], F32) for m in range(14)]
        Sx = [self.sb(es, "r_s%d" % m, [128, TT], F32) for m in range(14)]
        ltmp = [self.sb(es, "r_lt%d" % i, [128, TT], F32) for i in range(2)]
        LB = self.sb(es, "r_lb", [128, TT], BF16)
        SGd = self.sb(es, "r_sgd", [128, NQ, 128], BF16)
        NW = 2

        def wk(nm, dt=F32):
            return [self.sb(es, "r_%s%d" % (nm, i), [128, TT], dt) for i in range(NW)]

        logw = wk("logw"); ta = wk("ta"); cum = wk("cum"); egm1 = wk("egm1"); eg = wk("eg"); eig = wk("eig"); egc = wk("egc")
        kk = wk("kk"); kk2 = wk("kk2", BF16); rn = wk("rn"); kkn = wk("kkn"); kmod = wk("kmod"); bv = wk("bv")
        names = ["AT", "BT", "KT", "RT", "BGT", "KGT", "VB", "RK"]
        EXP = {n: [self.sb(es, "r_%s%d" % (n, c), [128, NQ, 128], BF16) for c in range(4)] for n in names}
        for n in names:
            for c in range(4):
                self.memset('pool', EXP[n][c].v, 0.0)
        GC = self.sb(es, "r_gc", [128, 4, NQ], F32)
        BKG = [self.sb(es, "r_bkg%d" % q, [128, 2, 4, 128], BF16) for q in range(NQ)]
        Vst = [self.sb(es, "r_vst%d" % q, [128, 4, 64], BF16) for q in range(NQ)]
        Nb = [[self.sb(es, "r_nb%d_%d" % (q, i), [128, 4, 128], BF16) for i in range(2)] for q in range(NQ)]
        Lb = [[self.sb(es, "r_lb%d_%d" % (q, i), [128, 4, 128], BF16) for i in range(2)] for q in range(NQ)]
        Pb = [self.sb(es, "r_pb%d" % q, [128, 4, 128], BF16) for q in range(NQ)]
        LakT = [self.sb(es, "r_lak%d" % q, [128, 4, 128], BF16) for q in range(NQ)]
        MrbT = [self.sb(es, "r_mrb%d" % q, [128, 4, 128], BF16) for q in range(NQ)]
        MrkT = [self.sb(es, "r_mrk%d" % q, [128, 4, 128], BF16) for q in range(NQ)]
        H = self.sb(es, "r_H", [128, 4, 64], F32)
        Hb = self.sb(es, "r_Hb", [128, 4, 64], BF16)
        Xb = [self.sb(es, "r_Xb%d" % i, [128, 4, 64], BF16) for i in range(2)]
        Ub = [self.sb(es, "r_Ub%d" % i, [128, 4, 64], BF16) for i in range(2)]

        def yt(nm, shape=(128, 4, 64), dt=F32):
            return [self.sb(es, "r_%s%d" % (nm, i), list(shape), dt) for i in range(2)]

        Yv = yt("Yv"); Ysq = yt("Ysq"); Yn = yt("Yn"); Bn = yt("Bn"); Yf = yt("Yf", dt=BF16)
        st1 = yt("st1", (128, 4)); st2 = yt("st2", (128, 4)); mean = yt("mean", (128, 4)); var = yt("var", (128, 4))
        YT = [self.sb(es, "r_YT%d" % i, [64, 4, 2, TT], BF16) for i in range(2)]
        ident = self.ident
        ntile = self.ntok // TT
        tiles_per_b = self.seq // TT
        yas_v = self.yas.rearrange("(c hl i) t -> i c hl t", hl=2, i=64)

        for it in range(ntile):
            tok0 = it * TT
            if S.need_reset():
                S.hard_barrier()
            if it % tiles_per_b == 0:
                for m in range(14):
                    self.memset('pool', pcarry[m].v, 0.0)
                self.memset('pool', H.v, 0.0)
                self.memset('pool', Hb.v, 0.0)
            self.norm_from_dram(src, tok0, NS, xbufs, sst, rst, junk, h)
            self.transpose_h(h, NS, hT)
            for m in range(14):
                if m % 2 == 0:
                    ps = self.next_ps()
                o = (m % 2) * TT
                for k in range(8):
                    self.mm(ps[:, o:o + TT], win[:, k, m * 128:(m + 1) * 128], hT[:, k, :], start=(k == 0), stop=(k == 7))
                pw = PW[m % 3]
                self.act(pw[:, 1:1 + TT], ps[:, o:o + TT], AF.Identity, bias=col("bin", m))
                self.cp('pool', pw[:, 0:1], pcarry[m].v)
                self.cp('pool', pcarry[m].v, pw[:, TT:TT + 1])
                tmp = ltmp[m % 2]
                self.ts('pool', tmp.v, pw[:, 0:TT], col("mu", m), None, ALU.mult)
                self.stt('dve', Sx[m].v, pw[:, 1:1 + TT], der[:, m:m + 1], tmp.v, ALU.mult, ALU.add)
            if RSTOP == 1:
                continue
            self.act(LB[0:64, :], Sx[12][0:64, :], AF.Tanh)
            self.cp('pool', LB[64:128, :], Sx[12][64:128, :])
            tmp = ltmp[0]
            self.act(tmp.v, Sx[13].v, AF.Tanh, scale=0.5)
            for hl in range(2):
                self.ts('dve', SGd[:, :, hl * 64:(hl + 1) * 64], tmp.v.re("p (q s) -> p q s", s=CH), 0.5, 0.5, ALU.mult, ALU.add)
            if RSTOP == 21:
                continue
            for c in range(4):
                w = it * 4 + c
                r_, k_, v_ = Sx[c], Sx[4 + c], Sx[8 + c]
                LW = logw[w % NW]; TA = ta[w % NW]; CU = cum[w % NW]; E1 = egm1[w % NW]; EG = eg[w % NW]; EI = eig[w % NW]
                EC = egc[w % NW]; KK = kk[w % NW]; K2 = kk2[w % NW]; RN = rn[w % NW]; KN = kkn[w % NW]; KM = kmod[w % NW]
                BV = bv[w % NW]
                ps = self.next_ps()
                self.mm(ps[:, 0:TT], wlora[:, 0, c * 128:(c + 1) * 128], LB.v, start=True, stop=True)
                self.mm(ps[:, TT:2 * TT], wlora[:, 1, c * 128:(c + 1) * 128], LB.v, start=True, stop=True)
                self.act(LW.v, ps[:, 0:TT], AF.Tanh, scale=0.5, bias=der[:, 14 + c:15 + c])
                self.act(TA.v, ps[:, TT:2 * TT], AF.Tanh, scale=0.5, bias=der[:, 18 + c:19 + c])
                self.ts('dve', LW.v, LW.v, 1.0, -0.30326532985631671, ALU.add, ALU.mult)
                self.S.op('dve', lambda e: e.tensor_tensor_scan(CU.ap, m01.ap, LW.ap, 0.0, ALU.mult, ALU.add),
                          [m01.v, LW.v], [CU.v])
                if RSTOP == 22:
                    continue
                cu3 = CU.v.re("p (q s) -> p q s", s=CH)
                cuC = cu3[:, :, CH - 1:CH]
                self.tt('pool', E1.v, CU.v, LW.v, ALU.subtract)
                self.act(E1.v, E1.v, AF.Exp)
                self.act(EG.v, CU.v, AF.Exp)
                self.act(EI.v, CU.v, AF.Exp, scale=-1.0)
                self.tt('pool', EC.v.re("p (q s) -> p q s", s=CH), cuC.bc([128, NQ, CH]), cu3, ALU.subtract)
                self.act(EC.v, EC.v, AF.Exp)
                self.act(GC[:, c, :], cuC.re("p q o -> p (q o)"), AF.Exp)
                if RSTOP == 23:
                    continue
                self.ts('pool', KK.v, k_.v, col("kk", c), None, ALU.mult)
                self.tt('pool', K2.v, KK.v, KK.v, ALU.mult)
                psn = self.next_ps()
                self.mm(psn[:, 0:TT], bones.v, K2.v, start=True, stop=True)
                self.cp('act', RN.v, psn[:, 0:TT])
                self.rsqrt(RN.v, RN.v)
                self.tt('pool', KN.v, KK.v, RN.v, ALU.mult)
                self.ts('dve', KM.v, TA.v, der[:, 22 + c:23 + c], der[:, 26 + c:27 + c], ALU.mult, ALU.add)
                self.tt('dve', KM.v, KM.v, k_.v, ALU.mult)
                self.stt('dve', BV.v, TA.v, 1.0, KN.v, ALU.add, ALU.mult)
                if RSTOP == 24:
                    continue
                for hl in range(2):
                    P_ = slice(hl * 64, (hl + 1) * 64)

                    def hv(t):
                        return t[P_, :].re("p (q s) -> p q s", s=CH)

                    def ov(n):
                        return EXP[n][c][P_, :, hl * 64:(hl + 1) * 64]

                    self.stt('dve', ov("AT"), hv(KN), -1.0, hv(E1), ALU.mult, ALU.mult)
                    self.stt('dve', ov("BT"), hv(BV), 0.5, hv(EI), ALU.mult, ALU.mult)
                    self.stt('dve', ov("BGT"), hv(BV), 0.5, hv(EC), ALU.mult, ALU.mult)
                    self.tt('pool', ov("KT"), hv(KM), hv(EI), ALU.mult)
                    self.tt('pool', ov("KGT"), hv(KM), hv(EC), ALU.mult)
                    self.tt('pool', ov("RT"), hv(r_), hv(EG), ALU.mult)
                    self.tt('pool', ov("RK"), hv(r_), hv(KM), ALU.mult)
                    self.cp('pool', ov("VB"), hv(v_))
            if RSTOP == 2:
                continue
            for q in range(NQ):
                psA = self.next_ps()
                pbA = psA.v.bitcast(BF16)
                for gi, n in enumerate(("BGT", "KGT")):
                    for c in range(4):
                        self.tr(pbA[:, (gi * 4 + c) * 128:(gi * 4 + c + 1) * 128], EXP[n][c][:, q, :], signal=(gi == 1 and c == 3))
                self.evac(BKG[q].v, pbA.re("p (g c m) -> p g c m", g=2, c=4))
                psB = self.next_ps()
                pbB = psB.v.bitcast(BF16)
                for c in range(4):
                    self.tr(pbB[:, c * 128:(c + 1) * 128], EXP["VB"][c][:, q, :], signal=(c == 3))
                for hl in range(2):
                    self.evac(Vst[q][hl * 64:(hl + 1) * 64, :, :],
                              pbB[hl * 64:(hl + 1) * 64, 0:512].re("p (c m) -> p c m", c=4)[:, :, hl * 64:(hl + 1) * 64])
                combos = [("BT", "AT", mask_su, Nb[q][0]), ("AT", "BT", mask_sl, Lb[q][0]), ("KT", "AT", mask_su, LakT[q]),
                          ("BT", "RT", mask_u, MrbT[q]), ("KT", "RT", mask_u, MrkT[q])]
                for (ln, rn_, mk, dstt) in combos:
                    ps = self.next_ps()
                    for c in range(4):
                        self.mm(ps[:, c * 128:(c + 1) * 128], EXP[ln][c][:, q, :], EXP[rn_][c][:, q, :], start=True, stop=True,
                                signal=(c == 3))
                    self.tt('dve', dstt.v, ps.v.re("p (c m) -> p c m", c=4), mk.v.un(1).bc([128, 4, 128]), ALU.mult)
                self.tt('pool', Pb[q].v, Nb[q][0].v, ident.v.un(1).bc([128, 4, 128]), ALU.add)
            if RSTOP == 3:
                continue
            cur = 0
            for lvl in range(1, 6):
                nxt = 1 - cur
                for q in range(NQ):
                    if lvl < 5:
                        ps = self.next_ps()
                        for c in range(4):
                            self.mm(ps[:, c * 128:(c + 1) * 128], Lb[q][cur][:, c, :], Nb[q][cur][:, c, :], start=True, stop=True,
                                    signal=(c == 3))
                        self.cp('act', Nb[q][nxt].v, ps.v.re("p (c m) -> p c m", c=4))
                    ps = self.next_ps()
                    for c in range(4):
                        self.mm(ps[:, c * 128:(c + 1) * 128], Nb[q][cur][:, c, :], Lb[q][cur][:, c, :], start=True, stop=True,
                                signal=(c == 3))
                    self.cp('act' if lvl == 5 else 'dve', Lb[q][nxt].v, ps.v.re("p (c m) -> p c m", c=4))
                for q in range(NQ):
                    ps = self.next_ps()
                    for c in range(4):
                        self.mm(ps[:, c * 128:(c + 1) * 128], Lb[q][nxt][:, c, :], Pb[q][:, c, :], start=True, stop=True,
                                signal=(c == 3))
                    self.tt('dve', Pb[q].v, Pb[q].v, ps.v.re("p (c m) -> p c m", c=4), ALU.add)
                cur = nxt
            if RSTOP == 4:
                continue
            YTt = YT[it % 2]
            for q in range(NQ):
                w = it * NQ + q
                xb_, ub_ = Xb[w % 2], Ub[w % 2]
                psx = self.next_ps()
                for c in range(4):
                    self.mm(psx[:, c * 64:(c + 1) * 64], EXP["AT"][c][:, q, :], Hb[:, c, :], start=True, stop=False, signal=False)
                    self.mm(psx[:, c * 64:(c + 1) * 64], LakT[q][:, c, :], Vst[q][:, c, :], start=False, stop=True, signal=(c == 3))
                self.cp('act', xb_.v, psx[:, 0:256].re("p (c m) -> p c m", c=4))
                psu = self.next_ps()
                for c in range(4):
                    self.mm(psu[:, c * 64:(c + 1) * 64], Pb[q][:, c, :], xb_[:, c, :], start=True, stop=True, signal=(c == 3))
                self.cp('dve', ub_.v, psu[:, 0:256].re("p (c m) -> p c m", c=4))
                psy = self.next_ps()
                for c in range(4):
                    o = psy[:, c * 64:(c + 1) * 64]
                    self.mm(o, EXP["RT"][c][:, q, :], Hb[:, c, :], start=True, stop=False, signal=False)
                    self.mm(o, MrbT[q][:, c, :], ub_[:, c, :], start=False, stop=False, signal=False)
                    self.mm(o, MrkT[q][:, c, :], Vst[q][:, c, :], start=False, stop=True, signal=False)
                for c in range(4):
                    self.mm(psy[:, 256 + c:257 + c], EXP["RK"][c][:, q, :], rkb[:, c:c + 1], start=True, stop=True, signal=(c == 3))
                psh = self.next_ps()
                for c in range(4):
                    o = psh[:, c * 64:(c + 1) * 64]
                    self.mm(o, BKG[q][:, 0, c, :], ub_[:, c, :], start=True, stop=False, signal=False)
                    self.mm(o, BKG[q][:, 1, c, :], Vst[q][:, c, :], start=False, stop=True, signal=(c == 3))
                self.tt('dve', H.v, H.v, GC[:, :, q:q + 1].bc([128, 4, 64]), ALU.mult)
                self.tt('dve', H.v, H.v, psh[:, 0:256].re("p (c m) -> p c m", c=4), ALU.add)
                self.cp('act', Hb.v, H.v)
                psg = self.next_ps()
                self.mm(psg.v, SGd[:, q, :], gup.v, start=True, stop=True)
                Y = Yv[w % 2]; Y2 = Ysq[w % 2]; YN = Yn[w % 2]; BN = Bn[w % 2]; YF = Yf[w % 2]
                s1 = st1[w % 2]; s2 = st2[w % 2]; mn = mean[w % 2]; vr = var[w % 2]
                py3 = psy[:, 0:256].re("p (c m) -> p c m", c=4)
                self.cp('act', Y.v, py3)
                self.act(Y2.v, py3, AF.Square)
                self.S.op('dve', lambda e: e.reduce_sum(out=s1.ap, in_=Y.ap, axis=AX.X), [Y.v], [s1.v])
                self.S.op('dve', lambda e: e.reduce_sum(out=s2.ap, in_=Y2.ap, axis=AX.X), [Y2.v], [s2.v])
                self.ts('dve', mn.v, s1.v, 1.0 / 64.0, None, ALU.mult)
                self.tt('dve', vr.v, mn.v, mn.v, ALU.mult)
                self.stt('dve', vr.v, s2.v, 1.0 / 64.0, vr.v, ALU.mult, ALU.subtract)
                self.ts('dve', vr.v, vr.v, LNX_EPS, None, ALU.add)
                self.rsqrt(vr.v, vr.v)
                self.tt('pool', YN.v, Y.v, mn.v.un(2).bc([128, 4, 64]), ALU.subtract)
                self.tt('pool', YN.v, YN.v, vr.v.un(2).bc([128, 4, 64]), ALU.mult)
                self.tt('pool', YN.v, YN.v, lnxg.v, ALU.mult)
                self.tt('pool', YN.v, YN.v, lnxb.v, ALU.add)
                self.tt('dve', BN.v, Vst[q].v, psy[:, 256:260].un(2).bc([128, 4, 64]), ALU.mult)
                self.tt('pool', YN.v, YN.v, BN.v, ALU.add)
                for hl in range(2):
                    P_ = slice(hl * 64, (hl + 1) * 64)
                    gv = psg[P_, :].re("p (c h m) -> p c h m", c=4, h=2)[:, :, hl, :]
                    self.tt('dve', YF[P_, :, :], YN[P_, :, :], gv, ALU.mult)
                pst = self.next_ps()
                pbt = pst.v.bitcast(BF16)
                for c in range(4):
                    self.tr(pbt[0:64, c * 128:(c + 1) * 128], YF[:, c, :], signal=(c == 3))
                self.evac(YTt[:, :, :, q * CH:(q + 1) * CH], pbt[0:64, 0:512].re("p (c h t) -> p c h t", c=4, h=2))
            self.dma('sp', yas_v[:, :, :, tok0:tok0 + TT], YTt.v)

    def phaseM(self, es, src, dst, final=False):
        S = self.S
        inp = self.inp
        TT = 512
        NS = 4
        self._stg_i = 0
        self._ev_i = 0
        use_a = 'R' in self.phases
        cols, ci = self.load_cols(es, "m_cols", [("g", inp["norm_mix_g"]), ("bin", inp["b_in"][2816:4864])])
        hb = self.sb(es, "m_hb", [128, 16], F32)
        self.ts('dve', hb.v, cols[:, ci["bin"]:ci["bin"] + 16], 0.5, None, ALU.mult)
        win = self.sb(es, "m_win", [128, 8, 2048], BF16)
        wa = self.sb(es, "m_wa", [128, 4, D], BF16)
        wb = self.sb(es, "m_wb", [128, 4, D], BF16)
        wmo = self.sb(es, "m_wmo", [128, 8, D], BF16)
        with ExitStack() as tes:
            stg = [self.sb(tes, "m_stg%d" % i, [128, 2048], F32) for i in range(3)]
            for k in range(8):
                self.load_weight(stg, win[:, k, :], inp["w_in"][k * 128:(k + 1) * 128, 2816:4864], 2048,
                                 scale=cols[:, ci["g"] + k: ci["g"] + k + 1])
                self.load_weight(stg, wmo[:, k, :], inp["w_mix_out"][k * 128:(k + 1) * 128, :], D)
            for c in range(4):
                self.load_weight(stg, wa[:, c, :], inp["w_branch_a"][c * 128:(c + 1) * 128, :], D, scale=0.5)
                self.load_weight(stg, wb[:, c, :], inp["w_branch_b"][c * 128:(c + 1) * 128, :], D, scale=0.25)
            S.barrier()
        xbufs, sst, rst, junk = self.norm_bufs(es, "m", NS)
        h = self.sb(es, "m_h", [128, NS, D], BF16)
        hT = self.sb(es, "m_hT", [128, 8, TT], BF16)
        TG = [self.sb(es, "m_tg%d" % m, [128, TT], BF16) for m in range(16)]
        YA = [self.sb(es, "m_ya%d" % i, [128, 4, TT], BF16) for i in range(2)]
        YB = [self.sb(es, "m_yb%d" % i, [128, 4, TT], BF16) for i in range(2)]
        MA = [self.sb(es, "m_ma%d" % i, [128, TT], F32) for i in range(3)]
        MG = [self.sb(es, "m_mg%d" % m, [128, TT], BF16) for m in range(8)]
        self._mb = [self.sb(es, "m_mb%d" % i, [128, TT], F32) for i in range(2)]
        ntile = self.ntok // TT

        def load_y(i):
            tok0 = i * TT
            if use_a:
                self.dma('sp', YA[i % 2].v, self.yas[:, tok0:tok0 + TT].rearrange("(c p) t -> p c t", p=128))
            self.dma('sp', YB[i % 2].v, self.ybs[:, tok0:tok0 + TT].rearrange("(c p) t -> p c t", p=128))

        load_y(0)
        for i in range(ntile):
            if S.need_reset():
                S.hard_barrier()
            tok0 = i * TT
            if i + 1 < ntile:
                load_y(i + 1)
            ya, yb = YA[i % 2], YB[i % 2]
            self.norm_from_dram(src, tok0, NS, xbufs, sst, rst, junk, h)
            self.transpose_h(h, NS, hT)
            for m in range(16):
                ps = self.next_ps()
                for k in range(8):
                    self.mm(ps.v, win[:, k, m * 128:(m + 1) * 128], hT[:, k, :], start=(k == 0), stop=(k == 7))
                self.act(TG[m].v, ps.v, AF.Tanh, scale=0.5, bias=hb[:, m:m + 1])
            for m in range(8):
                ma = MA[m % 3]
                if use_a:
                    ps = self.next_ps()
                    for c in range(4):
                        self.mm(ps.v, wa[:, c, m * 128:(m + 1) * 128], ya[:, c, :], start=(c == 0), stop=(c == 3))
                    self.stt('dve', ma.v, TG[m].v, 1.0, ps.v, ALU.add, ALU.mult)
                ps2 = self.next_ps()
                for c in range(4):
                    self.mm(ps2.v, wb[:, c, m * 128:(m + 1) * 128], yb[:, c, :], start=(c == 0), stop=(c == 3))
                if use_a:
                    mb = MA[(m + 1) % 3] if False else self._mb[m % 2]
                    self.stt('dve', mb.v, TG[8 + m].v, 1.0, ps2.v, ALU.add, ALU.mult)
                    self.tt('pool', MG[m].v, mb.v, ma.v, ALU.add)
                else:
                    self.stt('dve', MG[m].v, TG[8 + m].v, 1.0, ps2.v, ALU.add, ALU.mult)
            for s in range(NS):
                pss = [self.next_ps(), self.next_ps()]
                for half in range(2):
                    for m in range(8):
                        self.mm(pss[half].v, MG[m][:, s * 128:(s + 1) * 128], wmo[:, m, half * 512:(half + 1) * 512],
                                start=(m == 0), stop=(m == 7))
                xb = xbufs[self._xb_i % 2]
                self._xb_i += 1
                self.dma('sp', xb.v, src[tok0 + s * 128: tok0 + (s + 1) * 128, :])
                for half in range(2):
                    self.tt('dve', xb[:, half * 512:(half + 1) * 512], xb[:, half * 512:(half + 1) * 512], pss[half].v, ALU.add)
                self.dma('sp', dst[tok0 + s * 128: tok0 + (s + 1) * 128, :], xb.v)

    def evac(self, out, in_, i=None):
        if i is None:
            i = self._ev_i
            self._ev_i += 1
        return self.cp(('act', 'dve')[i % 2], out, in_)

    def phaseB(self, es, src, dst, final=False):
        S = self.S
        inp = self.inp
        TT = 512
        NS = 4
        self._stg_i = 0
        self._ev_i = 0
        cols, ci = self.load_cols(es, "b_cols", [("gx", inp["norm_x_g"]), ("gm", inp["norm_mem_g"])])
        gq = self.sb(es, "b_gq", [128, 8], F32)
        self.ts('dve', gq.v, cols[:, ci["gx"]:ci["gx"] + 8], 1.0 / 16.0, None, ALU.mult)
        wq = self.sb(es, "b_wq", [128, 8, D], BF16)
        wo = self.sb(es, "b_wo", [128, 8, D], BF16)
        kT = [self.sb(es, "b_kT%d" % b, [128, 8, NMEM], BF16) for b in range(self.nb)]
        Vt = [self.sb(es, "b_V%d" % b, [128, 2, D], BF16) for b in range(self.nb)]
        xts = [self.sb(es, "b_xt%d" % i, [128, NS, D], F32) for i in range(2)]
        h = self.sb(es, "b_h", [128, NS, D], BF16)
        hT = self.sb(es, "b_hT", [128, 8, TT], BF16)
        junk = self.sb(es, "b_junk", [128, D], BF16)
        ss = self.sb(es, "b_ss", [128, 4], F32)
        rstd = self.sb(es, "b_rstd", [128, 4], F32)
        with ExitStack() as tes:
            stg = [self.sb(tes, "b_stg%d" % i, [128, 2048], F32) for i in range(3)]
            wkv = self.sb(tes, "b_wkv", [128, 8, 2 * D], BF16)
            for k in range(8):
                self.load_weight(stg, wkv[:, k, :], inp["w_ckv"][k * 128:(k + 1) * 128, :], 2 * D,
                                 scale=cols[:, ci["gm"] + k: ci["gm"] + k + 1])
            for k in range(8):
                self.load_weight(stg, wq[:, k, :], inp["w_cq"][k * 128:(k + 1) * 128, :], D, scale=gq[:, k:k + 1])
                self.load_weight(stg, wo[:, k, :], inp["w_co"][k * 128:(k + 1) * 128, :], D)
            for b in range(self.nb):
                mt = xts[b % 2]
                self.dma('sp', mt[:, 0:2, :], inp["mem"][b * NMEM:(b + 1) * NMEM, :].rearrange("(s p) d -> p s d", p=128))
                self.rms_h(mt, 2, ss, rstd, junk, h)
                self.transpose_h(h, 2, hT[:, :, 0:NMEM])
                for m in range(8):
                    if m % 2 == 0:
                        ps = self.next_ps()
                    o = (m % 2) * NMEM
                    for k in range(8):
                        self.mm(ps[:, o:o + NMEM], wkv[:, k, m * 128:(m + 1) * 128], hT[:, k, 0:NMEM], start=(k == 0), stop=(k == 7))
                    if m % 2 == 1:
                        self.evac(kT[b][:, m - 1:m + 1, :], ps.v.re("p (a t) -> p a t", a=2))
                for mc in range(2):
                    for half in range(2):
                        ps = self.next_ps()
                        for k in range(8):
                            self.mm(ps.v, hT[:, k, mc * 128:(mc + 1) * 128], wkv[:, k, D + half * 512: D + (half + 1) * 512],
                                    start=(k == 0), stop=(k == 7))
                        self.evac(Vt[b][:, mc, half * 512:(half + 1) * 512], ps.v)
            S.barrier()
        qT = self.sb(es, "b_qT", [128, 8, TT], BF16)
        oT = self.sb(es, "b_oT", [128, 8, TT], BF16)
        prT = self.sb(es, "b_prT", [128, 2, 4, TT], BF16)
        prs = [self.sb(es, "b_pr%d" % i, [128, 4, NMEM], BF16) for i in range(2)]
        prn = [self.sb(es, "b_prn%d" % i, [128, 4, NMEM], BF16) for i in range(2)]
        mx = [self.sb(es, "b_mx%d" % i, [128, 4], F32) for i in range(2)]
        sm = [self.sb(es, "b_sm%d" % i, [128, 4], F32) for i in range(2)]
        ntile = self.ntok // TT
        tiles_per_b = self.seq // TT

        def load(i):
            self.dma('sp', xts[i % 2].v, src[i * TT:(i + 1) * TT, :].rearrange("(s p) d -> p s d", p=128))

        load(0)
        for i in range(ntile):
            if S.need_reset():
                S.hard_barrier()
            b = i // tiles_per_b
            xt = xts[i % 2]
            if i + 1 < ntile:
                load(i + 1)
            self.rms_h(xt, NS, ss, rstd, junk, h)
            self.transpose_h(h, NS, hT)
            for m in range(8):
                ps = self.next_ps()
                for k in range(8):
                    self.mm(ps.v, wq[:, k, m * 128:(m + 1) * 128], hT[:, k, :], start=(k == 0), stop=(k == 7))
                self.evac(qT[:, m, :], ps.v)
            for s in range(NS):
                pr = prs[s % 2]; pn = prn[s % 2]; mxs = mx[s % 2]; sms = sm[s % 2]
                banks = [self.next_ps(), self.next_ps()]
                for hh in range(4):
                    ps = banks[hh // 2]
                    o = (hh % 2) * NMEM
                    for c in range(2):
                        self.mm(ps[:, o:o + NMEM], qT[:, 2 * hh + c, s * 128:(s + 1) * 128], kT[b][:, 2 * hh + c, :],
                                start=(c == 0), stop=(c == 1))
                for g in range(2):
                    self.S.op('dve', lambda e: e.reduce_max(out=mxs.ap[:, 2 * g:2 * g + 2],
                                                            in_=banks[g].ap.rearrange("p (a t) -> p a t", a=2), axis=AX.X),
                              [banks[g].v], [mxs.v])
                self.ts('dve', mxs.v, mxs.v, -1.0, None, ALU.mult)
                for hh in range(4):
                    ps = banks[hh // 2]
                    o = (hh % 2) * NMEM
                    self.act(pr[:, hh, :], ps[:, o:o + NMEM], AF.Exp, bias=mxs[:, hh:hh + 1], accum=sms[:, hh:hh + 1])
                self.S.op('dve', lambda e: e.reciprocal(out=sms.ap, in_=sms.ap), [sms.v], [sms.v])
                self.tt('dve', pn.v, pr.v, sms.v.un(2).bc([128, 4, NMEM]), ALU.mult)
                ps = self.next_ps()
                pb = ps.v.bitcast(BF16)
                for hh in range(4):
                    for mc in range(2):
                        self.tr(pb[:, (hh * 2 + mc) * 128:(hh * 2 + mc + 1) * 128], pn[:, hh, mc * 128:(mc + 1) * 128],
                                signal=(hh == 3 and mc == 1))
                self.evac(prT[:, :, :, s * 128:(s + 1) * 128], pb.re("p (h m t) -> p m h t", h=4, m=2))
            for hh in range(4):
                for c in range(2):
                    ps = self.next_ps()
                    for mc in range(2):
                        self.mm(ps.v, Vt[b][:, mc, hh * 256 + c * 128: hh * 256 + (c + 1) * 128], prT[:, mc, hh, :],
                                start=(mc == 0), stop=(mc == 1))
                    self.evac(oT[:, 2 * hh + c, :], ps.v)
            for s in range(NS):
                pss = [self.next_ps(), self.next_ps()]
                for half in range(2):
                    for k in range(8):
                        self.mm(pss[half].v, oT[:, k, s * 128:(s + 1) * 128], wo[:, k, half * 512:(half + 1) * 512],
                                start=(k == 0), stop=(k == 7))
                for half in range(2):
                    self.tt('dve', xt[:, s, half * 512:(half + 1) * 512], xt[:, s, half * 512:(half + 1) * 512], pss[half].v, ALU.add)
                self.dma('sp', dst[i * TT + s * 128: i * TT + (s + 1) * 128, :], xt[:, s, :])

    def phaseC(self, es, src, dst, final=True):
        nc, S = self.nc, self.S
        TT = 256
        NS = TT // 128
        NJ = DFF // 128
        inp = self.inp
        self._stg_i = 0
        cols, ci = self.load_cols(es, "c_cols", [("g", inp["norm_ffn_g"]), ("cw0", inp["ffn_conv_w"][0]),
                                                 ("cw1", inp["ffn_conv_w"][1]), ("cw2", inp["ffn_conv_w"][2]),
                                                 ("cb", inp["ffn_conv_b"])])
        wi = self.sb(es, "c_wi", [128, 8, 2 * DFF], BF16)
        wo = self.sb(es, "c_wo", [128, NJ, D], BF16)
        gfin = self.sb(es, "c_gfin", [128, D], F32)
        self.dma('sp', gfin.v, inp["norm_final_g"].partition_broadcast(128))
        with ExitStack() as tes:
            stg = [self.sb(tes, "c_stg%d" % i, [128, 2048], F32) for i in range(3)]
            for k in range(8):
                self.load_weight(stg, wi[:, k, :], inp["w_ffn_in"][k * 128:(k + 1) * 128, :], 2 * DFF,
                                 scale=cols[:, ci["g"] + k: ci["g"] + k + 1])
            for j in range(NJ):
                self.load_weight(stg, wo[:, j, :], inp["w_ffn_out"][j * 128:(j + 1) * 128, :], D)
            S.barrier()
        xts = [self.sb(es, "c_xt%d" % i, [128, NS, D], F32) for i in range(2)]
        h = self.sb(es, "c_h", [128, NS, D], BF16)
        hT = self.sb(es, "c_hT", [128, 8, TT], BF16)
        actT = self.sb(es, "c_actT", [128, NJ, TT], BF16)
        junk = self.sb(es, "c_junk", [128, D], BF16)
        ss = self.sb(es, "c_ss", [128, 4], F32)
        rstd = self.sb(es, "c_rstd", [128, 4], F32)
        halo = self.sb(es, "c_halo", [128, NJ, 2], F32)
        NW = 3
        uw = [self.sb(es, "c_uw%d" % i, [128, 2 + TT], F32) for i in range(NW)]
        acc = [self.sb(es, "c_acc%d" % i, [128, TT], F32) for i in range(NW)]
        tmp = [self.sb(es, "c_tmp%d" % i, [128, TT], F32) for i in range(NW)]
        th = [self.sb(es, "c_th%d" % i, [128, TT], F32) for i in range(NW)]
        ntile = self.ntok // TT
        tiles_per_b = self.seq // TT

        def load(i):
            xt = xts[i % 2]
            self.dma('sp', xt.v, src[i * TT:(i + 1) * TT, :].rearrange("(s p) d -> p s d", p=128))

        load(0)
        for i in range(ntile):
            if S.need_reset():
                S.hard_barrier()
            xt = xts[i % 2]
            if i + 1 < ntile:
                load(i + 1)
            if i % tiles_per_b == 0:
                self.memset('pool', halo.v, 0.0)
            self.rms_h(xt, NS, ss, rstd, junk, h)
            self.transpose_h(h, NS, hT)
            for j in range(NJ):
                ps = self.next_ps()
                for half in range(2):
                    c0 = half * DFF + j * 128
                    for k in range(8):
                        self.mm(ps[:, half * TT:(half + 1) * TT], wi[:, k, c0:c0 + 128], hT[:, k, :], start=(k == 0), stop=(k == 7))
                u = uw[j % NW]; a = acc[j % NW]; tm = tmp[j % NW]; tg = th[j % NW]
                self.cp('act', u[:, 2:2 + TT], ps[:, 0:TT])
                self.cp('pool', u[:, 0:2], halo[:, j, :])
                w0 = cols[:, ci["cw0"] + j: ci["cw0"] + j + 1]
                w1 = cols[:, ci["cw1"] + j: ci["cw1"] + j + 1]
                w2 = cols[:, ci["cw2"] + j: ci["cw2"] + j + 1]
                cb = cols[:, ci["cb"] + j: ci["cb"] + j + 1]
                self.ts('dve', a.v, u[:, 0:TT], w0, cb, ALU.mult, ALU.add)
                self.stt('dve', a.v, u[:, 1:1 + TT], w1, a.v, ALU.mult, ALU.add)
                self.stt('dve', a.v, u[:, 2:2 + TT], w2, a.v, ALU.mult, ALU.add)
                self.cp('pool', halo[:, j, :], u[:, TT:TT + 2])
                self.act(tg.v, a.v, AF.Gelu_apprx_tanh)
                self.tt('dve', actT[:, j, :], tg.v, ps[:, TT:2 * TT], ALU.mult)
            for s in range(NS):
                pss = [self.next_ps(), self.next_ps()]
                for half in range(2):
                    for j in range(NJ):
                        self.mm(pss[half].v, actT[:, j, s * 128:(s + 1) * 128], wo[:, j, half * 512:(half + 1) * 512],
                                start=(j == 0), stop=(j == NJ - 1))
                for half in range(2):
                    self.tt('dve', xt[:, s, half * 512:(half + 1) * 512], xt[:, s, half * 512:(half + 1) * 512], pss[half].v, ALU.add)
                if final:
                    self.act(junk.v, xt[:, s, :], AF.Square, accum=ss[:, 2 + s:3 + s])
                    self.ts('dve', rstd[:, 2 + s:3 + s], ss[:, 2 + s:3 + s], 1.0 / D, NORM_EPS, ALU.mult, ALU.add)
                    self.rsqrt(rstd[:, 2 + s:3 + s], rstd[:, 2 + s:3 + s])
                    self.stt('dve', xt[:, s, :], xt[:, s, :], rstd[:, 2 + s:3 + s], gfin.v, ALU.mult, ALU.mult)
                self.dma('sp', dst[i * TT + s * 128: i * TT + (s + 1) * 128, :], xt[:, s, :])


_PROG_CACHE = {}


def get_prog(nb=4, seq=2048, phases="LRMBC"):
    key = (nb, seq, phases)
    if key not in _PROG_CACHE:
        _PROG_CACHE[key] = Prog(nb, seq, phases)
    return _PROG_CACHE[key]


_W_NAMES = ["norm_mix_g", "w_in", "b_in", "mu_shift", "w0", "w_lora_up", "a0", "a_lora_up", "g_lora_up", "k_k", "k_a",
            "r_k", "lnx_g", "lnx_b", "w_branch_a", "conv_b_w", "conv_b_b", "w_rg_a", "b_rg_a", "w_rg_x", "b_rg_x",
            "lru_lambda", "w_branch_b", "w_mix_out", "norm_x_g", "norm_mem_g", "w_cq", "w_ckv", "w_co", "norm_ffn_g",
            "w_ffn_in", "ffn_conv_w", "ffn_conv_b", "w_ffn_out", "norm_final_g"]


def run(inputs, ncores=8, nb=4, seq=2048, phases="LRMBC"):
    prog = get_prog(nb, seq, phases)
    shapes = {k: tuple(v.shape) for k, v in prog.inp.items()}
    shared = {}
    for k in _W_NAMES:
        a = np.ascontiguousarray(np.asarray(inputs[k], dtype=np.float32))
        shared[k] = a.reshape(shapes[k])
    x = np.asarray(inputs["x"], dtype=np.float32)
    mem = np.asarray(inputs["mem"], dtype=np.float32)
    in_maps = []
    for c in range(ncores):
        m = dict(shared)
        m["x"] = np.ascontiguousarray(x[c * nb:(c + 1) * nb, :seq]).reshape(nb * seq, D)
        m["mem"] = np.ascontiguousarray(mem[c * nb:(c + 1) * nb]).reshape(nb * NMEM, D)
        in_maps.append(m)
    res = run_bass_kernel_spmd(prog.nc, in_maps, core_ids=list(range(ncores)))
    outs = [np.asarray(r["out"]).reshape(nb, seq, D) for r in res.results]
    return np.concatenate(outs, axis=0)


def kernel(**inputs):
    return run(inputs).astype(np.float32)
```

```python
import numpy as np
from contextlib import ExitStack
import concourse.bass as bass
import concourse.mybir as mybir
from concourse.bass_utils import run_bass_kernel_spmd

F32 = mybir.dt.float32
BF16 = mybir.dt.bfloat16
AF = mybir.ActivationFunctionType
ALU = mybir.AluOpType
AX = mybir.AxisListType

D = 1024
NMEM = 256
AW = 512
RWKV_COLS = 1792
P_IN = 4864
DFF = 2816
NORM_EPS = 1e-6
LNX_EPS = 64e-5
SAME_ENGINE_SYNC = True
import os as _os
RSTOP = int(_os.environ.get("RSTOP", "0"))
LVAR = int(_os.environ.get("LVAR", "2"))


_ALL_TILES = []


class T:
    def __init__(self, ap, name=""):
        self.ap = ap if isinstance(ap, bass.AP) else ap[:]
        self.w = None
        self.r = []
        self.name = name
        self.dsem = None
        _ALL_TILES.append(self)

    def __getitem__(self, k):
        return V([self], self.ap[k])

    @property
    def v(self):
        return V([self], self.ap)


class V:
    def __init__(self, ts, ap):
        self.ts = ts
        self.ap = ap

    def __getitem__(self, k):
        return V(self.ts, self.ap[k])

    def re(self, pat, **kw):
        return V(self.ts, self.ap.rearrange(pat, **kw))

    def bc(self, shape):
        return V(self.ts, self.ap.to_broadcast(shape))

    def un(self, axis):
        return V(self.ts, self.ap.unsqueeze(axis))

    def bitcast(self, dt):
        return V(self.ts, self.ap.bitcast(dt))


def _ap(x):
    return x.ap if isinstance(x, V) else x


class Sync:
    def __init__(self, nc, es, n_dma_sems=64):
        self.nc = nc
        self.es = es
        self.engs = {'pe': nc.tensor, 'act': nc.scalar, 'dve': nc.vector, 'pool': nc.gpsimd, 'sp': nc.sync}
        self.sem = {}
        self.cnt = {}
        self.seen = {}
        for e in self.engs:
            self.sem[e] = es.enter_context(nc.semaphore("c_" + e))
            self.cnt[e] = 0
            self.seen[e] = {}
        self.dpool = [{'sem': es.enter_context(nc.semaphore("d%d" % i)), 'cnt': 0} for i in range(n_dma_sems)]
        self.dfree = list(range(n_dma_sems))
        self.n_inst = 0
        self.bar1 = es.enter_context(nc.semaphore("bar1"))
        self.bar2 = es.enter_context(nc.semaphore("bar2"))
        self.bar_k = 0

    def _wait(self, e, tok):
        if tok is None:
            return
        sem, val, src = tok
        if src == e and (e == 'pe' or not SAME_ENGINE_SYNC):
            return
        key = sem.name
        if self.seen[e].get(key, 0) >= val:
            return
        self.seen[e][key] = val
        self.engs[e].wait_ge(sem, val)

    def deps(self, e, reads, writes):
        for v in reads:
            if not isinstance(v, V):
                continue
            for t in v.ts:
                self._wait(e, t.w)
        for v in writes:
            for t in v.ts:
                self._wait(e, t.w)
                for tok in t.r:
                    self._wait(e, tok)

    def done(self, tok, reads, writes):
        for v in reads:
            if not isinstance(v, V):
                continue
            for t in v.ts:
                t.r.append(tok)
                if len(t.r) > 24:
                    t.r = t.r[-24:] if False else t.r
        for v in writes:
            for t in v.ts:
                t.w = tok
                t.r = []

    def op(self, e, fn, reads, writes, signal=True):
        self.deps(e, reads, writes)
        inst = fn(self.engs[e])
        self.n_inst += 1
        if signal:
            self.cnt[e] += 1
            inst.then_inc(self.sem[e], 1)
            tok = (self.sem[e], self.cnt[e], e)
        else:
            tok = (self.sem[e], self.cnt[e] + 1, e)
        self.done(tok, reads, writes)
        return inst

    def _dsem(self, t):
        if t.dsem is None:
            if not self.dfree:
                raise RuntimeError("out of dma semaphores")
            t.dsem = self.dpool[self.dfree.pop(0)]
        return t.dsem

    def dma(self, q, out, in_, **kw):
        reads = [in_] if isinstance(in_, V) else []
        writes = [out] if isinstance(out, V) else []
        self.deps(q, reads, writes)
        sbv = out if isinstance(out, V) else in_
        ds = self._dsem(sbv.ts[0])
        inst = self.engs[q].dma_start(out=_ap(out), in_=_ap(in_), **kw)
        self.n_inst += 1
        ds['cnt'] += 16
        inst.then_inc(ds['sem'], 16)
        tok = (ds['sem'], ds['cnt'], 'dma')
        self.done(tok, reads, writes)
        return tok

    def barrier(self):
        toks = [(self.sem[f], self.cnt[f], f) for f in self.engs if self.cnt[f] > 0]
        toks += [(d['sem'], d['cnt'], 'dma') for d in self.dpool if d['cnt'] > 0]
        for e in self.engs:
            for tok in toks:
                if tok[2] == e:
                    continue
                self._wait(e, tok)

    def release_dma_sems(self):
        self.dfree = list(range(len(self.dpool)))
        for t in _ALL_TILES:
            t.dsem = None

    def need_reset(self, limit=2600):
        return max(self.cnt.values()) > limit or max(d['cnt'] for d in self.dpool) > limit

    def hard_barrier(self):
        self.barrier()
        self.bar_k += 1
        k = self.bar_k
        for e in self.engs:
            self.engs[e].sem_inc(self.bar1, 1)
        sp = self.engs['sp']
        sp.wait_ge(self.bar1, len(self.engs) * k)
        for e in self.engs:
            if self.cnt[e] > 0:
                sp.sem_clear(self.sem[e])
        for d in self.dpool:
            if d['cnt'] > 0:
                sp.sem_clear(d['sem'])
        sp.sem_inc(self.bar2, 1)
        for e in self.engs:
            if e != 'sp':
                self.engs[e].wait_ge(self.bar2, k)
        for e in self.engs:
            self.cnt[e] = 0
            self.seen[e] = {}
        for d in self.dpool:
            d['cnt'] = 0
        for t in _ALL_TILES:
            t.w = None
            t.r = []


class Prog:
    def __init__(self, nb=4, seq=2048, phases="LRMBC", dbg=False):
        self.nb, self.seq, self.phases, self.dbg = nb, seq, phases, dbg
        self.ntok = nb * seq
        nc = self.nc = bass.Bass("TRN2", target_bir_lowering=False)
        self.inp = {}
        del _ALL_TILES[:]

        def din(name, shape):
            self.inp[name] = nc.dram_tensor(name, list(shape), F32, kind="ExternalInput").ap()
            return self.inp[name]

        ntok = self.ntok
        din("x", [ntok, D])
        din("mem", [nb * NMEM, D])
        din("norm_mix_g", [D]); din("w_in", [D, P_IN]); din("b_in", [P_IN]); din("mu_shift", [RWKV_COLS])
        din("w0", [AW]); din("w_lora_up", [64, AW]); din("a0", [AW]); din("a_lora_up", [64, AW])
        din("g_lora_up", [128, AW]); din("k_k", [AW]); din("k_a", [AW]); din("r_k", [AW])
        din("lnx_g", [AW]); din("lnx_b", [AW]); din("w_branch_a", [AW, D])
        din("conv_b_w", [4, AW]); din("conv_b_b", [AW]); din("w_rg_a", [8, 64, 64]); din("b_rg_a", [AW])
        din("w_rg_x", [8, 64, 64]); din("b_rg_x", [AW]); din("lru_lambda", [AW]); din("w_branch_b", [AW, D])
        din("w_mix_out", [D, D]); din("norm_x_g", [D]); din("norm_mem_g", [D]); din("w_cq", [D, D])
        din("w_ckv", [D, 2 * D]); din("w_co", [D, D]); din("norm_ffn_g", [D]); din("w_ffn_in", [D, 2 * DFF])
        din("ffn_conv_w", [3, DFF]); din("ffn_conv_b", [DFF]); din("w_ffn_out", [DFF, D]); din("norm_final_g", [D])
        self.out = nc.dram_tensor("out", [ntok, D], F32, kind="ExternalOutput").ap()
        self.x1 = nc.dram_tensor("x1s", [ntok, D], F32).ap()
        self.x2 = nc.dram_tensor("x2s", [ntok, D], F32).ap()
        self.dbg_out = {}

        with ExitStack() as es:
            self.es = es
            self.S = Sync(nc, es)
            self.ps = [T(es.enter_context(nc.psum_tensor("ps%d" % i, [128, 512], F32)), "ps%d" % i) for i in range(8)]
            self.ps_i = 0
            self.ident = self.sb(es, "ident", [128, 128], BF16)
            self.identf = self.sb(es, "identf", [128, 128], F32)
            for idt in (self.ident, self.identf):
                self.S.op('pool', lambda e: e.memset(idt.ap[:], 0.0), [], [idt.v])
                self.S.op('pool', lambda e: e.affine_select(idt.ap[:], idt.ap[:], pattern=[[-1, 128]],
                                                           compare_op=ALU.not_equal, fill=1.0, base=0,
                                                           channel_multiplier=1), [idt.v], [idt.v])
            self.mhalf = self.sb(es, "mhalf", [128, 512], F32)
            self.memset('pool', self.mhalf.v, -0.5)
            self.phalf = self.sb(es, "phalf", [128, 512], F32)
            self.memset('pool', self.phalf.v, 0.5)
            self.yas = nc.dram_tensor("yas", [AW, ntok], BF16).ap()
            self.ybs = nc.dram_tensor("ybs", [AW, ntok], BF16).ap()
            src = {'L': self.inp["x"], 'R': self.inp["x"], 'M': self.inp["x"], 'B': self.x1, 'C': self.x2}
            dst = {'L': None, 'R': None, 'M': self.x1, 'B': self.x2, 'C': self.out}
            order = [p for p in "LRMBC" if p in phases]
            chain = [p for p in order if p in "MBC"]
            for i, p in enumerate(order):
                s_ap, d_ap = src[p], dst[p]
                if p in chain:
                    if chain.index(p) == 0:
                        s_ap = self.inp["x"]
                    if chain.index(p) == len(chain) - 1:
                        d_ap = self.out
                with ExitStack() as pes:
                    getattr(self, "phase" + p)(pes, s_ap, d_ap, final=(p == 'C'))
                    self.S.hard_barrier()
                self.S.release_dma_sems()
            self.S.barrier()

    def sb(self, es, name, shape, dt):
        return T(es.enter_context(self.nc.sbuf_tensor(name, list(shape), dt)), name)

    def next_ps(self):
        t = self.ps[self.ps_i]
        self.ps_i = (self.ps_i + 1) % 8
        return t

    def act(self, out, in_, func, bias=0.0, scale=1.0, accum=None):
        rd = [in_] + [a for a in (bias, scale) if isinstance(a, V)]
        wr = [out] + ([accum] if accum is not None else [])
        kw = {}
        if accum is not None:
            kw['accum_out'] = accum.ap
        return self.S.op('act', lambda e: e.activation(out=out.ap, in_=in_.ap, func=func, bias=_ap(bias),
                                                       scale=_ap(scale), **kw), rd, wr)

    def tt(self, eng, out, in0, in1, op):
        return self.S.op(eng, lambda e: e.tensor_tensor(out=out.ap, in0=in0.ap, in1=in1.ap, op=op), [in0, in1], [out])

    def ts(self, eng, out, in0, s1, s2, op0, op1=None):
        rd = [in0] + [a for a in (s1, s2) if isinstance(a, V)]
        if op1 is None:
            return self.S.op(eng, lambda e: e.tensor_scalar(out=out.ap, in0=in0.ap, scalar1=_ap(s1), scalar2=None,
                                                            op0=op0), rd, [out])
        return self.S.op(eng, lambda e: e.tensor_scalar(out=out.ap, in0=in0.ap, scalar1=_ap(s1), scalar2=_ap(s2),
                                                        op0=op0, op1=op1), rd, [out])

    def stt(self, eng, out, in0, scalar, in1, op0, op1):
        rd = [in0, in1] + ([scalar] if isinstance(scalar, V) else [])
        return self.S.op(eng, lambda e: e.scalar_tensor_tensor(out=out.ap, in0=in0.ap, scalar=_ap(scalar), in1=in1.ap,
                                                               op0=op0, op1=op1), rd, [out])

    def cp(self, eng, out, in_):
        if eng == 'act':
            return self.act(out, in_, AF.Copy)
        return self.S.op(eng, lambda e: e.tensor_copy(out=out.ap, in_=in_.ap), [in_], [out])

    def rsqrt(self, out, in_):
        shp = list(in_.ap.shape)
        if shp[-1] > 8:
            self.act(out, in_, AF.Ln)
            return self.act(out, out, AF.Exp, scale=-0.5)
        mh = self.mhalf[0:shp[0], 0:shp[-1]]
        if len(shp) == 3:
            mh = mh.un(1).bc(shp)
        return self.tt('pool', out, in_, mh, ALU.pow)

    def memset(self, eng, out, val):
        return self.S.op(eng, lambda e: e.memset(out.ap, val), [], [out])

    def mm(self, out, lhsT, rhs, start, stop, signal=None):
        if signal is None:
            signal = stop
        return self.S.op('pe', lambda e: e.matmul(out.ap, lhsT=lhsT.ap, rhs=rhs.ap, start=start, stop=stop),
                         [lhsT, rhs], [out], signal=signal)

    def tr(self, out, in_, signal=True, f32=False):
        idt = self.identf if f32 else self.ident
        n = in_.ap.shape[0]
        return self.S.op('pe', lambda e: e.transpose(out.ap, in_.ap, idt.ap[0:n, 0:n]), [in_, idt.v], [out],
                         signal=signal)

    def dma(self, q, out, in_, **kw):
        return self.S.dma(q, out, in_, **kw)

    def load_weight(self, stg, dst, src_rows, ncols, scale=None, piece=2048):
        rows = src_rows.shape[0]
        c0 = 0
        while c0 < ncols:
            n = min(piece, ncols - c0)
            st = stg[self._stg_i % len(stg)]
            eng = ('dve', 'act')[self._stg_i % 2]
            self._stg_i += 1
            self.dma('sp', st[0:rows, 0:n], src_rows[:, c0:c0 + n])
            o = dst[:, c0:c0 + n]
            i = st[0:rows, 0:n]
            if scale is None:
                self.cp(eng, o, i)
            elif eng == 'act':
                if isinstance(scale, V):
                    self.act(o, i, AF.Copy, scale=scale)
                else:
                    self.act(o, i, AF.Copy, scale=float(scale))
            else:
                self.ts(eng, o, i, scale, None, ALU.mult)
            c0 += n

    def load_cols(self, es, name, specs):
        cols = {}
        n = 0
        for k, v in specs:
            cols[k] = n
            n += (v.shape[0] + 127) // 128
        res = self.sb(es, name, [128, n], F32)
        with ExitStack() as tes:
            ngrp = (n + 127) // 128
            stage = [self.sb(tes, name + "_st%d" % g, [128, 128], F32) for g in range(ngrp)]
            for st in stage:
                self.memset('pool', st.v, 0.0)
            for k, v in specs:
                L = v.shape[0]
                m = (L + 127) // 128
                c = cols[k]
                r = 0
                while r < m:
                    g, rr = divmod(c + r, 128)
                    cnt = min(m - r, 128 - rr)
                    if L >= 128:
                        self.dma('sp', stage[g][rr:rr + cnt, :], v[r * 128:(r + cnt) * 128].rearrange("(m p) -> m p", p=128))
                    else:
                        self.dma('sp', stage[g][rr:rr + 1, 0:L], v.rearrange("(m p) -> m p", m=1))
                    r += cnt
            for g in range(ngrp):
                w = min(128, n - g * 128)
                ps = self.next_ps()
                self.tr(ps[:, 0:w], stage[g][0:w, :], f32=True)
                self.cp('dve', res[:, g * 128:g * 128 + w], ps[:, 0:w])
            self.S.barrier()
        return res, cols

    def rms_h(self, xt, nsub, ss, rstd, junk, h):
        for s in range(nsub):
            self.act(junk.v, xt[:, s, :], AF.Square, accum=ss[:, s:s + 1])
        self.ts('dve', rstd[:, 0:nsub], ss[:, 0:nsub], 1.0 / D, NORM_EPS, ALU.mult, ALU.add)
        self.rsqrt(rstd[:, 0:nsub], rstd[:, 0:nsub])
        for s in range(nsub):
            self.ts('dve', h[:, s, :], xt[:, s, :], rstd[:, s:s + 1], None, ALU.mult)

    def transpose_h(self, h, nsub, hT, evac=('act', 'dve')):
        tw = nsub * 128
        per_bank = 1024 // tw
        c = 0
        i = 0
        while c < 8:
            ps = self.next_ps()
            pb = ps.v.bitcast(BF16)
            nchunk = min(per_bank, 8 - c)
            for cc in range(nchunk):
                for s in range(nsub):
                    last = (cc == nchunk - 1 and s == nsub - 1)
                    self.tr(pb[:, cc * tw + s * 128: cc * tw + (s + 1) * 128], h[:, s, (c + cc) * 128:(c + cc + 1) * 128],
                            signal=last)
            self.cp(evac[i % len(evac)], hT[:, c:c + nchunk, :], pb[:, 0:nchunk * tw].re("p (c t) -> p c t", c=nchunk))
            c += nchunk
            i += 1

    def norm_from_dram(self, src, tok0, NS, xbufs, sst, rst, junk, h):
        for s in range(NS):
            xb = xbufs[self._xb_i % len(xbufs)]
            self._xb_i += 1
            self.dma('sp', xb.v, src[tok0 + s * 128: tok0 + (s + 1) * 128, :])
            self.act(junk.v, xb.v, AF.Square, accum=sst[s].v)
            self.ts('dve', rst[s].v, sst[s].v, 1.0 / D, NORM_EPS, ALU.mult, ALU.add)
            self.rsqrt(rst[s].v, rst[s].v)
            self.ts('dve', h[:, s, :], xb.v, rst[s].v, None, ALU.mult)

    def norm_bufs(self, es, pfx, NS):
        xbufs = [self.sb(es, pfx + "_xb%d" % i, [128, D], F32) for i in range(2)]
        sst = [self.sb(es, pfx + "_ss%d" % i, [128, 1], F32) for i in range(NS)]
        rst = [self.sb(es, pfx + "_rs%d" % i, [128, 1], F32) for i in range(NS)]
        junk = self.sb(es, pfx + "_junk", [128, D], BF16)
        self._xb_i = 0
        return xbufs, sst, rst, junk

    def phaseL(self, es, src, dst, final=False):
        S = self.S
        inp = self.inp
        TT = 512
        NS = 4
        self._stg_i = 0
        self._ev_i = 0
        cols, ci = self.load_cols(es, "l_cols", [
            ("g", inp["norm_mix_g"]), ("bin", inp["b_in"][1792:2816]),
            ("cw0", inp["conv_b_w"][0]), ("cw1", inp["conv_b_w"][1]), ("cw2", inp["conv_b_w"][2]), ("cw3", inp["conv_b_w"][3]),
            ("cbb", inp["conv_b_b"]), ("bra", inp["b_rg_a"]), ("brx", inp["b_rg_x"]), ("lam", inp["lru_lambda"])])
        hb = self.sb(es, "l_hb", [128, 8], F32)
        self.ts('dve', hb[:, 0:4], cols[:, ci["bra"]:ci["bra"] + 4], 0.5, None, ALU.mult)
        self.ts('dve', hb[:, 4:8], cols[:, ci["brx"]:ci["brx"] + 4], 0.5, None, ALU.mult)
        cA = self.sb(es, "l_cA", [128, 8], F32)
        lt = self.sb(es, "l_lt", [128, 4], F32)
        self.act(lt.v, cols[:, ci["lam"]:ci["lam"] + 4], AF.Exp, scale=-1.0)
        self.ts('dve', lt.v, lt.v, 1.0, None, ALU.add)
        self.act(lt.v, lt.v, AF.Ln)
        self.ts('dve', cA[:, 0:4], lt.v, -4.0, None, ALU.mult)
        self.ts('dve', cA[:, 4:8], lt.v, -8.0, None, ALU.mult)
        win = self.sb(es, "l_win", [128, 8, 1024], BF16)
        wg = self.sb(es, "l_wg", [128, 2, 4, 128], BF16)
        with ExitStack() as tes:
            stg = [self.sb(tes, "l_stg%d" % i, [128, 1024], F32) for i in range(3)]
            for k in range(8):
                self.load_weight(stg, win[:, k, :], inp["w_in"][k * 128:(k + 1) * 128, 1792:2816], 1024,
                                 scale=cols[:, ci["g"] + k: ci["g"] + k + 1], piece=1024)
            for gi, nm in enumerate(("w_rg_a", "w_rg_x")):
                st = stg[gi]
                self.memset('pool', st.v, 0.0)
                for blk in range(8):
                    c, hl = divmod(blk, 2)
                    self.dma('sp', st[hl * 64:(hl + 1) * 64, c * 128 + hl * 64: c * 128 + (hl + 1) * 64], inp[nm][blk])
                self.cp('dve', wg[:, gi, :, :], st[:, 0:512].re("p (c m) -> p c m", c=4))
            S.barrier()
        xbufs, sst, rst, junk = self.norm_bufs(es, "l", NS)
        h = self.sb(es, "l_h", [128, NS, D], BF16)
        hT = self.sb(es, "l_hT", [128, 8, TT], BF16)
        PX = [self.sb(es, "l_px%d" % c, [128, 3 + TT], F32) for c in range(4)]
        GY = [self.sb(es, "l_gy%d" % c, [128, TT], F32) for c in range(4)]
        YB = [self.sb(es, "l_yb%d" % c, [128, TT], BF16) for c in range(4)]
        carry = [self.sb(es, "l_cy%d" % c, [128, 1], F32) for c in range(4)]
        NW = 2
        def wk(nm, dt=F32, n=NW):
            return [self.sb(es, "l_%s%d" % (nm, i), [128, TT], dt) for i in range(n)]
        acc = wk("acc", n=4); xbb = wk("xbb", BF16, n=4); ta = wk("ta", n=4); tx = wk("tx", n=4)
        av = wk("av"); a2 = wk("a2"); u = wk("u"); hl_ = wk("hl")
        ntile = self.ntok // TT
        tiles_per_b = self.seq // TT
        for i in range(ntile):
            if S.need_reset():
                S.hard_barrier()
            tok0 = i * TT
            first = (i % tiles_per_b == 0)
            if first:
                for c in range(4):
                    self.memset('pool', PX[c][:, 0:3], 0.0)
                    self.memset('pool', carry[c].v, 0.0)
            self.norm_from_dram(src, tok0, NS, xbufs, sst, rst, junk, h)
            self.transpose_h(h, NS, hT)
            for c in range(4):
                ps = self.next_ps()
                for k in range(8):
                    self.mm(ps.v, win[:, k, c * 128:(c + 1) * 128], hT[:, k, :], start=(k == 0), stop=(k == 7))
                self.act(PX[c][:, 3:3 + TT], ps.v, AF.Identity, bias=cols[:, ci["bin"] + c: ci["bin"] + c + 1])
            for c in range(4):
                ps = self.next_ps()
                for k in range(8):
                    self.mm(ps.v, win[:, k, 512 + c * 128: 512 + (c + 1) * 128], hT[:, k, :], start=(k == 0), stop=(k == 7))
                self.act(GY[c].v, ps.v, AF.Gelu_apprx_tanh, bias=cols[:, ci["bin"] + 4 + c: ci["bin"] + 5 + c])
            for c in range(4):
                A = acc[c]; XB = xbb[c]; TA = ta[c]; TX = tx[c]
                cw = [cols[:, ci["cw%d" % j] + c: ci["cw%d" % j] + c + 1] for j in range(4)]
                self.ts('dve', A.v, PX[c][:, 0:TT], cw[0], cols[:, ci["cbb"] + c: ci["cbb"] + c + 1], ALU.mult, ALU.add)
                for j in range(1, 4):
                    self.stt('dve', A.v, PX[c][:, j:j + TT], cw[j], A.v, ALU.mult, ALU.add)
                self.cp('pool', PX[c][:, 0:3], PX[c][:, TT:TT + 3])
                self.cp('dve', XB.v, A.v)
                psa = self.next_ps()
                self.mm(psa.v, wg[:, 0, c, :], XB.v, start=True, stop=True)
                psx = self.next_ps()
                self.mm(psx.v, wg[:, 1, c, :], XB.v, start=True, stop=True)
                self.act(TA.v, psa.v, AF.Tanh, scale=0.5, bias=hb[:, c:c + 1])
                self.act(TX.v, psx.v, AF.Tanh, scale=0.5, bias=hb[:, 4 + c:5 + c])
            for c in range(4):
                A = acc[c]; TA = ta[c]; TX = tx[c]; AV = av[c % NW]; A2 = a2[c % NW]; U = u[c % NW]; HL = hl_[c % NW]
                self.act(AV.v, TA.v, AF.Exp, scale=cA[:, c:c + 1], bias=cA[:, c:c + 1])
                self.act(A2.v, TA.v, AF.Exp, scale=cA[:, 4 + c:5 + c], bias=cA[:, 4 + c:5 + c])
                if LVAR == 1:
                    self.ts('dve', A2.v, A2.v, -1.0, 1.0, ALU.mult, ALU.add)
                    self.ts('dve', A2.v, A2.v, 0.0, None, ALU.max)
                    self.tt('pool', A2.v, A2.v, self.phalf[:, 0:TT], ALU.pow)
                elif LVAR == 2:
                    self.ts('dve', A2.v, A2.v, -1.0, 1.0, ALU.mult, ALU.add)
                    self.ts('dve', A2.v, A2.v, 1e-30, None, ALU.max)
                    self.act(A2.v, A2.v, AF.Ln)
                    self.act(A2.v, A2.v, AF.Exp, scale=0.5)
                else:
                    self.ts('dve', A2.v, A2.v, 0.99999994, -1.0, ALU.min, ALU.mult)
                    self.act(A2.v, A2.v, AF.Ln, bias=1.0)
                    self.act(A2.v, A2.v, AF.Exp, scale=0.5)
                if first:
                    self.memset('pool', A2[:, 0:1], 1.0)
                self.stt('dve', U.v, TX.v, 1.0, A.v, ALU.add, ALU.mult)
                self.tt('dve', U.v, U.v, A2.v, ALU.mult)
                self.S.op('dve', lambda e: e.tensor_tensor_scan(HL.ap, AV.ap, U.ap, carry[c].ap, ALU.mult, ALU.add),
                          [AV.v, U.v, carry[c].v], [HL.v])
                self.cp('pool', carry[c].v, HL[:, TT - 1:TT])
                self.tt('dve', YB[c].v, HL.v, GY[c].v, ALU.mult)
                self.dma('sp', self.ybs[c * 128:(c + 1) * 128, tok0:tok0 + TT], YB[c].v)

    def phaseR(self, es, src, dst, final=False):
        S = self.S
        inp = self.inp
        TT = 256
        NS = 2
        NQ = 4
        CH = 64
        self._stg_i = 0
        self._ev_i = 0
        cols, ci = self.load_cols(es, "r_cols", [
            ("g", inp["norm_mix_g"]), ("bin", inp["b_in"][0:RWKV_COLS]), ("mu", inp["mu_shift"]),
            ("w0", inp["w0"]), ("a0", inp["a0"]), ("kk", inp["k_k"]), ("ka", inp["k_a"]), ("rk", inp["r_k"])])

        def col(key, j):
            return cols[:, ci[key] + j: ci[key] + j + 1]

        der = self.sb(es, "r_der", [128, 32], F32)
        self.ts('dve', der[:, 0:14], cols[:, ci["mu"]:ci["mu"] + 14], -1.0, 1.0, ALU.mult, ALU.add)
        self.ts('dve', der[:, 14:18], cols[:, ci["w0"]:ci["w0"] + 4], 0.5, None, ALU.mult)
        self.ts('dve', der[:, 18:22], cols[:, ci["a0"]:ci["a0"] + 4], 0.5, None, ALU.mult)
        self.ts('dve', der[:, 22:26], cols[:, ci["ka"]:ci["ka"] + 4], 0.5, None, ALU.mult)
        self.ts('dve', der[:, 26:30], cols[:, ci["ka"]:ci["ka"] + 4], -0.5, 1.0, ALU.mult, ALU.add)
        rkb = self.sb(es, "r_rkb", [128, 4], BF16)
        self.cp('dve', rkb.v, cols[:, ci["rk"]:ci["rk"] + 4])
        m64 = [self.sb(es, "r_m64_%d" % i, [64, 64], BF16) for i in range(3)]
        masks = [self.sb(es, "r_mask%d" % i, [128, 128], BF16) for i in range(3)]
        specs = [([[1, 64]], -1, ALU.is_gt), ([[-1, 64]], 1, ALU.is_gt), ([[1, 64]], -1, ALU.is_ge)]
        for mi in range(3):
            pat, cm, op = specs[mi]
            self.memset('pool', m64[mi].v, 1.0)
            self.S.op('pool', lambda e: e.affine_select(m64[mi].ap, m64[mi].ap, pattern=pat, compare_op=op, fill=0.0,
                                                        base=0, channel_multiplier=cm), [m64[mi].v], [m64[mi].v])
            for a in range(2):
                for b_ in range(2):
                    self.dma('sp', masks[mi][a * 64:(a + 1) * 64, b_ * 64:(b_ + 1) * 64], m64[mi].v)
        mask_su, mask_sl, mask_u = masks
        bones = self.sb(es, "r_bones", [128, 128], BF16)
        self.memset('pool', bones.v, 0.0)
        self.memset('pool', bones[0:64, 0:64], 1.0)
        self.memset('pool', bones[64:128, 64:128], 1.0)
        m01 = self.sb(es, "r_m01", [128, TT], F32)
        self.memset('pool', m01.v, 1.0)
        self.memset('pool', m01.v.re("p (q s) -> p q s", s=CH)[:, :, 0:1], 0.0)
        lnxg = self.sb(es, "r_lnxg", [128, 4, 64], F32)
        lnxb = self.sb(es, "r_lnxb", [128, 4, 64], F32)
        for c in range(4):
            for hl in range(2):
                hh = 2 * c + hl
                self.dma('sp', lnxg[hl * 64:(hl + 1) * 64, c, :], inp["lnx_g"][hh * 64:(hh + 1) * 64].partition_broadcast(64))
                self.dma('sp', lnxb[hl * 64:(hl + 1) * 64, c, :], inp["lnx_b"][hh * 64:(hh + 1) * 64].partition_broadcast(64))
        win = self.sb(es, "r_win", [128, 8, RWKV_COLS], BF16)
        wlora = self.sb(es, "r_wlora", [128, 2, AW], BF16)
        gup = self.sb(es, "r_gup", [128, AW], BF16)
        with ExitStack() as tes:
            stg = [self.sb(tes, "r_stg%d" % i, [128, RWKV_COLS], F32) for i in range(3)]
            for k in range(8):
                self.load_weight(stg, win[:, k, :], inp["w_in"][k * 128:(k + 1) * 128, 0:RWKV_COLS], RWKV_COLS,
                                 scale=cols[:, ci["g"] + k: ci["g"] + k + 1], piece=RWKV_COLS)
            st = stg[0]
            self.memset('pool', st[:, 0:2 * AW], 0.0)
            self.dma('sp', st[0:64, 0:AW], inp["w_lora_up"])
            self.dma('sp', st[64:128, AW:2 * AW], inp["a_lora_up"])
            self.cp('dve', wlora.v, st[:, 0:2 * AW].re("p (a m) -> p a m", a=2))
            st = stg[1]
            self.dma('sp', st[:, 0:AW], inp["g_lora_up"])
            self.cp('dve', gup.v, st[:, 0:AW])
            S.barrier()
        xbufs, sst, rst, junk = self.norm_bufs(es, "r", NS)
        h = self.sb(es, "r_h", [128, NS, D], BF16)
        hT = self.sb(es, "r_hT", [128, 8, TT], BF16)
        PW = [self.sb(es, "r_pw%d" % i, [128, 1 + TT], F32) for i in range(3)]
        pcarry = [self.sb(es, "r_pc%d" % m, [128, 1], F32) for m in range(14)]
        Sx = [self.sb(es, "r_s%d" % m, [128, TT], F32) for m in range(14)]
        ltmp = [self.sb(es, "r_lt%d" % i, [128, TT], F32) for i in range(2)]
        LB = self.sb(es, "r_lb", [128, TT], BF16)
        SGd = self.sb(es, "r_sgd", [128, NQ, 128], BF16)
        NW = 2

        def wk(nm, dt=F32):
            return [self.sb(es, "r_%s%d" % (nm, i), [128, TT], dt) for i in range(NW)]

        logw = [self.sb(es, "r_logw%d" % i, [128, TT], F32) for i in range(4)]
        ta = [self.sb(es, "r_ta%d" % i, [128, TT], F32) for i in range(4)]
        cum = [self.sb(es, "r_cum%d" % i, [128, TT], F32) for i in range(4)]
        egm1 = wk("egm1"); eg = wk("eg"); eig = wk("eig"); egc = wk("egc")
        kk = wk("kk"); kk2 = wk("kk2", BF16); rn = wk("rn"); kkn = wk("kkn"); kmod = wk("kmod"); bv = wk("bv")
        names = ["AT", "BT", "KT", "RT", "BGT", "KGT", "VB", "RK"]
        EXP = {n: [self.sb(es, "r_%s%d" % (n, c), [128, NQ, 128], BF16) for c in range(4)] for n in names}
        for n in names:
            for c in range(4):
                self.memset('pool', EXP[n][c].v, 0.0)
        GC = self.sb(es, "r_gc", [128, 4, NQ], F32)
        BKG = [self.sb(es, "r_bkg%d" % q, [128, 2, 4, 128], BF16) for q in range(NQ)]
        Vst = [self.sb(es, "r_vst%d" % q, [128, 4, 64], BF16) for q in range(NQ)]
        Nb = [[self.sb(es, "r_nb%d_%d" % (q, i), [128, 4, 128], BF16) for i in range(2)] for q in range(NQ)]
        Lb = [[self.sb(es, "r_lb%d_%d" % (q, i), [128, 4, 128], BF16) for i in range(2)] for q in range(NQ)]
        Pb = [self.sb(es, "r_pb%d" % q, [128, 4, 128], BF16) for q in range(NQ)]
        LakT = [self.sb(es, "r_lak%d" % q, [128, 4, 128], BF16) for q in range(NQ)]
        MrbT = [self.sb(es, "r_mrb%d" % q, [128, 4, 128], BF16) for q in range(NQ)]
        MrkT = [self.sb(es, "r_mrk%d" % q, [128, 4, 128], BF16) for q in range(NQ)]
        H = self.sb(es, "r_H", [128, 4, 64], F32)
        Hb = self.sb(es, "r_Hb", [128, 4, 64], BF16)
        Xb = [self.sb(es, "r_Xb%d" % i, [128, 4, 64], BF16) for i in range(2)]
        Ub = [self.sb(es, "r_Ub%d" % i, [128, 4, 64], BF16) for i in range(2)]

        def yt(nm, shape=(128, 4, 64), dt=F32):
            return [self.sb(es, "r_%s%d" % (nm, i), list(shape), dt) for i in range(2)]

        Yv = yt("Yv"); Ysq = yt("Ysq"); Yn = yt("Yn"); Bn = yt("Bn"); Yf = yt("Yf", dt=BF16)
        st1 = yt("st1", (128, 4)); st2 = yt("st2", (128, 4)); mean = yt("mean", (128, 4)); var = yt("var", (128, 4))
        YT = [self.sb(es, "r_YT%d" % i, [64, 4, 2, TT], BF16) for i in range(2)]
        ident = self.ident
        ntile = self.ntok // TT
        tiles_per_b = self.seq // TT
        yas_v = self.yas.rearrange("(c hl i) t -> i c hl t", hl=2, i=64)

        for it in range(ntile):
            tok0 = it * TT
            if S.need_reset():
                S.hard_barrier()
            if it % tiles_per_b == 0:
                for m in range(14):
                    self.memset('pool', pcarry[m].v, 0.0)
                self.memset('pool', H.v, 0.0)
                self.memset('pool', Hb.v, 0.0)
            self.norm_from_dram(src, tok0, NS, xbufs, sst, rst, junk, h)
            self.transpose_h(h, NS, hT)
            for m in range(14):
                if m % 2 == 0:
                    ps = self.next_ps()
                o = (m % 2) * TT
                for k in range(8):
                    self.mm(ps[:, o:o + TT], win[:, k, m * 128:(m + 1) * 128], hT[:, k, :], start=(k == 0), stop=(k == 7))
                pw = PW[m % 3]
                self.act(pw[:, 1:1 + TT], ps[:, o:o + TT], AF.Identity, bias=col("bin", m))
                self.cp('pool', pw[:, 0:1], pcarry[m].v)
                self.cp('pool', pcarry[m].v, pw[:, TT:TT + 1])
                tmp = ltmp[m % 2]
                self.act(tmp.v, pw[:, 0:TT], AF.Copy, scale=col("mu", m))
                self.stt('dve', Sx[m].v, pw[:, 1:1 + TT], der[:, m:m + 1], tmp.v, ALU.mult, ALU.add)
            if RSTOP == 1:
                continue
            self.act(LB[0:64, :], Sx[12][0:64, :], AF.Tanh)
            self.cp('act', LB[64:128, :], Sx[12][64:128, :])
            tmp = ltmp[0]
            self.act(tmp.v, Sx[13].v, AF.Tanh, scale=0.5)
            for hl in range(2):
                self.ts('dve', SGd[:, :, hl * 64:(hl + 1) * 64], tmp.v.re("p (q s) -> p q s", s=CH), 0.5, 0.5, ALU.mult, ALU.add)
            if RSTOP == 21:
                continue
            for c in range(4):
                LW = logw[c]; TA = ta[c]; CU = cum[c]
                ps = self.next_ps()
                self.mm(ps[:, 0:TT], wlora[:, 0, c * 128:(c + 1) * 128], LB.v, start=True, stop=True)
                self.mm(ps[:, TT:2 * TT], wlora[:, 1, c * 128:(c + 1) * 128], LB.v, start=True, stop=True)
                self.act(LW.v, ps[:, 0:TT], AF.Tanh, scale=0.5, bias=der[:, 14 + c:15 + c])
                self.act(TA.v, ps[:, TT:2 * TT], AF.Tanh, scale=0.5, bias=der[:, 18 + c:19 + c])
                self.ts('dve', LW.v, LW.v, 1.0, -0.30326532985631671, ALU.add, ALU.mult)
                self.S.op('dve', lambda e: e.tensor_tensor_scan(CU.ap, m01.ap, LW.ap, 0.0, ALU.mult, ALU.add),
                          [m01.v, LW.v], [CU.v])
            for c in range(4):
                w = it * 4 + c
                r_, k_, v_ = Sx[c], Sx[4 + c], Sx[8 + c]
                LW = logw[c]; TA = ta[c]; CU = cum[c]
                E1 = egm1[w % NW]; EG = eg[w % NW]; EI = eig[w % NW]
                EC = egc[w % NW]; KK = kk[w % NW]; K2 = kk2[w % NW]; RN = rn[w % NW]; KN = kkn[w % NW]; KM = kmod[w % NW]
                BV = bv[w % NW]
                cu3 = CU.v.re("p (q s) -> p q s", s=CH)
                cuC = cu3[:, :, CH - 1:CH]
                self.tt('pool', E1.v, CU.v, LW.v, ALU.subtract)
                self.act(E1.v, E1.v, AF.Exp)
                self.act(EG.v, CU.v, AF.Exp)
                self.act(EI.v, CU.v, AF.Exp, scale=-1.0)
                self.tt('pool', EC.v.re("p (q s) -> p q s", s=CH), cuC.bc([128, NQ, CH]), cu3, ALU.subtract)
                self.act(EC.v, EC.v, AF.Exp)
                self.act(GC[:, c, :], cuC.re("p q o -> p (q o)"), AF.Exp)
                self.act(KK.v, k_.v, AF.Copy, scale=col("kk", c))
                self.act(K2.v, KK.v, AF.Square)
                psn = self.next_ps()
                self.mm(psn[:, 0:TT], bones.v, K2.v, start=True, stop=True)
                self.act(RN.v, psn[:, 0:TT], AF.Ln)
                self.act(RN.v, RN.v, AF.Exp, scale=-0.5)
                self.tt('pool', KN.v, KK.v, RN.v, ALU.mult)
                self.act(KM.v, TA.v, AF.Identity, scale=der[:, 22 + c:23 + c], bias=der[:, 26 + c:27 + c])
                self.tt('dve', KM.v, KM.v, k_.v, ALU.mult)
                self.stt('dve', BV.v, TA.v, 1.0, KN.v, ALU.add, ALU.mult)
                for hl in range(2):
                    P_ = slice(hl * 64, (hl + 1) * 64)

                    def hv(t):
                        return t[P_, :].re("p (q s) -> p q s", s=CH)

                    def ov(n):
                        return EXP[n][c][P_, :, hl * 64:(hl + 1) * 64]

                    self.stt('dve', ov("AT"), hv(KN), -1.0, hv(E1), ALU.mult, ALU.mult)
                    self.stt('dve', ov("BT"), hv(BV), 0.5, hv(EI), ALU.mult, ALU.mult)
                    self.stt('dve', ov("BGT"), hv(BV), 0.5, hv(EC), ALU.mult, ALU.mult)
                    self.tt('pool', ov("KT"), hv(KM), hv(EI), ALU.mult)
                    self.tt('pool', ov("KGT"), hv(KM), hv(EC), ALU.mult)
                    self.tt('pool', ov("RT"), hv(r_), hv(EG), ALU.mult)
                    self.tt('pool', ov("RK"), hv(r_), hv(KM), ALU.mult)
                    self.cp('act', ov("VB"), hv(v_))
            if RSTOP == 2:
                continue
            for q in range(NQ):
                psA = self.next_ps()
                pbA = psA.v.bitcast(BF16)
                for gi, n in enumerate(("BGT", "KGT")):
                    for c in range(4):
                        self.tr(pbA[:, (gi * 4 + c) * 128:(gi * 4 + c + 1) * 128], EXP[n][c][:, q, :], signal=(gi == 1 and c == 3))
                self.evac(BKG[q].v, pbA.re("p (g c m) -> p g c m", g=2, c=4))
                psB = self.next_ps()
                pbB = psB.v.bitcast(BF16)
                for c in range(4):
                    self.tr(pbB[:, c * 128:(c + 1) * 128], EXP["VB"][c][:, q, :], signal=(c == 3))
                for hl in range(2):
                    self.evac(Vst[q][hl * 64:(hl + 1) * 64, :, :],
                              pbB[hl * 64:(hl + 1) * 64, 0:512].re("p (c m) -> p c m", c=4)[:, :, hl * 64:(hl + 1) * 64])
                combos = [("BT", "AT", mask_su, Nb[q][0]), ("AT", "BT", mask_sl, Lb[q][0]), ("KT", "AT", mask_su, LakT[q]),
                          ("BT", "RT", mask_u, MrbT[q]), ("KT", "RT", mask_u, MrkT[q])]
                for (ln, rn_, mk, dstt) in combos:
                    ps = self.next_ps()
                    for c in range(4):
                        self.mm(ps[:, c * 128:(c + 1) * 128], EXP[ln][c][:, q, :], EXP[rn_][c][:, q, :], start=True, stop=True,
                                signal=(c == 3))
                    self.tt('dve', dstt.v, ps.v.re("p (c m) -> p c m", c=4), mk.v.un(1).bc([128, 4, 128]), ALU.mult)
                self.tt('pool', Pb[q].v, Nb[q][0].v, ident.v.un(1).bc([128, 4, 128]), ALU.add)
            if RSTOP == 3:
                continue
            cur = 0
            for lvl in range(1, 6):
                nxt = 1 - cur
                for q in range(NQ):
                    if lvl < 5:
                        ps = self.next_ps()
                        for c in range(4):
                            self.mm(ps[:, c * 128:(c + 1) * 128], Lb[q][cur][:, c, :], Nb[q][cur][:, c, :], start=True, stop=True,
                                    signal=(c == 3))
                        self.cp('act', Nb[q][nxt].v, ps.v.re("p (c m) -> p c m", c=4))
                    ps = self.next_ps()
                    for c in range(4):
                        self.mm(ps[:, c * 128:(c + 1) * 128], Nb[q][cur][:, c, :], Lb[q][cur][:, c, :], start=True, stop=True,
                                signal=(c == 3))
                    self.cp('act', Lb[q][nxt].v, ps.v.re("p (c m) -> p c m", c=4))
                for q in range(NQ):
                    ps = self.next_ps()
                    for c in range(4):
                        self.mm(ps[:, c * 128:(c + 1) * 128], Lb[q][nxt][:, c, :], Pb[q][:, c, :], start=True, stop=True,
                                signal=(c == 3))
                    self.tt('dve', Pb[q].v, Pb[q].v, ps.v.re("p (c m) -> p c m", c=4), ALU.add)
                cur = nxt
            if RSTOP == 4:
                continue
            YTt = YT[it % 2]
            for q in range(NQ):
                w = it * NQ + q
                xb_, ub_ = Xb[w % 2], Ub[w % 2]
                psx = self.next_ps()
                for c in range(4):
                    self.mm(psx[:, c * 64:(c + 1) * 64], EXP["AT"][c][:, q, :], Hb[:, c, :], start=True, stop=False, signal=False)
                    self.mm(psx[:, c * 64:(c + 1) * 64], LakT[q][:, c, :], Vst[q][:, c, :], start=False, stop=True, signal=(c == 3))
                self.cp('act', xb_.v, psx[:, 0:256].re("p (c m) -> p c m", c=4))
                psu = self.next_ps()
                for c in range(4):
                    self.mm(psu[:, c * 64:(c + 1) * 64], Pb[q][:, c, :], xb_[:, c, :], start=True, stop=True, signal=(c == 3))
                self.cp('dve', ub_.v, psu[:, 0:256].re("p (c m) -> p c m", c=4))
                psy = self.next_ps()
                for c in range(4):
                    o = psy[:, c * 64:(c + 1) * 64]
                    self.mm(o, EXP["RT"][c][:, q, :], Hb[:, c, :], start=True, stop=False, signal=False)
                    self.mm(o, MrbT[q][:, c, :], ub_[:, c, :], start=False, stop=False, signal=False)
                    self.mm(o, MrkT[q][:, c, :], Vst[q][:, c, :], start=False, stop=True, signal=False)
                for c in range(4):
                    self.mm(psy[:, 256 + c:257 + c], EXP["RK"][c][:, q, :], rkb[:, c:c + 1], start=True, stop=True, signal=(c == 3))
                psh = self.next_ps()
                for c in range(4):
                    o = psh[:, c * 64:(c + 1) * 64]
                    self.mm(o, BKG[q][:, 0, c, :], ub_[:, c, :], start=True, stop=False, signal=False)
                    self.mm(o, BKG[q][:, 1, c, :], Vst[q][:, c, :], start=False, stop=True, signal=(c == 3))
                self.tt('dve', H.v, H.v, GC[:, :, q:q + 1].bc([128, 4, 64]), ALU.mult)
                self.tt('dve', H.v, H.v, psh[:, 0:256].re("p (c m) -> p c m", c=4), ALU.add)
                self.cp('act', Hb.v, H.v)
                psg = self.next_ps()
                self.mm(psg.v, SGd[:, q, :], gup.v, start=True, stop=True)
                Y = Yv[w % 2]; Y2 = Ysq[w % 2]; YN = Yn[w % 2]; BN = Bn[w % 2]; YF = Yf[w % 2]
                s1 = st1[w % 2]; s2 = st2[w % 2]; mn = mean[w % 2]; vr = var[w % 2]
                py3 = psy[:, 0:256].re("p (c m) -> p c m", c=4)
                self.cp('act', Y.v, py3)
                self.act(Y2.v, py3, AF.Square)
                self.S.op('dve', lambda e: e.reduce_sum(out=s1.ap, in_=Y.ap, axis=AX.X), [Y.v], [s1.v])
                self.S.op('dve', lambda e: e.reduce_sum(out=s2.ap, in_=Y2.ap, axis=AX.X), [Y2.v], [s2.v])
                self.ts('dve', mn.v, s1.v, 1.0 / 64.0, None, ALU.mult)
                self.tt('dve', vr.v, mn.v, mn.v, ALU.mult)
                self.stt('dve', vr.v, s2.v, 1.0 / 64.0, vr.v, ALU.mult, ALU.subtract)
                self.ts('dve', vr.v, vr.v, LNX_EPS, None, ALU.add)
                self.rsqrt(vr.v, vr.v)
                self.tt('pool', YN.v, Y.v, mn.v.un(2).bc([128, 4, 64]), ALU.subtract)
                self.tt('pool', YN.v, YN.v, vr.v.un(2).bc([128, 4, 64]), ALU.mult)
                self.tt('pool', YN.v, YN.v, lnxg.v, ALU.mult)
                self.tt('pool', YN.v, YN.v, lnxb.v, ALU.add)
                self.tt('dve', BN.v, Vst[q].v, psy[:, 256:260].un(2).bc([128, 4, 64]), ALU.mult)
                self.tt('pool', YN.v, YN.v, BN.v, ALU.add)
                for hl in range(2):
                    P_ = slice(hl * 64, (hl + 1) * 64)
                    gv = psg[P_, :].re("p (c h m) -> p c h m", c=4, h=2)[:, :, hl, :]
                    self.tt('dve', YF[P_, :, :], YN[P_, :, :], gv, ALU.mult)
                pst = self.next_ps()
                pbt = pst.v.bitcast(BF16)
                for c in range(4):
                    self.tr(pbt[0:64, c * 128:(c + 1) * 128], YF[:, c, :], signal=(c == 3))
                self.evac(YTt[:, :, :, q * CH:(q + 1) * CH], pbt[0:64, 0:512].re("p (c h t) -> p c h t", c=4, h=2))
            self.dma('sp', yas_v[:, :, :, tok0:tok0 + TT], YTt.v)
        self._sbuf_left = self.nc.sbuf_bytes_remaining

    def phaseM(self, es, src, dst, final=False):
        S = self.S
        inp = self.inp
        TT = 512
        NS = 4
        self._stg_i = 0
        self._ev_i = 0
        use_a = 'R' in self.phases
        cols, ci = self.load_cols(es, "m_cols", [("g", inp["norm_mix_g"]), ("bin", inp["b_in"][2816:4864])])
        hb = self.sb(es, "m_hb", [128, 16], F32)
        self.ts('dve', hb.v, cols[:, ci["bin"]:ci["bin"] + 16], 0.5, None, ALU.mult)
        win = self.sb(es, "m_win", [128, 8, 2048], BF16)
        wa = self.sb(es, "m_wa", [128, 4, D], BF16)
        wb = self.sb(es, "m_wb", [128, 4, D], BF16)
        wmo = self.sb(es, "m_wmo", [128, 8, D], BF16)
        with ExitStack() as tes:
            stg = [self.sb(tes, "m_stg%d" % i, [128, 2048], F32) for i in range(3)]
            for k in range(8):
                self.load_weight(stg, win[:, k, :], inp["w_in"][k * 128:(k + 1) * 128, 2816:4864], 2048,
                                 scale=cols[:, ci["g"] + k: ci["g"] + k + 1])
                self.load_weight(stg, wmo[:, k, :], inp["w_mix_out"][k * 128:(k + 1) * 128, :], D)
            for c in range(4):
                self.load_weight(stg, wa[:, c, :], inp["w_branch_a"][c * 128:(c + 1) * 128, :], D, scale=0.5)
                self.load_weight(stg, wb[:, c, :], inp["w_branch_b"][c * 128:(c + 1) * 128, :], D, scale=0.25)
            S.barrier()
        xbufs, sst, rst, junk = self.norm_bufs(es, "m", NS)
        h = self.sb(es, "m_h", [128, NS, D], BF16)
        hT = self.sb(es, "m_hT", [128, 8, TT], BF16)
        TG = [self.sb(es, "m_tg%d" % m, [128, TT], BF16) for m in range(16)]
        YA = [self.sb(es, "m_ya%d" % i, [128, 4, TT], BF16) for i in range(2)]
        YB = [self.sb(es, "m_yb%d" % i, [128, 4, TT], BF16) for i in range(2)]
        MA = [self.sb(es, "m_ma%d" % i, [128, TT], F32) for i in range(3)]
        MG = [self.sb(es, "m_mg%d" % m, [128, TT], BF16) for m in range(8)]
        self._mb = [self.sb(es, "m_mb%d" % i, [128, TT], F32) for i in range(2)]
        ntile = self.ntok // TT

        def load_y(i):
            tok0 = i * TT
            if use_a:
                self.dma('sp', YA[i % 2].v, self.yas[:, tok0:tok0 + TT].rearrange("(c p) t -> p c t", p=128))
            self.dma('sp', YB[i % 2].v, self.ybs[:, tok0:tok0 + TT].rearrange("(c p) t -> p c t", p=128))

        load_y(0)
        for i in range(ntile):
            if S.need_reset():
                S.hard_barrier()
            tok0 = i * TT
            if i + 1 < ntile:
                load_y(i + 1)
            ya, yb = YA[i % 2], YB[i % 2]
            self.norm_from_dram(src, tok0, NS, xbufs, sst, rst, junk, h)
            self.transpose_h(h, NS, hT)
            for m in range(16):
                ps = self.next_ps()
                for k in range(8):
                    self.mm(ps.v, win[:, k, m * 128:(m + 1) * 128], hT[:, k, :], start=(k == 0), stop=(k == 7))
                self.act(TG[m].v, ps.v, AF.Tanh, scale=0.5, bias=hb[:, m:m + 1])
            for m in range(8):
                ma = MA[m % 3]
                if use_a:
                    ps = self.next_ps()
                    for c in range(4):
                        self.mm(ps.v, wa[:, c, m * 128:(m + 1) * 128], ya[:, c, :], start=(c == 0), stop=(c == 3))
                    self.stt('dve', ma.v, TG[m].v, 1.0, ps.v, ALU.add, ALU.mult)
                ps2 = self.next_ps()
                for c in range(4):
                    self.mm(ps2.v, wb[:, c, m * 128:(m + 1) * 128], yb[:, c, :], start=(c == 0), stop=(c == 3))
                if use_a:
                    mb = MA[(m + 1) % 3] if False else self._mb[m % 2]
                    self.stt('dve', mb.v, TG[8 + m].v, 1.0, ps2.v, ALU.add, ALU.mult)
                    self.tt('dve', MG[m].v, mb.v, ma.v, ALU.add)
                else:
                    self.stt('dve', MG[m].v, TG[8 + m].v, 1.0, ps2.v, ALU.add, ALU.mult)
            for s in range(NS):
                pss = [self.next_ps(), self.next_ps()]
                for half in range(2):
                    for m in range(8):
                        self.mm(pss[half].v, MG[m][:, s * 128:(s + 1) * 128], wmo[:, m, half * 512:(half + 1) * 512],
                                start=(m == 0), stop=(m == 7))
                xb = xbufs[self._xb_i % 2]
                self._xb_i += 1
                self.dma('sp', xb.v, src[tok0 + s * 128: tok0 + (s + 1) * 128, :])
                for half in range(2):
                    self.tt('dve', xb[:, half * 512:(half + 1) * 512], xb[:, half * 512:(half + 1) * 512], pss[half].v, ALU.add)
                self.dma('sp', dst[tok0 + s * 128: tok0 + (s + 1) * 128, :], xb.v)

    def evac(self, out, in_, i=None):
        if i is None:
            i = self._ev_i
            self._ev_i += 1
        return self.cp(('act', 'dve')[i % 2], out, in_)

    def phaseB(self, es, src, dst, final=False):
        S = self.S
        inp = self.inp
        TT = 512
        NS = 4
        self._stg_i = 0
        self._ev_i = 0
        cols, ci = self.load_cols(es, "b_cols", [("gx", inp["norm_x_g"]), ("gm", inp["norm_mem_g"])])
        gq = self.sb(es, "b_gq", [128, 8], F32)
        self.ts('dve', gq.v, cols[:, ci["gx"]:ci["gx"] + 8], 1.0 / 16.0, None, ALU.mult)
        wq = self.sb(es, "b_wq", [128, 8, D], BF16)
        wo = self.sb(es, "b_wo", [128, 8, D], BF16)
        kT = [self.sb(es, "b_kT%d" % b, [128, 8, NMEM], BF16) for b in range(self.nb)]
        Vt = [self.sb(es, "b_V%d" % b, [128, 2, D], BF16) for b in range(self.nb)]
        xts = [self.sb(es, "b_xt%d" % i, [128, NS, D], F32) for i in range(2)]
        h = self.sb(es, "b_h", [128, NS, D], BF16)
        hT = self.sb(es, "b_hT", [128, 8, TT], BF16)
        junk = self.sb(es, "b_junk", [128, D], BF16)
        ss = self.sb(es, "b_ss", [128, 4], F32)
        rstd = self.sb(es, "b_rstd", [128, 4], F32)
        with ExitStack() as tes:
            stg = [self.sb(tes, "b_stg%d" % i, [128, 2048], F32) for i in range(3)]
            wkv = self.sb(tes, "b_wkv", [128, 8, 2 * D], BF16)
            for k in range(8):
                self.load_weight(stg, wkv[:, k, :], inp["w_ckv"][k * 128:(k + 1) * 128, :], 2 * D,
                                 scale=cols[:, ci["gm"] + k: ci["gm"] + k + 1])
            for k in range(8):
                self.load_weight(stg, wq[:, k, :], inp["w_cq"][k * 128:(k + 1) * 128, :], D, scale=gq[:, k:k + 1])
                self.load_weight(stg, wo[:, k, :], inp["w_co"][k * 128:(k + 1) * 128, :], D)
            for b in range(self.nb):
                mt = xts[b % 2]
                self.dma('sp', mt[:, 0:2, :], inp["mem"][b * NMEM:(b + 1) * NMEM, :].rearrange("(s p) d -> p s d", p=128))
                self.rms_h(mt, 2, ss, rstd, junk, h)
                self.transpose_h(h, 2, hT[:, :, 0:NMEM])
                for m in range(8):
                    if m % 2 == 0:
                        ps = self.next_ps()
                    o = (m % 2) * NMEM
                    for k in range(8):
                        self.mm(ps[:, o:o + NMEM], wkv[:, k, m * 128:(m + 1) * 128], hT[:, k, 0:NMEM], start=(k == 0), stop=(k == 7))
                    if m % 2 == 1:
                        self.evac(kT[b][:, m - 1:m + 1, :], ps.v.re("p (a t) -> p a t", a=2))
                for mc in range(2):
                    for half in range(2):
                        ps = self.next_ps()
                        for k in range(8):
                            self.mm(ps.v, hT[:, k, mc * 128:(mc + 1) * 128], wkv[:, k, D + half * 512: D + (half + 1) * 512],
                                    start=(k == 0), stop=(k == 7))
                        self.evac(Vt[b][:, mc, half * 512:(half + 1) * 512], ps.v)
            S.barrier()
        qT = self.sb(es, "b_qT", [128, 8, TT], BF16)
        oT = self.sb(es, "b_oT", [128, 8, TT], BF16)
        prT = self.sb(es, "b_prT", [128, 2, 4, TT], BF16)
        prs = [self.sb(es, "b_pr%d" % i, [128, 4, NMEM], BF16) for i in range(2)]
        prn = [self.sb(es, "b_prn%d" % i, [128, 4, NMEM], BF16) for i in range(2)]
        mx = [self.sb(es, "b_mx%d" % i, [128, 4], F32) for i in range(2)]
        sm = [self.sb(es, "b_sm%d" % i, [128, 4], F32) for i in range(2)]
        ntile = self.ntok // TT
        tiles_per_b = self.seq // TT

        def load(i):
            self.dma('sp', xts[i % 2].v, src[i * TT:(i + 1) * TT, :].rearrange("(s p) d -> p s d", p=128))

        load(0)
        for i in range(ntile):
            if S.need_reset():
                S.hard_barrier()
            b = i // tiles_per_b
            xt = xts[i % 2]
            if i + 1 < ntile:
                load(i + 1)
            self.rms_h(xt, NS, ss, rstd, junk, h)
            self.transpose_h(h, NS, hT)
            for m in range(8):
                ps = self.next_ps()
                for k in range(8):
                    self.mm(ps.v, wq[:, k, m * 128:(m + 1) * 128], hT[:, k, :], start=(k == 0), stop=(k == 7))
                self.evac(qT[:, m, :], ps.v)
            for s in range(NS):
                pr = prs[s % 2]; pn = prn[s % 2]; mxs = mx[s % 2]; sms = sm[s % 2]
                banks = [self.next_ps(), self.next_ps()]
                for hh in range(4):
                    ps = banks[hh // 2]
                    o = (hh % 2) * NMEM
                    for c in range(2):
                        self.mm(ps[:, o:o + NMEM], qT[:, 2 * hh + c, s * 128:(s + 1) * 128], kT[b][:, 2 * hh + c, :],
                                start=(c == 0), stop=(c == 1))
                for g in range(2):
                    self.S.op('dve', lambda e: e.reduce_max(out=mxs.ap[:, 2 * g:2 * g + 2],
                                                            in_=banks[g].ap.rearrange("p (a t) -> p a t", a=2), axis=AX.X),
                              [banks[g].v], [mxs.v])
                self.ts('dve', mxs.v, mxs.v, -1.0, None, ALU.mult)
                for hh in range(4):
                    ps = banks[hh // 2]
                    o = (hh % 2) * NMEM
                    self.act(pr[:, hh, :], ps[:, o:o + NMEM], AF.Exp, bias=mxs[:, hh:hh + 1], accum=sms[:, hh:hh + 1])
                self.S.op('dve', lambda e: e.reciprocal(out=sms.ap, in_=sms.ap), [sms.v], [sms.v])
                self.tt('dve', pn.v, pr.v, sms.v.un(2).bc([128, 4, NMEM]), ALU.mult)
                ps = self.next_ps()
                pb = ps.v.bitcast(BF16)
                for hh in range(4):
                    for mc in range(2):
                        self.tr(pb[:, (hh * 2 + mc) * 128:(hh * 2 + mc + 1) * 128], pn[:, hh, mc * 128:(mc + 1) * 128],
                                signal=(hh == 3 and mc == 1))
                self.evac(prT[:, :, :, s * 128:(s + 1) * 128], pb.re("p (h m t) -> p m h t", h=4, m=2))
            for hh in range(4):
                for c in range(2):
                    ps = self.next_ps()
                    for mc in range(2):
                        self.mm(ps.v, Vt[b][:, mc, hh * 256 + c * 128: hh * 256 + (c + 1) * 128], prT[:, mc, hh, :],
                                start=(mc == 0), stop=(mc == 1))
                    self.evac(oT[:, 2 * hh + c, :], ps.v)
            for s in range(NS):
                pss = [self.next_ps(), self.next_ps()]
                for half in range(2):
                    for k in range(8):
                        self.mm(pss[half].v, oT[:, k, s * 128:(s + 1) * 128], wo[:, k, half * 512:(half + 1) * 512],
                                start=(k == 0), stop=(k == 7))
                for half in range(2):
                    self.tt('dve', xt[:, s, half * 512:(half + 1) * 512], xt[:, s, half * 512:(half + 1) * 512], pss[half].v, ALU.add)
                self.dma('sp', dst[i * TT + s * 128: i * TT + (s + 1) * 128, :], xt[:, s, :])

    def phaseC(self, es, src, dst, final=True):
        nc, S = self.nc, self.S
        TT = 256
        NS = TT // 128
        NJ = DFF // 128
        inp = self.inp
        self._stg_i = 0
        cols, ci = self.load_cols(es, "c_cols", [("g", inp["norm_ffn_g"]), ("cw0", inp["ffn_conv_w"][0]),
                                                 ("cw1", inp["ffn_conv_w"][1]), ("cw2", inp["ffn_conv_w"][2]),
                                                 ("cb", inp["ffn_conv_b"])])
        wi = self.sb(es, "c_wi", [128, 8, 2 * DFF], BF16)
        wo = self.sb(es, "c_wo", [128, NJ, D], BF16)
        gfin = self.sb(es, "c_gfin", [128, D], F32)
        self.dma('sp', gfin.v, inp["norm_final_g"].partition_broadcast(128))
        with ExitStack() as tes:
            stg = [self.sb(tes, "c_stg%d" % i, [128, 2048], F32) for i in range(3)]
            for k in range(8):
                self.load_weight(stg, wi[:, k, :], inp["w_ffn_in"][k * 128:(k + 1) * 128, :], 2 * DFF,
                                 scale=cols[:, ci["g"] + k: ci["g"] + k + 1])
            for j in range(NJ):
                self.load_weight(stg, wo[:, j, :], inp["w_ffn_out"][j * 128:(j + 1) * 128, :], D)
            S.barrier()
        xts = [self.sb(es, "c_xt%d" % i, [128, NS, D], F32) for i in range(2)]
        h = self.sb(es, "c_h", [128, NS, D], BF16)
        hT = self.sb(es, "c_hT", [128, 8, TT], BF16)
        actT = self.sb(es, "c_actT", [128, NJ, TT], BF16)
        junk = self.sb(es, "c_junk", [128, D], BF16)
        ss = self.sb(es, "c_ss", [128, 4], F32)
        rstd = self.sb(es, "c_rstd", [128, 4], F32)
        halo = self.sb(es, "c_halo", [128, NJ, 2], F32)
        NW = 3
        uw = [self.sb(es, "c_uw%d" % i, [128, 2 + TT], F32) for i in range(NW)]
        acc = [self.sb(es, "c_acc%d" % i, [128, TT], F32) for i in range(NW)]
        tmp = [self.sb(es, "c_tmp%d" % i, [128, TT], F32) for i in range(NW)]
        th = [self.sb(es, "c_th%d" % i, [128, TT], F32) for i in range(NW)]
        ntile = self.ntok // TT
        tiles_per_b = self.seq // TT

        def load(i):
            xt = xts[i % 2]
            self.dma('sp', xt.v, src[i * TT:(i + 1) * TT, :].rearrange("(s p) d -> p s d", p=128))

        load(0)
        for i in range(ntile):
            if S.need_reset():
                S.hard_barrier()
            xt = xts[i % 2]
            if i + 1 < ntile:
                load(i + 1)
            if i % tiles_per_b == 0:
                self.memset('pool', halo.v, 0.0)
            self.rms_h(xt, NS, ss, rstd, junk, h)
            self.transpose_h(h, NS, hT)
            for j in range(NJ):
                ps = self.next_ps()
                for half in range(2):
                    c0 = half * DFF + j * 128
                    for k in range(8):
                        self.mm(ps[:, half * TT:(half + 1) * TT], wi[:, k, c0:c0 + 128], hT[:, k, :], start=(k == 0), stop=(k == 7))
                u = uw[j % NW]; a = acc[j % NW]; tm = tmp[j % NW]; tg = th[j % NW]
                self.cp('act', u[:, 2:2 + TT], ps[:, 0:TT])
                self.cp('pool', u[:, 0:2], halo[:, j, :])
                w0 = cols[:, ci["cw0"] + j: ci["cw0"] + j + 1]
                w1 = cols[:, ci["cw1"] + j: ci["cw1"] + j + 1]
                w2 = cols[:, ci["cw2"] + j: ci["cw2"] + j + 1]
                cb = cols[:, ci["cb"] + j: ci["cb"] + j + 1]
                self.ts('dve', a.v, u[:, 0:TT], w0, cb, ALU.mult, ALU.add)
                self.stt('dve', a.v, u[:, 1:1 + TT], w1, a.v, ALU.mult, ALU.add)
                self.stt('dve', a.v, u[:, 2:2 + TT], w2, a.v, ALU.mult, ALU.add)
                self.cp('pool', halo[:, j, :], u[:, TT:TT + 2])
                self.act(tg.v, a.v, AF.Gelu_apprx_tanh)
                self.tt('dve', actT[:, j, :], tg.v, ps[:, TT:2 * TT], ALU.mult)
            for s in range(NS):
                pss = [self.next_ps(), self.next_ps()]
                for half in range(2):
                    for j in range(NJ):
                        self.mm(pss[half].v, actT[:, j, s * 128:(s + 1) * 128], wo[:, j, half * 512:(half + 1) * 512],
                                start=(j == 0), stop=(j == NJ - 1))
                for half in range(2):
                    self.tt('dve', xt[:, s, half * 512:(half + 1) * 512], xt[:, s, half * 512:(half + 1) * 512], pss[half].v, ALU.add)
                if final:
                    self.act(junk.v, xt[:, s, :], AF.Square, accum=ss[:, 2 + s:3 + s])
                    self.ts('dve', rstd[:, 2 + s:3 + s], ss[:, 2 + s:3 + s], 1.0 / D, NORM_EPS, ALU.mult, ALU.add)
                    self.rsqrt(rstd[:, 2 + s:3 + s], rstd[:, 2 + s:3 + s])
                    self.stt('dve', xt[:, s, :], xt[:, s, :], rstd[:, 2 + s:3 + s], gfin.v, ALU.mult, ALU.mult)
                self.dma('sp', dst[i * TT + s * 128: i * TT + (s + 1) * 128, :], xt[:, s, :])


_PROG_CACHE = {}


def get_prog(nb=4, seq=2048, phases="LRMBC"):
    key = (nb, seq, phases)
    if key not in _PROG_CACHE:
        _PROG_CACHE[key] = Prog(nb, seq, phases)
    return _PROG_CACHE[key]


_W_NAMES = ["norm_mix_g", "w_in", "b_in", "mu_shift", "w0", "w_lora_up", "a0", "a_lora_up", "g_lora_up", "k_k", "k_a",
            "r_k", "lnx_g", "lnx_b", "w_branch_a", "conv_b_w", "conv_b_b", "w_rg_a", "b_rg_a", "w_rg_x", "b_rg_x",
            "lru_lambda", "w_branch_b", "w_mix_out", "norm_x_g", "norm_mem_g", "w_cq", "w_ckv", "w_co", "norm_ffn_g",
            "w_ffn_in", "ffn_conv_w", "ffn_conv_b", "w_ffn_out", "norm_final_g"]


def run(inputs, ncores=8, nb=4, seq=2048, phases="LRMBC"):
    prog = get_prog(nb, seq, phases)
    shapes = {k: tuple(v.shape) for k, v in prog.inp.items()}
    shared = {}
    for k in _W_NAMES:
        a = np.ascontiguousarray(np.asarray(inputs[k], dtype=np.float32))
        shared[k] = a.reshape(shapes[k])
    x = np.asarray(inputs["x"], dtype=np.float32)
    mem = np.asarray(inputs["mem"], dtype=np.float32)
    in_maps = []
    for c in range(ncores):
        m = dict(shared)
        m["x"] = np.ascontiguousarray(x[c * nb:(c + 1) * nb, :seq]).reshape(nb * seq, D)
        m["mem"] = np.ascontiguousarray(mem[c * nb:(c + 1) * nb]).reshape(nb * NMEM, D)
        in_maps.append(m)
    res = run_bass_kernel_spmd(prog.nc, in_maps, core_ids=list(range(ncores)))
    outs = [np.asarray(r["out"]).reshape(nb, seq, D) for r in res.results]
    return np.concatenate(outs, axis=0)


def kernel(**inputs):
    return run(inputs).astype(np.float32)
```

```python
import numpy as np
from contextlib import ExitStack
import concourse.bass as bass
import concourse.mybir as mybir
from concourse.bass_utils import run_bass_kernel_spmd

F32 = mybir.dt.float32
BF16 = mybir.dt.bfloat16
AF = mybir.ActivationFunctionType
ALU = mybir.AluOpType
AX = mybir.AxisListType

D = 1024
NMEM = 256
AW = 512
RWKV_COLS = 1792
P_IN = 4864
DFF = 2816
NORM_EPS = 1e-6
LNX_EPS = 64e-5
SAME_ENGINE_SYNC = True
import os as _os
RSTOP = int(_os.environ.get("RSTOP", "0"))
LVAR = int(_os.environ.get("LVAR", "2"))


_ALL_TILES = []


class T:
    def __init__(self, ap, name=""):
        self.ap = ap if isinstance(ap, bass.AP) else ap[:]
        self.w = None
        self.r = []
        self.name = name
        self.dsem = None
        _ALL_TILES.append(self)

    def __getitem__(self, k):
        return V([self], self.ap[k])

    @property
    def v(self):
        return V([self], self.ap)


class V:
    def __init__(self, ts, ap):
        self.ts = ts
        self.ap = ap

    def __getitem__(self, k):
        return V(self.ts, self.ap[k])

    def re(self, pat, **kw):
        return V(self.ts, self.ap.rearrange(pat, **kw))

    def bc(self, shape):
        return V(self.ts, self.ap.to_broadcast(shape))

    def un(self, axis):
        return V(self.ts, self.ap.unsqueeze(axis))

    def bitcast(self, dt):
        return V(self.ts, self.ap.bitcast(dt))


def _ap(x):
    return x.ap if isinstance(x, V) else x


class Sync:
    def __init__(self, nc, es, n_dma_sems=64):
        self.nc = nc
        self.es = es
        self.engs = {'pe': nc.tensor, 'act': nc.scalar, 'dve': nc.vector, 'pool': nc.gpsimd, 'sp': nc.sync}
        self.sem = {}
        self.cnt = {}
        self.seen = {}
        for e in self.engs:
            self.sem[e] = es.enter_context(nc.semaphore("c_" + e))
            self.cnt[e] = 0
            self.seen[e] = {}
        self.dpool = [{'sem': es.enter_context(nc.semaphore("d%d" % i)), 'cnt': 0} for i in range(n_dma_sems)]
        self.dfree = list(range(n_dma_sems))
        self.n_inst = 0
        self.bar1 = es.enter_context(nc.semaphore("bar1"))
        self.bar2 = es.enter_context(nc.semaphore("bar2"))
        self.bar_k = 0

    def _wait(self, e, tok):
        if tok is None:
            return
        sem, val, src = tok
        if src == e and (e == 'pe' or not SAME_ENGINE_SYNC):
            return
        key = sem.name
        if self.seen[e].get(key, 0) >= val:
            return
        self.seen[e][key] = val
        self.engs[e].wait_ge(sem, val)

    def deps(self, e, reads, writes):
        for v in reads:
            if not isinstance(v, V):
                continue
            for t in v.ts:
                self._wait(e, t.w)
        for v in writes:
            for t in v.ts:
                self._wait(e, t.w)
                for tok in t.r:
                    self._wait(e, tok)

    def done(self, tok, reads, writes):
        for v in reads:
            if not isinstance(v, V):
                continue
            for t in v.ts:
                t.r.append(tok)
                if len(t.r) > 24:
                    t.r = t.r[-24:] if False else t.r
        for v in writes:
            for t in v.ts:
                t.w = tok
                t.r = []

    def op(self, e, fn, reads, writes, signal=True):
        self.deps(e, reads, writes)
        inst = fn(self.engs[e])
        self.n_inst += 1
        if signal:
            self.cnt[e] += 1
            inst.then_inc(self.sem[e], 1)
            tok = (self.sem[e], self.cnt[e], e)
        else:
            tok = (self.sem[e], self.cnt[e] + 1, e)
        self.done(tok, reads, writes)
        return inst

    def _dsem(self, t):
        if t.dsem is None:
            if not self.dfree:
                raise RuntimeError("out of dma semaphores")
            t.dsem = self.dpool[self.dfree.pop(0)]
        return t.dsem

    def dma(self, q, out, in_, **kw):
        reads = [in_] if isinstance(in_, V) else []
        writes = [out] if isinstance(out, V) else []
        self.deps(q, reads, writes)
        sbv = out if isinstance(out, V) else in_
        ds = self._dsem(sbv.ts[0])
        inst = self.engs[q].dma_start(out=_ap(out), in_=_ap(in_), **kw)
        self.n_inst += 1
        ds['cnt'] += 16
        inst.then_inc(ds['sem'], 16)
        tok = (ds['sem'], ds['cnt'], 'dma')
        self.done(tok, reads, writes)
        return tok

    def barrier(self):
        toks = [(self.sem[f], self.cnt[f], f) for f in self.engs if self.cnt[f] > 0]
        toks += [(d['sem'], d['cnt'], 'dma') for d in self.dpool if d['cnt'] > 0]
        for e in self.engs:
            for tok in toks:
                if tok[2] == e:
                    continue
                self._wait(e, tok)

    def release_dma_sems(self):
        self.dfree = list(range(len(self.dpool)))
        for t in _ALL_TILES:
            t.dsem = None

    def need_reset(self, limit=2600):
        return max(self.cnt.values()) > limit or max(d['cnt'] for d in self.dpool) > limit

    def hard_barrier(self):
        self.barrier()
        self.bar_k += 1
        k = self.bar_k
        for e in self.engs:
            self.engs[e].sem_inc(self.bar1, 1)
        sp = self.engs['sp']
        sp.wait_ge(self.bar1, len(self.engs) * k)
        for e in self.engs:
            if self.cnt[e] > 0:
                sp.sem_clear(self.sem[e])
        for d in self.dpool:
            if d['cnt'] > 0:
                sp.sem_clear(d['sem'])
        sp.sem_inc(self.bar2, 1)
        for e in self.engs:
            if e != 'sp':
                self.engs[e].wait_ge(self.bar2, k)
        for e in self.engs:
            self.cnt[e] = 0
            self.seen[e] = {}
        for d in self.dpool:
            d['cnt'] = 0
        for t in _ALL_TILES:
            t.w = None
            t.r = []


class Prog:
    def __init__(self, nb=4, seq=2048, phases="LRMBC", dbg=False):
        self.nb, self.seq, self.phases, self.dbg = nb, seq, phases, dbg
        self.ntok = nb * seq
        nc = self.nc = bass.Bass("TRN2", target_bir_lowering=False)
        self.inp = {}
        del _ALL_TILES[:]

        def din(name, shape):
            self.inp[name] = nc.dram_tensor(name, list(shape), F32, kind="ExternalInput").ap()
            return self.inp[name]

        ntok = self.ntok
        din("x", [ntok, D])
        din("mem", [nb * NMEM, D])
        din("norm_mix_g", [D]); din("w_in", [D, P_IN]); din("b_in", [P_IN]); din("mu_shift", [RWKV_COLS])
        din("w0", [AW]); din("w_lora_up", [64, AW]); din("a0", [AW]); din("a_lora_up", [64, AW])
        din("g_lora_up", [128, AW]); din("k_k", [AW]); din("k_a", [AW]); din("r_k", [AW])
        din("lnx_g", [AW]); din("lnx_b", [AW]); din("w_branch_a", [AW, D])
        din("conv_b_w", [4, AW]); din("conv_b_b", [AW]); din("w_rg_a", [8, 64, 64]); din("b_rg_a", [AW])
        din("w_rg_x", [8, 64, 64]); din("b_rg_x", [AW]); din("lru_lambda", [AW]); din("w_branch_b", [AW, D])
        din("w_mix_out", [D, D]); din("norm_x_g", [D]); din("norm_mem_g", [D]); din("w_cq", [D, D])
        din("w_ckv", [D, 2 * D]); din("w_co", [D, D]); din("norm_ffn_g", [D]); din("w_ffn_in", [D, 2 * DFF])
        din("ffn_conv_w", [3, DFF]); din("ffn_conv_b", [DFF]); din("w_ffn_out", [DFF, D]); din("norm_final_g", [D])
        self.out = nc.dram_tensor("out", [ntok, D], F32, kind="ExternalOutput").ap()
        self.x1 = nc.dram_tensor("x1s", [ntok, D], F32).ap()
        self.x2 = nc.dram_tensor("x2s", [ntok, D], F32).ap()
        self.dbg_out = {}

        with ExitStack() as es:
            self.es = es
            self.S = Sync(nc, es)
            self.ps = [T(es.enter_context(nc.psum_tensor("ps%d" % i, [128, 512], F32)), "ps%d" % i) for i in range(8)]
            self.ps_i = 0
            self.ident = self.sb(es, "ident", [128, 128], BF16)
            self.identf = self.sb(es, "identf", [128, 128], F32)
            for idt in (self.ident, self.identf):
                self.S.op('pool', lambda e: e.memset(idt.ap[:], 0.0), [], [idt.v])
                self.S.op('pool', lambda e: e.affine_select(idt.ap[:], idt.ap[:], pattern=[[-1, 128]],
                                                           compare_op=ALU.not_equal, fill=1.0, base=0,
                                                           channel_multiplier=1), [idt.v], [idt.v])
            self.mhalf = self.sb(es, "mhalf", [128, 512], F32)
            self.memset('pool', self.mhalf.v, -0.5)
            self.phalf = self.sb(es, "phalf", [128, 512], F32)
            self.memset('pool', self.phalf.v, 0.5)
            self.yas = nc.dram_tensor("yas", [AW, ntok], BF16).ap()
            self.ybs = nc.dram_tensor("ybs", [AW, ntok], BF16).ap()
            src = {'L': self.inp["x"], 'R': self.inp["x"], 'M': self.inp["x"], 'B': self.x1, 'C': self.x2}
            dst = {'L': None, 'R': None, 'M': self.x1, 'B': self.x2, 'C': self.out}
            order = [p for p in "LRMBC" if p in phases]
            chain = [p for p in order if p in "MBC"]
            for i, p in enumerate(order):
                s_ap, d_ap = src[p], dst[p]
                if p in chain:
                    if chain.index(p) == 0:
                        s_ap = self.inp["x"]
                    if chain.index(p) == len(chain) - 1:
                        d_ap = self.out
                with ExitStack() as pes:
                    getattr(self, "phase" + p)(pes, s_ap, d_ap, final=(p == 'C'))
                    self.S.hard_barrier()
                self.S.release_dma_sems()
            self.S.barrier()

    def sb(self, es, name, shape, dt):
        return T(es.enter_context(self.nc.sbuf_tensor(name, list(shape), dt)), name)

    def next_ps(self):
        t = self.ps[self.ps_i]
        self.ps_i = (self.ps_i + 1) % 8
        return t

    def act(self, out, in_, func, bias=0.0, scale=1.0, accum=None):
        rd = [in_] + [a for a in (bias, scale) if isinstance(a, V)]
        wr = [out] + ([accum] if accum is not None else [])
        kw = {}
        if accum is not None:
            kw['accum_out'] = accum.ap
        return self.S.op('act', lambda e: e.activation(out=out.ap, in_=in_.ap, func=func, bias=_ap(bias),
                                                       scale=_ap(scale), **kw), rd, wr)

    def tt(self, eng, out, in0, in1, op):
        return self.S.op(eng, lambda e: e.tensor_tensor(out=out.ap, in0=in0.ap, in1=in1.ap, op=op), [in0, in1], [out])

    def ts(self, eng, out, in0, s1, s2, op0, op1=None):
        rd = [in0] + [a for a in (s1, s2) if isinstance(a, V)]
        if op1 is None:
            return self.S.op(eng, lambda e: e.tensor_scalar(out=out.ap, in0=in0.ap, scalar1=_ap(s1), scalar2=None,
                                                            op0=op0), rd, [out])
        return self.S.op(eng, lambda e: e.tensor_scalar(out=out.ap, in0=in0.ap, scalar1=_ap(s1), scalar2=_ap(s2),
                                                        op0=op0, op1=op1), rd, [out])

    def stt(self, eng, out, in0, scalar, in1, op0, op1):
        rd = [in0, in1] + ([scalar] if isinstance(scalar, V) else [])
        return self.S.op(eng, lambda e: e.scalar_tensor_tensor(out=out.ap, in0=in0.ap, scalar=_ap(scalar), in1=in1.ap,
                                                               op0=op0, op1=op1), rd, [out])

    def cp(self, eng, out, in_):
        if eng == 'act':
            return self.act(out, in_, AF.Copy)
        return self.S.op(eng, lambda e: e.tensor_copy(out=out.ap, in_=in_.ap), [in_], [out])

    def rsqrt(self, out, in_):
        shp = list(in_.ap.shape)
        if shp[-1] > 8:
            self.act(out, in_, AF.Ln)
            return self.act(out, out, AF.Exp, scale=-0.5)
        mh = self.mhalf[0:shp[0], 0:shp[-1]]
        if len(shp) == 3:
            mh = mh.un(1).bc(shp)
        return self.tt('pool', out, in_, mh, ALU.pow)

    def memset(self, eng, out, val):
        return self.S.op(eng, lambda e: e.memset(out.ap, val), [], [out])

    def mm(self, out, lhsT, rhs, start, stop, signal=None):
        if signal is None:
            signal = stop
        return self.S.op('pe', lambda e: e.matmul(out.ap, lhsT=lhsT.ap, rhs=rhs.ap, start=start, stop=stop),
                         [lhsT, rhs], [out], signal=signal)

    def tr(self, out, in_, signal=True, f32=False):
        idt = self.identf if f32 else self.ident
        n = in_.ap.shape[0]
        return self.S.op('pe', lambda e: e.transpose(out.ap, in_.ap, idt.ap[0:n, 0:n]), [in_, idt.v], [out],
                         signal=signal)

    def dma(self, q, out, in_, **kw):
        return self.S.dma(q, out, in_, **kw)

    def load_weight(self, stg, dst, src_rows, ncols, scale=None, piece=2048):
        rows = src_rows.shape[0]
        c0 = 0
        while c0 < ncols:
            n = min(piece, ncols - c0)
            st = stg[self._stg_i % len(stg)]
            eng = ('dve', 'act')[self._stg_i % 2]
            self._stg_i += 1
            self.dma('sp', st[0:rows, 0:n], src_rows[:, c0:c0 + n])
            o = dst[:, c0:c0 + n]
            i = st[0:rows, 0:n]
            if scale is None:
                self.cp(eng, o, i)
            elif eng == 'act':
                if isinstance(scale, V):
                    self.act(o, i, AF.Copy, scale=scale)
                else:
                    self.act(o, i, AF.Copy, scale=float(scale))
            else:
                self.ts(eng, o, i, scale, None, ALU.mult)
            c0 += n

    def load_cols(self, es, name, specs):
        cols = {}
        n = 0
        for k, v in specs:
            cols[k] = n
            n += (v.shape[0] + 127) // 128
        res = self.sb(es, name, [128, n], F32)
        with ExitStack() as tes:
            ngrp = (n + 127) // 128
            stage = [self.sb(tes, name + "_st%d" % g, [128, 128], F32) for g in range(ngrp)]
            for st in stage:
                self.memset('pool', st.v, 0.0)
            for k, v in specs:
                L = v.shape[0]
                m = (L + 127) // 128
                c = cols[k]
                r = 0
                while r < m:
                    g, rr = divmod(c + r, 128)
                    cnt = min(m - r, 128 - rr)
                    if L >= 128:
                        self.dma('sp', stage[g][rr:rr + cnt, :], v[r * 128:(r + cnt) * 128].rearrange("(m p) -> m p", p=128))
                    else:
                        self.dma('sp', stage[g][rr:rr + 1, 0:L], v.rearrange("(m p) -> m p", m=1))
                    r += cnt
            for g in range(ngrp):
                w = min(128, n - g * 128)
                ps = self.next_ps()
                self.tr(ps[:, 0:w], stage[g][0:w, :], f32=True)
                self.cp('dve', res[:, g * 128:g * 128 + w], ps[:, 0:w])
            self.S.barrier()
        return res, cols

    def rms_h(self, xt, nsub, ss, rstd, junk, h):
        for s in range(nsub):
            self.act(junk.v, xt[:, s, :], AF.Square, accum=ss[:, s:s + 1])
        self.ts('dve', rstd[:, 0:nsub], ss[:, 0:nsub], 1.0 / D, NORM_EPS, ALU.mult, ALU.add)
        self.rsqrt(rstd[:, 0:nsub], rstd[:, 0:nsub])
        for s in range(nsub):
            self.ts('dve', h[:, s, :], xt[:, s, :], rstd[:, s:s + 1], None, ALU.mult)

    def transpose_h(self, h, nsub, hT, evac=('act', 'dve')):
        tw = nsub * 128
        per_bank = 1024 // tw
        c = 0
        i = 0
        while c < 8:
            ps = self.next_ps()
            pb = ps.v.bitcast(BF16)
            nchunk = min(per_bank, 8 - c)
            for cc in range(nchunk):
                for s in range(nsub):
                    last = (cc == nchunk - 1 and s == nsub - 1)
                    self.tr(pb[:, cc * tw + s * 128: cc * tw + (s + 1) * 128], h[:, s, (c + cc) * 128:(c + cc + 1) * 128],
                            signal=last)
            self.cp(evac[i % len(evac)], hT[:, c:c + nchunk, :], pb[:, 0:nchunk * tw].re("p (c t) -> p c t", c=nchunk))
            c += nchunk
            i += 1

    def norm_from_dram(self, src, tok0, NS, xbufs, sst, rst, junk, h):
        for s in range(NS):
            xb = xbufs[self._xb_i % len(xbufs)]
            self._xb_i += 1
            self.dma('sp', xb.v, src[tok0 + s * 128: tok0 + (s + 1) * 128, :])
            self.act(junk.v, xb.v, AF.Square, accum=sst[s].v)
            self.ts('dve', rst[s].v, sst[s].v, 1.0 / D, NORM_EPS, ALU.mult, ALU.add)
            self.rsqrt(rst[s].v, rst[s].v)
            self.ts('dve', h[:, s, :], xb.v, rst[s].v, None, ALU.mult)

    def norm_bufs(self, es, pfx, NS):
        xbufs = [self.sb(es, pfx + "_xb%d" % i, [128, D], F32) for i in range(2)]
        sst = [self.sb(es, pfx + "_ss%d" % i, [128, 1], F32) for i in range(NS)]
        rst = [self.sb(es, pfx + "_rs%d" % i, [128, 1], F32) for i in range(NS)]
        junk = self.sb(es, pfx + "_junk", [128, D], BF16)
        self._xb_i = 0
        return xbufs, sst, rst, junk

    def phaseL(self, es, src, dst, final=False):
        S = self.S
        inp = self.inp
        TT = 512
        NS = 4
        self._stg_i = 0
        self._ev_i = 0
        cols, ci = self.load_cols(es, "l_cols", [
            ("g", inp["norm_mix_g"]), ("bin", inp["b_in"][1792:2816]),
            ("cw0", inp["conv_b_w"][0]), ("cw1", inp["conv_b_w"][1]), ("cw2", inp["conv_b_w"][2]), ("cw3", inp["conv_b_w"][3]),
            ("cbb", inp["conv_b_b"]), ("bra", inp["b_rg_a"]), ("brx", inp["b_rg_x"]), ("lam", inp["lru_lambda"])])
        hb = self.sb(es, "l_hb", [128, 8], F32)
        self.ts('dve', hb[:, 0:4], cols[:, ci["bra"]:ci["bra"] + 4], 0.5, None, ALU.mult)
        self.ts('dve', hb[:, 4:8], cols[:, ci["brx"]:ci["brx"] + 4], 0.5, None, ALU.mult)
        cA = self.sb(es, "l_cA", [128, 8], F32)
        lt = self.sb(es, "l_lt", [128, 4], F32)
        self.act(lt.v, cols[:, ci["lam"]:ci["lam"] + 4], AF.Exp, scale=-1.0)
        self.ts('dve', lt.v, lt.v, 1.0, None, ALU.add)
        self.act(lt.v, lt.v, AF.Ln)
        self.ts('dve', cA[:, 0:4], lt.v, -4.0, None, ALU.mult)
        self.ts('dve', cA[:, 4:8], lt.v, -8.0, None, ALU.mult)
        win = self.sb(es, "l_win", [128, 8, 1024], BF16)
        wg = self.sb(es, "l_wg", [128, 2, 4, 128], BF16)
        with ExitStack() as tes:
            stg = [self.sb(tes, "l_stg%d" % i, [128, 1024], F32) for i in range(3)]
            for k in range(8):
                self.load_weight(stg, win[:, k, :], inp["w_in"][k * 128:(k + 1) * 128, 1792:2816], 1024,
                                 scale=cols[:, ci["g"] + k: ci["g"] + k + 1], piece=1024)
            for gi, nm in enumerate(("w_rg_a", "w_rg_x")):
                st = stg[gi]
                self.memset('pool', st.v, 0.0)
                for blk in range(8):
                    c, hl = divmod(blk, 2)
                    self.dma('sp', st[hl * 64:(hl + 1) * 64, c * 128 + hl * 64: c * 128 + (hl + 1) * 64], inp[nm][blk])
                self.cp('dve', wg[:, gi, :, :], st[:, 0:512].re("p (c m) -> p c m", c=4))
            S.barrier()
        xbufs, sst, rst, junk = self.norm_bufs(es, "l", NS)
        h = self.sb(es, "l_h", [128, NS, D], BF16)
        hT = self.sb(es, "l_hT", [128, 8, TT], BF16)
        PX = [self.sb(es, "l_px%d" % c, [128, 3 + TT], F32) for c in range(4)]
        GY = [self.sb(es, "l_gy%d" % c, [128, TT], F32) for c in range(4)]
        YB = [self.sb(es, "l_yb%d" % c, [128, TT], BF16) for c in range(4)]
        carry = [self.sb(es, "l_cy%d" % c, [128, 1], F32) for c in range(4)]
        NW = 2
        def wk(nm, dt=F32, n=NW):
            return [self.sb(es, "l_%s%d" % (nm, i), [128, TT], dt) for i in range(n)]
        acc = wk("acc", n=4); xbb = wk("xbb", BF16, n=4); ta = wk("ta", n=4); tx = wk("tx", n=4)
        av = wk("av"); a2 = wk("a2"); u = wk("u"); hl_ = wk("hl")
        ntile = self.ntok // TT
        tiles_per_b = self.seq // TT
        hs = [h, self.sb(es, "l_h1", [128, NS, D], BF16)]
        hTs = [hT, self.sb(es, "l_hT1", [128, 8, TT], BF16)]

        def front(i):
            self.norm_from_dram(src, i * TT, NS, xbufs, sst, rst, junk, hs[i % 2])
            self.transpose_h(hs[i % 2], NS, hTs[i % 2])

        def midA(i):
            hT_ = hTs[i % 2]
            first = (i % tiles_per_b == 0)
            if first:
                for c in range(4):
                    self.memset('pool', PX[c][:, 0:3], 0.0)
                    self.memset('pool', carry[c].v, 0.0)
            for c in range(4):
                ps = self.next_ps()
                for k in range(8):
                    self.mm(ps.v, win[:, k, c * 128:(c + 1) * 128], hT_[:, k, :], start=(k == 0), stop=(k == 7))
                self.act(PX[c][:, 3:3 + TT], ps.v, AF.Identity, bias=cols[:, ci["bin"] + c: ci["bin"] + c + 1])
            for c in range(4):
                ps = self.next_ps()
                for k in range(8):
                    self.mm(ps.v, win[:, k, 512 + c * 128: 512 + (c + 1) * 128], hT_[:, k, :], start=(k == 0), stop=(k == 7))
                self.act(GY[c].v, ps.v, AF.Gelu_apprx_tanh, bias=cols[:, ci["bin"] + 4 + c: ci["bin"] + 5 + c])
            for c in range(4):
                A = acc[c]; XB = xbb[c]; TA = ta[c]; TX = tx[c]
                cw = [cols[:, ci["cw%d" % j] + c: ci["cw%d" % j] + c + 1] for j in range(4)]
                self.ts('dve', A.v, PX[c][:, 0:TT], cw[0], cols[:, ci["cbb"] + c: ci["cbb"] + c + 1], ALU.mult, ALU.add)
                for j in range(1, 4):
                    self.stt('dve', A.v, PX[c][:, j:j + TT], cw[j], A.v, ALU.mult, ALU.add)
                self.cp('pool', PX[c][:, 0:3], PX[c][:, TT:TT + 3])
                self.cp('dve', XB.v, A.v)
                psa = self.next_ps()
                self.mm(psa.v, wg[:, 0, c, :], XB.v, start=True, stop=True)
                psx = self.next_ps()
                self.mm(psx.v, wg[:, 1, c, :], XB.v, start=True, stop=True)
                self.act(TA.v, psa.v, AF.Tanh, scale=0.5, bias=hb[:, c:c + 1])
                self.act(TX.v, psx.v, AF.Tanh, scale=0.5, bias=hb[:, 4 + c:5 + c])

        def midB(i):
            tok0 = i * TT
            first = (i % tiles_per_b == 0)
            for c in range(4):
                A = acc[c]; TA = ta[c]; TX = tx[c]; AV = av[c % NW]; A2 = a2[c % NW]; U = u[c % NW]; HL = hl_[c % NW]
                self.act(AV.v, TA.v, AF.Exp, scale=cA[:, c:c + 1], bias=cA[:, c:c + 1])
                self.act(A2.v, TA.v, AF.Exp, scale=cA[:, 4 + c:5 + c], bias=cA[:, 4 + c:5 + c])
                self.ts('dve', A2.v, A2.v, -1.0, 1.0, ALU.mult, ALU.add)
                self.ts('dve', A2.v, A2.v, 1e-30, None, ALU.max)
                self.act(A2.v, A2.v, AF.Ln)
                self.act(A2.v, A2.v, AF.Exp, scale=0.5)
                if first:
                    self.memset('pool', A2[:, 0:1], 1.0)
                self.stt('dve', U.v, TX.v, 1.0, A.v, ALU.add, ALU.mult)
                self.tt('dve', U.v, U.v, A2.v, ALU.mult)
                self.S.op('dve', lambda e: e.tensor_tensor_scan(HL.ap, AV.ap, U.ap, carry[c].ap, ALU.mult, ALU.add),
                          [AV.v, U.v, carry[c].v], [HL.v])
                self.cp('pool', carry[c].v, HL[:, TT - 1:TT])
                self.tt('dve', YB[c].v, HL.v, GY[c].v, ALU.mult)
                self.dma('sp', self.ybs[c * 128:(c + 1) * 128, tok0:tok0 + TT], YB[c].v)

        front(0)
        for i in range(ntile):
            if S.need_reset():
                S.hard_barrier()
            midA(i)
            if i + 1 < ntile:
                front(i + 1)
            midB(i)

    def phaseR(self, es, src, dst, final=False):
        S = self.S
        inp = self.inp
        TT = 256
        NS = 2
        NQ = 4
        CH = 64
        self._stg_i = 0
        self._ev_i = 0
        cols, ci = self.load_cols(es, "r_cols", [
            ("g", inp["norm_mix_g"]), ("bin", inp["b_in"][0:RWKV_COLS]), ("mu", inp["mu_shift"]),
            ("w0", inp["w0"]), ("a0", inp["a0"]), ("kk", inp["k_k"]), ("ka", inp["k_a"]), ("rk", inp["r_k"])])

        def col(key, j):
            return cols[:, ci[key] + j: ci[key] + j + 1]

        der = self.sb(es, "r_der", [128, 32], F32)
        self.ts('dve', der[:, 0:14], cols[:, ci["mu"]:ci["mu"] + 14], -1.0, 1.0, ALU.mult, ALU.add)
        self.ts('dve', der[:, 14:18], cols[:, ci["w0"]:ci["w0"] + 4], 0.5, None, ALU.mult)
        self.ts('dve', der[:, 18:22], cols[:, ci["a0"]:ci["a0"] + 4], 0.5, None, ALU.mult)
        self.ts('dve', der[:, 22:26], cols[:, ci["ka"]:ci["ka"] + 4], 0.5, None, ALU.mult)
        self.ts('dve', der[:, 26:30], cols[:, ci["ka"]:ci["ka"] + 4], -0.5, 1.0, ALU.mult, ALU.add)
        rkb = self.sb(es, "r_rkb", [128, 4], BF16)
        self.cp('dve', rkb.v, cols[:, ci["rk"]:ci["rk"] + 4])
        m64 = [self.sb(es, "r_m64_%d" % i, [64, 64], BF16) for i in range(3)]
        masks = [self.sb(es, "r_mask%d" % i, [128, 128], BF16) for i in range(3)]
        specs = [([[1, 64]], -1, ALU.is_gt), ([[-1, 64]], 1, ALU.is_gt), ([[1, 64]], -1, ALU.is_ge)]
        for mi in range(3):
            pat, cm, op = specs[mi]
            self.memset('pool', m64[mi].v, 1.0)
            self.S.op('pool', lambda e: e.affine_select(m64[mi].ap, m64[mi].ap, pattern=pat, compare_op=op, fill=0.0,
                                                        base=0, channel_multiplier=cm), [m64[mi].v], [m64[mi].v])
            for a in range(2):
                for b_ in range(2):
                    self.dma('sp', masks[mi][a * 64:(a + 1) * 64, b_ * 64:(b_ + 1) * 64], m64[mi].v)
        mask_su, mask_sl, mask_u = masks
        bones = self.sb(es, "r_bones", [128, 128], BF16)
        self.memset('pool', bones.v, 0.0)
        self.memset('pool', bones[0:64, 0:64], 1.0)
        self.memset('pool', bones[64:128, 64:128], 1.0)
        m01 = self.sb(es, "r_m01", [128, TT], F32)
        self.memset('pool', m01.v, 1.0)
        self.memset('pool', m01.v.re("p (q s) -> p q s", s=CH)[:, :, 0:1], 0.0)
        lnxg = self.sb(es, "r_lnxg", [128, 4, 64], F32)
        lnxb = self.sb(es, "r_lnxb", [128, 4, 64], F32)
        for c in range(4):
            for hl in range(2):
                hh = 2 * c + hl
                self.dma('sp', lnxg[hl * 64:(hl + 1) * 64, c, :], inp["lnx_g"][hh * 64:(hh + 1) * 64].partition_broadcast(64))
                self.dma('sp', lnxb[hl * 64:(hl + 1) * 64, c, :], inp["lnx_b"][hh * 64:(hh + 1) * 64].partition_broadcast(64))
        win = self.sb(es, "r_win", [128, 8, RWKV_COLS], BF16)
        wlora = self.sb(es, "r_wlora", [128, 2, AW], BF16)
        gup = self.sb(es, "r_gup", [128, AW], BF16)
        with ExitStack() as tes:
            stg = [self.sb(tes, "r_stg%d" % i, [128, RWKV_COLS], F32) for i in range(3)]
            for k in range(8):
                self.load_weight(stg, win[:, k, :], inp["w_in"][k * 128:(k + 1) * 128, 0:RWKV_COLS], RWKV_COLS,
                                 scale=cols[:, ci["g"] + k: ci["g"] + k + 1], piece=RWKV_COLS)
            st = stg[0]
            self.memset('pool', st[:, 0:2 * AW], 0.0)
            self.dma('sp', st[0:64, 0:AW], inp["w_lora_up"])
            self.dma('sp', st[64:128, AW:2 * AW], inp["a_lora_up"])
            self.cp('dve', wlora.v, st[:, 0:2 * AW].re("p (a m) -> p a m", a=2))
            st = stg[1]
            self.dma('sp', st[:, 0:AW], inp["g_lora_up"])
            self.cp('dve', gup.v, st[:, 0:AW])
            S.barrier()
        xbufs, sst, rst, junk = self.norm_bufs(es, "r", NS)
        h = self.sb(es, "r_h", [128, NS, D], BF16)
        hT = self.sb(es, "r_hT", [128, 8, TT], BF16)
        PW = [self.sb(es, "r_pw%d" % i, [128, 1 + TT], F32) for i in range(3)]
        pcarry = [self.sb(es, "r_pc%d" % m, [128, 1], F32) for m in range(14)]
        Sx = [self.sb(es, "r_s%d" % m, [128, TT], F32) for m in range(14)]
        ltmp = [self.sb(es, "r_lt%d" % i, [128, TT], F32) for i in range(2)]
        LB = self.sb(es, "r_lb", [128, TT], BF16)
        SGd = self.sb(es, "r_sgd", [128, NQ, 128], BF16)
        NW = 2

        def wk(nm, dt=F32):
            return [self.sb(es, "r_%s%d" % (nm, i), [128, TT], dt) for i in range(NW)]

        logw = [self.sb(es, "r_logw%d" % i, [128, TT], F32) for i in range(4)]
        ta = [self.sb(es, "r_ta%d" % i, [128, TT], F32) for i in range(4)]
        cum = [self.sb(es, "r_cum%d" % i, [128, TT], F32) for i in range(4)]
        egm1 = wk("egm1"); eg = wk("eg"); eig = wk("eig"); egc = wk("egc")
        kk = wk("kk"); kk2 = wk("kk2", BF16); rn = wk("rn"); kkn = wk("kkn"); kmod = wk("kmod"); bv = wk("bv")
        names = ["AT", "BT", "KT", "RT", "BGT", "KGT", "VB", "RK"]
        EXP = {n: [self.sb(es, "r_%s%d" % (n, c), [128, NQ, 128], BF16) for c in range(4)] for n in names}
        for n in names:
            for c in range(4):
                self.memset('pool', EXP[n][c].v, 0.0)
        GC = self.sb(es, "r_gc", [128, 4, NQ], F32)
        BKG = [self.sb(es, "r_bkg%d" % q, [128, 2, 4, 128], BF16) for q in range(NQ)]
        Vst = [self.sb(es, "r_vst%d" % q, [128, 4, 64], BF16) for q in range(NQ)]
        Nb = [[self.sb(es, "r_nb%d_%d" % (q, i), [128, 4, 128], BF16) for i in range(2)] for q in range(NQ)]
        Lb = [[self.sb(es, "r_lb%d_%d" % (q, i), [128, 4, 128], BF16) for i in range(2)] for q in range(NQ)]
        Pb = [self.sb(es, "r_pb%d" % q, [128, 4, 128], BF16) for q in range(NQ)]
        LakT = [self.sb(es, "r_lak%d" % q, [128, 4, 128], BF16) for q in range(NQ)]
        MrbT = [self.sb(es, "r_mrb%d" % q, [128, 4, 128], BF16) for q in range(NQ)]
        MrkT = [self.sb(es, "r_mrk%d" % q, [128, 4, 128], BF16) for q in range(NQ)]
        H = self.sb(es, "r_H", [128, 4, 64], F32)
        Hb = self.sb(es, "r_Hb", [128, 4, 64], BF16)
        Xb = [self.sb(es, "r_Xb%d" % i, [128, 4, 64], BF16) for i in range(2)]
        Ub = [self.sb(es, "r_Ub%d" % i, [128, 4, 64], BF16) for i in range(2)]

        def yt(nm, shape=(128, 4, 64), dt=F32):
            return [self.sb(es, "r_%s%d" % (nm, i), list(shape), dt) for i in range(2)]

        Yv = yt("Yv"); Ysq = yt("Ysq"); Yn = yt("Yn"); Bn = yt("Bn"); Yf = yt("Yf", dt=BF16)
        st1 = yt("st1", (128, 4)); st2 = yt("st2", (128, 4)); mean = yt("mean", (128, 4)); var = yt("var", (128, 4))
        YT = [self.sb(es, "r_YT%d" % i, [64, 4, 2, TT], BF16) for i in range(2)]
        ident = self.ident
        ntile = self.ntok // TT
        tiles_per_b = self.seq // TT
        yas_v = self.yas.rearrange("(c hl i) t -> i c hl t", hl=2, i=64)

        for it in range(ntile):
            tok0 = it * TT
            if S.need_reset():
                S.hard_barrier()
            if it % tiles_per_b == 0:
                for m in range(14):
                    self.memset('pool', pcarry[m].v, 0.0)
                self.memset('pool', H.v, 0.0)
                self.memset('pool', Hb.v, 0.0)
            self.norm_from_dram(src, tok0, NS, xbufs, sst, rst, junk, h)
            self.transpose_h(h, NS, hT)
            for m in range(14):
                if m % 2 == 0:
                    ps = self.next_ps()
                o = (m % 2) * TT
                for k in range(8):
                    self.mm(ps[:, o:o + TT], win[:, k, m * 128:(m + 1) * 128], hT[:, k, :], start=(k == 0), stop=(k == 7))
                pw = PW[m % 3]
                self.act(pw[:, 1:1 + TT], ps[:, o:o + TT], AF.Identity, bias=col("bin", m))
                self.cp('pool', pw[:, 0:1], pcarry[m].v)
                self.cp('pool', pcarry[m].v, pw[:, TT:TT + 1])
                tmp = ltmp[m % 2]
                self.act(tmp.v, pw[:, 0:TT], AF.Copy, scale=col("mu", m))
                self.stt('dve', Sx[m].v, pw[:, 1:1 + TT], der[:, m:m + 1], tmp.v, ALU.mult, ALU.add)
            if RSTOP == 1:
                continue
            self.act(LB[0:64, :], Sx[12][0:64, :], AF.Tanh)
            self.cp('act', LB[64:128, :], Sx[12][64:128, :])
            tmp = ltmp[0]
            self.act(tmp.v, Sx[13].v, AF.Tanh, scale=0.5)
            for hl in range(2):
                self.ts('dve', SGd[:, :, hl * 64:(hl + 1) * 64], tmp.v.re("p (q s) -> p q s", s=CH), 0.5, 0.5, ALU.mult, ALU.add)
            if RSTOP == 21:
                continue
            for c in range(4):
                LW = logw[c]; TA = ta[c]; CU = cum[c]
                ps = self.next_ps()
                self.mm(ps[:, 0:TT], wlora[:, 0, c * 128:(c + 1) * 128], LB.v, start=True, stop=True)
                self.mm(ps[:, TT:2 * TT], wlora[:, 1, c * 128:(c + 1) * 128], LB.v, start=True, stop=True)
                self.act(LW.v, ps[:, 0:TT], AF.Tanh, scale=0.5, bias=der[:, 14 + c:15 + c])
                self.act(TA.v, ps[:, TT:2 * TT], AF.Tanh, scale=0.5, bias=der[:, 18 + c:19 + c])
                self.ts('dve', LW.v, LW.v, 1.0, -0.30326532985631671, ALU.add, ALU.mult)
                self.S.op('dve', lambda e: e.tensor_tensor_scan(CU.ap, m01.ap, LW.ap, 0.0, ALU.mult, ALU.add),
                          [m01.v, LW.v], [CU.v])
            for c in range(4):
                w = it * 4 + c
                r_, k_, v_ = Sx[c], Sx[4 + c], Sx[8 + c]
                LW = logw[c]; TA = ta[c]; CU = cum[c]
                E1 = egm1[w % NW]; EG = eg[w % NW]; EI = eig[w % NW]
                EC = egc[w % NW]; KK = kk[w % NW]; K2 = kk2[w % NW]; RN = rn[w % NW]; KN = kkn[w % NW]; KM = kmod[w % NW]
                BV = bv[w % NW]
                cu3 = CU.v.re("p (q s) -> p q s", s=CH)
                cuC = cu3[:, :, CH - 1:CH]
                self.tt('pool', E1.v, CU.v, LW.v, ALU.subtract)
                self.act(E1.v, E1.v, AF.Exp)
                self.act(EG.v, CU.v, AF.Exp)
                self.act(EI.v, CU.v, AF.Exp, scale=-1.0)
                self.tt('pool', EC.v.re("p (q s) -> p q s", s=CH), cuC.bc([128, NQ, CH]), cu3, ALU.subtract)
                self.act(EC.v, EC.v, AF.Exp)
                self.act(GC[:, c, :], cuC.re("p q o -> p (q o)"), AF.Exp)
                self.act(KK.v, k_.v, AF.Copy, scale=col("kk", c))
                self.act(K2.v, KK.v, AF.Square)
                psn = self.next_ps()
                self.mm(psn[:, 0:TT], bones.v, K2.v, start=True, stop=True)
                self.act(RN.v, psn[:, 0:TT], AF.Ln)
                self.act(RN.v, RN.v, AF.Exp, scale=-0.5)
                self.tt('pool', KN.v, KK.v, RN.v, ALU.mult)
                self.act(KM.v, TA.v, AF.Identity, scale=der[:, 22 + c:23 + c], bias=der[:, 26 + c:27 + c])
                self.tt('dve', KM.v, KM.v, k_.v, ALU.mult)
                self.stt('dve', BV.v, TA.v, 1.0, KN.v, ALU.add, ALU.mult)
                for hl in range(2):
                    P_ = slice(hl * 64, (hl + 1) * 64)

                    def hv(t):
                        return t[P_, :].re("p (q s) -> p q s", s=CH)

                    def ov(n):
                        return EXP[n][c][P_, :, hl * 64:(hl + 1) * 64]

                    self.stt('dve', ov("AT"), hv(KN), -1.0, hv(E1), ALU.mult, ALU.mult)
                    self.stt('dve', ov("BT"), hv(BV), 0.5, hv(EI), ALU.mult, ALU.mult)
                    self.stt('dve', ov("BGT"), hv(BV), 0.5, hv(EC), ALU.mult, ALU.mult)
                    self.tt('pool', ov("KT"), hv(KM), hv(EI), ALU.mult)
                    self.tt('pool', ov("KGT"), hv(KM), hv(EC), ALU.mult)
                    self.tt('pool', ov("RT"), hv(r_), hv(EG), ALU.mult)
                    self.tt('pool', ov("RK"), hv(r_), hv(KM), ALU.mult)
                    self.cp('act', ov("VB"), hv(v_))
            if RSTOP == 2:
                continue
            for q in range(NQ):
                psA = self.next_ps()
                pbA = psA.v.bitcast(BF16)
                for gi, n in enumerate(("BGT", "KGT")):
                    for c in range(4):
                        self.tr(pbA[:, (gi * 4 + c) * 128:(gi * 4 + c + 1) * 128], EXP[n][c][:, q, :], signal=(gi == 1 and c == 3))
                self.evac(BKG[q].v, pbA.re("p (g c m) -> p g c m", g=2, c=4))
                psB = self.next_ps()
                pbB = psB.v.bitcast(BF16)
                for c in range(4):
                    self.tr(pbB[:, c * 128:(c + 1) * 128], EXP["VB"][c][:, q, :], signal=(c == 3))
                for hl in range(2):
                    self.evac(Vst[q][hl * 64:(hl + 1) * 64, :, :],
                              pbB[hl * 64:(hl + 1) * 64, 0:512].re("p (c m) -> p c m", c=4)[:, :, hl * 64:(hl + 1) * 64])
                combos = [("BT", "AT", mask_su, Nb[q][0]), ("AT", "BT", mask_sl, Lb[q][0]), ("KT", "AT", mask_su, LakT[q]),
                          ("BT", "RT", mask_u, MrbT[q]), ("KT", "RT", mask_u, MrkT[q])]
                for (ln, rn_, mk, dstt) in combos:
                    ps = self.next_ps()
                    for c in range(4):
                        self.mm(ps[:, c * 128:(c + 1) * 128], EXP[ln][c][:, q, :], EXP[rn_][c][:, q, :], start=True, stop=True,
                                signal=(c == 3))
                    self.tt('dve', dstt.v, ps.v.re("p (c m) -> p c m", c=4), mk.v.un(1).bc([128, 4, 128]), ALU.mult)
                self.tt('pool', Pb[q].v, Nb[q][0].v, ident.v.un(1).bc([128, 4, 128]), ALU.add)
            if RSTOP == 3:
                continue
            cur = 0
            for lvl in range(1, 6):
                nxt = 1 - cur
                for q in range(NQ):
                    if lvl < 5:
                        ps = self.next_ps()
                        for c in range(4):
                            self.mm(ps[:, c * 128:(c + 1) * 128], Lb[q][cur][:, c, :], Nb[q][cur][:, c, :], start=True, stop=True,
                                    signal=(c == 3))
                        self.cp('act', Nb[q][nxt].v, ps.v.re("p (c m) -> p c m", c=4))
                    ps = self.next_ps()
                    for c in range(4):
                        self.mm(ps[:, c * 128:(c + 1) * 128], Nb[q][cur][:, c, :], Lb[q][cur][:, c, :], start=True, stop=True,
                                signal=(c == 3))
                    self.cp('act', Lb[q][nxt].v, ps.v.re("p (c m) -> p c m", c=4))
                for q in range(NQ):
                    ps = self.next_ps()
                    for c in range(4):
                        self.mm(ps[:, c * 128:(c + 1) * 128], Lb[q][nxt][:, c, :], Pb[q][:, c, :], start=True, stop=True,
                                signal=(c == 3))
                    self.tt('dve', Pb[q].v, Pb[q].v, ps.v.re("p (c m) -> p c m", c=4), ALU.add)
                cur = nxt
            if RSTOP == 4:
                continue
            YTt = YT[it % 2]
            for q in range(NQ):
                w = it * NQ + q
                xb_, ub_ = Xb[w % 2], Ub[w % 2]
                psx = self.next_ps()
                for c in range(4):
                    self.mm(psx[:, c * 64:(c + 1) * 64], EXP["AT"][c][:, q, :], Hb[:, c, :], start=True, stop=False, signal=False)
                    self.mm(psx[:, c * 64:(c + 1) * 64], LakT[q][:, c, :], Vst[q][:, c, :], start=False, stop=True, signal=(c == 3))
                self.cp('act', xb_.v, psx[:, 0:256].re("p (c m) -> p c m", c=4))
                psu = self.next_ps()
                for c in range(4):
                    self.mm(psu[:, c * 64:(c + 1) * 64], Pb[q][:, c, :], xb_[:, c, :], start=True, stop=True, signal=(c == 3))
                self.cp('dve', ub_.v, psu[:, 0:256].re("p (c m) -> p c m", c=4))
                psy = self.next_ps()
                for c in range(4):
                    o = psy[:, c * 64:(c + 1) * 64]
                    self.mm(o, EXP["RT"][c][:, q, :], Hb[:, c, :], start=True, stop=False, signal=False)
                    self.mm(o, MrbT[q][:, c, :], ub_[:, c, :], start=False, stop=False, signal=False)
                    self.mm(o, MrkT[q][:, c, :], Vst[q][:, c, :], start=False, stop=True, signal=False)
                for c in range(4):
                    self.mm(psy[:, 256 + c:257 + c], EXP["RK"][c][:, q, :], rkb[:, c:c + 1], start=True, stop=True, signal=(c == 3))
                psh = self.next_ps()
                for c in range(4):
                    o = psh[:, c * 64:(c + 1) * 64]
                    self.mm(o, BKG[q][:, 0, c, :], ub_[:, c, :], start=True, stop=False, signal=False)
                    self.mm(o, BKG[q][:, 1, c, :], Vst[q][:, c, :], start=False, stop=True, signal=(c == 3))
                self.tt('dve', H.v, H.v, GC[:, :, q:q + 1].bc([128, 4, 64]), ALU.mult)
                self.tt('dve', H.v, H.v, psh[:, 0:256].re("p (c m) -> p c m", c=4), ALU.add)
                self.cp('act', Hb.v, H.v)
                psg = self.next_ps()
                self.mm(psg.v, SGd[:, q, :], gup.v, start=True, stop=True)
                Y = Yv[w % 2]; Y2 = Ysq[w % 2]; YN = Yn[w % 2]; BN = Bn[w % 2]; YF = Yf[w % 2]
                s1 = st1[w % 2]; s2 = st2[w % 2]; mn = mean[w % 2]; vr = var[w % 2]
                py3 = psy[:, 0:256].re("p (c m) -> p c m", c=4)
                self.cp('act', Y.v, py3)
                self.act(Y2.v, py3, AF.Square)
                self.S.op('dve', lambda e: e.reduce_sum(out=s1.ap, in_=Y.ap, axis=AX.X), [Y.v], [s1.v])
                self.S.op('dve', lambda e: e.reduce_sum(out=s2.ap, in_=Y2.ap, axis=AX.X), [Y2.v], [s2.v])
                self.ts('dve', mn.v, s1.v, 1.0 / 64.0, None, ALU.mult)
                self.tt('dve', vr.v, mn.v, mn.v, ALU.mult)
                self.stt('dve', vr.v, s2.v, 1.0 / 64.0, vr.v, ALU.mult, ALU.subtract)
                self.ts('dve', vr.v, vr.v, LNX_EPS, None, ALU.add)
                self.rsqrt(vr.v, vr.v)
                self.tt('pool', YN.v, Y.v, mn.v.un(2).bc([128, 4, 64]), ALU.subtract)
                self.tt('pool', YN.v, YN.v, vr.v.un(2).bc([128, 4, 64]), ALU.mult)
                self.tt('pool', YN.v, YN.v, lnxg.v, ALU.mult)
                self.tt('pool', YN.v, YN.v, lnxb.v, ALU.add)
                self.tt('dve', BN.v, Vst[q].v, psy[:, 256:260].un(2).bc([128, 4, 64]), ALU.mult)
                self.tt('pool', YN.v, YN.v, BN.v, ALU.add)
                for hl in range(2):
                    P_ = slice(hl * 64, (hl + 1) * 64)
                    gv = psg[P_, :].re("p (c h m) -> p c h m", c=4, h=2)[:, :, hl, :]
                    self.tt('dve', YF[P_, :, :], YN[P_, :, :], gv, ALU.mult)
                pst = self.next_ps()
                pbt = pst.v.bitcast(BF16)
                for c in range(4):
                    self.tr(pbt[0:64, c * 128:(c + 1) * 128], YF[:, c, :], signal=(c == 3))
                self.evac(YTt[:, :, :, q * CH:(q + 1) * CH], pbt[0:64, 0:512].re("p (c h t) -> p c h t", c=4, h=2))
            self.dma('sp', yas_v[:, :, :, tok0:tok0 + TT], YTt.v)
        self._sbuf_left = self.nc.sbuf_bytes_remaining

    def phaseM(self, es, src, dst, final=False):
        S = self.S
        inp = self.inp
        TT = 512
        NS = 4
        self._stg_i = 0
        self._ev_i = 0
        use_a = 'R' in self.phases
        cols, ci = self.load_cols(es, "m_cols", [("g", inp["norm_mix_g"]), ("bin", inp["b_in"][2816:4864])])
        hb = self.sb(es, "m_hb", [128, 16], F32)
        self.ts('dve', hb.v, cols[:, ci["bin"]:ci["bin"] + 16], 0.5, None, ALU.mult)
        win = self.sb(es, "m_win", [128, 8, 2048], BF16)
        wa = self.sb(es, "m_wa", [128, 4, D], BF16)
        wb = self.sb(es, "m_wb", [128, 4, D], BF16)
        wmo = self.sb(es, "m_wmo", [128, 8, D], BF16)
        with ExitStack() as tes:
            stg = [self.sb(tes, "m_stg%d" % i, [128, 2048], F32) for i in range(3)]
            for k in range(8):
                self.load_weight(stg, win[:, k, :], inp["w_in"][k * 128:(k + 1) * 128, 2816:4864], 2048,
                                 scale=cols[:, ci["g"] + k: ci["g"] + k + 1])
                self.load_weight(stg, wmo[:, k, :], inp["w_mix_out"][k * 128:(k + 1) * 128, :], D)
            for c in range(4):
                self.load_weight(stg, wa[:, c, :], inp["w_branch_a"][c * 128:(c + 1) * 128, :], D, scale=0.5)
                self.load_weight(stg, wb[:, c, :], inp["w_branch_b"][c * 128:(c + 1) * 128, :], D, scale=0.25)
            S.barrier()
        xbufs, sst, rst, junk = self.norm_bufs(es, "m", NS)
        xrb = [self.sb(es, "m_xr%d" % i, [128, D], F32) for i in range(2)]
        hs = [self.sb(es, "m_h%d" % i, [128, NS, D], BF16) for i in range(2)]
        hTs = [self.sb(es, "m_hT%d" % i, [128, 8, TT], BF16) for i in range(2)]
        TG = [self.sb(es, "m_tg%d" % m, [128, TT], BF16) for m in range(16)]
        YA = [self.sb(es, "m_ya%d" % i, [128, 4, TT], BF16) for i in range(2)]
        YB = [self.sb(es, "m_yb%d" % i, [128, 4, TT], BF16) for i in range(2)]
        MA = [self.sb(es, "m_ma%d" % i, [128, TT], F32) for i in range(3)]
        MG = [self.sb(es, "m_mg%d" % m, [128, TT], BF16) for m in range(8)]
        self._mb = [self.sb(es, "m_mb%d" % i, [128, TT], F32) for i in range(2)]
        ntile = self.ntok // TT

        def load_y(i):
            if i >= ntile:
                return
            tok0 = i * TT
            if use_a:
                self.dma('sp', YA[i % 2].v, self.yas[:, tok0:tok0 + TT].rearrange("(c p) t -> p c t", p=128))
            self.dma('sp', YB[i % 2].v, self.ybs[:, tok0:tok0 + TT].rearrange("(c p) t -> p c t", p=128))

        def front(i):
            self.norm_from_dram(src, i * TT, NS, xbufs, sst, rst, junk, hs[i % 2])
            self.transpose_h(hs[i % 2], NS, hTs[i % 2])

        def mid(i):
            hT = hTs[i % 2]
            ya, yb = YA[i % 2], YB[i % 2]
            for m in range(16):
                ps = self.next_ps()
                for k in range(8):
                    self.mm(ps.v, win[:, k, m * 128:(m + 1) * 128], hT[:, k, :], start=(k == 0), stop=(k == 7))
                self.act(TG[m].v, ps.v, AF.Tanh, scale=0.5, bias=hb[:, m:m + 1])
            for m in range(8):
                ma = MA[m % 3]
                if use_a:
                    ps = self.next_ps()
                    for c in range(4):
                        self.mm(ps.v, wa[:, c, m * 128:(m + 1) * 128], ya[:, c, :], start=(c == 0), stop=(c == 3))
                    self.stt('dve', ma.v, TG[m].v, 1.0, ps.v, ALU.add, ALU.mult)
                ps2 = self.next_ps()
                for c in range(4):
                    self.mm(ps2.v, wb[:, c, m * 128:(m + 1) * 128], yb[:, c, :], start=(c == 0), stop=(c == 3))
                if use_a:
                    mb = self._mb[m % 2]
                    self.stt('dve', mb.v, TG[8 + m].v, 1.0, ps2.v, ALU.add, ALU.mult)
                    self.tt('pool', MG[m].v, mb.v, ma.v, ALU.add)
                else:
                    self.stt('dve', MG[m].v, TG[8 + m].v, 1.0, ps2.v, ALU.add, ALU.mult)

        def back(i):
            tok0 = i * TT
            for s in range(NS):
                pss = [self.next_ps(), self.next_ps()]
                for half in range(2):
                    for m in range(8):
                        self.mm(pss[half].v, MG[m][:, s * 128:(s + 1) * 128], wmo[:, m, half * 512:(half + 1) * 512],
                                start=(m == 0), stop=(m == 7))
                xb = xrb[s % 2]
                self.dma('sp', xb.v, src[tok0 + s * 128: tok0 + (s + 1) * 128, :])
                for half in range(2):
                    self.tt('dve', xb[:, half * 512:(half + 1) * 512], xb[:, half * 512:(half + 1) * 512], pss[half].v, ALU.add)
                self.dma('sp', dst[tok0 + s * 128: tok0 + (s + 1) * 128, :], xb.v)

        load_y(0)
        front(0)
        for i in range(ntile):
            if S.need_reset():
                S.hard_barrier()
            load_y(i + 1)
            mid(i)
            if i + 1 < ntile:
                front(i + 1)
            back(i)

    def evac(self, out, in_, i=None):
        if i is None:
            i = self._ev_i
            self._ev_i += 1
        return self.cp(('act', 'dve')[i % 2], out, in_)

    def phaseB(self, es, src, dst, final=False):
        S = self.S
        inp = self.inp
        TT = 512
        NS = 4
        self._stg_i = 0
        self._ev_i = 0
        cols, ci = self.load_cols(es, "b_cols", [("gx", inp["norm_x_g"]), ("gm", inp["norm_mem_g"])])
        gq = self.sb(es, "b_gq", [128, 8], F32)
        self.ts('dve', gq.v, cols[:, ci["gx"]:ci["gx"] + 8], 1.0 / 16.0, None, ALU.mult)
        wq = self.sb(es, "b_wq", [128, 8, D], BF16)
        wo = self.sb(es, "b_wo", [128, 8, D], BF16)
        kT = [self.sb(es, "b_kT%d" % b, [128, 8, NMEM], BF16) for b in range(self.nb)]
        Vt = [self.sb(es, "b_V%d" % b, [128, 2, D], BF16) for b in range(self.nb)]
        xts = [self.sb(es, "b_xt%d" % i, [128, NS, D], F32) for i in range(2)]
        h = self.sb(es, "b_h", [128, NS, D], BF16)
        hT = self.sb(es, "b_hT", [128, 8, TT], BF16)
        junk = self.sb(es, "b_junk", [128, D], BF16)
        ss = self.sb(es, "b_ss", [128, 4], F32)
        rstd = self.sb(es, "b_rstd", [128, 4], F32)
        with ExitStack() as tes:
            stg = [self.sb(tes, "b_stg%d" % i, [128, 2048], F32) for i in range(3)]
            wkv = self.sb(tes, "b_wkv", [128, 8, 2 * D], BF16)
            for k in range(8):
                self.load_weight(stg, wkv[:, k, :], inp["w_ckv"][k * 128:(k + 1) * 128, :], 2 * D,
                                 scale=cols[:, ci["gm"] + k: ci["gm"] + k + 1])
            for k in range(8):
                self.load_weight(stg, wq[:, k, :], inp["w_cq"][k * 128:(k + 1) * 128, :], D, scale=gq[:, k:k + 1])
                self.load_weight(stg, wo[:, k, :], inp["w_co"][k * 128:(k + 1) * 128, :], D)
            for b in range(self.nb):
                mt = xts[b % 2]
                self.dma('sp', mt[:, 0:2, :], inp["mem"][b * NMEM:(b + 1) * NMEM, :].rearrange("(s p) d -> p s d", p=128))
                self.rms_h(mt, 2, ss, rstd, junk, h)
                self.transpose_h(h, 2, hT[:, :, 0:NMEM])
                for m in range(8):
                    if m % 2 == 0:
                        ps = self.next_ps()
                    o = (m % 2) * NMEM
                    for k in range(8):
                        self.mm(ps[:, o:o + NMEM], wkv[:, k, m * 128:(m + 1) * 128], hT[:, k, 0:NMEM], start=(k == 0), stop=(k == 7))
                    if m % 2 == 1:
                        self.evac(kT[b][:, m - 1:m + 1, :], ps.v.re("p (a t) -> p a t", a=2))
                for mc in range(2):
                    for half in range(2):
                        ps = self.next_ps()
                        for k in range(8):
                            self.mm(ps.v, hT[:, k, mc * 128:(mc + 1) * 128], wkv[:, k, D + half * 512: D + (half + 1) * 512],
                                    start=(k == 0), stop=(k == 7))
                        self.evac(Vt[b][:, mc, half * 512:(half + 1) * 512], ps.v)
            S.barrier()
        qT = self.sb(es, "b_qT", [128, 8, TT], BF16)
        oT = self.sb(es, "b_oT", [128, 8, TT], BF16)
        prT = self.sb(es, "b_prT", [128, 2, 4, TT], BF16)
        prs = [self.sb(es, "b_pr%d" % i, [128, 4, NMEM], BF16) for i in range(2)]
        prn = [self.sb(es, "b_prn%d" % i, [128, 4, NMEM], BF16) for i in range(2)]
        mx = [self.sb(es, "b_mx%d" % i, [128, 4], F32) for i in range(2)]
        sm = [self.sb(es, "b_sm%d" % i, [128, 4], F32) for i in range(2)]
        hs = [h, self.sb(es, "b_h1", [128, NS, D], BF16)]
        hTs = [hT, self.sb(es, "b_hT1", [128, 8, TT], BF16)]
        sss = [ss, self.sb(es, "b_ss1", [128, 4], F32)]
        rstds = [rstd, self.sb(es, "b_rstd1", [128, 4], F32)]
        ntile = self.ntok // TT
        tiles_per_b = self.seq // TT

        def load(i):
            if i < ntile:
                self.dma('sp', xts[i % 2].v, src[i * TT:(i + 1) * TT, :].rearrange("(s p) d -> p s d", p=128))

        def front(i):
            self.rms_h(xts[i % 2], NS, sss[i % 2], rstds[i % 2], junk, hs[i % 2])
            self.transpose_h(hs[i % 2], NS, hTs[i % 2])

        def mid(i):
            b = i // tiles_per_b
            hT_ = hTs[i % 2]
            for m in range(8):
                ps = self.next_ps()
                for k in range(8):
                    self.mm(ps.v, wq[:, k, m * 128:(m + 1) * 128], hT_[:, k, :], start=(k == 0), stop=(k == 7))
                self.evac(qT[:, m, :], ps.v)
            for s in range(NS):
                pr = prs[s % 2]; pn = prn[s % 2]; mxs = mx[s % 2]; sms = sm[s % 2]
                banks = [self.next_ps(), self.next_ps()]
                for hh in range(4):
                    ps = banks[hh // 2]
                    o = (hh % 2) * NMEM
                    for c in range(2):
                        self.mm(ps[:, o:o + NMEM], qT[:, 2 * hh + c, s * 128:(s + 1) * 128], kT[b][:, 2 * hh + c, :],
                                start=(c == 0), stop=(c == 1))
                for g in range(2):
                    self.S.op('dve', lambda e: e.reduce_max(out=mxs.ap[:, 2 * g:2 * g + 2],
                                                            in_=banks[g].ap.rearrange("p (a t) -> p a t", a=2), axis=AX.X),
                              [banks[g].v], [mxs.v])
                self.ts('dve', mxs.v, mxs.v, -1.0, None, ALU.mult)
                for hh in range(4):
                    ps = banks[hh // 2]
                    o = (hh % 2) * NMEM
                    self.act(pr[:, hh, :], ps[:, o:o + NMEM], AF.Exp, bias=mxs[:, hh:hh + 1], accum=sms[:, hh:hh + 1])
                self.S.op('dve', lambda e: e.reciprocal(out=sms.ap, in_=sms.ap), [sms.v], [sms.v])
                self.tt('dve', pn.v, pr.v, sms.v.un(2).bc([128, 4, NMEM]), ALU.mult)
                ps = self.next_ps()
                pb = ps.v.bitcast(BF16)
                for hh in range(4):
                    for mc in range(2):
                        self.tr(pb[:, (hh * 2 + mc) * 128:(hh * 2 + mc + 1) * 128], pn[:, hh, mc * 128:(mc + 1) * 128],
                                signal=(hh == 3 and mc == 1))
                self.evac(prT[:, :, :, s * 128:(s + 1) * 128], pb.re("p (h m t) -> p m h t", h=4, m=2))
            for hh in range(4):
                for c in range(2):
                    ps = self.next_ps()
                    for mc in range(2):
                        self.mm(ps.v, Vt[b][:, mc, hh * 256 + c * 128: hh * 256 + (c + 1) * 128], prT[:, mc, hh, :],
                                start=(mc == 0), stop=(mc == 1))
                    self.evac(oT[:, 2 * hh + c, :], ps.v)

        def back(i):
            xt = xts[i % 2]
            for s in range(NS):
                pss = [self.next_ps(), self.next_ps()]
                for half in range(2):
                    for k in range(8):
                        self.mm(pss[half].v, oT[:, k, s * 128:(s + 1) * 128], wo[:, k, half * 512:(half + 1) * 512],
                                start=(k == 0), stop=(k == 7))
                for half in range(2):
                    self.tt('dve', xt[:, s, half * 512:(half + 1) * 512], xt[:, s, half * 512:(half + 1) * 512], pss[half].v, ALU.add)
                self.dma('sp', dst[i * TT + s * 128: i * TT + (s + 1) * 128, :], xt[:, s, :])

        load(0)
        load(1)
        front(0)
        for i in range(ntile):
            if S.need_reset():
                S.hard_barrier()
            mid(i)
            if i + 1 < ntile:
                front(i + 1)
            back(i)
            load(i + 2)

    def phaseC(self, es, src, dst, final=True):
        nc, S = self.nc, self.S
        TT = 256
        NS = TT // 128
        NJ = DFF // 128
        inp = self.inp
        self._stg_i = 0
        cols, ci = self.load_cols(es, "c_cols", [("g", inp["norm_ffn_g"]), ("cw0", inp["ffn_conv_w"][0]),
                                                 ("cw1", inp["ffn_conv_w"][1]), ("cw2", inp["ffn_conv_w"][2]),
                                                 ("cb", inp["ffn_conv_b"])])
        wi = self.sb(es, "c_wi", [128, 8, 2 * DFF], BF16)
        wo = self.sb(es, "c_wo", [128, NJ, D], BF16)
        gfin = self.sb(es, "c_gfin", [128, D], F32)
        self.dma('sp', gfin.v, inp["norm_final_g"].partition_broadcast(128))
        with ExitStack() as tes:
            stg = [self.sb(tes, "c_stg%d" % i, [128, 2048], F32) for i in range(3)]
            for k in range(8):
                self.load_weight(stg, wi[:, k, :], inp["w_ffn_in"][k * 128:(k + 1) * 128, :], 2 * DFF,
                                 scale=cols[:, ci["g"] + k: ci["g"] + k + 1])
            for j in range(NJ):
                self.load_weight(stg, wo[:, j, :], inp["w_ffn_out"][j * 128:(j + 1) * 128, :], D)
            S.barrier()
        xts = [self.sb(es, "c_xt%d" % i, [128, NS, D], F32) for i in range(2)]
        hs = [self.sb(es, "c_h%d" % i, [128, NS, D], BF16) for i in range(2)]
        hTs = [self.sb(es, "c_hT%d" % i, [128, 8, TT], BF16) for i in range(2)]
        actT = [self.sb(es, "c_actT%d" % j, [128, TT], BF16) for j in range(NJ)]
        junk = self.sb(es, "c_junk", [128, D], BF16)
        sss = [self.sb(es, "c_ss%d" % i, [128, 4], F32) for i in range(3)]
        rstds = [self.sb(es, "c_rstd%d" % i, [128, 4], F32) for i in range(3)]
        halo = self.sb(es, "c_halo", [128, NJ, 2], F32)
        NW = 3
        uw = [self.sb(es, "c_uw%d" % i, [128, 2 + TT], F32) for i in range(NW)]
        acc = [self.sb(es, "c_acc%d" % i, [128, TT], F32) for i in range(NW)]
        th = [self.sb(es, "c_th%d" % i, [128, TT], F32) for i in range(NW)]
        ntile = self.ntok // TT
        tiles_per_b = self.seq // TT

        def load(i):
            if i < ntile:
                self.dma('sp', xts[i % 2].v, src[i * TT:(i + 1) * TT, :].rearrange("(s p) d -> p s d", p=128))

        def front(i):
            self.rms_h(xts[i % 2], NS, sss[i % 2], rstds[i % 2], junk, hs[i % 2])
            self.transpose_h(hs[i % 2], NS, hTs[i % 2])

        def mid(i):
            hT = hTs[i % 2]
            for j in range(NJ):
                ps = self.next_ps()
                for half in range(2):
                    c0 = half * DFF + j * 128
                    for k in range(8):
                        self.mm(ps[:, half * TT:(half + 1) * TT], wi[:, k, c0:c0 + 128], hT[:, k, :], start=(k == 0), stop=(k == 7))
                u = uw[j % NW]; a = acc[j % NW]; tg = th[j % NW]
                self.cp('act', u[:, 2:2 + TT], ps[:, 0:TT])
                self.cp('pool', u[:, 0:2], halo[:, j, :])
                w0 = cols[:, ci["cw0"] + j: ci["cw0"] + j + 1]
                w1 = cols[:, ci["cw1"] + j: ci["cw1"] + j + 1]
                w2 = cols[:, ci["cw2"] + j: ci["cw2"] + j + 1]
                cb = cols[:, ci["cb"] + j: ci["cb"] + j + 1]
                self.ts('dve', a.v, u[:, 0:TT], w0, cb, ALU.mult, ALU.add)
                self.stt('dve', a.v, u[:, 1:1 + TT], w1, a.v, ALU.mult, ALU.add)
                self.stt('dve', a.v, u[:, 2:2 + TT], w2, a.v, ALU.mult, ALU.add)
                self.cp('pool', halo[:, j, :], u[:, TT:TT + 2])
                self.act(tg.v, a.v, AF.Gelu_apprx_tanh)
                self.tt('dve', actT[j].v, tg.v, ps[:, TT:2 * TT], ALU.mult)

        def back(i):
            xt = xts[i % 2]
            ss, rstd = sss[2], rstds[2]
            for s in range(NS):
                pss = [self.next_ps(), self.next_ps()]
                for half in range(2):
                    for j in range(NJ):
                        self.mm(pss[half].v, actT[j][:, s * 128:(s + 1) * 128], wo[:, j, half * 512:(half + 1) * 512],
                                start=(j == 0), stop=(j == NJ - 1))
                for half in range(2):
                    self.tt('dve', xt[:, s, half * 512:(half + 1) * 512], xt[:, s, half * 512:(half + 1) * 512], pss[half].v, ALU.add)
                if final:
                    self.act(junk.v, xt[:, s, :], AF.Square, accum=ss[:, s:s + 1])
                    self.ts('dve', rstd[:, s:s + 1], ss[:, s:s + 1], 1.0 / D, NORM_EPS, ALU.mult, ALU.add)
                    self.rsqrt(rstd[:, s:s + 1], rstd[:, s:s + 1])
                    self.stt('dve', xt[:, s, :], xt[:, s, :], rstd[:, s:s + 1], gfin.v, ALU.mult, ALU.mult)
                self.dma('sp', dst[i * TT + s * 128: i * TT + (s + 1) * 128, :], xt[:, s, :])

        load(0)
        load(1)
        front(0)
        for i in range(ntile):
            if S.need_reset():
                S.hard_barrier()
            if i % tiles_per_b == 0:
                self.memset('pool', halo.v, 0.0)
            mid(i)
            if i + 1 < ntile:
                front(i + 1)
            back(i)
            load(i + 2)


_PROG_CACHE = {}


def get_prog(nb=4, seq=2048, phases="LRMBC"):
    key = (nb, seq, phases)
    if key not in _PROG_CACHE:
        _PROG_CACHE[key] = Prog(nb, seq, phases)
    return _PROG_CACHE[key]


_W_NAMES = ["norm_mix_g", "w_in", "b_in", "mu_shift", "w0", "w_lora_up", "a0", "a_lora_up", "g_lora_up", "k_k", "k_a",
            "r_k", "lnx_g", "lnx_b", "w_branch_a", "conv_b_w", "conv_b_b", "w_rg_a", "b_rg_a", "w_rg_x", "b_rg_x",
            "lru_lambda", "w_branch_b", "w_mix_out", "norm_x_g", "norm_mem_g", "w_cq", "w_ckv", "w_co", "norm_ffn_g",
            "w_ffn_in", "ffn_conv_w", "ffn_conv_b", "w_ffn_out", "norm_final_g"]


def run(inputs, ncores=8, nb=4, seq=2048, phases="LRMBC"):
    prog = get_prog(nb, seq, phases)
    shapes = {k: tuple(v.shape) for k, v in prog.inp.items()}
    shared = {}
    for k in _W_NAMES:
        a = np.ascontiguousarray(np.asarray(inputs[k], dtype=np.float32))
        shared[k] = a.reshape(shapes[k])
    x = np.asarray(inputs["x"], dtype=np.float32)
    mem = np.asarray(inputs["mem"], dtype=np.float32)
    in_maps = []
    for c in range(ncores):
        m = dict(shared)
        m["x"] = np.ascontiguousarray(x[c * nb:(c + 1) * nb, :seq]).reshape(nb * seq, D)
        m["mem"] = np.ascontiguousarray(mem[c * nb:(c + 1) * nb]).reshape(nb * NMEM, D)
        in_maps.append(m)
    res = run_bass_kernel_spmd(prog.nc, in_maps, core_ids=list(range(ncores)))
    outs = [np.asarray(r["out"]).reshape(nb, seq, D) for r in res.results]
    return np.concatenate(outs, axis=0)


def kernel(**inputs):
    return run(inputs).astype(np.float32)
```

```python
import numpy as np
from contextlib import ExitStack
import concourse.bass as bass
import concourse.mybir as mybir
from concourse.bass_utils import run_bass_kernel_spmd

F32 = mybir.dt.float32
BF16 = mybir.dt.bfloat16
AF = mybir.ActivationFunctionType
ALU = mybir.AluOpType
AX = mybir.AxisListType

D = 1024
NMEM = 256
AW = 512
RWKV_COLS = 1792
P_IN = 4864
DFF = 2816
NORM_EPS = 1e-6
LNX_EPS = 64e-5
SAME_ENGINE_SYNC = True
import os as _os
RSTOP = int(_os.environ.get("RSTOP", "0"))
LVAR = int(_os.environ.get("LVAR", "2"))


_ALL_TILES = []


class T:
    def __init__(self, ap, name=""):
        self.ap = ap if isinstance(ap, bass.AP) else ap[:]
        self.w = None
        self.r = []
        self.name = name
        self.dsem = None
        _ALL_TILES.append(self)

    def __getitem__(self, k):
        return V([self], self.ap[k])

    @property
    def v(self):
        return V([self], self.ap)


class V:
    def __init__(self, ts, ap):
        self.ts = ts
        self.ap = ap

    def __getitem__(self, k):
        return V(self.ts, self.ap[k])

    def re(self, pat, **kw):
        return V(self.ts, self.ap.rearrange(pat, **kw))

    def bc(self, shape):
        return V(self.ts, self.ap.to_broadcast(shape))

    def un(self, axis):
        return V(self.ts, self.ap.unsqueeze(axis))

    def bitcast(self, dt):
        return V(self.ts, self.ap.bitcast(dt))


def _ap(x):
    return x.ap if isinstance(x, V) else x


class Sync:
    def __init__(self, nc, es, n_dma_sems=64):
        self.nc = nc
        self.es = es
        self.engs = {'pe': nc.tensor, 'act': nc.scalar, 'dve': nc.vector, 'pool': nc.gpsimd, 'sp': nc.sync}
        self.sem = {}
        self.cnt = {}
        self.seen = {}
        for e in self.engs:
            self.sem[e] = es.enter_context(nc.semaphore("c_" + e))
            self.cnt[e] = 0
            self.seen[e] = {}
        self.dpool = [{'sem': es.enter_context(nc.semaphore("d%d" % i)), 'cnt': 0} for i in range(n_dma_sems)]
        self.dfree = list(range(n_dma_sems))
        self.n_inst = 0
        self.bar1 = es.enter_context(nc.semaphore("bar1"))
        self.bar2 = es.enter_context(nc.semaphore("bar2"))
        self.bar_k = 0

    def _wait(self, e, tok):
        if tok is None:
            return
        sem, val, src = tok
        if src == e and (e == 'pe' or not SAME_ENGINE_SYNC):
            return
        key = sem.name
        if self.seen[e].get(key, 0) >= val:
            return
        self.seen[e][key] = val
        self.engs[e].wait_ge(sem, val)

    def deps(self, e, reads, writes):
        for v in reads:
            if not isinstance(v, V):
                continue
            for t in v.ts:
                self._wait(e, t.w)
        for v in writes:
            for t in v.ts:
                self._wait(e, t.w)
                for tok in t.r:
                    self._wait(e, tok)

    def done(self, tok, reads, writes):
        for v in reads:
            if not isinstance(v, V):
                continue
            for t in v.ts:
                t.r.append(tok)
                if len(t.r) > 24:
                    t.r = t.r[-24:] if False else t.r
        for v in writes:
            for t in v.ts:
                t.w = tok
                t.r = []

    def op(self, e, fn, reads, writes, signal=True):
        self.deps(e, reads, writes)
        inst = fn(self.engs[e])
        self.n_inst += 1
        if signal:
            self.cnt[e] += 1
            inst.then_inc(self.sem[e], 1)
            tok = (self.sem[e], self.cnt[e], e)
        else:
            tok = (self.sem[e], self.cnt[e] + 1, e)
        self.done(tok, reads, writes)
        return inst

    def _dsem(self, t):
        if t.dsem is None:
            if not self.dfree:
                raise RuntimeError("out of dma semaphores")
            t.dsem = self.dpool[self.dfree.pop(0)]
        return t.dsem

    def dma(self, q, out, in_, **kw):
        reads = [in_] if isinstance(in_, V) else []
        writes = [out] if isinstance(out, V) else []
        self.deps(q, reads, writes)
        sbv = out if isinstance(out, V) else in_
        ds = self._dsem(sbv.ts[0])
        inst = self.engs[q].dma_start(out=_ap(out), in_=_ap(in_), **kw)
        self.n_inst += 1
        ds['cnt'] += 16
        inst.then_inc(ds['sem'], 16)
        tok = (ds['sem'], ds['cnt'], 'dma')
        self.done(tok, reads, writes)
        return tok

    def barrier(self):
        toks = [(self.sem[f], self.cnt[f], f) for f in self.engs if self.cnt[f] > 0]
        toks += [(d['sem'], d['cnt'], 'dma') for d in self.dpool if d['cnt'] > 0]
        for e in self.engs:
            for tok in toks:
                if tok[2] == e:
                    continue
                self._wait(e, tok)

    def release_dma_sems(self):
        self.dfree = list(range(len(self.dpool)))
        for t in _ALL_TILES:
            t.dsem = None

    def need_reset(self, limit=2600):
        return max(self.cnt.values()) > limit or max(d['cnt'] for d in self.dpool) > limit

    def hard_barrier(self):
        self.barrier()
        self.bar_k += 1
        k = self.bar_k
        for e in self.engs:
            self.engs[e].sem_inc(self.bar1, 1)
        sp = self.engs['sp']
        sp.wait_ge(self.bar1, len(self.engs) * k)
        for e in self.engs:
            if self.cnt[e] > 0:
                sp.sem_clear(self.sem[e])
        for d in self.dpool:
            if d['cnt'] > 0:
                sp.sem_clear(d['sem'])
        sp.sem_inc(self.bar2, 1)
        for e in self.engs:
            if e != 'sp':
                self.engs[e].wait_ge(self.bar2, k)
        for e in self.engs:
            self.cnt[e] = 0
            self.seen[e] = {}
        for d in self.dpool:
            d['cnt'] = 0
        for t in _ALL_TILES:
            t.w = None
            t.r = []


class Prog:
    def __init__(self, nb=4, seq=2048, phases="LRMBC", dbg=False):
        self.nb, self.seq, self.phases, self.dbg = nb, seq, phases, dbg
        self.ntok = nb * seq
        nc = self.nc = bass.Bass("TRN2", target_bir_lowering=False)
        self.inp = {}
        del _ALL_TILES[:]

        def din(name, shape):
            self.inp[name] = nc.dram_tensor(name, list(shape), F32, kind="ExternalInput").ap()
            return self.inp[name]

        ntok = self.ntok
        din("x", [ntok, D])
        din("mem", [nb * NMEM, D])
        din("norm_mix_g", [D]); din("w_in", [D, P_IN]); din("b_in", [P_IN]); din("mu_shift", [RWKV_COLS])
        din("w0", [AW]); din("w_lora_up", [64, AW]); din("a0", [AW]); din("a_lora_up", [64, AW])
        din("g_lora_up", [128, AW]); din("k_k", [AW]); din("k_a", [AW]); din("r_k", [AW])
        din("lnx_g", [AW]); din("lnx_b", [AW]); din("w_branch_a", [AW, D])
        din("conv_b_w", [4, AW]); din("conv_b_b", [AW]); din("w_rg_a", [8, 64, 64]); din("b_rg_a", [AW])
        din("w_rg_x", [8, 64, 64]); din("b_rg_x", [AW]); din("lru_lambda", [AW]); din("w_branch_b", [AW, D])
        din("w_mix_out", [D, D]); din("norm_x_g", [D]); din("norm_mem_g", [D]); din("w_cq", [D, D])
        din("w_ckv", [D, 2 * D]); din("w_co", [D, D]); din("norm_ffn_g", [D]); din("w_ffn_in", [D, 2 * DFF])
        din("ffn_conv_w", [3, DFF]); din("ffn_conv_b", [DFF]); din("w_ffn_out", [DFF, D]); din("norm_final_g", [D])
        self.out = nc.dram_tensor("out", [ntok, D], F32, kind="ExternalOutput").ap()
        self.x1 = nc.dram_tensor("x1s", [ntok, D], F32).ap()
        self.x2 = nc.dram_tensor("x2s", [ntok, D], F32).ap()
        self.dbg_out = {}

        with ExitStack() as es:
            self.es = es
            self.S = Sync(nc, es)
            self.ps = [T(es.enter_context(nc.psum_tensor("ps%d" % i, [128, 512], F32)), "ps%d" % i) for i in range(8)]
            self.ps_i = 0
            self.ident = self.sb(es, "ident", [128, 128], BF16)
            self.identf = self.sb(es, "identf", [128, 128], F32)
            for idt in (self.ident, self.identf):
                self.S.op('pool', lambda e: e.memset(idt.ap[:], 0.0), [], [idt.v])
                self.S.op('pool', lambda e: e.affine_select(idt.ap[:], idt.ap[:], pattern=[[-1, 128]],
                                                           compare_op=ALU.not_equal, fill=1.0, base=0,
                                                           channel_multiplier=1), [idt.v], [idt.v])
            self.mhalf = self.sb(es, "mhalf", [128, 512], F32)
            self.memset('pool', self.mhalf.v, -0.5)
            self.phalf = self.sb(es, "phalf", [128, 512], F32)
            self.memset('pool', self.phalf.v, 0.5)
            self.yas = nc.dram_tensor("yas", [AW, ntok], BF16).ap()
            self.ybs = nc.dram_tensor("ybs", [AW, ntok], BF16).ap()
            src = {'L': self.inp["x"], 'R': self.inp["x"], 'M': self.inp["x"], 'B': self.x1, 'C': self.x2}
            dst = {'L': None, 'R': None, 'M': self.x1, 'B': self.x2, 'C': self.out}
            order = [p for p in "LRMBC" if p in phases]
            chain = [p for p in order if p in "MBC"]
            for i, p in enumerate(order):
                s_ap, d_ap = src[p], dst[p]
                if p in chain:
                    if chain.index(p) == 0:
                        s_ap = self.inp["x"]
                    if chain.index(p) == len(chain) - 1:
                        d_ap = self.out
                with ExitStack() as pes:
                    getattr(self, "phase" + p)(pes, s_ap, d_ap, final=(p == 'C'))
                    self.S.hard_barrier()
                self.S.release_dma_sems()
            self.S.barrier()

    def sb(self, es, name, shape, dt):
        return T(es.enter_context(self.nc.sbuf_tensor(name, list(shape), dt)), name)

    def next_ps(self):
        t = self.ps[self.ps_i]
        self.ps_i = (self.ps_i + 1) % 8
        return t

    def act(self, out, in_, func, bias=0.0, scale=1.0, accum=None):
        rd = [in_] + [a for a in (bias, scale) if isinstance(a, V)]
        wr = [out] + ([accum] if accum is not None else [])
        kw = {}
        if accum is not None:
            kw['accum_out'] = accum.ap
        return self.S.op('act', lambda e: e.activation(out=out.ap, in_=in_.ap, func=func, bias=_ap(bias),
                                                       scale=_ap(scale), **kw), rd, wr)

    def tt(self, eng, out, in0, in1, op):
        return self.S.op(eng, lambda e: e.tensor_tensor(out=out.ap, in0=in0.ap, in1=in1.ap, op=op), [in0, in1], [out])

    def ts(self, eng, out, in0, s1, s2, op0, op1=None):
        rd = [in0] + [a for a in (s1, s2) if isinstance(a, V)]
        if op1 is None:
            return self.S.op(eng, lambda e: e.tensor_scalar(out=out.ap, in0=in0.ap, scalar1=_ap(s1), scalar2=None,
                                                            op0=op0), rd, [out])
        return self.S.op(eng, lambda e: e.tensor_scalar(out=out.ap, in0=in0.ap, scalar1=_ap(s1), scalar2=_ap(s2),
                                                        op0=op0, op1=op1), rd, [out])

    def stt(self, eng, out, in0, scalar, in1, op0, op1):
        rd = [in0, in1] + ([scalar] if isinstance(scalar, V) else [])
        return self.S.op(eng, lambda e: e.scalar_tensor_tensor(out=out.ap, in0=in0.ap, scalar=_ap(scalar), in1=in1.ap,
                                                               op0=op0, op1=op1), rd, [out])

    def cp(self, eng, out, in_):
        if eng == 'act':
            return self.act(out, in_, AF.Copy)
        return self.S.op(eng, lambda e: e.tensor_copy(out=out.ap, in_=in_.ap), [in_], [out])

    def rsqrt(self, out, in_):
        shp = list(in_.ap.shape)
        if shp[-1] > 8:
            self.act(out, in_, AF.Ln)
            return self.act(out, out, AF.Exp, scale=-0.5)
        mh = self.mhalf[0:shp[0], 0:shp[-1]]
        if len(shp) == 3:
            mh = mh.un(1).bc(shp)
        return self.tt('pool', out, in_, mh, ALU.pow)

    def memset(self, eng, out, val):
        return self.S.op(eng, lambda e: e.memset(out.ap, val), [], [out])

    def mm(self, out, lhsT, rhs, start, stop, signal=None):
        if signal is None:
            signal = stop
        return self.S.op('pe', lambda e: e.matmul(out.ap, lhsT=lhsT.ap, rhs=rhs.ap, start=start, stop=stop),
                         [lhsT, rhs], [out], signal=signal)

    def tr(self, out, in_, signal=True, f32=False):
        idt = self.identf if f32 else self.ident
        n = in_.ap.shape[0]
        return self.S.op('pe', lambda e: e.transpose(out.ap, in_.ap, idt.ap[0:n, 0:n]), [in_, idt.v], [out],
                         signal=signal)

    def dma(self, q, out, in_, **kw):
        return self.S.dma(q, out, in_, **kw)

    def load_weight(self, stg, dst, src_rows, ncols, scale=None, piece=2048):
        rows = src_rows.shape[0]
        c0 = 0
        while c0 < ncols:
            n = min(piece, ncols - c0)
            st = stg[self._stg_i % len(stg)]
            eng = ('dve', 'act')[self._stg_i % 2]
            self._stg_i += 1
            self.dma('sp', st[0:rows, 0:n], src_rows[:, c0:c0 + n])
            o = dst[:, c0:c0 + n]
            i = st[0:rows, 0:n]
            if scale is None:
                self.cp(eng, o, i)
            elif eng == 'act':
                if isinstance(scale, V):
                    self.act(o, i, AF.Copy, scale=scale)
                else:
                    self.act(o, i, AF.Copy, scale=float(scale))
            else:
                self.ts(eng, o, i, scale, None, ALU.mult)
            c0 += n

    def load_cols(self, es, name, specs):
        cols = {}
        n = 0
        for k, v in specs:
            cols[k] = n
            n += (v.shape[0] + 127) // 128
        res = self.sb(es, name, [128, n], F32)
        with ExitStack() as tes:
            ngrp = (n + 127) // 128
            stage = [self.sb(tes, name + "_st%d" % g, [128, 128], F32) for g in range(ngrp)]
            for st in stage:
                self.memset('pool', st.v, 0.0)
            for k, v in specs:
                L = v.shape[0]
                m = (L + 127) // 128
                c = cols[k]
                r = 0
                while r < m:
                    g, rr = divmod(c + r, 128)
                    cnt = min(m - r, 128 - rr)
                    if L >= 128:
                        self.dma('sp', stage[g][rr:rr + cnt, :], v[r * 128:(r + cnt) * 128].rearrange("(m p) -> m p", p=128))
                    else:
                        self.dma('sp', stage[g][rr:rr + 1, 0:L], v.rearrange("(m p) -> m p", m=1))
                    r += cnt
            for g in range(ngrp):
                w = min(128, n - g * 128)
                ps = self.next_ps()
                self.tr(ps[:, 0:w], stage[g][0:w, :], f32=True)
                self.cp('dve', res[:, g * 128:g * 128 + w], ps[:, 0:w])
            self.S.barrier()
        return res, cols

    def rms_h(self, xt, nsub, ss, rstd, junk, h):
        for s in range(nsub):
            self.act(junk.v, xt[:, s, :], AF.Square, accum=ss[:, s:s + 1])
        self.ts('dve', rstd[:, 0:nsub], ss[:, 0:nsub], 1.0 / D, NORM_EPS, ALU.mult, ALU.add)
        self.rsqrt(rstd[:, 0:nsub], rstd[:, 0:nsub])
        for s in range(nsub):
            self.ts('dve', h[:, s, :], xt[:, s, :], rstd[:, s:s + 1], None, ALU.mult)

    def transpose_h(self, h, nsub, hT, evac=('act', 'dve')):
        tw = nsub * 128
        per_bank = 1024 // tw
        c = 0
        i = 0
        while c < 8:
            ps = self.next_ps()
            pb = ps.v.bitcast(BF16)
            nchunk = min(per_bank, 8 - c)
            for cc in range(nchunk):
                for s in range(nsub):
                    last = (cc == nchunk - 1 and s == nsub - 1)
                    self.tr(pb[:, cc * tw + s * 128: cc * tw + (s + 1) * 128], h[:, s, (c + cc) * 128:(c + cc + 1) * 128],
                            signal=last)
            self.cp(evac[i % len(evac)], hT[:, c:c + nchunk, :], pb[:, 0:nchunk * tw].re("p (c t) -> p c t", c=nchunk))
            c += nchunk
            i += 1

    def norm_from_dram(self, src, tok0, NS, xbufs, sst, rst, junk, h):
        for s in range(NS):
            xb = xbufs[self._xb_i % len(xbufs)]
            self._xb_i += 1
            self.dma('sp', xb.v, src[tok0 + s * 128: tok0 + (s + 1) * 128, :])
            self.act(junk.v, xb.v, AF.Square, accum=sst[s].v)
            self.ts('dve', rst[s].v, sst[s].v, 1.0 / D, NORM_EPS, ALU.mult, ALU.add)
            self.rsqrt(rst[s].v, rst[s].v)
            self.ts('dve', h[:, s, :], xb.v, rst[s].v, None, ALU.mult)

    def norm_bufs(self, es, pfx, NS):
        xbufs = [self.sb(es, pfx + "_xb%d" % i, [128, D], F32) for i in range(2)]
        sst = [self.sb(es, pfx + "_ss%d" % i, [128, 1], F32) for i in range(NS)]
        rst = [self.sb(es, pfx + "_rs%d" % i, [128, 1], F32) for i in range(NS)]
        junk = self.sb(es, pfx + "_junk", [128, D], BF16)
        self._xb_i = 0
        return xbufs, sst, rst, junk

    def phaseL(self, es, src, dst, final=False):
        S = self.S
        inp = self.inp
        TT = 512
        NS = 4
        self._stg_i = 0
        self._ev_i = 0
        cols, ci = self.load_cols(es, "l_cols", [
            ("g", inp["norm_mix_g"]), ("bin", inp["b_in"][1792:2816]),
            ("cw0", inp["conv_b_w"][0]), ("cw1", inp["conv_b_w"][1]), ("cw2", inp["conv_b_w"][2]), ("cw3", inp["conv_b_w"][3]),
            ("cbb", inp["conv_b_b"]), ("bra", inp["b_rg_a"]), ("brx", inp["b_rg_x"]), ("lam", inp["lru_lambda"])])
        hb = self.sb(es, "l_hb", [128, 8], F32)
        self.ts('dve', hb[:, 0:4], cols[:, ci["bra"]:ci["bra"] + 4], 0.5, None, ALU.mult)
        self.ts('dve', hb[:, 4:8], cols[:, ci["brx"]:ci["brx"] + 4], 0.5, None, ALU.mult)
        cA = self.sb(es, "l_cA", [128, 8], F32)
        lt = self.sb(es, "l_lt", [128, 4], F32)
        self.act(lt.v, cols[:, ci["lam"]:ci["lam"] + 4], AF.Exp, scale=-1.0)
        self.ts('dve', lt.v, lt.v, 1.0, None, ALU.add)
        self.act(lt.v, lt.v, AF.Ln)
        self.ts('dve', cA[:, 0:4], lt.v, -4.0, None, ALU.mult)
        self.ts('dve', cA[:, 4:8], lt.v, -8.0, None, ALU.mult)
        win = self.sb(es, "l_win", [128, 8, 1024], BF16)
        wg = self.sb(es, "l_wg", [128, 2, 4, 128], BF16)
        with ExitStack() as tes:
            stg = [self.sb(tes, "l_stg%d" % i, [128, 1024], F32) for i in range(3)]
            for k in range(8):
                self.load_weight(stg, win[:, k, :], inp["w_in"][k * 128:(k + 1) * 128, 1792:2816], 1024,
                                 scale=cols[:, ci["g"] + k: ci["g"] + k + 1], piece=1024)
            for gi, nm in enumerate(("w_rg_a", "w_rg_x")):
                st = stg[gi]
                self.memset('pool', st.v, 0.0)
                for blk in range(8):
                    c, hl = divmod(blk, 2)
                    self.dma('sp', st[hl * 64:(hl + 1) * 64, c * 128 + hl * 64: c * 128 + (hl + 1) * 64], inp[nm][blk])
                self.cp('dve', wg[:, gi, :, :], st[:, 0:512].re("p (c m) -> p c m", c=4))
            S.barrier()
        xbufs, sst, rst, junk = self.norm_bufs(es, "l", NS)
        h = self.sb(es, "l_h", [128, NS, D], BF16)
        hT = self.sb(es, "l_hT", [128, 8, TT], BF16)
        PX = [self.sb(es, "l_px%d" % c, [128, 3 + TT], F32) for c in range(4)]
        GY = [self.sb(es, "l_gy%d" % c, [128, TT], F32) for c in range(4)]
        YB = [self.sb(es, "l_yb%d" % c, [128, TT], BF16) for c in range(4)]
        carry = [self.sb(es, "l_cy%d" % c, [128, 1], F32) for c in range(4)]
        NW = 2
        def wk(nm, dt=F32, n=NW):
            return [self.sb(es, "l_%s%d" % (nm, i), [128, TT], dt) for i in range(n)]
        acc = wk("acc", n=4); xbb = wk("xbb", BF16, n=4); ta = wk("ta", n=4); tx = wk("tx", n=4)
        av = wk("av"); a2 = wk("a2"); u = wk("u"); hl_ = wk("hl")
        ntile = self.ntok // TT
        tiles_per_b = self.seq // TT
        hs = [h, self.sb(es, "l_h1", [128, NS, D], BF16)]
        hTs = [hT, self.sb(es, "l_hT1", [128, 8, TT], BF16)]

        def front(i):
            self.norm_from_dram(src, i * TT, NS, xbufs, sst, rst, junk, hs[i % 2])
            self.transpose_h(hs[i % 2], NS, hTs[i % 2])

        def midA(i):
            hT_ = hTs[i % 2]
            first = (i % tiles_per_b == 0)
            if first:
                for c in range(4):
                    self.memset('pool', PX[c][:, 0:3], 0.0)
                    self.memset('pool', carry[c].v, 0.0)
            for c in range(4):
                ps = self.next_ps()
                for k in range(8):
                    self.mm(ps.v, win[:, k, c * 128:(c + 1) * 128], hT_[:, k, :], start=(k == 0), stop=(k == 7))
                self.act(PX[c][:, 3:3 + TT], ps.v, AF.Identity, bias=cols[:, ci["bin"] + c: ci["bin"] + c + 1])
            for c in range(4):
                ps = self.next_ps()
                for k in range(8):
                    self.mm(ps.v, win[:, k, 512 + c * 128: 512 + (c + 1) * 128], hT_[:, k, :], start=(k == 0), stop=(k == 7))
                self.act(GY[c].v, ps.v, AF.Gelu_apprx_tanh, bias=cols[:, ci["bin"] + 4 + c: ci["bin"] + 5 + c])
            for c in range(4):
                A = acc[c]; XB = xbb[c]; TA = ta[c]; TX = tx[c]
                cw = [cols[:, ci["cw%d" % j] + c: ci["cw%d" % j] + c + 1] for j in range(4)]
                self.ts('dve', A.v, PX[c][:, 0:TT], cw[0], cols[:, ci["cbb"] + c: ci["cbb"] + c + 1], ALU.mult, ALU.add)
                for j in range(1, 4):
                    self.stt('dve', A.v, PX[c][:, j:j + TT], cw[j], A.v, ALU.mult, ALU.add)
                self.cp('pool', PX[c][:, 0:3], PX[c][:, TT:TT + 3])
                self.cp('dve', XB.v, A.v)
                psa = self.next_ps()
                self.mm(psa.v, wg[:, 0, c, :], XB.v, start=True, stop=True)
                psx = self.next_ps()
                self.mm(psx.v, wg[:, 1, c, :], XB.v, start=True, stop=True)
                self.act(TA.v, psa.v, AF.Tanh, scale=0.5, bias=hb[:, c:c + 1])
                self.act(TX.v, psx.v, AF.Tanh, scale=0.5, bias=hb[:, 4 + c:5 + c])

        def midB(i):
            tok0 = i * TT
            first = (i % tiles_per_b == 0)
            for c in range(4):
                A = acc[c]; TA = ta[c]; TX = tx[c]; AV = av[c % NW]; A2 = a2[c % NW]; U = u[c % NW]; HL = hl_[c % NW]
                self.act(AV.v, TA.v, AF.Exp, scale=cA[:, c:c + 1], bias=cA[:, c:c + 1])
                self.act(A2.v, TA.v, AF.Exp, scale=cA[:, 4 + c:5 + c], bias=cA[:, 4 + c:5 + c])
                self.ts('dve', A2.v, A2.v, -1.0, 1.0, ALU.mult, ALU.add)
                self.ts('dve', A2.v, A2.v, 1e-30, None, ALU.max)
                self.act(A2.v, A2.v, AF.Ln)
                self.act(A2.v, A2.v, AF.Exp, scale=0.5)
                if first:
                    self.memset('pool', A2[:, 0:1], 1.0)
                self.stt('dve', U.v, TX.v, 1.0, A.v, ALU.add, ALU.mult)
                self.tt('dve', U.v, U.v, A2.v, ALU.mult)
                self.S.op('dve', lambda e: e.tensor_tensor_scan(HL.ap, AV.ap, U.ap, carry[c].ap, ALU.mult, ALU.add),
                          [AV.v, U.v, carry[c].v], [HL.v])
                self.cp('pool', carry[c].v, HL[:, TT - 1:TT])
                self.tt('dve', YB[c].v, HL.v, GY[c].v, ALU.mult)
                self.dma('sp', self.ybs[c * 128:(c + 1) * 128, tok0:tok0 + TT], YB[c].v)

        front(0)
        for i in range(ntile):
            if S.need_reset():
                S.hard_barrier()
            midA(i)
            if i + 1 < ntile:
                front(i + 1)
            midB(i)

    def phaseR(self, es, src, dst, final=False):
        S = self.S
        inp = self.inp
        TT = 256
        NS = 2
        NQ = 4
        CH = 64
        self._stg_i = 0
        self._ev_i = 0
        cols, ci = self.load_cols(es, "r_cols", [
            ("g", inp["norm_mix_g"]), ("bin", inp["b_in"][0:RWKV_COLS]), ("mu", inp["mu_shift"]),
            ("w0", inp["w0"]), ("a0", inp["a0"]), ("kk", inp["k_k"]), ("ka", inp["k_a"]), ("rk", inp["r_k"])])

        def col(key, j):
            return cols[:, ci[key] + j: ci[key] + j + 1]

        der = self.sb(es, "r_der", [128, 32], F32)
        self.ts('dve', der[:, 0:14], cols[:, ci["mu"]:ci["mu"] + 14], -1.0, 1.0, ALU.mult, ALU.add)
        self.ts('dve', der[:, 14:18], cols[:, ci["w0"]:ci["w0"] + 4], 0.5, None, ALU.mult)
        self.ts('dve', der[:, 18:22], cols[:, ci["a0"]:ci["a0"] + 4], 0.5, None, ALU.mult)
        self.ts('dve', der[:, 22:26], cols[:, ci["ka"]:ci["ka"] + 4], 0.5, None, ALU.mult)
        self.ts('dve', der[:, 26:30], cols[:, ci["ka"]:ci["ka"] + 4], -0.5, 1.0, ALU.mult, ALU.add)
        rkb = self.sb(es, "r_rkb", [128, 4], BF16)
        self.cp('dve', rkb.v, cols[:, ci["rk"]:ci["rk"] + 4])
        m64 = [self.sb(es, "r_m64_%d" % i, [64, 64], BF16) for i in range(3)]
        masks = [self.sb(es, "r_mask%d" % i, [128, 128], BF16) for i in range(3)]
        specs = [([[1, 64]], -1, ALU.is_gt), ([[-1, 64]], 1, ALU.is_gt), ([[1, 64]], -1, ALU.is_ge)]
        for mi in range(3):
            pat, cm, op = specs[mi]
            self.memset('pool', m64[mi].v, 1.0)
            self.S.op('pool', lambda e: e.affine_select(m64[mi].ap, m64[mi].ap, pattern=pat, compare_op=op, fill=0.0,
                                                        base=0, channel_multiplier=cm), [m64[mi].v], [m64[mi].v])
            for a in range(2):
                for b_ in range(2):
                    self.dma('sp', masks[mi][a * 64:(a + 1) * 64, b_ * 64:(b_ + 1) * 64], m64[mi].v)
        mask_su, mask_sl, mask_u = masks
        bones = self.sb(es, "r_bones", [128, 128], BF16)
        self.memset('pool', bones.v, 0.0)
        self.memset('pool', bones[0:64, 0:64], 1.0)
        self.memset('pool', bones[64:128, 64:128], 1.0)
        m01 = self.sb(es, "r_m01", [128, TT], F32)
        self.memset('pool', m01.v, 1.0)
        self.memset('pool', m01.v.re("p (q s) -> p q s", s=CH)[:, :, 0:1], 0.0)
        lnxg = self.sb(es, "r_lnxg", [128, 4, 64], F32)
        lnxb = self.sb(es, "r_lnxb", [128, 4, 64], F32)
        for c in range(4):
            for hl in range(2):
                hh = 2 * c + hl
                self.dma('sp', lnxg[hl * 64:(hl + 1) * 64, c, :], inp["lnx_g"][hh * 64:(hh + 1) * 64].partition_broadcast(64))
                self.dma('sp', lnxb[hl * 64:(hl + 1) * 64, c, :], inp["lnx_b"][hh * 64:(hh + 1) * 64].partition_broadcast(64))
        win = self.sb(es, "r_win", [128, 8, RWKV_COLS], BF16)
        wlora = self.sb(es, "r_wlora", [128, 2, AW], BF16)
        gup = self.sb(es, "r_gup", [128, AW], BF16)
        with ExitStack() as tes:
            stg = [self.sb(tes, "r_stg%d" % i, [128, RWKV_COLS], F32) for i in range(3)]
            for k in range(8):
                self.load_weight(stg, win[:, k, :], inp["w_in"][k * 128:(k + 1) * 128, 0:RWKV_COLS], RWKV_COLS,
                                 scale=cols[:, ci["g"] + k: ci["g"] + k + 1], piece=RWKV_COLS)
            st = stg[0]
            self.memset('pool', st[:, 0:2 * AW], 0.0)
            self.dma('sp', st[0:64, 0:AW], inp["w_lora_up"])
            self.dma('sp', st[64:128, AW:2 * AW], inp["a_lora_up"])
            self.cp('dve', wlora.v, st[:, 0:2 * AW].re("p (a m) -> p a m", a=2))
            st = stg[1]
            self.dma('sp', st[:, 0:AW], inp["g_lora_up"])
            self.cp('dve', gup.v, st[:, 0:AW])
            S.barrier()
        xbufs, sst, rst, junk = self.norm_bufs(es, "r", NS)
        h = self.sb(es, "r_h", [128, NS, D], BF16)
        hT = self.sb(es, "r_hT", [128, 8, TT], BF16)
        pcarry = [self.sb(es, "r_pc%d" % m, [128, 1], F32) for m in range(14)]
        Sx = [self.sb(es, "r_s%d" % m, [128, TT], F32) for m in range(14)]
        ltmp = [self.sb(es, "r_lt%d" % i, [128, TT], F32) for i in range(2)]
        LB = self.sb(es, "r_lb", [128, TT], BF16)
        SGd = self.sb(es, "r_sgd", [128, NQ, 128], BF16)
        NW = 2

        def wk(nm, dt=F32):
            return [self.sb(es, "r_%s%d" % (nm, i), [128, TT], dt) for i in range(NW)]

        logw = [self.sb(es, "r_logw%d" % i, [128, TT], F32) for i in range(4)]
        ta = [self.sb(es, "r_ta%d" % i, [128, TT], F32) for i in range(4)]
        cum = [self.sb(es, "r_cum%d" % i, [128, TT], F32) for i in range(4)]
        egm1 = wk("egm1"); eg = wk("eg"); eig = wk("eig"); egc = wk("egc")
        kk = wk("kk"); kk2 = wk("kk2", BF16); rn = wk("rn"); kkn = wk("kkn"); kmod = wk("kmod"); bv = wk("bv")
        names = ["AT", "BT", "KT", "RT", "BGT", "KGT", "VB", "RK"]
        EXP = {n: [self.sb(es, "r_%s%d" % (n, c), [128, NQ, 128], BF16) for c in range(4)] for n in names}
        for n in names:
            for c in range(4):
                self.memset('pool', EXP[n][c].v, 0.0)
        GC = self.sb(es, "r_gc", [128, 4, NQ], F32)
        BKG = [self.sb(es, "r_bkg%d" % q, [128, 2, 4, 128], BF16) for q in range(NQ)]
        Vst = [self.sb(es, "r_vst%d" % q, [128, 4, 64], BF16) for q in range(NQ)]
        Nb = [[self.sb(es, "r_nb%d_%d" % (q, i), [128, 4, 128], BF16) for i in range(2)] for q in range(NQ)]
        Lb = [[self.sb(es, "r_lb%d_%d" % (q, i), [128, 4, 128], BF16) for i in range(2)] for q in range(NQ)]
        Pb = [self.sb(es, "r_pb%d" % q, [128, 4, 128], BF16) for q in range(NQ)]
        LakT = [self.sb(es, "r_lak%d" % q, [128, 4, 128], BF16) for q in range(NQ)]
        MrbT = [self.sb(es, "r_mrb%d" % q, [128, 4, 128], BF16) for q in range(NQ)]
        MrkT = [self.sb(es, "r_mrk%d" % q, [128, 4, 128], BF16) for q in range(NQ)]
        H = self.sb(es, "r_H", [128, 4, 64], F32)
        Hb = self.sb(es, "r_Hb", [128, 4, 64], BF16)
        Xb = [self.sb(es, "r_Xb%d" % i, [128, 4, 64], BF16) for i in range(2)]
        Ub = [self.sb(es, "r_Ub%d" % i, [128, 4, 64], BF16) for i in range(2)]

        def yt(nm, shape=(128, 4, 64), dt=F32):
            return [self.sb(es, "r_%s%d" % (nm, i), list(shape), dt) for i in range(2)]

        Yv = yt("Yv"); Ysq = yt("Ysq"); Yn = yt("Yn"); Bn = yt("Bn"); Yf = yt("Yf", dt=BF16)
        st1 = yt("st1", (128, 4)); st2 = yt("st2", (128, 4)); mean = yt("mean", (128, 4)); var = yt("var", (128, 4))
        YT = [self.sb(es, "r_YT%d" % i, [64, 4, 2, TT], BF16) for i in range(2)]
        ident = self.ident
        ntile = self.ntok // TT
        tiles_per_b = self.seq // TT
        yas_v = self.yas.rearrange("(c hl i) t -> i c hl t", hl=2, i=64)

        PWraw = [es.enter_context(self.nc.sbuf_tensor("r_pwx%d" % i, [128, 1 + TT], F32)) for i in range(3)]
        PWh = [T(r[:, 0:1], "r_pwh") for r in PWraw]
        PWb = [T(r[:, 1:1 + TT], "r_pwb") for r in PWraw]

        def s12_pieces(it):
            tok0 = it * TT

            def p0():
                if it % tiles_per_b == 0:
                    for m in range(14):
                        self.memset('pool', pcarry[m].v, 0.0)
                self.norm_from_dram(src, tok0, NS, xbufs, sst, rst, junk, h)
                self.transpose_h(h, NS, hT)

            def proj(ms):
                def f():
                    ps = None
                    for idx, m in enumerate(ms):
                        if idx % 2 == 0:
                            ps = self.next_ps()
                        o = (idx % 2) * TT
                        for k in range(8):
                            self.mm(ps[:, o:o + TT], win[:, k, m * 128:(m + 1) * 128], hT[:, k, :], start=(k == 0), stop=(k == 7))
                        pi = m % 3
                        self.cp('dve', PWh[pi].v, pcarry[m].v)
                        self.act(PWb[pi].v, ps[:, o:o + TT], AF.Identity, bias=col("bin", m))
                        self.cp('dve', pcarry[m].v, PWb[pi][:, TT - 1:TT])
                        tmp = ltmp[m % 2]
                        self.act(tmp.v, V([PWh[pi], PWb[pi]], PWraw[pi][:, 0:TT]), AF.Copy, scale=col("mu", m))
                        self.stt('dve', Sx[m].v, PWb[pi].v, der[:, m:m + 1], tmp.v, ALU.mult, ALU.add)
                return f

            return [p0, proj([0, 1, 2, 3]), proj([4, 5, 6, 7]), proj([8, 9, 10, 11]), proj([12, 13])]

        def stage3(it):
            if it % tiles_per_b == 0:
                self.memset('pool', H.v, 0.0)
                self.memset('pool', Hb.v, 0.0)
            for _ in range(1):
                if RSTOP == 1:
                    return
                self.act(LB[0:64, :], Sx[12][0:64, :], AF.Tanh)
                self.cp('act', LB[64:128, :], Sx[12][64:128, :])
                tmp = ltmp[0]
                self.act(tmp.v, Sx[13].v, AF.Tanh, scale=0.5)
                for hl in range(2):
                    self.ts('dve', SGd[:, :, hl * 64:(hl + 1) * 64], tmp.v.re("p (q s) -> p q s", s=CH), 0.5, 0.5, ALU.mult, ALU.add)
                if RSTOP == 21:
                    return
                for c in range(4):
                    LW = logw[c]; TA = ta[c]; CU = cum[c]
                    ps = self.next_ps()
                    self.mm(ps[:, 0:TT], wlora[:, 0, c * 128:(c + 1) * 128], LB.v, start=True, stop=True)
                    self.mm(ps[:, TT:2 * TT], wlora[:, 1, c * 128:(c + 1) * 128], LB.v, start=True, stop=True)
                    self.act(LW.v, ps[:, 0:TT], AF.Tanh, scale=0.5, bias=der[:, 14 + c:15 + c])
                    self.act(TA.v, ps[:, TT:2 * TT], AF.Tanh, scale=0.5, bias=der[:, 18 + c:19 + c])
                    self.ts('dve', LW.v, LW.v, 1.0, -0.30326532985631671, ALU.add, ALU.mult)
                    self.S.op('dve', lambda e: e.tensor_tensor_scan(CU.ap, m01.ap, LW.ap, 0.0, ALU.mult, ALU.add),
                              [m01.v, LW.v], [CU.v])
                for c in range(4):
                    w = it * 4 + c
                    r_, k_, v_ = Sx[c], Sx[4 + c], Sx[8 + c]
                    LW = logw[c]; TA = ta[c]; CU = cum[c]
                    E1 = egm1[w % NW]; EG = eg[w % NW]; EI = eig[w % NW]
                    EC = egc[w % NW]; KK = kk[w % NW]; K2 = kk2[w % NW]; RN = rn[w % NW]; KN = kkn[w % NW]; KM = kmod[w % NW]
                    BV = bv[w % NW]
                    cu3 = CU.v.re("p (q s) -> p q s", s=CH)
                    cuC = cu3[:, :, CH - 1:CH]
                    self.tt('pool', E1.v, CU.v, LW.v, ALU.subtract)
                    self.act(E1.v, E1.v, AF.Exp)
                    self.act(EG.v, CU.v, AF.Exp)
                    self.act(EI.v, CU.v, AF.Exp, scale=-1.0)
                    self.tt('pool', EC.v.re("p (q s) -> p q s", s=CH), cuC.bc([128, NQ, CH]), cu3, ALU.subtract)
                    self.act(EC.v, EC.v, AF.Exp)
                    self.act(GC[:, c, :], cuC.re("p q o -> p (q o)"), AF.Exp)
                    self.act(KK.v, k_.v, AF.Copy, scale=col("kk", c))
                    self.act(K2.v, KK.v, AF.Square)
                    psn = self.next_ps()
                    self.mm(psn[:, 0:TT], bones.v, K2.v, start=True, stop=True)
                    self.act(RN.v, psn[:, 0:TT], AF.Ln)
                    self.act(RN.v, RN.v, AF.Exp, scale=-0.5)
                    self.tt('pool', KN.v, KK.v, RN.v, ALU.mult)
                    self.act(KM.v, TA.v, AF.Identity, scale=der[:, 22 + c:23 + c], bias=der[:, 26 + c:27 + c])
                    self.tt('dve', KM.v, KM.v, k_.v, ALU.mult)
                    self.stt('dve', BV.v, TA.v, 1.0, KN.v, ALU.add, ALU.mult)
                    for hl in range(2):
                        P_ = slice(hl * 64, (hl + 1) * 64)

                        def hv(t):
                            return t[P_, :].re("p (q s) -> p q s", s=CH)

                        def ov(n):
                            return EXP[n][c][P_, :, hl * 64:(hl + 1) * 64]

                        self.stt('dve', ov("AT"), hv(KN), -1.0, hv(E1), ALU.mult, ALU.mult)
                        self.stt('dve', ov("BT"), hv(BV), 0.5, hv(EI), ALU.mult, ALU.mult)
                        self.stt('dve', ov("BGT"), hv(BV), 0.5, hv(EC), ALU.mult, ALU.mult)
                        self.tt('pool', ov("KT"), hv(KM), hv(EI), ALU.mult)
                        self.tt('pool', ov("KGT"), hv(KM), hv(EC), ALU.mult)
                        self.tt('pool', ov("RT"), hv(r_), hv(EG), ALU.mult)
                        self.tt('pool', ov("RK"), hv(r_), hv(KM), ALU.mult)
                        self.cp('act', ov("VB"), hv(v_))

        def stage45(it):
            for _ in range(1):
                if RSTOP == 2:
                    return
                for q in range(NQ):
                    psA = self.next_ps()
                    pbA = psA.v.bitcast(BF16)
                    for gi, n in enumerate(("BGT", "KGT")):
                        for c in range(4):
                            self.tr(pbA[:, (gi * 4 + c) * 128:(gi * 4 + c + 1) * 128], EXP[n][c][:, q, :], signal=(gi == 1 and c == 3))
                    self.evac(BKG[q].v, pbA.re("p (g c m) -> p g c m", g=2, c=4))
                    psB = self.next_ps()
                    pbB = psB.v.bitcast(BF16)
                    for c in range(4):
                        self.tr(pbB[:, c * 128:(c + 1) * 128], EXP["VB"][c][:, q, :], signal=(c == 3))
                    for hl in range(2):
                        self.evac(Vst[q][hl * 64:(hl + 1) * 64, :, :],
                                  pbB[hl * 64:(hl + 1) * 64, 0:512].re("p (c m) -> p c m", c=4)[:, :, hl * 64:(hl + 1) * 64])
                    combos = [("BT", "AT", mask_su, Nb[q][0]), ("AT", "BT", mask_sl, Lb[q][0]), ("KT", "AT", mask_su, LakT[q]),
                              ("BT", "RT", mask_u, MrbT[q]), ("KT", "RT", mask_u, MrkT[q])]
                    for (ln, rn_, mk, dstt) in combos:
                        ps = self.next_ps()
                        for c in range(4):
                            self.mm(ps[:, c * 128:(c + 1) * 128], EXP[ln][c][:, q, :], EXP[rn_][c][:, q, :], start=True, stop=True,
                                    signal=(c == 3))
                        self.tt('dve', dstt.v, ps.v.re("p (c m) -> p c m", c=4), mk.v.un(1).bc([128, 4, 128]), ALU.mult)
                    self.tt('pool', Pb[q].v, Nb[q][0].v, ident.v.un(1).bc([128, 4, 128]), ALU.add)
                if RSTOP == 3:
                    return
                cur = 0
                for lvl in range(1, 6):
                    nxt = 1 - cur
                    for q in range(NQ):
                        if lvl < 5:
                            ps = self.next_ps()
                            for c in range(4):
                                self.mm(ps[:, c * 128:(c + 1) * 128], Lb[q][cur][:, c, :], Nb[q][cur][:, c, :], start=True, stop=True,
                                        signal=(c == 3))
                            self.cp('act', Nb[q][nxt].v, ps.v.re("p (c m) -> p c m", c=4))
                        ps = self.next_ps()
                        for c in range(4):
                            self.mm(ps[:, c * 128:(c + 1) * 128], Nb[q][cur][:, c, :], Lb[q][cur][:, c, :], start=True, stop=True,
                                    signal=(c == 3))
                        self.cp('act', Lb[q][nxt].v, ps.v.re("p (c m) -> p c m", c=4))
                    for q in range(NQ):
                        ps = self.next_ps()
                        for c in range(4):
                            self.mm(ps[:, c * 128:(c + 1) * 128], Lb[q][nxt][:, c, :], Pb[q][:, c, :], start=True, stop=True,
                                    signal=(c == 3))
                        self.tt('dve', Pb[q].v, Pb[q].v, ps.v.re("p (c m) -> p c m", c=4), ALU.add)
                    cur = nxt

        def chain_q(it, q, YTt):
                w = it * NQ + q
                xb_, ub_ = Xb[w % 2], Ub[w % 2]
                psx = self.next_ps()
                for c in range(4):
                    self.mm(psx[:, c * 64:(c + 1) * 64], EXP["AT"][c][:, q, :], Hb[:, c, :], start=True, stop=False, signal=False)
                    self.mm(psx[:, c * 64:(c + 1) * 64], LakT[q][:, c, :], Vst[q][:, c, :], start=False, stop=True, signal=(c == 3))
                self.cp('act', xb_.v, psx[:, 0:256].re("p (c m) -> p c m", c=4))
                psu = self.next_ps()
                for c in range(4):
                    self.mm(psu[:, c * 64:(c + 1) * 64], Pb[q][:, c, :], xb_[:, c, :], start=True, stop=True, signal=(c == 3))
                self.cp('dve', ub_.v, psu[:, 0:256].re("p (c m) -> p c m", c=4))
                psy = self.next_ps()
                for c in range(4):
                    o = psy[:, c * 64:(c + 1) * 64]
                    self.mm(o, EXP["RT"][c][:, q, :], Hb[:, c, :], start=True, stop=False, signal=False)
                    self.mm(o, MrbT[q][:, c, :], ub_[:, c, :], start=False, stop=False, signal=False)
                    self.mm(o, MrkT[q][:, c, :], Vst[q][:, c, :], start=False, stop=True, signal=False)
                for c in range(4):
                    self.mm(psy[:, 256 + c:257 + c], EXP["RK"][c][:, q, :], rkb[:, c:c + 1], start=True, stop=True, signal=(c == 3))
                psh = self.next_ps()
                for c in range(4):
                    o = psh[:, c * 64:(c + 1) * 64]
                    self.mm(o, BKG[q][:, 0, c, :], ub_[:, c, :], start=True, stop=False, signal=False)
                    self.mm(o, BKG[q][:, 1, c, :], Vst[q][:, c, :], start=False, stop=True, signal=(c == 3))
                self.tt('dve', H.v, H.v, GC[:, :, q:q + 1].bc([128, 4, 64]), ALU.mult)
                self.tt('dve', H.v, H.v, psh[:, 0:256].re("p (c m) -> p c m", c=4), ALU.add)
                self.cp('act', Hb.v, H.v)
                psg = self.next_ps()
                self.mm(psg.v, SGd[:, q, :], gup.v, start=True, stop=True)
                Y = Yv[w % 2]; Y2 = Ysq[w % 2]; YN = Yn[w % 2]; BN = Bn[w % 2]; YF = Yf[w % 2]
                s1 = st1[w % 2]; s2 = st2[w % 2]; mn = mean[w % 2]; vr = var[w % 2]
                py3 = psy[:, 0:256].re("p (c m) -> p c m", c=4)
                self.cp('act', Y.v, py3)
                self.act(Y2.v, py3, AF.Square)
                self.S.op('dve', lambda e: e.reduce_sum(out=s1.ap, in_=Y.ap, axis=AX.X), [Y.v], [s1.v])
                self.S.op('dve', lambda e: e.reduce_sum(out=s2.ap, in_=Y2.ap, axis=AX.X), [Y2.v], [s2.v])
                self.ts('dve', mn.v, s1.v, 1.0 / 64.0, None, ALU.mult)
                self.tt('dve', vr.v, mn.v, mn.v, ALU.mult)
                self.stt('dve', vr.v, s2.v, 1.0 / 64.0, vr.v, ALU.mult, ALU.subtract)
                self.ts('dve', vr.v, vr.v, LNX_EPS, None, ALU.add)
                self.rsqrt(vr.v, vr.v)
                self.tt('pool', YN.v, Y.v, mn.v.un(2).bc([128, 4, 64]), ALU.subtract)
                self.tt('pool', YN.v, YN.v, vr.v.un(2).bc([128, 4, 64]), ALU.mult)
                self.tt('pool', YN.v, YN.v, lnxg.v, ALU.mult)
                self.tt('pool', YN.v, YN.v, lnxb.v, ALU.add)
                self.tt('dve', BN.v, Vst[q].v, psy[:, 256:260].un(2).bc([128, 4, 64]), ALU.mult)
                self.tt('pool', YN.v, YN.v, BN.v, ALU.add)
                for hl in range(2):
                    P_ = slice(hl * 64, (hl + 1) * 64)
                    gv = psg[P_, :].re("p (c h m) -> p c h m", c=4, h=2)[:, :, hl, :]
                    self.tt('dve', YF[P_, :, :], YN[P_, :, :], gv, ALU.mult)
                pst = self.next_ps()
                pbt = pst.v.bitcast(BF16)
                for c in range(4):
                    self.tr(pbt[0:64, c * 128:(c + 1) * 128], YF[:, c, :], signal=(c == 3))
                self.evac(YTt[:, :, :, q * CH:(q + 1) * CH], pbt[0:64, 0:512].re("p (c h t) -> p c h t", c=4, h=2))

        for p in s12_pieces(0):
            p()
        for it in range(ntile):
            tok0 = it * TT
            if S.need_reset():
                S.hard_barrier()
            stage3(it)
            stage45(it)
            pieces = s12_pieces(it + 1) if it + 1 < ntile else []
            YTt = YT[it % 2]
            for q in range(NQ):
                if RSTOP not in (4, 11):
                    chain_q(it, q, YTt)
                if pieces:
                    pieces.pop(0)()
            while pieces:
                pieces.pop(0)()
            self.dma('sp', yas_v[:, :, :, tok0:tok0 + TT], YTt.v)
        self._sbuf_left = self.nc.sbuf_bytes_remaining

    def phaseM(self, es, src, dst, final=False):
        S = self.S
        inp = self.inp
        TT = 512
        NS = 4
        self._stg_i = 0
        self._ev_i = 0
        use_a = 'R' in self.phases
        cols, ci = self.load_cols(es, "m_cols", [("g", inp["norm_mix_g"]), ("bin", inp["b_in"][2816:4864])])
        hb = self.sb(es, "m_hb", [128, 16], F32)
        self.ts('dve', hb.v, cols[:, ci["bin"]:ci["bin"] + 16], 0.5, None, ALU.mult)
        win = self.sb(es, "m_win", [128, 8, 2048], BF16)
        wa = self.sb(es, "m_wa", [128, 4, D], BF16)
        wb = self.sb(es, "m_wb", [128, 4, D], BF16)
        wmo = self.sb(es, "m_wmo", [128, 8, D], BF16)
        with ExitStack() as tes:
            stg = [self.sb(tes, "m_stg%d" % i, [128, 2048], F32) for i in range(3)]
            for k in range(8):
                self.load_weight(stg, win[:, k, :], inp["w_in"][k * 128:(k + 1) * 128, 2816:4864], 2048,
                                 scale=cols[:, ci["g"] + k: ci["g"] + k + 1])
                self.load_weight(stg, wmo[:, k, :], inp["w_mix_out"][k * 128:(k + 1) * 128, :], D)
            for c in range(4):
                self.load_weight(stg, wa[:, c, :], inp["w_branch_a"][c * 128:(c + 1) * 128, :], D, scale=0.5)
                self.load_weight(stg, wb[:, c, :], inp["w_branch_b"][c * 128:(c + 1) * 128, :], D, scale=0.25)
            S.barrier()
        xbufs, sst, rst, junk = self.norm_bufs(es, "m", NS)
        xrb = [self.sb(es, "m_xr%d" % i, [128, D], F32) for i in range(2)]
        hs = [self.sb(es, "m_h%d" % i, [128, NS, D], BF16) for i in range(2)]
        hTs = [self.sb(es, "m_hT%d" % i, [128, 8, TT], BF16) for i in range(2)]
        TG = [self.sb(es, "m_tg%d" % m, [128, TT], BF16) for m in range(16)]
        YA = [self.sb(es, "m_ya%d" % i, [128, 4, TT], BF16) for i in range(2)]
        YB = [self.sb(es, "m_yb%d" % i, [128, 4, TT], BF16) for i in range(2)]
        MA = [self.sb(es, "m_ma%d" % i, [128, TT], F32) for i in range(3)]
        MG = [self.sb(es, "m_mg%d" % m, [128, TT], BF16) for m in range(8)]
        self._mb = [self.sb(es, "m_mb%d" % i, [128, TT], F32) for i in range(2)]
        ntile = self.ntok // TT

        def load_y(i):
            if i >= ntile:
                return
            tok0 = i * TT
            if use_a:
                self.dma('sp', YA[i % 2].v, self.yas[:, tok0:tok0 + TT].rearrange("(c p) t -> p c t", p=128))
            self.dma('sp', YB[i % 2].v, self.ybs[:, tok0:tok0 + TT].rearrange("(c p) t -> p c t", p=128))

        def front(i):
            self.norm_from_dram(src, i * TT, NS, xbufs, sst, rst, junk, hs[i % 2])
            self.transpose_h(hs[i % 2], NS, hTs[i % 2])

        def mid(i):
            hT = hTs[i % 2]
            ya, yb = YA[i % 2], YB[i % 2]
            for m in range(16):
                ps = self.next_ps()
                for k in range(8):
                    self.mm(ps.v, win[:, k, m * 128:(m + 1) * 128], hT[:, k, :], start=(k == 0), stop=(k == 7))
                self.act(TG[m].v, ps.v, AF.Tanh, scale=0.5, bias=hb[:, m:m + 1])
            for m in range(8):
                ma = MA[m % 3]
                if use_a:
                    ps = self.next_ps()
                    for c in range(4):
                        self.mm(ps.v, wa[:, c, m * 128:(m + 1) * 128], ya[:, c, :], start=(c == 0), stop=(c == 3))
                    self.stt('dve', ma.v, TG[m].v, 1.0, ps.v, ALU.add, ALU.mult)
                ps2 = self.next_ps()
                for c in range(4):
                    self.mm(ps2.v, wb[:, c, m * 128:(m + 1) * 128], yb[:, c, :], start=(c == 0), stop=(c == 3))
                if use_a:
                    mb = self._mb[m % 2]
                    self.stt('dve', mb.v, TG[8 + m].v, 1.0, ps2.v, ALU.add, ALU.mult)
                    self.tt('pool', MG[m].v, mb.v, ma.v, ALU.add)
                else:
                    self.stt('dve', MG[m].v, TG[8 + m].v, 1.0, ps2.v, ALU.add, ALU.mult)

        def back(i):
            tok0 = i * TT
            for s in range(NS):
                pss = [self.next_ps(), self.next_ps()]
                for half in range(2):
                    for m in range(8):
                        self.mm(pss[half].v, MG[m][:, s * 128:(s + 1) * 128], wmo[:, m, half * 512:(half + 1) * 512],
                                start=(m == 0), stop=(m == 7))
                xb = xrb[s % 2]
                self.dma('sp', xb.v, src[tok0 + s * 128: tok0 + (s + 1) * 128, :])
                for half in range(2):
                    self.tt('dve', xb[:, half * 512:(half + 1) * 512], xb[:, half * 512:(half + 1) * 512], pss[half].v, ALU.add)
                self.dma('sp', dst[tok0 + s * 128: tok0 + (s + 1) * 128, :], xb.v)

        load_y(0)
        front(0)
        for i in range(ntile):
            if S.need_reset():
                S.hard_barrier()
            load_y(i + 1)
            mid(i)
            if i + 1 < ntile:
                front(i + 1)
            back(i)

    def evac(self, out, in_, i=None):
        if i is None:
            i = self._ev_i
            self._ev_i += 1
        return self.cp(('act', 'dve')[i % 2], out, in_)

    def phaseB(self, es, src, dst, final=False):
        S = self.S
        inp = self.inp
        TT = 512
        NS = 4
        self._stg_i = 0
        self._ev_i = 0
        cols, ci = self.load_cols(es, "b_cols", [("gx", inp["norm_x_g"]), ("gm", inp["norm_mem_g"])])
        gq = self.sb(es, "b_gq", [128, 8], F32)
        self.ts('dve', gq.v, cols[:, ci["gx"]:ci["gx"] + 8], 1.0 / 16.0, None, ALU.mult)
        wq = self.sb(es, "b_wq", [128, 8, D], BF16)
        wo = self.sb(es, "b_wo", [128, 8, D], BF16)
        kT = [self.sb(es, "b_kT%d" % b, [128, 8, NMEM], BF16) for b in range(self.nb)]
        Vt = [self.sb(es, "b_V%d" % b, [128, 2, D], BF16) for b in range(self.nb)]
        xts = [self.sb(es, "b_xt%d" % i, [128, NS, D], F32) for i in range(2)]
        h = self.sb(es, "b_h", [128, NS, D], BF16)
        hT = self.sb(es, "b_hT", [128, 8, TT], BF16)
        junk = self.sb(es, "b_junk", [128, D], BF16)
        ss = self.sb(es, "b_ss", [128, 4], F32)
        rstd = self.sb(es, "b_rstd", [128, 4], F32)
        with ExitStack() as tes:
            stg = [self.sb(tes, "b_stg%d" % i, [128, 2048], F32) for i in range(3)]
            wkv = self.sb(tes, "b_wkv", [128, 8, 2 * D], BF16)
            for k in range(8):
                self.load_weight(stg, wkv[:, k, :], inp["w_ckv"][k * 128:(k + 1) * 128, :], 2 * D,
                                 scale=cols[:, ci["gm"] + k: ci["gm"] + k + 1])
            for k in range(8):
                self.load_weight(stg, wq[:, k, :], inp["w_cq"][k * 128:(k + 1) * 128, :], D, scale=gq[:, k:k + 1])
                self.load_weight(stg, wo[:, k, :], inp["w_co"][k * 128:(k + 1) * 128, :], D)
            for b in range(self.nb):
                mt = xts[b % 2]
                self.dma('sp', mt[:, 0:2, :], inp["mem"][b * NMEM:(b + 1) * NMEM, :].rearrange("(s p) d -> p s d", p=128))
                self.rms_h(mt, 2, ss, rstd, junk, h)
                self.transpose_h(h, 2, hT[:, :, 0:NMEM])
                for m in range(8):
                    if m % 2 == 0:
                        ps = self.next_ps()
                    o = (m % 2) * NMEM
                    for k in range(8):
                        self.mm(ps[:, o:o + NMEM], wkv[:, k, m * 128:(m + 1) * 128], hT[:, k, 0:NMEM], start=(k == 0), stop=(k == 7))
                    if m % 2 == 1:
                        self.evac(kT[b][:, m - 1:m + 1, :], ps.v.re("p (a t) -> p a t", a=2))
                for mc in range(2):
                    for half in range(2):
                        ps = self.next_ps()
                        for k in range(8):
                            self.mm(ps.v, hT[:, k, mc * 128:(mc + 1) * 128], wkv[:, k, D + half * 512: D + (half + 1) * 512],
                                    start=(k == 0), stop=(k == 7))
                        self.evac(Vt[b][:, mc, half * 512:(half + 1) * 512], ps.v)
            S.barrier()
        qT = self.sb(es, "b_qT", [128, 8, TT], BF16)
        oT = self.sb(es, "b_oT", [128, 8, TT], BF16)
        prT = self.sb(es, "b_prT", [128, 2, 4, TT], BF16)
        prs = [self.sb(es, "b_pr%d" % i, [128, 4, NMEM], BF16) for i in range(2)]
        prn = [self.sb(es, "b_prn%d" % i, [128, 4, NMEM], BF16) for i in range(2)]
        mx = [self.sb(es, "b_mx%d" % i, [128, 4], F32) for i in range(2)]
        sm = [self.sb(es, "b_sm%d" % i, [128, 4], F32) for i in range(2)]
        hs = [h, self.sb(es, "b_h1", [128, NS, D], BF16)]
        hTs = [hT, self.sb(es, "b_hT1", [128, 8, TT], BF16)]
        sss = [ss, self.sb(es, "b_ss1", [128, 4], F32)]
        rstds = [rstd, self.sb(es, "b_rstd1", [128, 4], F32)]
        ntile = self.ntok // TT
        tiles_per_b = self.seq // TT

        def load(i):
            if i < ntile:
                self.dma('sp', xts[i % 2].v, src[i * TT:(i + 1) * TT, :].rearrange("(s p) d -> p s d", p=128))

        def front(i):
            self.rms_h(xts[i % 2], NS, sss[i % 2], rstds[i % 2], junk, hs[i % 2])
            self.transpose_h(hs[i % 2], NS, hTs[i % 2])

        def mid(i):
            b = i // tiles_per_b
            hT_ = hTs[i % 2]
            for m in range(8):
                ps = self.next_ps()
                for k in range(8):
                    self.mm(ps.v, wq[:, k, m * 128:(m + 1) * 128], hT_[:, k, :], start=(k == 0), stop=(k == 7))
                self.evac(qT[:, m, :], ps.v)
            for s in range(NS):
                pr = prs[s % 2]; pn = prn[s % 2]; mxs = mx[s % 2]; sms = sm[s % 2]
                banks = [self.next_ps(), self.next_ps()]
                for hh in range(4):
                    ps = banks[hh // 2]
                    o = (hh % 2) * NMEM
                    for c in range(2):
                        self.mm(ps[:, o:o + NMEM], qT[:, 2 * hh + c, s * 128:(s + 1) * 128], kT[b][:, 2 * hh + c, :],
                                start=(c == 0), stop=(c == 1))
                for g in range(2):
                    self.S.op('dve', lambda e: e.reduce_max(out=mxs.ap[:, 2 * g:2 * g + 2],
                                                            in_=banks[g].ap.rearrange("p (a t) -> p a t", a=2), axis=AX.X),
                              [banks[g].v], [mxs.v])
                self.ts('dve', mxs.v, mxs.v, -1.0, None, ALU.mult)
                for hh in range(4):
                    ps = banks[hh // 2]
                    o = (hh % 2) * NMEM
                    self.act(pr[:, hh, :], ps[:, o:o + NMEM], AF.Exp, bias=mxs[:, hh:hh + 1], accum=sms[:, hh:hh + 1])
                self.S.op('dve', lambda e: e.reciprocal(out=sms.ap, in_=sms.ap), [sms.v], [sms.v])
                self.tt('dve', pn.v, pr.v, sms.v.un(2).bc([128, 4, NMEM]), ALU.mult)
                ps = self.next_ps()
                pb = ps.v.bitcast(BF16)
                for hh in range(4):
                    for mc in range(2):
                        self.tr(pb[:, (hh * 2 + mc) * 128:(hh * 2 + mc + 1) * 128], pn[:, hh, mc * 128:(mc + 1) * 128],
                                signal=(hh == 3 and mc == 1))
                self.evac(prT[:, :, :, s * 128:(s + 1) * 128], pb.re("p (h m t) -> p m h t", h=4, m=2))
            for hh in range(4):
                for c in range(2):
                    ps = self.next_ps()
                    for mc in range(2):
                        self.mm(ps.v, Vt[b][:, mc, hh * 256 + c * 128: hh * 256 + (c + 1) * 128], prT[:, mc, hh, :],
                                start=(mc == 0), stop=(mc == 1))
                    self.evac(oT[:, 2 * hh + c, :], ps.v)

        def back(i):
            xt = xts[i % 2]
            for s in range(NS):
                pss = [self.next_ps(), self.next_ps()]
                for half in range(2):
                    for k in range(8):
                        self.mm(pss[half].v, oT[:, k, s * 128:(s + 1) * 128], wo[:, k, half * 512:(half + 1) * 512],
                                start=(k == 0), stop=(k == 7))
                for half in range(2):
                    self.tt('dve', xt[:, s, half * 512:(half + 1) * 512], xt[:, s, half * 512:(half + 1) * 512], pss[half].v, ALU.add)
                self.dma('sp', dst[i * TT + s * 128: i * TT + (s + 1) * 128, :], xt[:, s, :])

        load(0)
        load(1)
        front(0)
        for i in range(ntile):
            if S.need_reset():
                S.hard_barrier()
            mid(i)
            if i + 1 < ntile:
                front(i + 1)
            back(i)
            load(i + 2)

    def phaseC(self, es, src, dst, final=True):
        nc, S = self.nc, self.S
        TT = 256
        NS = TT // 128
        NJ = DFF // 128
        inp = self.inp
        self._stg_i = 0
        cols, ci = self.load_cols(es, "c_cols", [("g", inp["norm_ffn_g"]), ("cw0", inp["ffn_conv_w"][0]),
                                                 ("cw1", inp["ffn_conv_w"][1]), ("cw2", inp["ffn_conv_w"][2]),
                                                 ("cb", inp["ffn_conv_b"])])
        wi = self.sb(es, "c_wi", [128, 8, 2 * DFF], BF16)
        wo = self.sb(es, "c_wo", [128, NJ, D], BF16)
        gfin = self.sb(es, "c_gfin", [128, D], F32)
        self.dma('sp', gfin.v, inp["norm_final_g"].partition_broadcast(128))
        with ExitStack() as tes:
            stg = [self.sb(tes, "c_stg%d" % i, [128, 2048], F32) for i in range(3)]
            for k in range(8):
                self.load_weight(stg, wi[:, k, :], inp["w_ffn_in"][k * 128:(k + 1) * 128, :], 2 * DFF,
                                 scale=cols[:, ci["g"] + k: ci["g"] + k + 1])
            for j in range(NJ):
                self.load_weight(stg, wo[:, j, :], inp["w_ffn_out"][j * 128:(j + 1) * 128, :], D)
            S.barrier()
        xts = [self.sb(es, "c_xt%d" % i, [128, NS, D], F32) for i in range(2)]
        hs = [self.sb(es, "c_h%d" % i, [128, NS, D], BF16) for i in range(2)]
        hTs = [self.sb(es, "c_hT%d" % i, [128, 8, TT], BF16) for i in range(2)]
        actT = [self.sb(es, "c_actT%d" % j, [128, TT], BF16) for j in range(NJ)]
        junk = self.sb(es, "c_junk", [128, D], BF16)
        sss = [self.sb(es, "c_ss%d" % i, [128, 4], F32) for i in range(3)]
        rstds = [self.sb(es, "c_rstd%d" % i, [128, 4], F32) for i in range(3)]
        halos = [self.sb(es, "c_halo%d" % j, [128, 2], F32) for j in range(NJ)]
        NW = 4
        uraw = [es.enter_context(self.nc.sbuf_tensor("c_uw%d" % i, [128, 2 + TT], F32)) for i in range(NW)]
        uh = [T(r[:, 0:2], "c_uh") for r in uraw]
        ub = [T(r[:, 2:2 + TT], "c_ub") for r in uraw]

        def uview(jj, a, b_):
            return V([uh[jj], ub[jj]], uraw[jj][:, a:b_])

        acc = [self.sb(es, "c_acc%d" % i, [128, TT], F32) for i in range(NW)]
        th = [self.sb(es, "c_th%d" % i, [128, TT], F32) for i in range(NW)]
        ntile = self.ntok // TT
        tiles_per_b = self.seq // TT

        def load(i):
            if i < ntile:
                self.dma('sp', xts[i % 2].v, src[i * TT:(i + 1) * TT, :].rearrange("(s p) d -> p s d", p=128))

        def front(i):
            self.rms_h(xts[i % 2], NS, sss[i % 2], rstds[i % 2], junk, hs[i % 2])
            self.transpose_h(hs[i % 2], NS, hTs[i % 2])

        def mid(i):
            hT = hTs[i % 2]
            st = {}
            for j in range(NJ + 2):
                if j < NJ:
                    ps = self.next_ps()
                    for half in range(2):
                        c0 = half * DFF + j * 128
                        for k in range(8):
                            self.mm(ps[:, half * TT:(half + 1) * TT], wi[:, k, c0:c0 + 128], hT[:, k, :], start=(k == 0), stop=(k == 7))
                    jj = j % NW
                    a = acc[jj]
                    w0 = cols[:, ci["cw0"] + j: ci["cw0"] + j + 1]
                    w1 = cols[:, ci["cw1"] + j: ci["cw1"] + j + 1]
                    w2 = cols[:, ci["cw2"] + j: ci["cw2"] + j + 1]
                    cb = cols[:, ci["cb"] + j: ci["cb"] + j + 1]
                    self.cp('dve', uh[jj].v, halos[j].v)
                    self.cp('act', ub[jj].v, ps[:, 0:TT])
                    self.act(a.v, ps[:, 0:TT], AF.Identity, scale=w2, bias=cb)
                    self.stt('dve', a.v, uview(jj, 1, 1 + TT), w1, a.v, ALU.mult, ALU.add)
                    self.stt('dve', a.v, uview(jj, 0, TT), w0, a.v, ALU.mult, ALU.add)
                    self.cp('dve', halos[j].v, ub[jj][:, TT - 2:TT])
                    st[j] = ps
                if 0 <= j - 1 < NJ:
                    jj = (j - 1) % NW
                    self.act(th[jj].v, acc[jj].v, AF.Gelu_apprx_tanh)
                if 0 <= j - 2 < NJ:
                    jj = (j - 2) % NW
                    self.tt('dve', actT[j - 2].v, th[jj].v, st[j - 2][:, TT:2 * TT], ALU.mult)

        def back(i):
            xt = xts[i % 2]
            ss, rstd = sss[2], rstds[2]
            for s in range(NS):
                pss = [self.next_ps(), self.next_ps()]
                for half in range(2):
                    for j in range(NJ):
                        self.mm(pss[half].v, actT[j][:, s * 128:(s + 1) * 128], wo[:, j, half * 512:(half + 1) * 512],
                                start=(j == 0), stop=(j == NJ - 1))
                for half in range(2):
                    self.tt('dve', xt[:, s, half * 512:(half + 1) * 512], xt[:, s, half * 512:(half + 1) * 512], pss[half].v, ALU.add)
                if final:
                    self.act(junk.v, xt[:, s, :], AF.Square, accum=ss[:, s:s + 1])
                    self.ts('dve', rstd[:, s:s + 1], ss[:, s:s + 1], 1.0 / D, NORM_EPS, ALU.mult, ALU.add)
                    self.rsqrt(rstd[:, s:s + 1], rstd[:, s:s + 1])
                    self.stt('dve', xt[:, s, :], xt[:, s, :], rstd[:, s:s + 1], gfin.v, ALU.mult, ALU.mult)
                self.dma('sp', dst[i * TT + s * 128: i * TT + (s + 1) * 128, :], xt[:, s, :])

        load(0)
        load(1)
        front(0)
        for i in range(ntile):
            if S.need_reset():
                S.hard_barrier()
            if i % tiles_per_b == 0:
                for j in range(NJ):
                    self.memset('pool', halos[j].v, 0.0)
            mid(i)
            if i + 1 < ntile:
                front(i + 1)
            back(i)
            load(i + 2)


_PROG_CACHE = {}


def get_prog(nb=4, seq=2048, phases="LRMBC"):
    key = (nb, seq, phases)
    if key not in _PROG_CACHE:
        _PROG_CACHE[key] = Prog(nb, seq, phases)
    return _PROG_CACHE[key]


_W_NAMES = ["norm_mix_g", "w_in", "b_in", "mu_shift", "w0", "w_lora_up", "a0", "a_lora_up", "g_lora_up", "k_k", "k_a",
            "r_k", "lnx_g", "lnx_b", "w_branch_a", "conv_b_w", "conv_b_b", "w_rg_a", "b_rg_a", "w_rg_x", "b_rg_x",
            "lru_lambda", "w_branch_b", "w_mix_out", "norm_x_g", "norm_mem_g", "w_cq", "w_ckv", "w_co", "norm_ffn_g",
            "w_ffn_in", "ffn_conv_w", "ffn_conv_b", "w_ffn_out", "norm_final_g"]


def run(inputs, ncores=8, nb=4, seq=2048, phases="LRMBC"):
    prog = get_prog(nb, seq, phases)
    shapes = {k: tuple(v.shape) for k, v in prog.inp.items()}
    shared = {}
    for k in _W_NAMES:
        a = np.ascontiguousarray(np.asarray(inputs[k], dtype=np.float32))
        shared[k] = a.reshape(shapes[k])
    x = np.asarray(inputs["x"], dtype=np.float32)
    mem = np.asarray(inputs["mem"], dtype=np.float32)
    in_maps = []
    for c in range(ncores):
        m = dict(shared)
        m["x"] = np.ascontiguousarray(x[c * nb:(c + 1) * nb, :seq]).reshape(nb * seq, D)
        m["mem"] = np.ascontiguousarray(mem[c * nb:(c + 1) * nb]).reshape(nb * NMEM, D)
        in_maps.append(m)
    res = run_bass_kernel_spmd(prog.nc, in_maps, core_ids=list(range(ncores)))
    outs = [np.asarray(r["out"]).reshape(nb, seq, D) for r in res.results]
    return np.concatenate(outs, axis=0)


def kernel(**inputs):
    return run(inputs).astype(np.float32)
```

```python
import numpy as np
from contextlib import ExitStack
import concourse.bass as bass
import concourse.mybir as mybir
from concourse.bass_utils import run_bass_kernel_spmd

F32 = mybir.dt.float32
BF16 = mybir.dt.bfloat16
AF = mybir.ActivationFunctionType
ALU = mybir.AluOpType
AX = mybir.AxisListType

D = 1024
NMEM = 256
AW = 512
RWKV_COLS = 1792
P_IN = 4864
DFF = 2816
NORM_EPS = 1e-6
LNX_EPS = 64e-5
SAME_ENGINE_SYNC = True
import os as _os
RSTOP = int(_os.environ.get("RSTOP", "0"))
LVAR = int(_os.environ.get("LVAR", "2"))


_ALL_TILES = []


class T:
    def __init__(self, ap, name=""):
        self.ap = ap if isinstance(ap, bass.AP) else ap[:]
        self.w = None
        self.r = []
        self.name = name
        self.dsem = None
        _ALL_TILES.append(self)

    def __getitem__(self, k):
        return V([self], self.ap[k])

    @property
    def v(self):
        return V([self], self.ap)


class V:
    def __init__(self, ts, ap):
        self.ts = ts
        self.ap = ap

    def __getitem__(self, k):
        return V(self.ts, self.ap[k])

    def re(self, pat, **kw):
        return V(self.ts, self.ap.rearrange(pat, **kw))

    def bc(self, shape):
        return V(self.ts, self.ap.to_broadcast(shape))

    def un(self, axis):
        return V(self.ts, self.ap.unsqueeze(axis))

    def bitcast(self, dt):
        return V(self.ts, self.ap.bitcast(dt))


def _ap(x):
    return x.ap if isinstance(x, V) else x


class Sync:
    def __init__(self, nc, es, n_dma_sems=64):
        self.nc = nc
        self.es = es
        self.engs = {'pe': nc.tensor, 'act': nc.scalar, 'dve': nc.vector, 'pool': nc.gpsimd, 'sp': nc.sync}
        self.sem = {}
        self.cnt = {}
        self.seen = {}
        for e in self.engs:
            self.sem[e] = es.enter_context(nc.semaphore("c_" + e))
            self.cnt[e] = 0
            self.seen[e] = {}
        self.dpool = [{'sem': es.enter_context(nc.semaphore("d%d" % i)), 'cnt': 0} for i in range(n_dma_sems)]
        self.dfree = list(range(n_dma_sems))
        self.n_inst = 0
        self.bar1 = es.enter_context(nc.semaphore("bar1"))
        self.bar2 = es.enter_context(nc.semaphore("bar2"))
        self.bar_k = 0

    def _wait(self, e, tok):
        if tok is None:
            return
        sem, val, src = tok
        if src == e and (e == 'pe' or not SAME_ENGINE_SYNC):
            return
        key = sem.name
        if self.seen[e].get(key, 0) >= val:
            return
        self.seen[e][key] = val
        self.engs[e].wait_ge(sem, val)

    def deps(self, e, reads, writes):
        for v in reads:
            if not isinstance(v, V):
                continue
            for t in v.ts:
                self._wait(e, t.w)
        for v in writes:
            for t in v.ts:
                self._wait(e, t.w)
                for tok in t.r:
                    self._wait(e, tok)

    def done(self, tok, reads, writes):
        for v in reads:
            if not isinstance(v, V):
                continue
            for t in v.ts:
                t.r.append(tok)
                if len(t.r) > 24:
                    t.r = t.r[-24:] if False else t.r
        for v in writes:
            for t in v.ts:
                t.w = tok
                t.r = []

    def op(self, e, fn, reads, writes, signal=True):
        self.deps(e, reads, writes)
        inst = fn(self.engs[e])
        self.n_inst += 1
        if signal:
            self.cnt[e] += 1
            inst.then_inc(self.sem[e], 1)
            tok = (self.sem[e], self.cnt[e], e)
        else:
            tok = (self.sem[e], self.cnt[e] + 1, e)
        self.done(tok, reads, writes)
        return inst

    def _dsem(self, t):
        if t.dsem is None:
            if not self.dfree:
                raise RuntimeError("out of dma semaphores")
            t.dsem = self.dpool[self.dfree.pop(0)]
        return t.dsem

    def dma(self, q, out, in_, **kw):
        reads = [in_] if isinstance(in_, V) else []
        writes = [out] if isinstance(out, V) else []
        self.deps(q, reads, writes)
        sbv = out if isinstance(out, V) else in_
        ds = self._dsem(sbv.ts[0])
        inst = self.engs[q].dma_start(out=_ap(out), in_=_ap(in_), **kw)
        self.n_inst += 1
        ds['cnt'] += 16
        inst.then_inc(ds['sem'], 16)
        tok = (ds['sem'], ds['cnt'], 'dma')
        self.done(tok, reads, writes)
        return tok

    def barrier(self):
        toks = [(self.sem[f], self.cnt[f], f) for f in self.engs if self.cnt[f] > 0]
        toks += [(d['sem'], d['cnt'], 'dma') for d in self.dpool if d['cnt'] > 0]
        for e in self.engs:
            for tok in toks:
                if tok[2] == e:
                    continue
                self._wait(e, tok)

    def release_dma_sems(self):
        self.dfree = list(range(len(self.dpool)))
        for t in _ALL_TILES:
            t.dsem = None

    def need_reset(self, limit=2600):
        return max(self.cnt.values()) > limit or max(d['cnt'] for d in self.dpool) > limit

    def hard_barrier(self):
        self.barrier()
        self.bar_k += 1
        k = self.bar_k
        for e in self.engs:
            self.engs[e].sem_inc(self.bar1, 1)
        sp = self.engs['sp']
        sp.wait_ge(self.bar1, len(self.engs) * k)
        for e in self.engs:
            if self.cnt[e] > 0:
                sp.sem_clear(self.sem[e])
        for d in self.dpool:
            if d['cnt'] > 0:
                sp.sem_clear(d['sem'])
        sp.sem_inc(self.bar2, 1)
        for e in self.engs:
            if e != 'sp':
                self.engs[e].wait_ge(self.bar2, k)
        for e in self.engs:
            self.cnt[e] = 0
            self.seen[e] = {}
        for d in self.dpool:
            d['cnt'] = 0
        for t in _ALL_TILES:
            t.w = None
            t.r = []


class Prog:
    def __init__(self, nb=4, seq=2048, phases="LRMBC", dbg=False):
        self.nb, self.seq, self.phases, self.dbg = nb, seq, phases, dbg
        self.ntok = nb * seq
        nc = self.nc = bass.Bass("TRN2", target_bir_lowering=False)
        self.inp = {}
        del _ALL_TILES[:]

        def din(name, shape):
            self.inp[name] = nc.dram_tensor(name, list(shape), F32, kind="ExternalInput").ap()
            return self.inp[name]

        ntok = self.ntok
        din("x", [ntok, D])
        din("mem", [nb * NMEM, D])
        din("norm_mix_g", [D]); din("w_in", [D, P_IN]); din("b_in", [P_IN]); din("mu_shift", [RWKV_COLS])
        din("w0", [AW]); din("w_lora_up", [64, AW]); din("a0", [AW]); din("a_lora_up", [64, AW])
        din("g_lora_up", [128, AW]); din("k_k", [AW]); din("k_a", [AW]); din("r_k", [AW])
        din("lnx_g", [AW]); din("lnx_b", [AW]); din("w_branch_a", [AW, D])
        din("conv_b_w", [4, AW]); din("conv_b_b", [AW]); din("w_rg_a", [8, 64, 64]); din("b_rg_a", [AW])
        din("w_rg_x", [8, 64, 64]); din("b_rg_x", [AW]); din("lru_lambda", [AW]); din("w_branch_b", [AW, D])
        din("w_mix_out", [D, D]); din("norm_x_g", [D]); din("norm_mem_g", [D]); din("w_cq", [D, D])
        din("w_ckv", [D, 2 * D]); din("w_co", [D, D]); din("norm_ffn_g", [D]); din("w_ffn_in", [D, 2 * DFF])
        din("ffn_conv_w", [3, DFF]); din("ffn_conv_b", [DFF]); din("w_ffn_out", [DFF, D]); din("norm_final_g", [D])
        self.out = nc.dram_tensor("out", [ntok, D], F32, kind="ExternalOutput").ap()
        self.x1 = nc.dram_tensor("x1s", [ntok, D], F32).ap()
        self.x2 = nc.dram_tensor("x2s", [ntok, D], F32).ap()
        self.dbg_out = {}

        with ExitStack() as es:
            self.es = es
            self.S = Sync(nc, es)
            self.ps = [T(es.enter_context(nc.psum_tensor("ps%d" % i, [128, 512], F32)), "ps%d" % i) for i in range(8)]
            self.ps_i = 0
            self.ident = self.sb(es, "ident", [128, 128], BF16)
            self.identf = self.sb(es, "identf", [128, 128], F32)
            for idt in (self.ident, self.identf):
                self.S.op('pool', lambda e: e.memset(idt.ap[:], 0.0), [], [idt.v])
                self.S.op('pool', lambda e: e.affine_select(idt.ap[:], idt.ap[:], pattern=[[-1, 128]],
                                                           compare_op=ALU.not_equal, fill=1.0, base=0,
                                                           channel_multiplier=1), [idt.v], [idt.v])
            self.mhalf = self.sb(es, "mhalf", [128, 512], F32)
            self.memset('pool', self.mhalf.v, -0.5)
            self.phalf = self.sb(es, "phalf", [128, 512], F32)
            self.memset('pool', self.phalf.v, 0.5)
            self.yas = nc.dram_tensor("yas", [AW, ntok], BF16).ap()
            self.ybs = nc.dram_tensor("ybs", [AW, ntok], BF16).ap()
            src = {'L': self.inp["x"], 'R': self.inp["x"], 'M': self.inp["x"], 'B': self.x1, 'C': self.x2}
            dst = {'L': None, 'R': None, 'M': self.x1, 'B': self.x2, 'C': self.out}
            order = [p for p in "LRMBC" if p in phases]
            chain = [p for p in order if p in "MBC"]
            for i, p in enumerate(order):
                s_ap, d_ap = src[p], dst[p]
                if p in chain:
                    if chain.index(p) == 0:
                        s_ap = self.inp["x"]
                    if chain.index(p) == len(chain) - 1:
                        d_ap = self.out
                with ExitStack() as pes:
                    getattr(self, "phase" + p)(pes, s_ap, d_ap, final=(p == 'C'))
                    self.S.hard_barrier()
                self.S.release_dma_sems()
            self.S.barrier()

    def sb(self, es, name, shape, dt):
        return T(es.enter_context(self.nc.sbuf_tensor(name, list(shape), dt)), name)

    ps_allowed = None

    def next_ps(self):
        if self.ps_allowed is not None:
            self._psa_i = getattr(self, "_psa_i", 0) + 1
            return self.ps[self.ps_allowed[self._psa_i % len(self.ps_allowed)]]
        t = self.ps[self.ps_i]
        self.ps_i = (self.ps_i + 1) % 8
        return t

    def act(self, out, in_, func, bias=0.0, scale=1.0, accum=None):
        rd = [in_] + [a for a in (bias, scale) if isinstance(a, V)]
        wr = [out] + ([accum] if accum is not None else [])
        kw = {}
        if accum is not None:
            kw['accum_out'] = accum.ap
        return self.S.op('act', lambda e: e.activation(out=out.ap, in_=in_.ap, func=func, bias=_ap(bias),
                                                       scale=_ap(scale), **kw), rd, wr)

    def tt(self, eng, out, in0, in1, op):
        return self.S.op(eng, lambda e: e.tensor_tensor(out=out.ap, in0=in0.ap, in1=in1.ap, op=op), [in0, in1], [out])

    def ts(self, eng, out, in0, s1, s2, op0, op1=None):
        rd = [in0] + [a for a in (s1, s2) if isinstance(a, V)]
        if op1 is None:
            return self.S.op(eng, lambda e: e.tensor_scalar(out=out.ap, in0=in0.ap, scalar1=_ap(s1), scalar2=None,
                                                            op0=op0), rd, [out])
        return self.S.op(eng, lambda e: e.tensor_scalar(out=out.ap, in0=in0.ap, scalar1=_ap(s1), scalar2=_ap(s2),
                                                        op0=op0, op1=op1), rd, [out])

    def stt(self, eng, out, in0, scalar, in1, op0, op1):
        rd = [in0, in1] + ([scalar] if isinstance(scalar, V) else [])
        return self.S.op(eng, lambda e: e.scalar_tensor_tensor(out=out.ap, in0=in0.ap, scalar=_ap(scalar), in1=in1.ap,
                                                               op0=op0, op1=op1), rd, [out])

    def cp(self, eng, out, in_):
        if eng == 'act':
            return self.act(out, in_, AF.Copy)
        return self.S.op(eng, lambda e: e.tensor_copy(out=out.ap, in_=in_.ap), [in_], [out])

    def rsqrt(self, out, in_):
        shp = list(in_.ap.shape)
        if shp[-1] > 8:
            self.act(out, in_, AF.Ln)
            return self.act(out, out, AF.Exp, scale=-0.5)
        mh = self.mhalf[0:shp[0], 0:shp[-1]]
        if len(shp) == 3:
            mh = mh.un(1).bc(shp)
        return self.tt('pool', out, in_, mh, ALU.pow)

    def memset(self, eng, out, val):
        return self.S.op(eng, lambda e: e.memset(out.ap, val), [], [out])

    def mm(self, out, lhsT, rhs, start, stop, signal=None):
        if signal is None:
            signal = stop
        return self.S.op('pe', lambda e: e.matmul(out.ap, lhsT=lhsT.ap, rhs=rhs.ap, start=start, stop=stop),
                         [lhsT, rhs], [out], signal=signal)

    def tr(self, out, in_, signal=True, f32=False):
        idt = self.identf if f32 else self.ident
        n = in_.ap.shape[0]
        return self.S.op('pe', lambda e: e.transpose(out.ap, in_.ap, idt.ap[0:n, 0:n]), [in_, idt.v], [out],
                         signal=signal)

    def dma(self, q, out, in_, **kw):
        return self.S.dma(q, out, in_, **kw)

    def load_weight(self, stg, dst, src_rows, ncols, scale=None, piece=2048):
        rows = src_rows.shape[0]
        c0 = 0
        while c0 < ncols:
            n = min(piece, ncols - c0)
            st = stg[self._stg_i % len(stg)]
            eng = ('dve', 'act')[self._stg_i % 2]
            self._stg_i += 1
            self.dma('sp', st[0:rows, 0:n], src_rows[:, c0:c0 + n])
            o = dst[:, c0:c0 + n]
            i = st[0:rows, 0:n]
            if scale is None:
                self.cp(eng, o, i)
            elif eng == 'act':
                if isinstance(scale, V):
                    self.act(o, i, AF.Copy, scale=scale)
                else:
                    self.act(o, i, AF.Copy, scale=float(scale))
            else:
                self.ts(eng, o, i, scale, None, ALU.mult)
            c0 += n

    def load_cols(self, es, name, specs):
        cols = {}
        n = 0
        for k, v in specs:
            cols[k] = n
            n += (v.shape[0] + 127) // 128
        res = self.sb(es, name, [128, n], F32)
        with ExitStack() as tes:
            ngrp = (n + 127) // 128
            stage = [self.sb(tes, name + "_st%d" % g, [128, 128], F32) for g in range(ngrp)]
            for st in stage:
                self.memset('pool', st.v, 0.0)
            for k, v in specs:
                L = v.shape[0]
                m = (L + 127) // 128
                c = cols[k]
                r = 0
                while r < m:
                    g, rr = divmod(c + r, 128)
                    cnt = min(m - r, 128 - rr)
                    if L >= 128:
                        self.dma('sp', stage[g][rr:rr + cnt, :], v[r * 128:(r + cnt) * 128].rearrange("(m p) -> m p", p=128))
                    else:
                        self.dma('sp', stage[g][rr:rr + 1, 0:L], v.rearrange("(m p) -> m p", m=1))
                    r += cnt
            for g in range(ngrp):
                w = min(128, n - g * 128)
                ps = self.next_ps()
                self.tr(ps[:, 0:w], stage[g][0:w, :], f32=True)
                self.cp('dve', res[:, g * 128:g * 128 + w], ps[:, 0:w])
            self.S.barrier()
        return res, cols

    def rms_h(self, xt, nsub, ss, rstd, junk, h):
        for s in range(nsub):
            self.act(junk.v, xt[:, s, :], AF.Square, accum=ss[:, s:s + 1])
        self.ts('dve', rstd[:, 0:nsub], ss[:, 0:nsub], 1.0 / D, NORM_EPS, ALU.mult, ALU.add)
        self.rsqrt(rstd[:, 0:nsub], rstd[:, 0:nsub])
        for s in range(nsub):
            self.ts('dve', h[:, s, :], xt[:, s, :], rstd[:, s:s + 1], None, ALU.mult)

    def transpose_h(self, h, nsub, hT, evac=('act', 'dve')):
        tw = nsub * 128
        per_bank = 1024 // tw
        c = 0
        i = 0
        while c < 8:
            ps = self.next_ps()
            pb = ps.v.bitcast(BF16)
            nchunk = min(per_bank, 8 - c)
            for cc in range(nchunk):
                for s in range(nsub):
                    last = (cc == nchunk - 1 and s == nsub - 1)
                    self.tr(pb[:, cc * tw + s * 128: cc * tw + (s + 1) * 128], h[:, s, (c + cc) * 128:(c + cc + 1) * 128],
                            signal=last)
            self.cp(evac[i % len(evac)], hT[:, c:c + nchunk, :], pb[:, 0:nchunk * tw].re("p (c t) -> p c t", c=nchunk))
            c += nchunk
            i += 1

    def norm_from_dram(self, src, tok0, NS, xbufs, sst, rst, junk, h):
        for s in range(NS):
            xb = xbufs[self._xb_i % len(xbufs)]
            self._xb_i += 1
            self.dma('sp', xb.v, src[tok0 + s * 128: tok0 + (s + 1) * 128, :])
            self.act(junk.v, xb.v, AF.Square, accum=sst[s].v)
            self.ts('dve', rst[s].v, sst[s].v, 1.0 / D, NORM_EPS, ALU.mult, ALU.add)
            self.rsqrt(rst[s].v, rst[s].v)
            self.ts('dve', h[:, s, :], xb.v, rst[s].v, None, ALU.mult)

    def norm_bufs(self, es, pfx, NS):
        xbufs = [self.sb(es, pfx + "_xb%d" % i, [128, D], F32) for i in range(2)]
        sst = [self.sb(es, pfx + "_ss%d" % i, [128, 1], F32) for i in range(NS)]
        rst = [self.sb(es, pfx + "_rs%d" % i, [128, 1], F32) for i in range(NS)]
        junk = self.sb(es, pfx + "_junk", [128, D], BF16)
        self._xb_i = 0
        return xbufs, sst, rst, junk

    def phaseL(self, es, src, dst, final=False):
        S = self.S
        inp = self.inp
        TT = 512
        NS = 4
        self._stg_i = 0
        self._ev_i = 0
        cols, ci = self.load_cols(es, "l_cols", [
            ("g", inp["norm_mix_g"]), ("bin", inp["b_in"][1792:2816]),
            ("cw0", inp["conv_b_w"][0]), ("cw1", inp["conv_b_w"][1]), ("cw2", inp["conv_b_w"][2]), ("cw3", inp["conv_b_w"][3]),
            ("cbb", inp["conv_b_b"]), ("bra", inp["b_rg_a"]), ("brx", inp["b_rg_x"]), ("lam", inp["lru_lambda"])])
        hb = self.sb(es, "l_hb", [128, 8], F32)
        self.ts('dve', hb[:, 0:4], cols[:, ci["bra"]:ci["bra"] + 4], 0.5, None, ALU.mult)
        self.ts('dve', hb[:, 4:8], cols[:, ci["brx"]:ci["brx"] + 4], 0.5, None, ALU.mult)
        cA = self.sb(es, "l_cA", [128, 8], F32)
        lt = self.sb(es, "l_lt", [128, 4], F32)
        self.act(lt.v, cols[:, ci["lam"]:ci["lam"] + 4], AF.Exp, scale=-1.0)
        self.ts('dve', lt.v, lt.v, 1.0, None, ALU.add)
        self.act(lt.v, lt.v, AF.Ln)
        self.ts('dve', cA[:, 0:4], lt.v, -4.0, None, ALU.mult)
        self.ts('dve', cA[:, 4:8], lt.v, -8.0, None, ALU.mult)
        win = self.sb(es, "l_win", [128, 8, 1024], BF16)
        wg = self.sb(es, "l_wg", [128, 2, 4, 128], BF16)
        with ExitStack() as tes:
            stg = [self.sb(tes, "l_stg%d" % i, [128, 1024], F32) for i in range(3)]
            for k in range(8):
                self.load_weight(stg, win[:, k, :], inp["w_in"][k * 128:(k + 1) * 128, 1792:2816], 1024,
                                 scale=cols[:, ci["g"] + k: ci["g"] + k + 1], piece=1024)
            for gi, nm in enumerate(("w_rg_a", "w_rg_x")):
                st = stg[gi]
                self.memset('pool', st.v, 0.0)
                for blk in range(8):
                    c, hl = divmod(blk, 2)
                    self.dma('sp', st[hl * 64:(hl + 1) * 64, c * 128 + hl * 64: c * 128 + (hl + 1) * 64], inp[nm][blk])
                self.cp('dve', wg[:, gi, :, :], st[:, 0:512].re("p (c m) -> p c m", c=4))
            S.barrier()
        xbufs, sst, rst, junk = self.norm_bufs(es, "l", NS)
        h = self.sb(es, "l_h", [128, NS, D], BF16)
        hT = self.sb(es, "l_hT", [128, 8, TT], BF16)
        PX = [self.sb(es, "l_px%d" % c, [128, 3 + TT], F32) for c in range(4)]
        GY = [self.sb(es, "l_gy%d" % c, [128, TT], F32) for c in range(4)]
        YB = [self.sb(es, "l_yb%d" % c, [128, TT], BF16) for c in range(4)]
        carry = [self.sb(es, "l_cy%d" % c, [128, 1], F32) for c in range(4)]
        NW = 2
        def wk(nm, dt=F32, n=NW):
            return [self.sb(es, "l_%s%d" % (nm, i), [128, TT], dt) for i in range(n)]
        acc = wk("acc", n=4); xbb = wk("xbb", BF16, n=4); ta = wk("ta", n=4); tx = wk("tx", n=4)
        av = wk("av"); a2 = wk("a2"); u = wk("u"); hl_ = wk("hl")
        ntile = self.ntok // TT
        tiles_per_b = self.seq // TT
        hs = [h, self.sb(es, "l_h1", [128, NS, D], BF16)]
        hTs = [hT, self.sb(es, "l_hT1", [128, 8, TT], BF16)]

        def front(i):
            self.norm_from_dram(src, i * TT, NS, xbufs, sst, rst, junk, hs[i % 2])
            self.transpose_h(hs[i % 2], NS, hTs[i % 2])

        def midA(i):
            hT_ = hTs[i % 2]
            first = (i % tiles_per_b == 0)
            if first:
                for c in range(4):
                    self.memset('pool', PX[c][:, 0:3], 0.0)
                    self.memset('pool', carry[c].v, 0.0)
            for c in range(4):
                ps = self.next_ps()
                for k in range(8):
                    self.mm(ps.v, win[:, k, c * 128:(c + 1) * 128], hT_[:, k, :], start=(k == 0), stop=(k == 7))
                self.act(PX[c][:, 3:3 + TT], ps.v, AF.Identity, bias=cols[:, ci["bin"] + c: ci["bin"] + c + 1])
            for c in range(4):
                ps = self.next_ps()
                for k in range(8):
                    self.mm(ps.v, win[:, k, 512 + c * 128: 512 + (c + 1) * 128], hT_[:, k, :], start=(k == 0), stop=(k == 7))
                self.act(GY[c].v, ps.v, AF.Gelu_apprx_tanh, bias=cols[:, ci["bin"] + 4 + c: ci["bin"] + 5 + c])
            for c in range(4):
                A = acc[c]; XB = xbb[c]; TA = ta[c]; TX = tx[c]
                cw = [cols[:, ci["cw%d" % j] + c: ci["cw%d" % j] + c + 1] for j in range(4)]
                self.ts('dve', A.v, PX[c][:, 0:TT], cw[0], cols[:, ci["cbb"] + c: ci["cbb"] + c + 1], ALU.mult, ALU.add)
                for j in range(1, 4):
                    self.stt('dve', A.v, PX[c][:, j:j + TT], cw[j], A.v, ALU.mult, ALU.add)
                self.cp('pool', PX[c][:, 0:3], PX[c][:, TT:TT + 3])
                self.cp('dve', XB.v, A.v)
                psa = self.next_ps()
                self.mm(psa.v, wg[:, 0, c, :], XB.v, start=True, stop=True)
                psx = self.next_ps()
                self.mm(psx.v, wg[:, 1, c, :], XB.v, start=True, stop=True)
                self.act(TA.v, psa.v, AF.Tanh, scale=0.5, bias=hb[:, c:c + 1])
                self.act(TX.v, psx.v, AF.Tanh, scale=0.5, bias=hb[:, 4 + c:5 + c])

        def midB(i):
            tok0 = i * TT
            first = (i % tiles_per_b == 0)
            for c in range(4):
                A = acc[c]; TA = ta[c]; TX = tx[c]; AV = av[c % NW]; A2 = a2[c % NW]; U = u[c % NW]; HL = hl_[c % NW]
                self.act(AV.v, TA.v, AF.Exp, scale=cA[:, c:c + 1], bias=cA[:, c:c + 1])
                self.act(A2.v, TA.v, AF.Exp, scale=cA[:, 4 + c:5 + c], bias=cA[:, 4 + c:5 + c])
                self.ts('dve', A2.v, A2.v, -1.0, 1.0, ALU.mult, ALU.add)
                self.ts('dve', A2.v, A2.v, 1e-30, None, ALU.max)
                self.act(A2.v, A2.v, AF.Ln)
                self.act(A2.v, A2.v, AF.Exp, scale=0.5)
                if first:
                    self.memset('pool', A2[:, 0:1], 1.0)
                self.stt('dve', U.v, TX.v, 1.0, A.v, ALU.add, ALU.mult)
                self.tt('dve', U.v, U.v, A2.v, ALU.mult)
                self.S.op('dve', lambda e: e.tensor_tensor_scan(HL.ap, AV.ap, U.ap, carry[c].ap, ALU.mult, ALU.add),
                          [AV.v, U.v, carry[c].v], [HL.v])
                self.cp('pool', carry[c].v, HL[:, TT - 1:TT])
                self.tt('dve', YB[c].v, HL.v, GY[c].v, ALU.mult)
                self.dma('sp', self.ybs[c * 128:(c + 1) * 128, tok0:tok0 + TT], YB[c].v)

        front(0)
        for i in range(ntile):
            if S.need_reset():
                S.hard_barrier()
            midA(i)
            if i + 1 < ntile:
                front(i + 1)
            midB(i)

    def phaseR(self, es, src, dst, final=False):
        S = self.S
        inp = self.inp
        TT = 256
        NS = 2
        NQ = 4
        CH = 64
        self._stg_i = 0
        self._ev_i = 0
        cols, ci = self.load_cols(es, "r_cols", [
            ("g", inp["norm_mix_g"]), ("bin", inp["b_in"][0:RWKV_COLS]), ("mu", inp["mu_shift"]),
            ("w0", inp["w0"]), ("a0", inp["a0"]), ("kk", inp["k_k"]), ("ka", inp["k_a"]), ("rk", inp["r_k"])])

        def col(key, j):
            return cols[:, ci[key] + j: ci[key] + j + 1]

        der = self.sb(es, "r_der", [128, 32], F32)
        self.ts('dve', der[:, 0:14], cols[:, ci["mu"]:ci["mu"] + 14], -1.0, 1.0, ALU.mult, ALU.add)
        self.ts('dve', der[:, 14:18], cols[:, ci["w0"]:ci["w0"] + 4], 0.5, None, ALU.mult)
        self.ts('dve', der[:, 18:22], cols[:, ci["a0"]:ci["a0"] + 4], 0.5, None, ALU.mult)
        self.ts('dve', der[:, 22:26], cols[:, ci["ka"]:ci["ka"] + 4], 0.5, None, ALU.mult)
        self.ts('dve', der[:, 26:30], cols[:, ci["ka"]:ci["ka"] + 4], -0.5, 1.0, ALU.mult, ALU.add)
        rkb = self.sb(es, "r_rkb", [128, 4], BF16)
        self.cp('dve', rkb.v, cols[:, ci["rk"]:ci["rk"] + 4])
        m64 = [self.sb(es, "r_m64_%d" % i, [64, 64], BF16) for i in range(3)]
        masks = [self.sb(es, "r_mask%d" % i, [128, 128], BF16) for i in range(3)]
        specs = [([[1, 64]], -1, ALU.is_gt), ([[-1, 64]], 1, ALU.is_gt), ([[1, 64]], -1, ALU.is_ge)]
        for mi in range(3):
            pat, cm, op = specs[mi]
            self.memset('pool', m64[mi].v, 1.0)
            self.S.op('pool', lambda e: e.affine_select(m64[mi].ap, m64[mi].ap, pattern=pat, compare_op=op, fill=0.0,
                                                        base=0, channel_multiplier=cm), [m64[mi].v], [m64[mi].v])
            for a in range(2):
                for b_ in range(2):
                    self.dma('sp', masks[mi][a * 64:(a + 1) * 64, b_ * 64:(b_ + 1) * 64], m64[mi].v)
        mask_su, mask_sl, mask_u = masks
        bones = self.sb(es, "r_bones", [128, 128], BF16)
        self.memset('pool', bones.v, 0.0)
        self.memset('pool', bones[0:64, 0:64], 1.0)
        self.memset('pool', bones[64:128, 64:128], 1.0)
        m01 = self.sb(es, "r_m01", [128, TT], F32)
        self.memset('pool', m01.v, 1.0)
        self.memset('pool', m01.v.re("p (q s) -> p q s", s=CH)[:, :, 0:1], 0.0)
        lnxg = self.sb(es, "r_lnxg", [128, 4, 64], F32)
        lnxb = self.sb(es, "r_lnxb", [128, 4, 64], F32)
        for c in range(4):
            for hl in range(2):
                hh = 2 * c + hl
                self.dma('sp', lnxg[hl * 64:(hl + 1) * 64, c, :], inp["lnx_g"][hh * 64:(hh + 1) * 64].partition_broadcast(64))
                self.dma('sp', lnxb[hl * 64:(hl + 1) * 64, c, :], inp["lnx_b"][hh * 64:(hh + 1) * 64].partition_broadcast(64))
        win = self.sb(es, "r_win", [128, 8, RWKV_COLS], BF16)
        wlora = self.sb(es, "r_wlora", [128, 2, AW], BF16)
        gup = self.sb(es, "r_gup", [128, AW], BF16)
        with ExitStack() as tes:
            stg = [self.sb(tes, "r_stg%d" % i, [128, RWKV_COLS], F32) for i in range(3)]
            for k in range(8):
                self.load_weight(stg, win[:, k, :], inp["w_in"][k * 128:(k + 1) * 128, 0:RWKV_COLS], RWKV_COLS,
                                 scale=cols[:, ci["g"] + k: ci["g"] + k + 1], piece=RWKV_COLS)
            st = stg[0]
            self.memset('pool', st[:, 0:2 * AW], 0.0)
            self.dma('sp', st[0:64, 0:AW], inp["w_lora_up"])
            self.dma('sp', st[64:128, AW:2 * AW], inp["a_lora_up"])
            self.cp('dve', wlora.v, st[:, 0:2 * AW].re("p (a m) -> p a m", a=2))
            st = stg[1]
            self.dma('sp', st[:, 0:AW], inp["g_lora_up"])
            self.cp('dve', gup.v, st[:, 0:AW])
            S.barrier()
        xbufs, sst, rst, junk = self.norm_bufs(es, "r", NS)
        h = self.sb(es, "r_h", [128, NS, D], BF16)
        hT = self.sb(es, "r_hT", [128, 8, TT], BF16)
        pcarry = [self.sb(es, "r_pc%d" % m, [128, 1], F32) for m in range(14)]
        Sx = [self.sb(es, "r_s%d" % m, [128, TT], F32) for m in range(14)]
        ltmp = [self.sb(es, "r_lt%d" % i, [128, TT], F32) for i in range(2)]
        LB = self.sb(es, "r_lb", [128, TT], BF16)
        SGd = self.sb(es, "r_sgd", [128, NQ, 128], BF16)
        NW = 2

        def wk(nm, dt=F32):
            return [self.sb(es, "r_%s%d" % (nm, i), [128, TT], dt) for i in range(NW)]

        logw = [self.sb(es, "r_logw%d" % i, [128, TT], F32) for i in range(4)]
        ta = [self.sb(es, "r_ta%d" % i, [128, TT], F32) for i in range(4)]
        cum = [self.sb(es, "r_cum%d" % i, [128, TT], F32) for i in range(4)]
        egm1 = wk("egm1"); eg = wk("eg"); eig = wk("eig"); egc = wk("egc")
        kk = wk("kk"); kk2 = wk("kk2", BF16); rn = wk("rn"); kkn = wk("kkn"); kmod = wk("kmod"); bv = wk("bv")
        names = ["AT", "BT", "KT", "RT", "BGT", "KGT", "VB", "RK"]
        EXP = {n: [self.sb(es, "r_%s%d" % (n, c), [128, NQ, 128], BF16) for c in range(4)] for n in names}
        for n in names:
            for c in range(4):
                self.memset('pool', EXP[n][c].v, 0.0)
        GC = self.sb(es, "r_gc", [128, 4, NQ], F32)
        BKG = [self.sb(es, "r_bkg%d" % q, [128, 2, 4, 128], BF16) for q in range(NQ)]
        Vst = [self.sb(es, "r_vst%d" % q, [128, 4, 64], BF16) for q in range(NQ)]
        Nb = [[self.sb(es, "r_nb%d_%d" % (q, i), [128, 4, 128], BF16) for i in range(2)] for q in range(NQ)]
        Lb = [[self.sb(es, "r_lb%d_%d" % (q, i), [128, 4, 128], BF16) for i in range(2)] for q in range(NQ)]
        Pb = [self.sb(es, "r_pb%d" % q, [128, 4, 128], BF16) for q in range(NQ)]
        LakT = [self.sb(es, "r_lak%d" % q, [128, 4, 128], BF16) for q in range(NQ)]
        MrbT = [self.sb(es, "r_mrb%d" % q, [128, 4, 128], BF16) for q in range(NQ)]
        MrkT = [self.sb(es, "r_mrk%d" % q, [128, 4, 128], BF16) for q in range(NQ)]
        H = self.sb(es, "r_H", [128, 4, 64], F32)
        Hb = self.sb(es, "r_Hb", [128, 4, 64], BF16)
        Xb = [self.sb(es, "r_Xb%d" % i, [128, 4, 64], BF16) for i in range(2)]
        Ub = [self.sb(es, "r_Ub%d" % i, [128, 4, 64], BF16) for i in range(2)]

        def yt(nm, shape=(128, 4, 64), dt=F32):
            return [self.sb(es, "r_%s%d" % (nm, i), list(shape), dt) for i in range(2)]

        Yv = yt("Yv"); Ysq = yt("Ysq"); Yn = yt("Yn"); Bn = yt("Bn"); Yf = yt("Yf", dt=BF16)
        st1 = yt("st1", (128, 4)); st2 = yt("st2", (128, 4)); mean = yt("mean", (128, 4)); var = yt("var", (128, 4))
        YT = [self.sb(es, "r_YT%d" % i, [64, 4, 2, TT], BF16) for i in range(2)]
        ident = self.ident
        ntile = self.ntok // TT
        tiles_per_b = self.seq // TT
        yas_v = self.yas.rearrange("(c hl i) t -> i c hl t", hl=2, i=64)

        PWraw = [es.enter_context(self.nc.sbuf_tensor("r_pwx%d" % i, [128, 1 + TT], F32)) for i in range(3)]
        PWh = [T(r[:, 0:1], "r_pwh") for r in PWraw]
        PWb = [T(r[:, 1:1 + TT], "r_pwb") for r in PWraw]

        def s12_pieces(it):
            tok0 = it * TT

            def p0():
                if it % tiles_per_b == 0:
                    for m in range(14):
                        self.memset('pool', pcarry[m].v, 0.0)
                self.norm_from_dram(src, tok0, NS, xbufs, sst, rst, junk, h)
                self.transpose_h(h, NS, hT)

            def proj(ms):
                def f():
                    ps = None
                    for idx, m in enumerate(ms):
                        if idx % 2 == 0:
                            ps = self.next_ps()
                        o = (idx % 2) * TT
                        for k in range(8):
                            self.mm(ps[:, o:o + TT], win[:, k, m * 128:(m + 1) * 128], hT[:, k, :], start=(k == 0), stop=(k == 7))
                        pi = m % 3
                        self.cp('dve', PWh[pi].v, pcarry[m].v)
                        self.act(PWb[pi].v, ps[:, o:o + TT], AF.Identity, bias=col("bin", m))
                        self.cp('dve', pcarry[m].v, PWb[pi][:, TT - 1:TT])
                        tmp = ltmp[m % 2]
                        self.act(tmp.v, V([PWh[pi], PWb[pi]], PWraw[pi][:, 0:TT]), AF.Copy, scale=col("mu", m))
                        self.stt('dve', Sx[m].v, PWb[pi].v, der[:, m:m + 1], tmp.v, ALU.mult, ALU.add)
                return f

            return [p0] + [proj([m, m + 1]) for m in range(0, 14, 2)]

        def stage3(it):
            if it % tiles_per_b == 0:
                self.memset('pool', H.v, 0.0)
                self.memset('pool', Hb.v, 0.0)
            for _ in range(1):
                if RSTOP == 1:
                    return
                self.act(LB[0:64, :], Sx[12][0:64, :], AF.Tanh)
                self.cp('act', LB[64:128, :], Sx[12][64:128, :])
                tmp = ltmp[0]
                self.act(tmp.v, Sx[13].v, AF.Tanh, scale=0.5)
                for hl in range(2):
                    self.ts('dve', SGd[:, :, hl * 64:(hl + 1) * 64], tmp.v.re("p (q s) -> p q s", s=CH), 0.5, 0.5, ALU.mult, ALU.add)
                if RSTOP == 21:
                    return
                for c in range(4):
                    LW = logw[c]; TA = ta[c]; CU = cum[c]
                    ps = self.next_ps()
                    self.mm(ps[:, 0:TT], wlora[:, 0, c * 128:(c + 1) * 128], LB.v, start=True, stop=True)
                    self.mm(ps[:, TT:2 * TT], wlora[:, 1, c * 128:(c + 1) * 128], LB.v, start=True, stop=True)
                    self.act(LW.v, ps[:, 0:TT], AF.Tanh, scale=0.5, bias=der[:, 14 + c:15 + c])
                    self.act(TA.v, ps[:, TT:2 * TT], AF.Tanh, scale=0.5, bias=der[:, 18 + c:19 + c])
                    self.ts('dve', LW.v, LW.v, 1.0, -0.30326532985631671, ALU.add, ALU.mult)
                    self.S.op('dve', lambda e: e.tensor_tensor_scan(CU.ap, m01.ap, LW.ap, 0.0, ALU.mult, ALU.add),
                              [m01.v, LW.v], [CU.v])
                def stageB(c):
                    w = it * 4 + c
                    r_, k_, v_ = Sx[c], Sx[4 + c], Sx[8 + c]
                    LW = logw[c]; TA = ta[c]; CU = cum[c]
                    E1 = egm1[w % NW]; EG = eg[w % NW]; EI = eig[w % NW]
                    EC = egc[w % NW]; KK = kk[w % NW]; K2 = kk2[w % NW]; RN = rn[w % NW]; KN = kkn[w % NW]; KM = kmod[w % NW]
                    BV = bv[w % NW]
                    cu3 = CU.v.re("p (q s) -> p q s", s=CH)
                    cuC = cu3[:, :, CH - 1:CH]
                    self.act(KK.v, k_.v, AF.Copy, scale=col("kk", c))
                    self.tt('pool', E1.v, CU.v, LW.v, ALU.subtract)
                    yield
                    self.act(K2.v, KK.v, AF.Square)
                    self.tt('pool', EC.v.re("p (q s) -> p q s", s=CH), cuC.bc([128, NQ, CH]), cu3, ALU.subtract)
                    yield
                    psn = self.next_ps()
                    self.mm(psn[:, 0:TT], bones.v, K2.v, start=True, stop=True)
                    self.act(E1.v, E1.v, AF.Exp)
                    yield
                    self.act(RN.v, psn[:, 0:TT], AF.Ln)
                    yield
                    self.act(RN.v, RN.v, AF.Exp, scale=-0.5)
                    yield
                    self.tt('pool', KN.v, KK.v, RN.v, ALU.mult)
                    self.act(EI.v, CU.v, AF.Exp, scale=-1.0)
                    yield
                    self.act(EC.v, EC.v, AF.Exp)
                    self.act(KM.v, TA.v, AF.Identity, scale=der[:, 22 + c:23 + c], bias=der[:, 26 + c:27 + c])
                    yield
                    self.stt('dve', BV.v, TA.v, 1.0, KN.v, ALU.add, ALU.mult)
                    self.tt('dve', KM.v, KM.v, k_.v, ALU.mult)
                    self.act(EG.v, CU.v, AF.Exp)
                    self.act(GC[:, c, :], cuC.re("p q o -> p (q o)"), AF.Exp)
                    yield
                    for hl in range(2):
                        P_ = slice(hl * 64, (hl + 1) * 64)

                        def hv(t):
                            return t[P_, :].re("p (q s) -> p q s", s=CH)

                        def ov(n):
                            return EXP[n][c][P_, :, hl * 64:(hl + 1) * 64]

                        self.stt('dve', ov("AT"), hv(KN), -1.0, hv(E1), ALU.mult, ALU.mult)
                        self.tt('pool', ov("KT"), hv(KM), hv(EI), ALU.mult)
                        self.cp('act', ov("VB"), hv(v_))
                        yield
                        self.stt('dve', ov("BT"), hv(BV), 0.5, hv(EI), ALU.mult, ALU.mult)
                        self.tt('pool', ov("KGT"), hv(KM), hv(EC), ALU.mult)
                        yield
                        self.stt('dve', ov("BGT"), hv(BV), 0.5, hv(EC), ALU.mult, ALU.mult)
                        self.tt('pool', ov("RT"), hv(r_), hv(EG), ALU.mult)
                        yield
                        self.tt('pool', ov("RK"), hv(r_), hv(KM), ALU.mult)
                        yield

                for pair in ((0, 1), (2, 3)):
                    gens = [stageB(c) for c in pair]
                    alive = True
                    while alive:
                        alive = False
                        for g in gens:
                            try:
                                next(g)
                                alive = True
                            except StopIteration:
                                pass

        def stage45(it):
            for _ in range(1):
                if RSTOP == 2:
                    return
                for q in range(NQ):
                    psA = self.next_ps()
                    pbA = psA.v.bitcast(BF16)
                    for gi, n in enumerate(("BGT", "KGT")):
                        for c in range(4):
                            self.tr(pbA[:, (gi * 4 + c) * 128:(gi * 4 + c + 1) * 128], EXP[n][c][:, q, :], signal=(gi == 1 and c == 3))
                    self.evac(BKG[q].v, pbA.re("p (g c m) -> p g c m", g=2, c=4))
                    psB = self.next_ps()
                    pbB = psB.v.bitcast(BF16)
                    for c in range(4):
                        self.tr(pbB[:, c * 128:(c + 1) * 128], EXP["VB"][c][:, q, :], signal=(c == 3))
                    for hl in range(2):
                        self.evac(Vst[q][hl * 64:(hl + 1) * 64, :, :],
                                  pbB[hl * 64:(hl + 1) * 64, 0:512].re("p (c m) -> p c m", c=4)[:, :, hl * 64:(hl + 1) * 64])
                    combos = [("BT", "AT", mask_su, Nb[q][0]), ("AT", "BT", mask_sl, Lb[q][0]), ("KT", "AT", mask_su, LakT[q]),
                              ("BT", "RT", mask_u, MrbT[q]), ("KT", "RT", mask_u, MrkT[q])]
                    for (ln, rn_, mk, dstt) in combos:
                        ps = self.next_ps()
                        for c in range(4):
                            self.mm(ps[:, c * 128:(c + 1) * 128], EXP[ln][c][:, q, :], EXP[rn_][c][:, q, :], start=True, stop=True,
                                    signal=(c == 3))
                        self.tt('dve', dstt.v, ps.v.re("p (c m) -> p c m", c=4), mk.v.un(1).bc([128, 4, 128]), ALU.mult)
                    self.tt('pool', Pb[q].v, Nb[q][0].v, ident.v.un(1).bc([128, 4, 128]), ALU.add)
                if RSTOP == 3:
                    return
                cur = 0
                for lvl in range(1, 6):
                    nxt = 1 - cur
                    for q in range(NQ):
                        if lvl < 5:
                            ps = self.next_ps()
                            for c in range(4):
                                self.mm(ps[:, c * 128:(c + 1) * 128], Lb[q][cur][:, c, :], Nb[q][cur][:, c, :], start=True, stop=True,
                                        signal=(c == 3))
                            self.cp('act', Nb[q][nxt].v, ps.v.re("p (c m) -> p c m", c=4))
                        ps = self.next_ps()
                        for c in range(4):
                            self.mm(ps[:, c * 128:(c + 1) * 128], Nb[q][cur][:, c, :], Lb[q][cur][:, c, :], start=True, stop=True,
                                    signal=(c == 3))
                        self.cp('act', Lb[q][nxt].v, ps.v.re("p (c m) -> p c m", c=4))
                    for q in range(NQ):
                        ps = self.next_ps()
                        for c in range(4):
                            self.mm(ps[:, c * 128:(c + 1) * 128], Lb[q][nxt][:, c, :], Pb[q][:, c, :], start=True, stop=True,
                                    signal=(c == 3))
                        self.tt('dve', Pb[q].v, Pb[q].v, ps.v.re("p (c m) -> p c m", c=4), ALU.add)
                    cur = nxt

        PS = self.ps

        def crit(it, q):
            w = it * NQ + q
            xb_, ub_ = Xb[w % 2], Ub[w % 2]
            psx, psu, psh, psy = PS[0], PS[1], PS[2], PS[3 + (q % 2)]
            for c in range(4):
                self.mm(psx[:, c * 64:(c + 1) * 64], EXP["AT"][c][:, q, :], Hb[:, c, :], start=True, stop=False, signal=False)
                self.mm(psx[:, c * 64:(c + 1) * 64], LakT[q][:, c, :], Vst[q][:, c, :], start=False, stop=True, signal=(c == 3))
            self.cp('act', xb_.v, psx[:, 0:256].re("p (c m) -> p c m", c=4))
            yield
            for c in range(4):
                self.mm(psu[:, c * 64:(c + 1) * 64], Pb[q][:, c, :], xb_[:, c, :], start=True, stop=True, signal=(c == 3))
            self.cp('dve', ub_.v, psu[:, 0:256].re("p (c m) -> p c m", c=4))
            yield
            for c in range(4):
                o = psh[:, c * 64:(c + 1) * 64]
                self.mm(o, BKG[q][:, 0, c, :], ub_[:, c, :], start=True, stop=False, signal=False)
                self.mm(o, BKG[q][:, 1, c, :], Vst[q][:, c, :], start=False, stop=True, signal=(c == 3))
            for c in range(4):
                o = psy[:, c * 64:(c + 1) * 64]
                self.mm(o, EXP["RT"][c][:, q, :], Hb[:, c, :], start=True, stop=False, signal=False)
                self.mm(o, MrbT[q][:, c, :], ub_[:, c, :], start=False, stop=False, signal=False)
                self.mm(o, MrkT[q][:, c, :], Vst[q][:, c, :], start=False, stop=True, signal=False)
            for c in range(4):
                self.mm(psy[:, 256 + c:257 + c], EXP["RK"][c][:, q, :], rkb[:, c:c + 1], start=True, stop=True, signal=(c == 3))
            self.tt('dve', H.v, H.v, GC[:, :, q:q + 1].bc([128, 4, 64]), ALU.mult)
            self.tt('dve', H.v, H.v, psh[:, 0:256].re("p (c m) -> p c m", c=4), ALU.add)
            self.cp('act', Hb.v, H.v)

        def post(it, q, YTt):
            w = it * NQ + q
            psy = PS[3 + (q % 2)]
            psg = PS[5]
            self.mm(psg.v, SGd[:, q, :], gup.v, start=True, stop=True)
            Y = Yv[w % 2]; Y2 = Ysq[w % 2]; YN = Yn[w % 2]; BN = Bn[w % 2]; YF = Yf[w % 2]
            s1 = st1[w % 2]; s2 = st2[w % 2]; mn = mean[w % 2]; vr = var[w % 2]
            py3 = psy[:, 0:256].re("p (c m) -> p c m", c=4)
            self.cp('act', Y.v, py3)
            self.act(Y2.v, py3, AF.Square)
            self.S.op('dve', lambda e: e.reduce_sum(out=s1.ap, in_=Y.ap, axis=AX.X), [Y.v], [s1.v])
            self.S.op('dve', lambda e: e.reduce_sum(out=s2.ap, in_=Y2.ap, axis=AX.X), [Y2.v], [s2.v])
            self.ts('dve', mn.v, s1.v, 1.0 / 64.0, None, ALU.mult)
            self.tt('dve', vr.v, mn.v, mn.v, ALU.mult)
            self.stt('dve', vr.v, s2.v, 1.0 / 64.0, vr.v, ALU.mult, ALU.subtract)
            self.ts('dve', vr.v, vr.v, LNX_EPS, None, ALU.add)
            self.rsqrt(vr.v, vr.v)
            self.tt('pool', YN.v, Y.v, mn.v.un(2).bc([128, 4, 64]), ALU.subtract)
            self.tt('pool', YN.v, YN.v, vr.v.un(2).bc([128, 4, 64]), ALU.mult)
            self.tt('pool', YN.v, YN.v, lnxg.v, ALU.mult)
            self.tt('pool', YN.v, YN.v, lnxb.v, ALU.add)
            self.tt('dve', BN.v, Vst[q].v, psy[:, 256:260].un(2).bc([128, 4, 64]), ALU.mult)
            self.tt('pool', YN.v, YN.v, BN.v, ALU.add)
            for hl in range(2):
                P_ = slice(hl * 64, (hl + 1) * 64)
                gv = psg[P_, :].re("p (c h m) -> p c h m", c=4, h=2)[:, :, hl, :]
                self.tt('dve', YF[P_, :, :], YN[P_, :, :], gv, ALU.mult)
            pst = self.next_ps()
            pbt = pst.v.bitcast(BF16)
            for c in range(4):
                self.tr(pbt[0:64, c * 128:(c + 1) * 128], YF[:, c, :], signal=(c == 3))
            self.evac(YTt[:, :, :, q * CH:(q + 1) * CH], pbt[0:64, 0:512].re("p (c h t) -> p c h t", c=4, h=2))

        for p in s12_pieces(0):
            p()
        for it in range(ntile):
            tok0 = it * TT
            if S.need_reset():
                S.hard_barrier()
            stage3(it)
            stage45(it)
            pieces = s12_pieces(it + 1) if it + 1 < ntile else []
            YTt = YT[it % 2]
            self.ps_allowed = [6, 7]

            def filler():
                if pieces:
                    pieces.pop(0)()

            for q in range(NQ + 1):
                if q < NQ:
                    for _ in crit(it, q):
                        filler()
                if q >= 1:
                    post(it, q - 1, YTt)
            while pieces:
                pieces.pop(0)()
            self.ps_allowed = None
            self.dma('sp', yas_v[:, :, :, tok0:tok0 + TT], YTt.v)
        self._sbuf_left = self.nc.sbuf_bytes_remaining

    def phaseM(self, es, src, dst, final=False):
        S = self.S
        inp = self.inp
        TT = 512
        NS = 4
        self._stg_i = 0
        self._ev_i = 0
        use_a = 'R' in self.phases
        cols, ci = self.load_cols(es, "m_cols", [("g", inp["norm_mix_g"]), ("bin", inp["b_in"][2816:4864])])
        hb = self.sb(es, "m_hb", [128, 16], F32)
        self.ts('dve', hb.v, cols[:, ci["bin"]:ci["bin"] + 16], 0.5, None, ALU.mult)
        win = self.sb(es, "m_win", [128, 8, 2048], BF16)
        wa = self.sb(es, "m_wa", [128, 4, D], BF16)
        wb = self.sb(es, "m_wb", [128, 4, D], BF16)
        wmo = self.sb(es, "m_wmo", [128, 8, D], BF16)
        with ExitStack() as tes:
            stg = [self.sb(tes, "m_stg%d" % i, [128, 2048], F32) for i in range(3)]
            for k in range(8):
                self.load_weight(stg, win[:, k, :], inp["w_in"][k * 128:(k + 1) * 128, 2816:4864], 2048,
                                 scale=cols[:, ci["g"] + k: ci["g"] + k + 1])
                self.load_weight(stg, wmo[:, k, :], inp["w_mix_out"][k * 128:(k + 1) * 128, :], D)
            for c in range(4):
                self.load_weight(stg, wa[:, c, :], inp["w_branch_a"][c * 128:(c + 1) * 128, :], D, scale=0.5)
                self.load_weight(stg, wb[:, c, :], inp["w_branch_b"][c * 128:(c + 1) * 128, :], D, scale=0.25)
            S.barrier()
        xbufs, sst, rst, junk = self.norm_bufs(es, "m", NS)
        xrb = [self.sb(es, "m_xr%d" % i, [128, D], F32) for i in range(NS)]
        hs = [self.sb(es, "m_h%d" % i, [128, NS, D], BF16) for i in range(2)]
        hTs = [self.sb(es, "m_hT%d" % i, [128, 8, TT], BF16) for i in range(2)]
        TG = [self.sb(es, "m_tg%d" % m, [128, TT], BF16) for m in range(16)]
        YA = [self.sb(es, "m_ya%d" % i, [128, 4, TT], BF16) for i in range(2)]
        YB = [self.sb(es, "m_yb%d" % i, [128, 4, TT], BF16) for i in range(2)]
        MA = [self.sb(es, "m_ma%d" % i, [128, TT], F32) for i in range(3)]
        MG = [self.sb(es, "m_mg%d" % m, [128, TT], BF16) for m in range(8)]
        self._mb = [self.sb(es, "m_mb%d" % i, [128, TT], F32) for i in range(2)]
        ntile = self.ntok // TT

        def load_y(i):
            if i >= ntile:
                return
            tok0 = i * TT
            if use_a:
                self.dma('sp', YA[i % 2].v, self.yas[:, tok0:tok0 + TT].rearrange("(c p) t -> p c t", p=128))
            self.dma('sp', YB[i % 2].v, self.ybs[:, tok0:tok0 + TT].rearrange("(c p) t -> p c t", p=128))

        def front(i):
            self.norm_from_dram(src, i * TT, NS, xbufs, sst, rst, junk, hs[i % 2])
            self.transpose_h(hs[i % 2], NS, hTs[i % 2])

        def mid(i):
            hT = hTs[i % 2]
            ya, yb = YA[i % 2], YB[i % 2]
            for m in range(16):
                ps = self.next_ps()
                for k in range(8):
                    self.mm(ps.v, win[:, k, m * 128:(m + 1) * 128], hT[:, k, :], start=(k == 0), stop=(k == 7))
                self.act(TG[m].v, ps.v, AF.Tanh, scale=0.5, bias=hb[:, m:m + 1])
            for m in range(8):
                ma = MA[m % 3]
                if use_a:
                    ps = self.next_ps()
                    for c in range(4):
                        self.mm(ps.v, wa[:, c, m * 128:(m + 1) * 128], ya[:, c, :], start=(c == 0), stop=(c == 3))
                    self.stt('dve', ma.v, TG[m].v, 1.0, ps.v, ALU.add, ALU.mult)
                ps2 = self.next_ps()
                for c in range(4):
                    self.mm(ps2.v, wb[:, c, m * 128:(m + 1) * 128], yb[:, c, :], start=(c == 0), stop=(c == 3))
                if use_a:
                    mb = self._mb[m % 2]
                    self.stt('dve', mb.v, TG[8 + m].v, 1.0, ps2.v, ALU.add, ALU.mult)
                    self.tt('pool', MG[m].v, mb.v, ma.v, ALU.add)
                else:
                    self.stt('dve', MG[m].v, TG[8 + m].v, 1.0, ps2.v, ALU.add, ALU.mult)

        def back(i):
            tok0 = i * TT
            for s in range(NS):
                self.dma('sp', xrb[s].v, src[tok0 + s * 128: tok0 + (s + 1) * 128, :])
            for s in range(NS):
                pss = [self.next_ps(), self.next_ps()]
                for half in range(2):
                    for m in range(8):
                        self.mm(pss[half].v, MG[m][:, s * 128:(s + 1) * 128], wmo[:, m, half * 512:(half + 1) * 512],
                                start=(m == 0), stop=(m == 7))
                xb = xrb[s]
                for half in range(2):
                    self.tt('dve', xb[:, half * 512:(half + 1) * 512], xb[:, half * 512:(half + 1) * 512], pss[half].v, ALU.add)
                self.dma('sp', dst[tok0 + s * 128: tok0 + (s + 1) * 128, :], xb.v)

        load_y(0)
        front(0)
        for i in range(ntile):
            if S.need_reset():
                S.hard_barrier()
            load_y(i + 1)
            mid(i)
            if i + 1 < ntile:
                front(i + 1)
            back(i)

    def evac(self, out, in_, i=None):
        if i is None:
            i = self._ev_i
            self._ev_i += 1
        return self.cp(('act', 'dve')[i % 2], out, in_)

    def phaseB(self, es, src, dst, final=False):
        S = self.S
        inp = self.inp
        TT = 512
        NS = 4
        self._stg_i = 0
        self._ev_i = 0
        cols, ci = self.load_cols(es, "b_cols", [("gx", inp["norm_x_g"]), ("gm", inp["norm_mem_g"])])
        gq = self.sb(es, "b_gq", [128, 8], F32)
        self.ts('dve', gq.v, cols[:, ci["gx"]:ci["gx"] + 8], 1.0 / 16.0, None, ALU.mult)
        wq = self.sb(es, "b_wq", [128, 8, D], BF16)
        wo = self.sb(es, "b_wo", [128, 8, D], BF16)
        kT = [self.sb(es, "b_kT%d" % b, [128, 8, NMEM], BF16) for b in range(self.nb)]
        Vt = [self.sb(es, "b_V%d" % b, [128, 2, D], BF16) for b in range(self.nb)]
        xts = [self.sb(es, "b_xt%d" % i, [128, NS, D], F32) for i in range(2)]
        h = self.sb(es, "b_h", [128, NS, D], BF16)
        hT = self.sb(es, "b_hT", [128, 8, TT], BF16)
        junk = self.sb(es, "b_junk", [128, D], BF16)
        ss = self.sb(es, "b_ss", [128, 4], F32)
        rstd = self.sb(es, "b_rstd", [128, 4], F32)
        with ExitStack() as tes:
            stg = [self.sb(tes, "b_stg%d" % i, [128, 2048], F32) for i in range(3)]
            wkv = self.sb(tes, "b_wkv", [128, 8, 2 * D], BF16)
            for k in range(8):
                self.load_weight(stg, wkv[:, k, :], inp["w_ckv"][k * 128:(k + 1) * 128, :], 2 * D,
                                 scale=cols[:, ci["gm"] + k: ci["gm"] + k + 1])
            for k in range(8):
                self.load_weight(stg, wq[:, k, :], inp["w_cq"][k * 128:(k + 1) * 128, :], D, scale=gq[:, k:k + 1])
                self.load_weight(stg, wo[:, k, :], inp["w_co"][k * 128:(k + 1) * 128, :], D)
            for b in range(self.nb):
                mt = xts[b % 2]
                self.dma('sp', mt[:, 0:2, :], inp["mem"][b * NMEM:(b + 1) * NMEM, :].rearrange("(s p) d -> p s d", p=128))
                self.rms_h(mt, 2, ss, rstd, junk, h)
                self.transpose_h(h, 2, hT[:, :, 0:NMEM])
                for m in range(8):
                    if m % 2 == 0:
                        ps = self.next_ps()
                    o = (m % 2) * NMEM
                    for k in range(8):
                        self.mm(ps[:, o:o + NMEM], wkv[:, k, m * 128:(m + 1) * 128], hT[:, k, 0:NMEM], start=(k == 0), stop=(k == 7))
                    if m % 2 == 1:
                        self.evac(kT[b][:, m - 1:m + 1, :], ps.v.re("p (a t) -> p a t", a=2))
                for mc in range(2):
                    for half in range(2):
                        ps = self.next_ps()
                        for k in range(8):
                            self.mm(ps.v, hT[:, k, mc * 128:(mc + 1) * 128], wkv[:, k, D + half * 512: D + (half + 1) * 512],
                                    start=(k == 0), stop=(k == 7))
                        self.evac(Vt[b][:, mc, half * 512:(half + 1) * 512], ps.v)
            S.barrier()
        qT = self.sb(es, "b_qT", [128, 8, TT], BF16)
        oT = self.sb(es, "b_oT", [128, 8, TT], BF16)
        prT = self.sb(es, "b_prT", [128, 2, 4, TT], BF16)
        prs = [self.sb(es, "b_pr%d" % i, [128, 4, NMEM], BF16) for i in range(2)]
        prn = [self.sb(es, "b_prn%d" % i, [128, 4, NMEM], BF16) for i in range(2)]
        mx = [self.sb(es, "b_mx%d" % i, [128, 4], F32) for i in range(2)]
        sm = [self.sb(es, "b_sm%d" % i, [128, 4], F32) for i in range(2)]
        hs = [h, self.sb(es, "b_h1", [128, NS, D], BF16)]
        hTs = [hT, self.sb(es, "b_hT1", [128, 8, TT], BF16)]
        sss = [ss, self.sb(es, "b_ss1", [128, 4], F32)]
        rstds = [rstd, self.sb(es, "b_rstd1", [128, 4], F32)]
        ntile = self.ntok // TT
        tiles_per_b = self.seq // TT

        def load(i):
            if i < ntile:
                self.dma('sp', xts[i % 2].v, src[i * TT:(i + 1) * TT, :].rearrange("(s p) d -> p s d", p=128))

        def front(i):
            self.rms_h(xts[i % 2], NS, sss[i % 2], rstds[i % 2], junk, hs[i % 2])
            self.transpose_h(hs[i % 2], NS, hTs[i % 2])

        def mid(i):
            b = i // tiles_per_b
            hT_ = hTs[i % 2]
            for m in range(8):
                ps = self.next_ps()
                for k in range(8):
                    self.mm(ps.v, wq[:, k, m * 128:(m + 1) * 128], hT_[:, k, :], start=(k == 0), stop=(k == 7))
                self.evac(qT[:, m, :], ps.v)
            def scores(s):
                banks = [self.next_ps(), self.next_ps()]
                for hh in range(4):
                    ps = banks[hh // 2]
                    o = (hh % 2) * NMEM
                    for c in range(2):
                        self.mm(ps[:, o:o + NMEM], qT[:, 2 * hh + c, s * 128:(s + 1) * 128], kT[b][:, 2 * hh + c, :],
                                start=(c == 0), stop=(c == 1))
                return banks

            nxt = scores(0)
            for s in range(NS):
                pr = prs[s % 2]; pn = prn[s % 2]; mxs = mx[s % 2]; sms = sm[s % 2]
                banks = nxt
                if s + 1 < NS:
                    nxt = scores(s + 1)
                for g in range(2):
                    self.S.op('dve', lambda e: e.reduce_max(out=mxs.ap[:, 2 * g:2 * g + 2],
                                                            in_=banks[g].ap.rearrange("p (a t) -> p a t", a=2), axis=AX.X),
                              [banks[g].v], [mxs.v])
                self.ts('dve', mxs.v, mxs.v, -1.0, None, ALU.mult)
                for hh in range(4):
                    ps = banks[hh // 2]
                    o = (hh % 2) * NMEM
                    self.act(pr[:, hh, :], ps[:, o:o + NMEM], AF.Exp, bias=mxs[:, hh:hh + 1], accum=sms[:, hh:hh + 1])
                self.S.op('dve', lambda e: e.reciprocal(out=sms.ap, in_=sms.ap), [sms.v], [sms.v])
                self.tt('dve', pn.v, pr.v, sms.v.un(2).bc([128, 4, NMEM]), ALU.mult)
                ps = self.next_ps()
                pb = ps.v.bitcast(BF16)
                for hh in range(4):
                    for mc in range(2):
                        self.tr(pb[:, (hh * 2 + mc) * 128:(hh * 2 + mc + 1) * 128], pn[:, hh, mc * 128:(mc + 1) * 128],
                                signal=(hh == 3 and mc == 1))
                self.evac(prT[:, :, :, s * 128:(s + 1) * 128], pb.re("p (h m t) -> p m h t", h=4, m=2))
            for hh in range(4):
                for c in range(2):
                    ps = self.next_ps()
                    for mc in range(2):
                        self.mm(ps.v, Vt[b][:, mc, hh * 256 + c * 128: hh * 256 + (c + 1) * 128], prT[:, mc, hh, :],
                                start=(mc == 0), stop=(mc == 1))
                    self.evac(oT[:, 2 * hh + c, :], ps.v)

        def back(i):
            xt = xts[i % 2]
            for s in range(NS):
                pss = [self.next_ps(), self.next_ps()]
                for half in range(2):
                    for k in range(8):
                        self.mm(pss[half].v, oT[:, k, s * 128:(s + 1) * 128], wo[:, k, half * 512:(half + 1) * 512],
                                start=(k == 0), stop=(k == 7))
                for half in range(2):
                    self.tt('dve', xt[:, s, half * 512:(half + 1) * 512], xt[:, s, half * 512:(half + 1) * 512], pss[half].v, ALU.add)
                self.dma('sp', dst[i * TT + s * 128: i * TT + (s + 1) * 128, :], xt[:, s, :])

        load(0)
        load(1)
        front(0)
        for i in range(ntile):
            if S.need_reset():
                S.hard_barrier()
            mid(i)
            if i + 1 < ntile:
                front(i + 1)
            back(i)
            load(i + 2)

    def phaseC(self, es, src, dst, final=True):
        nc, S = self.nc, self.S
        TT = 256
        NS = TT // 128
        NJ = DFF // 128
        inp = self.inp
        self._stg_i = 0
        cols, ci = self.load_cols(es, "c_cols", [("g", inp["norm_ffn_g"]), ("cw0", inp["ffn_conv_w"][0]),
                                                 ("cw1", inp["ffn_conv_w"][1]), ("cw2", inp["ffn_conv_w"][2]),
                                                 ("cb", inp["ffn_conv_b"])])
        wi = self.sb(es, "c_wi", [128, 8, 2 * DFF], BF16)
        wo = self.sb(es, "c_wo", [128, NJ, D], BF16)
        gfin = self.sb(es, "c_gfin", [128, D], F32)
        self.dma('sp', gfin.v, inp["norm_final_g"].partition_broadcast(128))
        with ExitStack() as tes:
            stg = [self.sb(tes, "c_stg%d" % i, [128, 2048], F32) for i in range(3)]
            for k in range(8):
                self.load_weight(stg, wi[:, k, :], inp["w_ffn_in"][k * 128:(k + 1) * 128, :], 2 * DFF,
                                 scale=cols[:, ci["g"] + k: ci["g"] + k + 1])
            for j in range(NJ):
                self.load_weight(stg, wo[:, j, :], inp["w_ffn_out"][j * 128:(j + 1) * 128, :], D)
            S.barrier()
        xts = [self.sb(es, "c_xt%d" % i, [128, NS, D], F32) for i in range(2)]
        hs = [self.sb(es, "c_h%d" % i, [128, NS, D], BF16) for i in range(2)]
        hTs = [self.sb(es, "c_hT%d" % i, [128, 8, TT], BF16) for i in range(2)]
        actT = [self.sb(es, "c_actT%d" % j, [128, TT], BF16) for j in range(NJ)]
        junk = self.sb(es, "c_junk", [128, D], BF16)
        sss = [self.sb(es, "c_ss%d" % i, [128, 4], F32) for i in range(3)]
        rstds = [self.sb(es, "c_rstd%d" % i, [128, 4], F32) for i in range(3)]
        halos = [self.sb(es, "c_halo%d" % j, [128, 2], F32) for j in range(NJ)]
        NW = 4
        uraw = [es.enter_context(self.nc.sbuf_tensor("c_uw%d" % i, [128, 2 + TT], F32)) for i in range(NW)]
        uh = [T(r[:, 0:2], "c_uh") for r in uraw]
        ub = [T(r[:, 2:2 + TT], "c_ub") for r in uraw]

        def uview(jj, a, b_):
            return V([uh[jj], ub[jj]], uraw[jj][:, a:b_])

        acc = [self.sb(es, "c_acc%d" % i, [128, TT], F32) for i in range(NW)]
        th = [self.sb(es, "c_th%d" % i, [128, TT], F32) for i in range(NW)]
        ntile = self.ntok // TT
        tiles_per_b = self.seq // TT

        def load(i):
            if i < ntile:
                self.dma('sp', xts[i % 2].v, src[i * TT:(i + 1) * TT, :].rearrange("(s p) d -> p s d", p=128))

        def front(i):
            self.rms_h(xts[i % 2], NS, sss[i % 2], rstds[i % 2], junk, hs[i % 2])
            self.transpose_h(hs[i % 2], NS, hTs[i % 2])

        def mid(i):
            hT = hTs[i % 2]
            st = {}
            for j in range(NJ + 2):
                if j < NJ:
                    ps = self.next_ps()
                    for half in range(2):
                        c0 = half * DFF + j * 128
                        for k in range(8):
                            self.mm(ps[:, half * TT:(half + 1) * TT], wi[:, k, c0:c0 + 128], hT[:, k, :], start=(k == 0), stop=(k == 7))
                    jj = j % NW
                    a = acc[jj]
                    w0 = cols[:, ci["cw0"] + j: ci["cw0"] + j + 1]
                    w1 = cols[:, ci["cw1"] + j: ci["cw1"] + j + 1]
                    w2 = cols[:, ci["cw2"] + j: ci["cw2"] + j + 1]
                    cb = cols[:, ci["cb"] + j: ci["cb"] + j + 1]
                    self.cp('dve', uh[jj].v, halos[j].v)
                    self.cp('act', ub[jj].v, ps[:, 0:TT])
                    self.act(a.v, ps[:, 0:TT], AF.Identity, scale=w2, bias=cb)
                    self.stt('dve', a.v, uview(jj, 1, 1 + TT), w1, a.v, ALU.mult, ALU.add)
                    self.stt('dve', a.v, uview(jj, 0, TT), w0, a.v, ALU.mult, ALU.add)
                    self.cp('dve', halos[j].v, ub[jj][:, TT - 2:TT])
                    st[j] = ps
                if 0 <= j - 1 < NJ:
                    jj = (j - 1) % NW
                    self.act(th[jj].v, acc[jj].v, AF.Gelu_apprx_tanh)
                if 0 <= j - 2 < NJ:
                    jj = (j - 2) % NW
                    self.tt('dve', actT[j - 2].v, th[jj].v, st[j - 2][:, TT:2 * TT], ALU.mult)

        def back(i):
            xt = xts[i % 2]
            ss, rstd = sss[2], rstds[2]
            for s in range(NS):
                pss = [self.next_ps(), self.next_ps()]
                for half in range(2):
                    for j in range(NJ):
                        self.mm(pss[half].v, actT[j][:, s * 128:(s + 1) * 128], wo[:, j, half * 512:(half + 1) * 512],
                                start=(j == 0), stop=(j == NJ - 1))
                for half in range(2):
                    self.tt('dve', xt[:, s, half * 512:(half + 1) * 512], xt[:, s, half * 512:(half + 1) * 512], pss[half].v, ALU.add)
                if final:
                    self.act(junk.v, xt[:, s, :], AF.Square, accum=ss[:, s:s + 1])
                    self.ts('dve', rstd[:, s:s + 1], ss[:, s:s + 1], 1.0 / D, NORM_EPS, ALU.mult, ALU.add)
                    self.rsqrt(rstd[:, s:s + 1], rstd[:, s:s + 1])
                    self.stt('dve', xt[:, s, :], xt[:, s, :], rstd[:, s:s + 1], gfin.v, ALU.mult, ALU.mult)
                self.dma('sp', dst[i * TT + s * 128: i * TT + (s + 1) * 128, :], xt[:, s, :])

        load(0)
        load(1)
        front(0)
        for i in range(ntile):
            if S.need_reset():
                S.hard_barrier()
            if i % tiles_per_b == 0:
                for j in range(NJ):
                    self.memset('pool', halos[j].v, 0.0)
            mid(i)
            if i + 1 < ntile:
                front(i + 1)
            back(i)
            load(i + 2)


_PROG_CACHE = {}


def get_prog(nb=4, seq=2048, phases="LRMBC"):
    key = (nb, seq, phases)
    if key not in _PROG_CACHE:
        _PROG_CACHE[key] = Prog(nb, seq, phases)
    return _PROG_CACHE[key]


_W_NAMES = ["norm_mix_g", "w_in", "b_in", "mu_shift", "w0", "w_lora_up", "a0", "a_lora_up", "g_lora_up", "k_k", "k_a",
            "r_k", "lnx_g", "lnx_b", "w_branch_a", "conv_b_w", "conv_b_b", "w_rg_a", "b_rg_a", "w_rg_x", "b_rg_x",
            "lru_lambda", "w_branch_b", "w_mix_out", "norm_x_g", "norm_mem_g", "w_cq", "w_ckv", "w_co", "norm_ffn_g",
            "w_ffn_in", "ffn_conv_w", "ffn_conv_b", "w_ffn_out", "norm_final_g"]


def run(inputs, ncores=8, nb=4, seq=2048, phases="LRMBC"):
    prog = get_prog(nb, seq, phases)
    shapes = {k: tuple(v.shape) for k, v in prog.inp.items()}
    shared = {}
    for k in _W_NAMES:
        a = np.ascontiguousarray(np.asarray(inputs[k], dtype=np.float32))
        shared[k] = a.reshape(shapes[k])
    x = np.asarray(inputs["x"], dtype=np.float32)
    mem = np.asarray(inputs["mem"], dtype=np.float32)
    in_maps = []
    for c in range(ncores):
        m = dict(shared)
        m["x"] = np.ascontiguousarray(x[c * nb:(c + 1) * nb, :seq]).reshape(nb * seq, D)
        m["mem"] = np.ascontiguousarray(mem[c * nb:(c + 1) * nb]).reshape(nb * NMEM, D)
        in_maps.append(m)
    res = run_bass_kernel_spmd(prog.nc, in_maps, core_ids=list(range(ncores)))
    outs = [np.asarray(r["out"]).reshape(nb, seq, D) for r in res.results]
    return np.concatenate(outs, axis=0)


def kernel(**inputs):
    return run(inputs).astype(np.float32)
```

```python
import numpy as np
from contextlib import ExitStack
import concourse.bass as bass
import concourse.mybir as mybir
from concourse.bass_utils import run_bass_kernel_spmd

F32 = mybir.dt.float32
BF16 = mybir.dt.bfloat16
AF = mybir.ActivationFunctionType
ALU = mybir.AluOpType
AX = mybir.AxisListType

D = 1024
NMEM = 256
AW = 512
RWKV_COLS = 1792
P_IN = 4864
DFF = 2816
NORM_EPS = 1e-6
LNX_EPS = 64e-5
SAME_ENGINE_SYNC = True
import os as _os
RSTOP = int(_os.environ.get("RSTOP", "0"))
LVAR = int(_os.environ.get("LVAR", "2"))


_ALL_TILES = []


class T:
    def __init__(self, ap, name=""):
        self.ap = ap if isinstance(ap, bass.AP) else ap[:]
        self.w = None
        self.r = []
        self.name = name
        self.dsem = None
        _ALL_TILES.append(self)

    def __getitem__(self, k):
        return V([self], self.ap[k])

    @property
    def v(self):
        return V([self], self.ap)


class V:
    def __init__(self, ts, ap):
        self.ts = ts
        self.ap = ap

    def __getitem__(self, k):
        return V(self.ts, self.ap[k])

    def re(self, pat, **kw):
        return V(self.ts, self.ap.rearrange(pat, **kw))

    def bc(self, shape):
        return V(self.ts, self.ap.to_broadcast(shape))

    def un(self, axis):
        return V(self.ts, self.ap.unsqueeze(axis))

    def bitcast(self, dt):
        return V(self.ts, self.ap.bitcast(dt))


def _ap(x):
    return x.ap if isinstance(x, V) else x


class Sync:
    def __init__(self, nc, es, n_dma_sems=64):
        self.nc = nc
        self.es = es
        self.engs = {'pe': nc.tensor, 'act': nc.scalar, 'dve': nc.vector, 'pool': nc.gpsimd, 'sp': nc.sync}
        self.sem = {}
        self.cnt = {}
        self.seen = {}
        for e in self.engs:
            self.sem[e] = es.enter_context(nc.semaphore("c_" + e))
            self.cnt[e] = 0
            self.seen[e] = {}
        self.dpool = [{'sem': es.enter_context(nc.semaphore("d%d" % i)), 'cnt': 0} for i in range(n_dma_sems)]
        self.dfree = list(range(n_dma_sems))
        self.n_inst = 0
        self.bar1 = es.enter_context(nc.semaphore("bar1"))
        self.bar2 = es.enter_context(nc.semaphore("bar2"))
        self.bar_k = 0

    def _wait(self, e, tok):
        if tok is None:
            return
        sem, val, src = tok
        if src == e and (e == 'pe' or not SAME_ENGINE_SYNC):
            return
        key = sem.name
        if self.seen[e].get(key, 0) >= val:
            return
        self.seen[e][key] = val
        self.engs[e].wait_ge(sem, val)

    def deps(self, e, reads, writes):
        for v in reads:
            if not isinstance(v, V):
                continue
            for t in v.ts:
                self._wait(e, t.w)
        for v in writes:
            for t in v.ts:
                self._wait(e, t.w)
                for tok in t.r:
                    self._wait(e, tok)

    def done(self, tok, reads, writes):
        for v in reads:
            if not isinstance(v, V):
                continue
            for t in v.ts:
                t.r.append(tok)
                if len(t.r) > 24:
                    t.r = t.r[-24:] if False else t.r
        for v in writes:
            for t in v.ts:
                t.w = tok
                t.r = []

    def op(self, e, fn, reads, writes, signal=True):
        self.deps(e, reads, writes)
        inst = fn(self.engs[e])
        self.n_inst += 1
        if signal:
            self.cnt[e] += 1
            inst.then_inc(self.sem[e], 1)
            tok = (self.sem[e], self.cnt[e], e)
        else:
            tok = (self.sem[e], self.cnt[e] + 1, e)
        self.done(tok, reads, writes)
        return inst

    def _dsem(self, t):
        if t.dsem is None:
            if not self.dfree:
                raise RuntimeError("out of dma semaphores")
            t.dsem = self.dpool[self.dfree.pop(0)]
        return t.dsem

    def dma(self, q, out, in_, **kw):
        reads = [in_] if isinstance(in_, V) else []
        writes = [out] if isinstance(out, V) else []
        self.deps(q, reads, writes)
        sbv = out if isinstance(out, V) else in_
        ds = self._dsem(sbv.ts[0])
        inst = self.engs[q].dma_start(out=_ap(out), in_=_ap(in_), **kw)
        self.n_inst += 1
        ds['cnt'] += 16
        inst.then_inc(ds['sem'], 16)
        tok = (ds['sem'], ds['cnt'], 'dma')
        self.done(tok, reads, writes)
        return tok

    def barrier(self):
        toks = [(self.sem[f], self.cnt[f], f) for f in self.engs if self.cnt[f] > 0]
        toks += [(d['sem'], d['cnt'], 'dma') for d in self.dpool if d['cnt'] > 0]
        for e in self.engs:
            for tok in toks:
                if tok[2] == e:
                    continue
                self._wait(e, tok)

    def release_dma_sems(self):
        self.dfree = list(range(len(self.dpool)))
        for t in _ALL_TILES:
            t.dsem = None

    def need_reset(self, limit=2600):
        return max(self.cnt.values()) > limit or max(d['cnt'] for d in self.dpool) > limit

    def hard_barrier(self):
        self.barrier()
        self.bar_k += 1
        k = self.bar_k
        for e in self.engs:
            self.engs[e].sem_inc(self.bar1, 1)
        sp = self.engs['sp']
        sp.wait_ge(self.bar1, len(self.engs) * k)
        for e in self.engs:
            if self.cnt[e] > 0:
                sp.sem_clear(self.sem[e])
        for d in self.dpool:
            if d['cnt'] > 0:
                sp.sem_clear(d['sem'])
        sp.sem_inc(self.bar2, 1)
        for e in self.engs:
            if e != 'sp':
                self.engs[e].wait_ge(self.bar2, k)
        for e in self.engs:
            self.cnt[e] = 0
            self.seen[e] = {}
        for d in self.dpool:
            d['cnt'] = 0
        for t in _ALL_TILES:
            t.w = None
            t.r = []


class Prog:
    def __init__(self, nb=4, seq=2048, phases="LRMBC", dbg=False):
        self.nb, self.seq, self.phases, self.dbg = nb, seq, phases, dbg
        self.ntok = nb * seq
        nc = self.nc = bass.Bass("TRN2", target_bir_lowering=False)
        self.inp = {}
        del _ALL_TILES[:]

        def din(name, shape):
            self.inp[name] = nc.dram_tensor(name, list(shape), F32, kind="ExternalInput").ap()
            return self.inp[name]

        ntok = self.ntok
        din("x", [ntok, D])
        din("mem", [nb * NMEM, D])
        din("norm_mix_g", [D]); din("w_in", [D, P_IN]); din("b_in", [P_IN]); din("mu_shift", [RWKV_COLS])
        din("w0", [AW]); din("w_lora_up", [64, AW]); din("a0", [AW]); din("a_lora_up", [64, AW])
        din("g_lora_up", [128, AW]); din("k_k", [AW]); din("k_a", [AW]); din("r_k", [AW])
        din("lnx_g", [AW]); din("lnx_b", [AW]); din("w_branch_a", [AW, D])
        din("conv_b_w", [4, AW]); din("conv_b_b", [AW]); din("w_rg_a", [8, 64, 64]); din("b_rg_a", [AW])
        din("w_rg_x", [8, 64, 64]); din("b_rg_x", [AW]); din("lru_lambda", [AW]); din("w_branch_b", [AW, D])
        din("w_mix_out", [D, D]); din("norm_x_g", [D]); din("norm_mem_g", [D]); din("w_cq", [D, D])
        din("w_ckv", [D, 2 * D]); din("w_co", [D, D]); din("norm_ffn_g", [D]); din("w_ffn_in", [D, 2 * DFF])
        din("ffn_conv_w", [3, DFF]); din("ffn_conv_b", [DFF]); din("w_ffn_out", [DFF, D]); din("norm_final_g", [D])
        self.out = nc.dram_tensor("out", [ntok, D], F32, kind="ExternalOutput").ap()
        self.x1 = nc.dram_tensor("x1s", [ntok, D], F32).ap()
        self.x2 = nc.dram_tensor("x2s", [ntok, D], F32).ap()
        self.dbg_out = {}

        with ExitStack() as es:
            self.es = es
            self.S = Sync(nc, es)
            self.ps = [T(es.enter_context(nc.psum_tensor("ps%d" % i, [128, 512], F32)), "ps%d" % i) for i in range(8)]
            self.ps_i = 0
            self.ident = self.sb(es, "ident", [128, 128], BF16)
            self.identf = self.sb(es, "identf", [128, 128], F32)
            for idt in (self.ident, self.identf):
                self.S.op('pool', lambda e: e.memset(idt.ap[:], 0.0), [], [idt.v])
                self.S.op('pool', lambda e: e.affine_select(idt.ap[:], idt.ap[:], pattern=[[-1, 128]],
                                                           compare_op=ALU.not_equal, fill=1.0, base=0,
                                                           channel_multiplier=1), [idt.v], [idt.v])
            self.mhalf = self.sb(es, "mhalf", [128, 512], F32)
            self.memset('pool', self.mhalf.v, -0.5)
            self.phalf = self.sb(es, "phalf", [128, 512], F32)
            self.memset('pool', self.phalf.v, 0.5)
            self.yas = nc.dram_tensor("yas", [AW, ntok], BF16).ap()
            self.ybs = nc.dram_tensor("ybs", [AW, ntok], BF16).ap()
            src = {'L': self.inp["x"], 'R': self.inp["x"], 'M': self.inp["x"], 'B': self.x1, 'C': self.x2}
            dst = {'L': None, 'R': None, 'M': self.x1, 'B': self.x2, 'C': self.out}
            order = [p for p in "LRMBC" if p in phases]
            chain = [p for p in order if p in "MBC"]
            for i, p in enumerate(order):
                s_ap, d_ap = src[p], dst[p]
                if p in chain:
                    if chain.index(p) == 0:
                        s_ap = self.inp["x"]
                    if chain.index(p) == len(chain) - 1:
                        d_ap = self.out
                with ExitStack() as pes:
                    getattr(self, "phase" + p)(pes, s_ap, d_ap, final=(p == 'C'))
                    self.S.hard_barrier()
                self.S.release_dma_sems()
            self.S.barrier()

    def sb(self, es, name, shape, dt):
        return T(es.enter_context(self.nc.sbuf_tensor(name, list(shape), dt)), name)

    ps_allowed = None

    def next_ps(self):
        if self.ps_allowed is not None:
            self._psa_i = getattr(self, "_psa_i", 0) + 1
            return self.ps[self.ps_allowed[self._psa_i % len(self.ps_allowed)]]
        t = self.ps[self.ps_i]
        self.ps_i = (self.ps_i + 1) % 8
        return t

    def act(self, out, in_, func, bias=0.0, scale=1.0, accum=None):
        rd = [in_] + [a for a in (bias, scale) if isinstance(a, V)]
        wr = [out] + ([accum] if accum is not None else [])
        kw = {}
        if accum is not None:
            kw['accum_out'] = accum.ap
        return self.S.op('act', lambda e: e.activation(out=out.ap, in_=in_.ap, func=func, bias=_ap(bias),
                                                       scale=_ap(scale), **kw), rd, wr)

    def tt(self, eng, out, in0, in1, op):
        return self.S.op(eng, lambda e: e.tensor_tensor(out=out.ap, in0=in0.ap, in1=in1.ap, op=op), [in0, in1], [out])

    def ts(self, eng, out, in0, s1, s2, op0, op1=None):
        rd = [in0] + [a for a in (s1, s2) if isinstance(a, V)]
        if op1 is None:
            return self.S.op(eng, lambda e: e.tensor_scalar(out=out.ap, in0=in0.ap, scalar1=_ap(s1), scalar2=None,
                                                            op0=op0), rd, [out])
        return self.S.op(eng, lambda e: e.tensor_scalar(out=out.ap, in0=in0.ap, scalar1=_ap(s1), scalar2=_ap(s2),
                                                        op0=op0, op1=op1), rd, [out])

    def stt(self, eng, out, in0, scalar, in1, op0, op1):
        rd = [in0, in1] + ([scalar] if isinstance(scalar, V) else [])
        return self.S.op(eng, lambda e: e.scalar_tensor_tensor(out=out.ap, in0=in0.ap, scalar=_ap(scalar), in1=in1.ap,
                                                               op0=op0, op1=op1), rd, [out])

    def cp(self, eng, out, in_):
        if eng == 'act':
            return self.act(out, in_, AF.Copy)
        return self.S.op(eng, lambda e: e.tensor_copy(out=out.ap, in_=in_.ap), [in_], [out])

    def rsqrt(self, out, in_):
        shp = list(in_.ap.shape)
        if shp[-1] > 8:
            self.act(out, in_, AF.Ln)
            return self.act(out, out, AF.Exp, scale=-0.5)
        mh = self.mhalf[0:shp[0], 0:shp[-1]]
        if len(shp) == 3:
            mh = mh.un(1).bc(shp)
        return self.tt('pool', out, in_, mh, ALU.pow)

    def memset(self, eng, out, val):
        return self.S.op(eng, lambda e: e.memset(out.ap, val), [], [out])

    def mm(self, out, lhsT, rhs, start, stop, signal=None):
        if signal is None:
            signal = stop
        return self.S.op('pe', lambda e: e.matmul(out.ap, lhsT=lhsT.ap, rhs=rhs.ap, start=start, stop=stop),
                         [lhsT, rhs], [out], signal=signal)

    def tr(self, out, in_, signal=True, f32=False):
        idt = self.identf if f32 else self.ident
        n = in_.ap.shape[0]
        return self.S.op('pe', lambda e: e.transpose(out.ap, in_.ap, idt.ap[0:n, 0:n]), [in_, idt.v], [out],
                         signal=signal)

    def dma(self, q, out, in_, **kw):
        return self.S.dma(q, out, in_, **kw)

    def load_weight(self, stg, dst, src_rows, ncols, scale=None, piece=2048):
        rows = src_rows.shape[0]
        c0 = 0
        while c0 < ncols:
            n = min(piece, ncols - c0)
            st = stg[self._stg_i % len(stg)]
            eng = ('dve', 'act')[self._stg_i % 2]
            self._stg_i += 1
            self.dma('sp', st[0:rows, 0:n], src_rows[:, c0:c0 + n])
            o = dst[:, c0:c0 + n]
            i = st[0:rows, 0:n]
            if scale is None:
                self.cp(eng, o, i)
            elif eng == 'act':
                if isinstance(scale, V):
                    self.act(o, i, AF.Copy, scale=scale)
                else:
                    self.act(o, i, AF.Copy, scale=float(scale))
            else:
                self.ts(eng, o, i, scale, None, ALU.mult)
            c0 += n

    def load_cols(self, es, name, specs):
        cols = {}
        n = 0
        for k, v in specs:
            cols[k] = n
            n += (v.shape[0] + 127) // 128
        res = self.sb(es, name, [128, n], F32)
        with ExitStack() as tes:
            ngrp = (n + 127) // 128
            stage = [self.sb(tes, name + "_st%d" % g, [128, 128], F32) for g in range(ngrp)]
            for st in stage:
                self.memset('pool', st.v, 0.0)
            for k, v in specs:
                L = v.shape[0]
                m = (L + 127) // 128
                c = cols[k]
                r = 0
                while r < m:
                    g, rr = divmod(c + r, 128)
                    cnt = min(m - r, 128 - rr)
                    if L >= 128:
                        self.dma('sp', stage[g][rr:rr + cnt, :], v[r * 128:(r + cnt) * 128].rearrange("(m p) -> m p", p=128))
                    else:
                        self.dma('sp', stage[g][rr:rr + 1, 0:L], v.rearrange("(m p) -> m p", m=1))
                    r += cnt
            for g in range(ngrp):
                w = min(128, n - g * 128)
                ps = self.next_ps()
                self.tr(ps[:, 0:w], stage[g][0:w, :], f32=True)
                self.cp('dve', res[:, g * 128:g * 128 + w], ps[:, 0:w])
            self.S.barrier()
        return res, cols

    def rms_h(self, xt, nsub, ss, rstd, junk, h):
        for s in range(nsub):
            self.act(junk.v, xt[:, s, :], AF.Square, accum=ss[:, s:s + 1])
        self.ts('dve', rstd[:, 0:nsub], ss[:, 0:nsub], 1.0 / D, NORM_EPS, ALU.mult, ALU.add)
        self.rsqrt(rstd[:, 0:nsub], rstd[:, 0:nsub])
        for s in range(nsub):
            self.ts('dve', h[:, s, :], xt[:, s, :], rstd[:, s:s + 1], None, ALU.mult)

    def transpose_h(self, h, nsub, hT, evac=('act', 'dve')):
        tw = nsub * 128
        per_bank = 1024 // tw
        c = 0
        i = 0
        while c < 8:
            ps = self.next_ps()
            pb = ps.v.bitcast(BF16)
            nchunk = min(per_bank, 8 - c)
            for cc in range(nchunk):
                for s in range(nsub):
                    last = (cc == nchunk - 1 and s == nsub - 1)
                    self.tr(pb[:, cc * tw + s * 128: cc * tw + (s + 1) * 128], h[:, s, (c + cc) * 128:(c + cc + 1) * 128],
                            signal=last)
            self.cp(evac[i % len(evac)], hT[:, c:c + nchunk, :], pb[:, 0:nchunk * tw].re("p (c t) -> p c t", c=nchunk))
            c += nchunk
            i += 1

    def norm_from_dram(self, src, tok0, NS, xbufs, sst, rst, junk, h):
        for s in range(NS):
            xb = xbufs[self._xb_i % len(xbufs)]
            self._xb_i += 1
            self.dma('sp', xb.v, src[tok0 + s * 128: tok0 + (s + 1) * 128, :])
            self.act(junk.v, xb.v, AF.Square, accum=sst[s].v)
            self.ts('dve', rst[s].v, sst[s].v, 1.0 / D, NORM_EPS, ALU.mult, ALU.add)
            self.rsqrt(rst[s].v, rst[s].v)
            self.ts('dve', h[:, s, :], xb.v, rst[s].v, None, ALU.mult)

    def norm_bufs(self, es, pfx, NS):
        xbufs = [self.sb(es, pfx + "_xb%d" % i, [128, D], F32) for i in range(2)]
        sst = [self.sb(es, pfx + "_ss%d" % i, [128, 1], F32) for i in range(NS)]
        rst = [self.sb(es, pfx + "_rs%d" % i, [128, 1], F32) for i in range(NS)]
        junk = self.sb(es, pfx + "_junk", [128, D], BF16)
        self._xb_i = 0
        return xbufs, sst, rst, junk

    def phaseL(self, es, src, dst, final=False):
        S = self.S
        inp = self.inp
        TT = 512
        NS = 4
        self._stg_i = 0
        self._ev_i = 0
        cols, ci = self.load_cols(es, "l_cols", [
            ("g", inp["norm_mix_g"]), ("bin", inp["b_in"][1792:2816]),
            ("cw0", inp["conv_b_w"][0]), ("cw1", inp["conv_b_w"][1]), ("cw2", inp["conv_b_w"][2]), ("cw3", inp["conv_b_w"][3]),
            ("cbb", inp["conv_b_b"]), ("bra", inp["b_rg_a"]), ("brx", inp["b_rg_x"]), ("lam", inp["lru_lambda"])])
        hb = self.sb(es, "l_hb", [128, 8], F32)
        self.ts('dve', hb[:, 0:4], cols[:, ci["bra"]:ci["bra"] + 4], 0.5, None, ALU.mult)
        self.ts('dve', hb[:, 4:8], cols[:, ci["brx"]:ci["brx"] + 4], 0.5, None, ALU.mult)
        cA = self.sb(es, "l_cA", [128, 8], F32)
        lt = self.sb(es, "l_lt", [128, 4], F32)
        self.act(lt.v, cols[:, ci["lam"]:ci["lam"] + 4], AF.Exp, scale=-1.0)
        self.ts('dve', lt.v, lt.v, 1.0, None, ALU.add)
        self.act(lt.v, lt.v, AF.Ln)
        self.ts('dve', cA[:, 0:4], lt.v, -4.0, None, ALU.mult)
        self.ts('dve', cA[:, 4:8], lt.v, -8.0, None, ALU.mult)
        win = self.sb(es, "l_win", [128, 8, 1024], BF16)
        wg = self.sb(es, "l_wg", [128, 2, 4, 128], BF16)
        with ExitStack() as tes:
            stg = [self.sb(tes, "l_stg%d" % i, [128, 1024], F32) for i in range(3)]
            for k in range(8):
                self.load_weight(stg, win[:, k, :], inp["w_in"][k * 128:(k + 1) * 128, 1792:2816], 1024,
                                 scale=cols[:, ci["g"] + k: ci["g"] + k + 1], piece=1024)
            for gi, nm in enumerate(("w_rg_a", "w_rg_x")):
                st = stg[gi]
                self.memset('pool', st.v, 0.0)
                for blk in range(8):
                    c, hl = divmod(blk, 2)
                    self.dma('sp', st[hl * 64:(hl + 1) * 64, c * 128 + hl * 64: c * 128 + (hl + 1) * 64], inp[nm][blk])
                self.cp('dve', wg[:, gi, :, :], st[:, 0:512].re("p (c m) -> p c m", c=4))
            S.barrier()
        xbufs, sst, rst, junk = self.norm_bufs(es, "l", NS)
        h = self.sb(es, "l_h", [128, NS, D], BF16)
        hT = self.sb(es, "l_hT", [128, 8, TT], BF16)
        PX = [self.sb(es, "l_px%d" % c, [128, 3 + TT], F32) for c in range(4)]
        GY = [self.sb(es, "l_gy%d" % c, [128, TT], F32) for c in range(4)]
        YB = [self.sb(es, "l_yb%d" % c, [128, TT], BF16) for c in range(4)]
        carry = [self.sb(es, "l_cy%d" % c, [128, 1], F32) for c in range(4)]
        NW = 2
        def wk(nm, dt=F32, n=NW):
            return [self.sb(es, "l_%s%d" % (nm, i), [128, TT], dt) for i in range(n)]
        acc = wk("acc", n=4); xbb = wk("xbb", BF16, n=4); ta = wk("ta", n=4); tx = wk("tx", n=4)
        av = wk("av"); a2 = wk("a2"); u = wk("u"); hl_ = wk("hl")
        ntile = self.ntok // TT
        tiles_per_b = self.seq // TT
        hs = [h, self.sb(es, "l_h1", [128, NS, D], BF16)]
        hTs = [hT, self.sb(es, "l_hT1", [128, 8, TT], BF16)]

        def front(i):
            self.norm_from_dram(src, i * TT, NS, xbufs, sst, rst, junk, hs[i % 2])
            self.transpose_h(hs[i % 2], NS, hTs[i % 2])

        def midA(i):
            hT_ = hTs[i % 2]
            first = (i % tiles_per_b == 0)
            if first:
                for c in range(4):
                    self.memset('pool', PX[c][:, 0:3], 0.0)
                    self.memset('pool', carry[c].v, 0.0)
            for c in range(4):
                ps = self.next_ps()
                for k in range(8):
                    self.mm(ps.v, win[:, k, c * 128:(c + 1) * 128], hT_[:, k, :], start=(k == 0), stop=(k == 7))
                self.act(PX[c][:, 3:3 + TT], ps.v, AF.Identity, bias=cols[:, ci["bin"] + c: ci["bin"] + c + 1])
            for c in range(4):
                ps = self.next_ps()
                for k in range(8):
                    self.mm(ps.v, win[:, k, 512 + c * 128: 512 + (c + 1) * 128], hT_[:, k, :], start=(k == 0), stop=(k == 7))
                self.act(GY[c].v, ps.v, AF.Gelu_apprx_tanh, bias=cols[:, ci["bin"] + 4 + c: ci["bin"] + 5 + c])
            for c in range(4):
                A = acc[c]; XB = xbb[c]; TA = ta[c]; TX = tx[c]
                cw = [cols[:, ci["cw%d" % j] + c: ci["cw%d" % j] + c + 1] for j in range(4)]
                self.ts('dve', A.v, PX[c][:, 0:TT], cw[0], cols[:, ci["cbb"] + c: ci["cbb"] + c + 1], ALU.mult, ALU.add)
                for j in range(1, 4):
                    self.stt('dve', A.v, PX[c][:, j:j + TT], cw[j], A.v, ALU.mult, ALU.add)
                self.cp('pool', PX[c][:, 0:3], PX[c][:, TT:TT + 3])
                self.cp('dve', XB.v, A.v)
                psa = self.next_ps()
                self.mm(psa.v, wg[:, 0, c, :], XB.v, start=True, stop=True)
                psx = self.next_ps()
                self.mm(psx.v, wg[:, 1, c, :], XB.v, start=True, stop=True)
                self.act(TA.v, psa.v, AF.Tanh, scale=0.5, bias=hb[:, c:c + 1])
                self.act(TX.v, psx.v, AF.Tanh, scale=0.5, bias=hb[:, 4 + c:5 + c])

        def midB(i):
            tok0 = i * TT
            first = (i % tiles_per_b == 0)
            for c in range(4):
                A = acc[c]; TA = ta[c]; TX = tx[c]; AV = av[c % NW]; A2 = a2[c % NW]; U = u[c % NW]; HL = hl_[c % NW]
                self.act(AV.v, TA.v, AF.Exp, scale=cA[:, c:c + 1], bias=cA[:, c:c + 1])
                self.act(A2.v, TA.v, AF.Exp, scale=cA[:, 4 + c:5 + c], bias=cA[:, 4 + c:5 + c])
                self.ts('dve', A2.v, A2.v, -1.0, 1.0, ALU.mult, ALU.add)
                self.ts('dve', A2.v, A2.v, 1e-30, None, ALU.max)
                self.act(A2.v, A2.v, AF.Ln)
                self.act(A2.v, A2.v, AF.Exp, scale=0.5)
                if first:
                    self.memset('pool', A2[:, 0:1], 1.0)
                self.stt('dve', U.v, TX.v, 1.0, A.v, ALU.add, ALU.mult)
                self.tt('dve', U.v, U.v, A2.v, ALU.mult)
                self.S.op('dve', lambda e: e.tensor_tensor_scan(HL.ap, AV.ap, U.ap, carry[c].ap, ALU.mult, ALU.add),
                          [AV.v, U.v, carry[c].v], [HL.v])
                self.cp('pool', carry[c].v, HL[:, TT - 1:TT])
                self.tt('dve', YB[c].v, HL.v, GY[c].v, ALU.mult)
                self.dma('sp', self.ybs[c * 128:(c + 1) * 128, tok0:tok0 + TT], YB[c].v)

        front(0)
        for i in range(ntile):
            if S.need_reset():
                S.hard_barrier()
            midA(i)
            if i + 1 < ntile:
                front(i + 1)
            midB(i)

    def phaseR(self, es, src, dst, final=False):
        S = self.S
        inp = self.inp
        TT = 256
        NS = 2
        NQ = 4
        CH = 64
        self._stg_i = 0
        self._ev_i = 0
        cols, ci = self.load_cols(es, "r_cols", [
            ("g", inp["norm_mix_g"]), ("bin", inp["b_in"][0:RWKV_COLS]), ("mu", inp["mu_shift"]),
            ("w0", inp["w0"]), ("a0", inp["a0"]), ("kk", inp["k_k"]), ("ka", inp["k_a"]), ("rk", inp["r_k"])])

        def col(key, j):
            return cols[:, ci[key] + j: ci[key] + j + 1]

        der = self.sb(es, "r_der", [128, 32], F32)
        self.ts('dve', der[:, 0:14], cols[:, ci["mu"]:ci["mu"] + 14], -1.0, 1.0, ALU.mult, ALU.add)
        self.ts('dve', der[:, 14:18], cols[:, ci["w0"]:ci["w0"] + 4], 0.5, None, ALU.mult)
        self.ts('dve', der[:, 18:22], cols[:, ci["a0"]:ci["a0"] + 4], 0.5, None, ALU.mult)
        self.ts('dve', der[:, 22:26], cols[:, ci["ka"]:ci["ka"] + 4], 0.5, None, ALU.mult)
        self.ts('dve', der[:, 26:30], cols[:, ci["ka"]:ci["ka"] + 4], -0.5, 1.0, ALU.mult, ALU.add)
        rkb = self.sb(es, "r_rkb", [128, 4], BF16)
        self.cp('dve', rkb.v, cols[:, ci["rk"]:ci["rk"] + 4])
        m64 = [self.sb(es, "r_m64_%d" % i, [64, 64], BF16) for i in range(3)]
        masks = [self.sb(es, "r_mask%d" % i, [128, 128], BF16) for i in range(3)]
        specs = [([[1, 64]], -1, ALU.is_gt), ([[-1, 64]], 1, ALU.is_gt), ([[1, 64]], -1, ALU.is_ge)]
        for mi in range(3):
            pat, cm, op = specs[mi]
            self.memset('pool', m64[mi].v, 1.0)
            self.S.op('pool', lambda e: e.affine_select(m64[mi].ap, m64[mi].ap, pattern=pat, compare_op=op, fill=0.0,
                                                        base=0, channel_multiplier=cm), [m64[mi].v], [m64[mi].v])
            for a in range(2):
                for b_ in range(2):
                    self.dma('sp', masks[mi][a * 64:(a + 1) * 64, b_ * 64:(b_ + 1) * 64], m64[mi].v)
        mask_su, mask_sl, mask_u = masks
        bones = self.sb(es, "r_bones", [128, 128], BF16)
        self.memset('pool', bones.v, 0.0)
        self.memset('pool', bones[0:64, 0:64], 1.0)
        self.memset('pool', bones[64:128, 64:128], 1.0)
        m01 = self.sb(es, "r_m01", [128, TT], F32)
        self.memset('pool', m01.v, 1.0)
        self.memset('pool', m01.v.re("p (q s) -> p q s", s=CH)[:, :, 0:1], 0.0)
        lnxg = self.sb(es, "r_lnxg", [128, 4, 64], F32)
        lnxb = self.sb(es, "r_lnxb", [128, 4, 64], F32)
        for c in range(4):
            for hl in range(2):
                hh = 2 * c + hl
                self.dma('sp', lnxg[hl * 64:(hl + 1) * 64, c, :], inp["lnx_g"][hh * 64:(hh + 1) * 64].partition_broadcast(64))
                self.dma('sp', lnxb[hl * 64:(hl + 1) * 64, c, :], inp["lnx_b"][hh * 64:(hh + 1) * 64].partition_broadcast(64))
        win = self.sb(es, "r_win", [128, 8, RWKV_COLS], BF16)
        wlora = self.sb(es, "r_wlora", [128, 2, AW], BF16)
        gup = self.sb(es, "r_gup", [128, AW], BF16)
        with ExitStack() as tes:
            stg = [self.sb(tes, "r_stg%d" % i, [128, RWKV_COLS], F32) for i in range(3)]
            for k in range(8):
                self.load_weight(stg, win[:, k, :], inp["w_in"][k * 128:(k + 1) * 128, 0:RWKV_COLS], RWKV_COLS,
                                 scale=cols[:, ci["g"] + k: ci["g"] + k + 1], piece=RWKV_COLS)
            st = stg[0]
            self.memset('pool', st[:, 0:2 * AW], 0.0)
            self.dma('sp', st[0:64, 0:AW], inp["w_lora_up"])
            self.dma('sp', st[64:128, AW:2 * AW], inp["a_lora_up"])
            self.cp('dve', wlora.v, st[:, 0:2 * AW].re("p (a m) -> p a m", a=2))
            st = stg[1]
            self.dma('sp', st[:, 0:AW], inp["g_lora_up"])
            self.cp('dve', gup.v, st[:, 0:AW])
            S.barrier()
        xbufs, sst, rst, junk = self.norm_bufs(es, "r", NS)
        h = self.sb(es, "r_h", [128, NS, D], BF16)
        hT = self.sb(es, "r_hT", [128, 8, TT], BF16)
        pcarry = [self.sb(es, "r_pc%d" % m, [128, 1], F32) for m in range(14)]
        Sx = [self.sb(es, "r_s%d" % m, [128, TT], F32) for m in range(14)]
        ltmp = [self.sb(es, "r_lt%d" % i, [128, TT], F32) for i in range(2)]
        LB = self.sb(es, "r_lb", [128, TT], BF16)
        SGd = self.sb(es, "r_sgd", [128, NQ, 128], BF16)
        NW = 2

        def wk(nm, dt=F32):
            return [self.sb(es, "r_%s%d" % (nm, i), [128, TT], dt) for i in range(NW)]

        logw = [self.sb(es, "r_logw%d" % i, [128, TT], F32) for i in range(4)]
        ta = [self.sb(es, "r_ta%d" % i, [128, TT], F32) for i in range(4)]
        cum = [self.sb(es, "r_cum%d" % i, [128, TT], F32) for i in range(4)]
        egm1 = wk("egm1"); eg = wk("eg"); eig = wk("eig"); egc = wk("egc")
        kk = wk("kk"); kk2 = wk("kk2", BF16); rn = wk("rn"); kkn = wk("kkn"); kmod = wk("kmod"); bv = wk("bv")
        names = ["AT", "BT", "KT", "RT", "BGT", "KGT", "VB", "RK"]
        EXP = {n: [self.sb(es, "r_%s%d" % (n, c), [128, NQ, 128], BF16) for c in range(4)] for n in names}
        for n in names:
            for c in range(4):
                self.memset('pool', EXP[n][c].v, 0.0)
        GC = self.sb(es, "r_gc", [128, 4, NQ], F32)
        BKG = [self.sb(es, "r_bkg%d" % q, [128, 2, 4, 128], BF16) for q in range(NQ)]
        Vst = [self.sb(es, "r_vst%d" % q, [128, 4, 64], BF16) for q in range(NQ)]
        Nb = [[self.sb(es, "r_nb%d_%d" % (q, i), [128, 4, 128], BF16) for i in range(2)] for q in range(NQ)]
        Lb = [[self.sb(es, "r_lb%d_%d" % (q, i), [128, 4, 128], BF16) for i in range(2)] for q in range(NQ)]
        Pb = [self.sb(es, "r_pb%d" % q, [128, 4, 128], BF16) for q in range(NQ)]
        LakT = [self.sb(es, "r_lak%d" % q, [128, 4, 128], BF16) for q in range(NQ)]
        MrbT = [self.sb(es, "r_mrb%d" % q, [128, 4, 128], BF16) for q in range(NQ)]
        MrkT = [self.sb(es, "r_mrk%d" % q, [128, 4, 128], BF16) for q in range(NQ)]
        H = self.sb(es, "r_H", [128, 4, 64], F32)
        Hb = self.sb(es, "r_Hb", [128, 4, 64], BF16)
        Xb = [self.sb(es, "r_Xb%d" % i, [128, 4, 64], BF16) for i in range(2)]
        Ub = [self.sb(es, "r_Ub%d" % i, [128, 4, 64], BF16) for i in range(2)]

        def yt(nm, shape=(128, 4, 64), dt=F32):
            return [self.sb(es, "r_%s%d" % (nm, i), list(shape), dt) for i in range(2)]

        Yv = yt("Yv"); Ysq = yt("Ysq"); Yn = yt("Yn"); Bn = yt("Bn"); Yf = yt("Yf", dt=BF16)
        st1 = yt("st1", (128, 4)); st2 = yt("st2", (128, 4)); mean = yt("mean", (128, 4)); var = yt("var", (128, 4))
        YT = [self.sb(es, "r_YT%d" % i, [64, 4, 2, TT], BF16) for i in range(2)]
        ident = self.ident
        ntile = self.ntok // TT
        tiles_per_b = self.seq // TT
        yas_v = self.yas.rearrange("(c hl i) t -> i c hl t", hl=2, i=64)

        PWraw = [es.enter_context(self.nc.sbuf_tensor("r_pwx%d" % i, [128, 1 + TT], F32)) for i in range(3)]
        PWh = [T(r[:, 0:1], "r_pwh") for r in PWraw]
        PWb = [T(r[:, 1:1 + TT], "r_pwb") for r in PWraw]

        def s12_pieces(it):
            tok0 = it * TT

            def p0():
                if it % tiles_per_b == 0:
                    for m in range(14):
                        self.memset('pool', pcarry[m].v, 0.0)
                self.norm_from_dram(src, tok0, NS, xbufs, sst, rst, junk, h)
                self.transpose_h(h, NS, hT)

            def proj(ms):
                def f():
                    ps = None
                    for idx, m in enumerate(ms):
                        if idx % 2 == 0:
                            ps = self.next_ps()
                        o = (idx % 2) * TT
                        for k in range(8):
                            self.mm(ps[:, o:o + TT], win[:, k, m * 128:(m + 1) * 128], hT[:, k, :], start=(k == 0), stop=(k == 7))
                        pi = m % 3
                        self.cp('dve', PWh[pi].v, pcarry[m].v)
                        self.act(PWb[pi].v, ps[:, o:o + TT], AF.Identity, bias=col("bin", m))
                        self.cp('dve', pcarry[m].v, PWb[pi][:, TT - 1:TT])
                        tmp = ltmp[m % 2]
                        self.act(tmp.v, V([PWh[pi], PWb[pi]], PWraw[pi][:, 0:TT]), AF.Copy, scale=col("mu", m))
                        self.stt('dve', Sx[m].v, PWb[pi].v, der[:, m:m + 1], tmp.v, ALU.mult, ALU.add)
                return f

            return [p0] + [proj([m]) for m in range(14)]

        def stage3(it):
            if it % tiles_per_b == 0:
                self.memset('pool', H.v, 0.0)
                self.memset('pool', Hb.v, 0.0)
            for _ in range(1):
                if RSTOP == 1:
                    return
                self.act(LB[0:64, :], Sx[12][0:64, :], AF.Tanh)
                self.cp('act', LB[64:128, :], Sx[12][64:128, :])
                tmp = ltmp[0]
                self.act(tmp.v, Sx[13].v, AF.Tanh, scale=0.5)
                for hl in range(2):
                    self.ts('dve', SGd[:, :, hl * 64:(hl + 1) * 64], tmp.v.re("p (q s) -> p q s", s=CH), 0.5, 0.5, ALU.mult, ALU.add)
                if RSTOP == 21:
                    return
                for c in range(4):
                    LW = logw[c]; TA = ta[c]; CU = cum[c]
                    ps = self.next_ps()
                    self.mm(ps[:, 0:TT], wlora[:, 0, c * 128:(c + 1) * 128], LB.v, start=True, stop=True)
                    self.mm(ps[:, TT:2 * TT], wlora[:, 1, c * 128:(c + 1) * 128], LB.v, start=True, stop=True)
                    self.act(LW.v, ps[:, 0:TT], AF.Tanh, scale=0.5, bias=der[:, 14 + c:15 + c])
                    self.act(TA.v, ps[:, TT:2 * TT], AF.Tanh, scale=0.5, bias=der[:, 18 + c:19 + c])
                    self.ts('dve', LW.v, LW.v, 1.0, -0.30326532985631671, ALU.add, ALU.mult)
                    self.S.op('dve', lambda e: e.tensor_tensor_scan(CU.ap, m01.ap, LW.ap, 0.0, ALU.mult, ALU.add),
                              [m01.v, LW.v], [CU.v])
                def stageB(c):
                    w = it * 4 + c
                    r_, k_, v_ = Sx[c], Sx[4 + c], Sx[8 + c]
                    LW = logw[c]; TA = ta[c]; CU = cum[c]
                    E1 = egm1[w % NW]; EG = eg[w % NW]; EI = eig[w % NW]
                    EC = egc[w % NW]; KK = kk[w % NW]; K2 = kk2[w % NW]; RN = rn[w % NW]; KN = kkn[w % NW]; KM = kmod[w % NW]
                    BV = bv[w % NW]
                    cu3 = CU.v.re("p (q s) -> p q s", s=CH)
                    cuC = cu3[:, :, CH - 1:CH]
                    self.act(KK.v, k_.v, AF.Copy, scale=col("kk", c))
                    self.tt('pool', E1.v, CU.v, LW.v, ALU.subtract)
                    yield
                    self.act(K2.v, KK.v, AF.Square)
                    self.tt('pool', EC.v.re("p (q s) -> p q s", s=CH), cuC.bc([128, NQ, CH]), cu3, ALU.subtract)
                    yield
                    psn = self.next_ps()
                    self.mm(psn[:, 0:TT], bones.v, K2.v, start=True, stop=True)
                    self.act(E1.v, E1.v, AF.Exp)
                    yield
                    self.act(RN.v, psn[:, 0:TT], AF.Ln)
                    yield
                    self.act(RN.v, RN.v, AF.Exp, scale=-0.5)
                    yield
                    self.tt('pool', KN.v, KK.v, RN.v, ALU.mult)
                    self.act(EI.v, CU.v, AF.Exp, scale=-1.0)
                    yield
                    self.act(EC.v, EC.v, AF.Exp)
                    self.act(KM.v, TA.v, AF.Identity, scale=der[:, 22 + c:23 + c], bias=der[:, 26 + c:27 + c])
                    yield
                    self.stt('dve', BV.v, TA.v, 1.0, KN.v, ALU.add, ALU.mult)
                    self.tt('dve', KM.v, KM.v, k_.v, ALU.mult)
                    self.act(EG.v, CU.v, AF.Exp)
                    self.act(GC[:, c, :], cuC.re("p q o -> p (q o)"), AF.Exp)
                    yield
                    for hl in range(2):
                        P_ = slice(hl * 64, (hl + 1) * 64)

                        def hv(t):
                            return t[P_, :].re("p (q s) -> p q s", s=CH)

                        def ov(n):
                            return EXP[n][c][P_, :, hl * 64:(hl + 1) * 64]

                        self.stt('dve', ov("AT"), hv(KN), -1.0, hv(E1), ALU.mult, ALU.mult)
                        self.tt('pool', ov("KT"), hv(KM), hv(EI), ALU.mult)
                        self.cp('act', ov("VB"), hv(v_))
                        yield
                        self.stt('dve', ov("BT"), hv(BV), 0.5, hv(EI), ALU.mult, ALU.mult)
                        self.tt('pool', ov("KGT"), hv(KM), hv(EC), ALU.mult)
                        yield
                        self.stt('dve', ov("BGT"), hv(BV), 0.5, hv(EC), ALU.mult, ALU.mult)
                        self.tt('pool', ov("RT"), hv(r_), hv(EG), ALU.mult)
                        yield
                        self.tt('pool', ov("RK"), hv(r_), hv(KM), ALU.mult)
                        yield

                for pair in ((0, 1), (2, 3)):
                    gens = [stageB(c) for c in pair]
                    alive = True
                    while alive:
                        alive = False
                        for g in gens:
                            try:
                                next(g)
                                alive = True
                            except StopIteration:
                                pass

        def stage45(it):
            for _ in range(1):
                if RSTOP == 2:
                    return
                for q in range(NQ):
                    psA = self.next_ps()
                    pbA = psA.v.bitcast(BF16)
                    for gi, n in enumerate(("BGT", "KGT")):
                        for c in range(4):
                            self.tr(pbA[:, (gi * 4 + c) * 128:(gi * 4 + c + 1) * 128], EXP[n][c][:, q, :], signal=(gi == 1 and c == 3))
                    self.evac(BKG[q].v, pbA.re("p (g c m) -> p g c m", g=2, c=4))
                    psB = self.next_ps()
                    pbB = psB.v.bitcast(BF16)
                    for c in range(4):
                        self.tr(pbB[:, c * 128:(c + 1) * 128], EXP["VB"][c][:, q, :], signal=(c == 3))
                    for hl in range(2):
                        self.evac(Vst[q][hl * 64:(hl + 1) * 64, :, :],
                                  pbB[hl * 64:(hl + 1) * 64, 0:512].re("p (c m) -> p c m", c=4)[:, :, hl * 64:(hl + 1) * 64])
                    combos = [("BT", "AT", mask_su, Nb[q][0]), ("AT", "BT", mask_sl, Lb[q][0]), ("KT", "AT", mask_su, LakT[q]),
                              ("BT", "RT", mask_u, MrbT[q]), ("KT", "RT", mask_u, MrkT[q])]
                    for (ln, rn_, mk, dstt) in combos:
                        ps = self.next_ps()
                        for c in range(4):
                            self.mm(ps[:, c * 128:(c + 1) * 128], EXP[ln][c][:, q, :], EXP[rn_][c][:, q, :], start=True, stop=True,
                                    signal=(c == 3))
                        self.tt('dve', dstt.v, ps.v.re("p (c m) -> p c m", c=4), mk.v.un(1).bc([128, 4, 128]), ALU.mult)
                    self.tt('pool', Pb[q].v, Nb[q][0].v, ident.v.un(1).bc([128, 4, 128]), ALU.add)
                if RSTOP == 3:
                    return
                cur = 0
                for lvl in range(1, 6):
                    nxt = 1 - cur
                    for q in range(NQ):
                        if lvl < 5:
                            ps = self.next_ps()
                            for c in range(4):
                                self.mm(ps[:, c * 128:(c + 1) * 128], Lb[q][cur][:, c, :], Nb[q][cur][:, c, :], start=True, stop=True,
                                        signal=(c == 3))
                            self.cp('act', Nb[q][nxt].v, ps.v.re("p (c m) -> p c m", c=4))
                        ps = self.next_ps()
                        for c in range(4):
                            self.mm(ps[:, c * 128:(c + 1) * 128], Nb[q][cur][:, c, :], Lb[q][cur][:, c, :], start=True, stop=True,
                                    signal=(c == 3))
                        self.cp('act', Lb[q][nxt].v, ps.v.re("p (c m) -> p c m", c=4))
                    for q in range(NQ):
                        ps = self.next_ps()
                        for c in range(4):
                            self.mm(ps[:, c * 128:(c + 1) * 128], Lb[q][nxt][:, c, :], Pb[q][:, c, :], start=True, stop=True,
                                    signal=(c == 3))
                        self.tt('dve', Pb[q].v, Pb[q].v, ps.v.re("p (c m) -> p c m", c=4), ALU.add)
                    cur = nxt

        PS = self.ps

        def crit(it, q):
            w = it * NQ + q
            xb_, ub_ = Xb[w % 2], Ub[w % 2]
            psx, psu, psh, psy = PS[0], PS[1], PS[2], PS[3 + (q % 2)]
            for c in range(4):
                self.mm(psx[:, c * 64:(c + 1) * 64], EXP["AT"][c][:, q, :], Hb[:, c, :], start=True, stop=False, signal=False)
                self.mm(psx[:, c * 64:(c + 1) * 64], LakT[q][:, c, :], Vst[q][:, c, :], start=False, stop=True, signal=(c == 3))
            self.cp('act', xb_.v, psx[:, 0:256].re("p (c m) -> p c m", c=4))
            yield
            for c in range(4):
                self.mm(psu[:, c * 64:(c + 1) * 64], Pb[q][:, c, :], xb_[:, c, :], start=True, stop=True, signal=(c == 3))
            self.cp('dve', ub_.v, psu[:, 0:256].re("p (c m) -> p c m", c=4))
            yield
            for c in range(4):
                o = psh[:, c * 64:(c + 1) * 64]
                self.mm(o, BKG[q][:, 0, c, :], ub_[:, c, :], start=True, stop=False, signal=False)
                self.mm(o, BKG[q][:, 1, c, :], Vst[q][:, c, :], start=False, stop=True, signal=(c == 3))
            for c in range(4):
                o = psy[:, c * 64:(c + 1) * 64]
                self.mm(o, EXP["RT"][c][:, q, :], Hb[:, c, :], start=True, stop=False, signal=False)
                self.mm(o, MrbT[q][:, c, :], ub_[:, c, :], start=False, stop=False, signal=False)
                self.mm(o, MrkT[q][:, c, :], Vst[q][:, c, :], start=False, stop=True, signal=False)
            for c in range(4):
                self.mm(psy[:, 256 + c:257 + c], EXP["RK"][c][:, q, :], rkb[:, c:c + 1], start=True, stop=True, signal=(c == 3))
            yield
            self.tt('dve', H.v, H.v, GC[:, :, q:q + 1].bc([128, 4, 64]), ALU.mult)
            self.tt('dve', H.v, H.v, psh[:, 0:256].re("p (c m) -> p c m", c=4), ALU.add)
            self.cp('act', Hb.v, H.v)

        def post_a(it, q):
            w = it * NQ + q
            psy = PS[3 + (q % 2)]
            psg = PS[5]
            self.mm(psg.v, SGd[:, q, :], gup.v, start=True, stop=True)
            Y = Yv[w % 2]; Y2 = Ysq[w % 2]; YN = Yn[w % 2]; BN = Bn[w % 2]; YF = Yf[w % 2]
            s1 = st1[w % 2]; s2 = st2[w % 2]; mn = mean[w % 2]; vr = var[w % 2]
            py3 = psy[:, 0:256].re("p (c m) -> p c m", c=4)
            self.cp('act', Y.v, py3)
            self.act(Y2.v, py3, AF.Square)
            self.tt('dve', BN.v, Vst[q].v, psy[:, 256:260].un(2).bc([128, 4, 64]), ALU.mult)
            self.S.op('dve', lambda e: e.reduce_sum(out=s1.ap, in_=Y.ap, axis=AX.X), [Y.v], [s1.v])
            self.S.op('dve', lambda e: e.reduce_sum(out=s2.ap, in_=Y2.ap, axis=AX.X), [Y2.v], [s2.v])
            self.ts('dve', mn.v, s1.v, 1.0 / 64.0, None, ALU.mult)
            self.tt('dve', vr.v, mn.v, mn.v, ALU.mult)
            self.stt('dve', vr.v, s2.v, 1.0 / 64.0, vr.v, ALU.mult, ALU.subtract)
            self.ts('dve', vr.v, vr.v, LNX_EPS, None, ALU.add)
            self.act(vr.v, vr.v, AF.Ln)
            self.act(vr.v, vr.v, AF.Exp, scale=-0.5)
            self.tt('dve', YN.v, Y.v, mn.v.un(2).bc([128, 4, 64]), ALU.subtract)
            self.tt('dve', YN.v, YN.v, vr.v.un(2).bc([128, 4, 64]), ALU.mult)
            self.tt('pool', YN.v, YN.v, lnxg.v, ALU.mult)
            self.tt('pool', YN.v, YN.v, lnxb.v, ALU.add)
            self.tt('pool', YN.v, YN.v, BN.v, ALU.add)
            for hl in range(2):
                P_ = slice(hl * 64, (hl + 1) * 64)
                gv = psg[P_, :].re("p (c h m) -> p c h m", c=4, h=2)[:, :, hl, :]
                self.tt('dve', YF[P_, :, :], YN[P_, :, :], gv, ALU.mult)

        def post_b(it, q, YTt):
            w = it * NQ + q
            YF = Yf[w % 2]
            pst = self.next_ps()
            pbt = pst.v.bitcast(BF16)
            for c in range(4):
                self.tr(pbt[0:64, c * 128:(c + 1) * 128], YF[:, c, :], signal=(c == 3))
            self.evac(YTt[:, :, :, q * CH:(q + 1) * CH], pbt[0:64, 0:512].re("p (c h t) -> p c h t", c=4, h=2))

        for p in s12_pieces(0):
            p()
        for it in range(ntile):
            tok0 = it * TT
            if S.need_reset():
                S.hard_barrier()
            stage3(it)
            stage45(it)
            pieces = s12_pieces(it + 1) if it + 1 < ntile else []
            YTt = YT[it % 2]
            self.ps_allowed = [6, 7]

            def filler():
                if pieces:
                    pieces.pop(0)()

            for q in range(NQ + 2):
                if q < NQ:
                    for _ in crit(it, q):
                        filler()
                if q == NQ:
                    while pieces:
                        pieces.pop(0)()
                if 1 <= q <= NQ:
                    post_a(it, q - 1)
                if q >= 2:
                    post_b(it, q - 2, YTt)
            self.ps_allowed = None
            self.dma('sp', yas_v[:, :, :, tok0:tok0 + TT], YTt.v)
        self._sbuf_left = self.nc.sbuf_bytes_remaining

    def phaseM(self, es, src, dst, final=False):
        S = self.S
        inp = self.inp
        TT = 512
        NS = 4
        self._stg_i = 0
        self._ev_i = 0
        use_a = 'R' in self.phases
        cols, ci = self.load_cols(es, "m_cols", [("g", inp["norm_mix_g"]), ("bin", inp["b_in"][2816:4864])])
        hb = self.sb(es, "m_hb", [128, 16], F32)
        self.ts('dve', hb.v, cols[:, ci["bin"]:ci["bin"] + 16], 0.5, None, ALU.mult)
        win = self.sb(es, "m_win", [128, 8, 2048], BF16)
        wa = self.sb(es, "m_wa", [128, 4, D], BF16)
        wb = self.sb(es, "m_wb", [128, 4, D], BF16)
        wmo = self.sb(es, "m_wmo", [128, 8, D], BF16)
        with ExitStack() as tes:
            stg = [self.sb(tes, "m_stg%d" % i, [128, 2048], F32) for i in range(3)]
            for k in range(8):
                self.load_weight(stg, win[:, k, :], inp["w_in"][k * 128:(k + 1) * 128, 2816:4864], 2048,
                                 scale=cols[:, ci["g"] + k: ci["g"] + k + 1])
                self.load_weight(stg, wmo[:, k, :], inp["w_mix_out"][k * 128:(k + 1) * 128, :], D)
            for c in range(4):
                self.load_weight(stg, wa[:, c, :], inp["w_branch_a"][c * 128:(c + 1) * 128, :], D, scale=0.5)
                self.load_weight(stg, wb[:, c, :], inp["w_branch_b"][c * 128:(c + 1) * 128, :], D, scale=0.25)
            S.barrier()
        xbufs, sst, rst, junk = self.norm_bufs(es, "m", NS)
        xrb = [self.sb(es, "m_xr%d" % i, [128, D], F32) for i in range(NS)]
        hs = [self.sb(es, "m_h%d" % i, [128, NS, D], BF16) for i in range(2)]
        hTs = [self.sb(es, "m_hT%d" % i, [128, 8, TT], BF16) for i in range(2)]
        TG = [self.sb(es, "m_tg%d" % m, [128, TT], BF16) for m in range(16)]
        YA = [self.sb(es, "m_ya%d" % i, [128, 4, TT], BF16) for i in range(2)]
        YB = [self.sb(es, "m_yb%d" % i, [128, 4, TT], BF16) for i in range(2)]
        MA = [self.sb(es, "m_ma%d" % i, [128, TT], F32) for i in range(3)]
        MG = [self.sb(es, "m_mg%d" % m, [128, TT], BF16) for m in range(8)]
        self._mb = [self.sb(es, "m_mb%d" % i, [128, TT], F32) for i in range(2)]
        ntile = self.ntok // TT

        def load_y(i):
            if i >= ntile:
                return
            tok0 = i * TT
            if use_a:
                self.dma('sp', YA[i % 2].v, self.yas[:, tok0:tok0 + TT].rearrange("(c p) t -> p c t", p=128))
            self.dma('sp', YB[i % 2].v, self.ybs[:, tok0:tok0 + TT].rearrange("(c p) t -> p c t", p=128))

        def front(i):
            self.norm_from_dram(src, i * TT, NS, xbufs, sst, rst, junk, hs[i % 2])
            self.transpose_h(hs[i % 2], NS, hTs[i % 2])

        def mid(i):
            hT = hTs[i % 2]
            ya, yb = YA[i % 2], YB[i % 2]
            for m in range(16):
                ps = self.next_ps()
                for k in range(8):
                    self.mm(ps.v, win[:, k, m * 128:(m + 1) * 128], hT[:, k, :], start=(k == 0), stop=(k == 7))
                self.act(TG[m].v, ps.v, AF.Tanh, scale=0.5, bias=hb[:, m:m + 1])
            for m in range(8):
                ma = MA[m % 3]
                if use_a:
                    ps = self.next_ps()
                    for c in range(4):
                        self.mm(ps.v, wa[:, c, m * 128:(m + 1) * 128], ya[:, c, :], start=(c == 0), stop=(c == 3))
                    self.stt('dve', ma.v, TG[m].v, 1.0, ps.v, ALU.add, ALU.mult)
                ps2 = self.next_ps()
                for c in range(4):
                    self.mm(ps2.v, wb[:, c, m * 128:(m + 1) * 128], yb[:, c, :], start=(c == 0), stop=(c == 3))
                if use_a:
                    mb = self._mb[m % 2]
                    self.stt('dve', mb.v, TG[8 + m].v, 1.0, ps2.v, ALU.add, ALU.mult)
                    self.tt('pool', MG[m].v, mb.v, ma.v, ALU.add)
                else:
                    self.stt('dve', MG[m].v, TG[8 + m].v, 1.0, ps2.v, ALU.add, ALU.mult)

        def back(i):
            tok0 = i * TT
            for s in range(NS):
                self.dma('sp', xrb[s].v, src[tok0 + s * 128: tok0 + (s + 1) * 128, :])
            for s in range(NS):
                pss = [self.next_ps(), self.next_ps()]
                for half in range(2):
                    for m in range(8):
                        self.mm(pss[half].v, MG[m][:, s * 128:(s + 1) * 128], wmo[:, m, half * 512:(half + 1) * 512],
                                start=(m == 0), stop=(m == 7))
                xb = xrb[s]
                for half in range(2):
                    self.tt('dve', xb[:, half * 512:(half + 1) * 512], xb[:, half * 512:(half + 1) * 512], pss[half].v, ALU.add)
                self.dma('sp', dst[tok0 + s * 128: tok0 + (s + 1) * 128, :], xb.v)

        load_y(0)
        front(0)
        for i in range(ntile):
            if S.need_reset():
                S.hard_barrier()
            load_y(i + 1)
            mid(i)
            if i + 1 < ntile:
                front(i + 1)
            back(i)

    def evac(self, out, in_, i=None):
        if i is None:
            i = self._ev_i
            self._ev_i += 1
        return self.cp(('act', 'dve')[i % 2], out, in_)

    def phaseB(self, es, src, dst, final=False):
        S = self.S
        inp = self.inp
        TT = 512
        NS = 4
        self._stg_i = 0
        self._ev_i = 0
        cols, ci = self.load_cols(es, "b_cols", [("gx", inp["norm_x_g"]), ("gm", inp["norm_mem_g"])])
        gq = self.sb(es, "b_gq", [128, 8], F32)
        self.ts('dve', gq.v, cols[:, ci["gx"]:ci["gx"] + 8], 1.0 / 16.0, None, ALU.mult)
        wq = self.sb(es, "b_wq", [128, 8, D], BF16)
        wo = self.sb(es, "b_wo", [128, 8, D], BF16)
        kT = [self.sb(es, "b_kT%d" % b, [128, 8, NMEM], BF16) for b in range(self.nb)]
        Vt = [self.sb(es, "b_V%d" % b, [128, 2, D], BF16) for b in range(self.nb)]
        xts = [self.sb(es, "b_xt%d" % i, [128, NS, D], F32) for i in range(2)]
        h = self.sb(es, "b_h", [128, NS, D], BF16)
        hT = self.sb(es, "b_hT", [128, 8, TT], BF16)
        junk = self.sb(es, "b_junk", [128, D], BF16)
        ss = self.sb(es, "b_ss", [128, 4], F32)
        rstd = self.sb(es, "b_rstd", [128, 4], F32)
        with ExitStack() as tes:
            stg = [self.sb(tes, "b_stg%d" % i, [128, 2048], F32) for i in range(3)]
            wkv = self.sb(tes, "b_wkv", [128, 8, 2 * D], BF16)
            for k in range(8):
                self.load_weight(stg, wkv[:, k, :], inp["w_ckv"][k * 128:(k + 1) * 128, :], 2 * D,
                                 scale=cols[:, ci["gm"] + k: ci["gm"] + k + 1])
            for k in range(8):
                self.load_weight(stg, wq[:, k, :], inp["w_cq"][k * 128:(k + 1) * 128, :], D, scale=gq[:, k:k + 1])
                self.load_weight(stg, wo[:, k, :], inp["w_co"][k * 128:(k + 1) * 128, :], D)
            for b in range(self.nb):
                mt = xts[b % 2]
                self.dma('sp', mt[:, 0:2, :], inp["mem"][b * NMEM:(b + 1) * NMEM, :].rearrange("(s p) d -> p s d", p=128))
                self.rms_h(mt, 2, ss, rstd, junk, h)
                self.transpose_h(h, 2, hT[:, :, 0:NMEM])
                for m in range(8):
                    if m % 2 == 0:
                        ps = self.next_ps()
                    o = (m % 2) * NMEM
                    for k in range(8):
                        self.mm(ps[:, o:o + NMEM], wkv[:, k, m * 128:(m + 1) * 128], hT[:, k, 0:NMEM], start=(k == 0), stop=(k == 7))
                    if m % 2 == 1:
                        self.evac(kT[b][:, m - 1:m + 1, :], ps.v.re("p (a t) -> p a t", a=2))
                for mc in range(2):
                    for half in range(2):
                        ps = self.next_ps()
                        for k in range(8):
                            self.mm(ps.v, hT[:, k, mc * 128:(mc + 1) * 128], wkv[:, k, D + half * 512: D + (half + 1) * 512],
                                    start=(k == 0), stop=(k == 7))
                        self.evac(Vt[b][:, mc, half * 512:(half + 1) * 512], ps.v)
            S.barrier()
        qT = self.sb(es, "b_qT", [128, 8, TT], BF16)
        oT = self.sb(es, "b_oT", [128, 8, TT], BF16)
        prT = self.sb(es, "b_prT", [128, 2, 4, TT], BF16)
        prs = [self.sb(es, "b_pr%d" % i, [128, 4, NMEM], BF16) for i in range(2)]
        prn = [self.sb(es, "b_prn%d" % i, [128, 4, NMEM], BF16) for i in range(2)]
        mx = [self.sb(es, "b_mx%d" % i, [128, 4], F32) for i in range(2)]
        sm = [self.sb(es, "b_sm%d" % i, [128, 4], F32) for i in range(2)]
        hs = [h, self.sb(es, "b_h1", [128, NS, D], BF16)]
        hTs = [hT, self.sb(es, "b_hT1", [128, 8, TT], BF16)]
        sss = [ss, self.sb(es, "b_ss1", [128, 4], F32)]
        rstds = [rstd, self.sb(es, "b_rstd1", [128, 4], F32)]
        ntile = self.ntok // TT
        tiles_per_b = self.seq // TT

        def load(i):
            if i < ntile:
                self.dma('sp', xts[i % 2].v, src[i * TT:(i + 1) * TT, :].rearrange("(s p) d -> p s d", p=128))

        def front(i):
            self.rms_h(xts[i % 2], NS, sss[i % 2], rstds[i % 2], junk, hs[i % 2])
            self.transpose_h(hs[i % 2], NS, hTs[i % 2])

        def mid(i):
            b = i // tiles_per_b
            hT_ = hTs[i % 2]
            for m in range(8):
                ps = self.next_ps()
                for k in range(8):
                    self.mm(ps.v, wq[:, k, m * 128:(m + 1) * 128], hT_[:, k, :], start=(k == 0), stop=(k == 7))
                self.evac(qT[:, m, :], ps.v)
            def scores(s):
                banks = [self.next_ps(), self.next_ps()]
                for hh in range(4):
                    ps = banks[hh // 2]
                    o = (hh % 2) * NMEM
                    for c in range(2):
                        self.mm(ps[:, o:o + NMEM], qT[:, 2 * hh + c, s * 128:(s + 1) * 128], kT[b][:, 2 * hh + c, :],
                                start=(c == 0), stop=(c == 1))
                return banks

            nxt = scores(0)
            for s in range(NS):
                pr = prs[s % 2]; pn = prn[s % 2]; mxs = mx[s % 2]; sms = sm[s % 2]
                banks = nxt
                if s + 1 < NS:
                    nxt = scores(s + 1)
                for g in range(2):
                    self.S.op('dve', lambda e: e.reduce_max(out=mxs.ap[:, 2 * g:2 * g + 2],
                                                            in_=banks[g].ap.rearrange("p (a t) -> p a t", a=2), axis=AX.X),
                              [banks[g].v], [mxs.v])
                self.ts('dve', mxs.v, mxs.v, -1.0, None, ALU.mult)
                for hh in range(4):
                    ps = banks[hh // 2]
                    o = (hh % 2) * NMEM
                    self.act(pr[:, hh, :], ps[:, o:o + NMEM], AF.Exp, bias=mxs[:, hh:hh + 1], accum=sms[:, hh:hh + 1])
                self.S.op('dve', lambda e: e.reciprocal(out=sms.ap, in_=sms.ap), [sms.v], [sms.v])
                self.tt('dve', pn.v, pr.v, sms.v.un(2).bc([128, 4, NMEM]), ALU.mult)
                ps = self.next_ps()
                pb = ps.v.bitcast(BF16)
                for hh in range(4):
                    for mc in range(2):
                        self.tr(pb[:, (hh * 2 + mc) * 128:(hh * 2 + mc + 1) * 128], pn[:, hh, mc * 128:(mc + 1) * 128],
                                signal=(hh == 3 and mc == 1))
                self.evac(prT[:, :, :, s * 128:(s + 1) * 128], pb.re("p (h m t) -> p m h t", h=4, m=2))
            for hh in range(4):
                for c in range(2):
                    ps = self.next_ps()
                    for mc in range(2):
                        self.mm(ps.v, Vt[b][:, mc, hh * 256 + c * 128: hh * 256 + (c + 1) * 128], prT[:, mc, hh, :],
                                start=(mc == 0), stop=(mc == 1))
                    self.evac(oT[:, 2 * hh + c, :], ps.v)

        def back(i):
            xt = xts[i % 2]
            for s in range(NS):
                pss = [self.next_ps(), self.next_ps()]
                for half in range(2):
                    for k in range(8):
                        self.mm(pss[half].v, oT[:, k, s * 128:(s + 1) * 128], wo[:, k, half * 512:(half + 1) * 512],
                                start=(k == 0), stop=(k == 7))
                for half in range(2):
                    self.tt('dve', xt[:, s, half * 512:(half + 1) * 512], xt[:, s, half * 512:(half + 1) * 512], pss[half].v, ALU.add)
                self.dma('sp', dst[i * TT + s * 128: i * TT + (s + 1) * 128, :], xt[:, s, :])

        load(0)
        load(1)
        front(0)
        for i in range(ntile):
            if S.need_reset():
                S.hard_barrier()
            mid(i)
            if i + 1 < ntile:
                front(i + 1)
            back(i)
            load(i + 2)

    def phaseC(self, es, src, dst, final=True):
        nc, S = self.nc, self.S
        TT = 256
        NS = TT // 128
        NJ = DFF // 128
        inp = self.inp
        self._stg_i = 0
        cols, ci = self.load_cols(es, "c_cols", [("g", inp["norm_ffn_g"]), ("cw0", inp["ffn_conv_w"][0]),
                                                 ("cw1", inp["ffn_conv_w"][1]), ("cw2", inp["ffn_conv_w"][2]),
                                                 ("cb", inp["ffn_conv_b"])])
        wi = self.sb(es, "c_wi", [128, 8, 2 * DFF], BF16)
        wo = self.sb(es, "c_wo", [128, NJ, D], BF16)
        gfin = self.sb(es, "c_gfin", [128, D], F32)
        self.dma('sp', gfin.v, inp["norm_final_g"].partition_broadcast(128))
        with ExitStack() as tes:
            stg = [self.sb(tes, "c_stg%d" % i, [128, 2048], F32) for i in range(3)]
            for k in range(8):
                self.load_weight(stg, wi[:, k, :], inp["w_ffn_in"][k * 128:(k + 1) * 128, :], 2 * DFF,
                                 scale=cols[:, ci["g"] + k: ci["g"] + k + 1])
            for j in range(NJ):
                self.load_weight(stg, wo[:, j, :], inp["w_ffn_out"][j * 128:(j + 1) * 128, :], D)
            S.barrier()
        xts = [self.sb(es, "c_xt%d" % i, [128, NS, D], F32) for i in range(2)]
        hs = [self.sb(es, "c_h%d" % i, [128, NS, D], BF16) for i in range(2)]
        hTs = [self.sb(es, "c_hT%d" % i, [128, 8, TT], BF16) for i in range(2)]
        actT = [self.sb(es, "c_actT%d" % j, [128, TT], BF16) for j in range(NJ)]
        junk = self.sb(es, "c_junk", [128, D], BF16)
        sss = [self.sb(es, "c_ss%d" % i, [128, 4], F32) for i in range(3)]
        rstds = [self.sb(es, "c_rstd%d" % i, [128, 4], F32) for i in range(3)]
        halos = [self.sb(es, "c_halo%d" % j, [128, 2], F32) for j in range(NJ)]
        NW = 4
        uraw = [es.enter_context(self.nc.sbuf_tensor("c_uw%d" % i, [128, 2 + TT], F32)) for i in range(NW)]
        uh = [T(r[:, 0:2], "c_uh") for r in uraw]
        ub = [T(r[:, 2:2 + TT], "c_ub") for r in uraw]

        def uview(jj, a, b_):
            return V([uh[jj], ub[jj]], uraw[jj][:, a:b_])

        acc = [self.sb(es, "c_acc%d" % i, [128, TT], F32) for i in range(NW)]
        th = [self.sb(es, "c_th%d" % i, [128, TT], F32) for i in range(NW)]
        ntile = self.ntok // TT
        tiles_per_b = self.seq // TT

        def load(i):
            if i < ntile:
                self.dma('sp', xts[i % 2].v, src[i * TT:(i + 1) * TT, :].rearrange("(s p) d -> p s d", p=128))

        def front(i):
            self.rms_h(xts[i % 2], NS, sss[i % 2], rstds[i % 2], junk, hs[i % 2])
            self.transpose_h(hs[i % 2], NS, hTs[i % 2])

        def mid(i):
            hT = hTs[i % 2]
            st = {}
            for j in range(NJ + 2):
                if j < NJ:
                    ps = self.next_ps()
                    for half in range(2):
                        c0 = half * DFF + j * 128
                        for k in range(8):
                            self.mm(ps[:, half * TT:(half + 1) * TT], wi[:, k, c0:c0 + 128], hT[:, k, :], start=(k == 0), stop=(k == 7))
                    jj = j % NW
                    a = acc[jj]
                    w0 = cols[:, ci["cw0"] + j: ci["cw0"] + j + 1]
                    w1 = cols[:, ci["cw1"] + j: ci["cw1"] + j + 1]
                    w2 = cols[:, ci["cw2"] + j: ci["cw2"] + j + 1]
                    cb = cols[:, ci["cb"] + j: ci["cb"] + j + 1]
                    self.cp('dve', uh[jj].v, halos[j].v)
                    self.cp('act', ub[jj].v, ps[:, 0:TT])
                    self.act(a.v, ps[:, 0:TT], AF.Identity, scale=w2, bias=cb)
                    self.stt('dve', a.v, uview(jj, 1, 1 + TT), w1, a.v, ALU.mult, ALU.add)
                    self.stt('dve', a.v, uview(jj, 0, TT), w0, a.v, ALU.mult, ALU.add)
                    self.cp('dve', halos[j].v, ub[jj][:, TT - 2:TT])
                    st[j] = ps
                if 0 <= j - 1 < NJ:
                    jj = (j - 1) % NW
                    self.act(th[jj].v, acc[jj].v, AF.Gelu_apprx_tanh)
                if 0 <= j - 2 < NJ:
                    jj = (j - 2) % NW
                    self.tt('dve', actT[j - 2].v, th[jj].v, st[j - 2][:, TT:2 * TT], ALU.mult)

        def back(i):
            xt = xts[i % 2]
            ss, rstd = sss[2], rstds[2]
            for s in range(NS):
                pss = [self.next_ps(), self.next_ps()]
                for half in range(2):
                    for j in range(NJ):
                        self.mm(pss[half].v, actT[j][:, s * 128:(s + 1) * 128], wo[:, j, half * 512:(half + 1) * 512],
                                start=(j == 0), stop=(j == NJ - 1))
                for half in range(2):
                    self.tt('dve', xt[:, s, half * 512:(half + 1) * 512], xt[:, s, half * 512:(half + 1) * 512], pss[half].v, ALU.add)
                if final:
                    self.act(junk.v, xt[:, s, :], AF.Square, accum=ss[:, s:s + 1])
                    self.ts('dve', rstd[:, s:s + 1], ss[:, s:s + 1], 1.0 / D, NORM_EPS, ALU.mult, ALU.add)
                    self.rsqrt(rstd[:, s:s + 1], rstd[:, s:s + 1])
                    self.stt('dve', xt[:, s, :], xt[:, s, :], rstd[:, s:s + 1], gfin.v, ALU.mult, ALU.mult)
                self.dma('sp', dst[i * TT + s * 128: i * TT + (s + 1) * 128, :], xt[:, s, :])

        load(0)
        load(1)
        front(0)
        for i in range(ntile):
            if S.need_reset():
                S.hard_barrier()
            if i % tiles_per_b == 0:
                for j in range(NJ):
                    self.memset('pool', halos[j].v, 0.0)
            mid(i)
            if i + 1 < ntile:
                front(i + 1)
            back(i)
            load(i + 2)


_PROG_CACHE = {}


def get_prog(nb=4, seq=2048, phases="LRMBC"):
    key = (nb, seq, phases)
    if key not in _PROG_CACHE:
        _PROG_CACHE[key] = Prog(nb, seq, phases)
    return _PROG_CACHE[key]


_W_NAMES = ["norm_mix_g", "w_in", "b_in", "mu_shift", "w0", "w_lora_up", "a0", "a_lora_up", "g_lora_up", "k_k", "k_a",
            "r_k", "lnx_g", "lnx_b", "w_branch_a", "conv_b_w", "conv_b_b", "w_rg_a", "b_rg_a", "w_rg_x", "b_rg_x",
            "lru_lambda", "w_branch_b", "w_mix_out", "norm_x_g", "norm_mem_g", "w_cq", "w_ckv", "w_co", "norm_ffn_g",
            "w_ffn_in", "ffn_conv_w", "ffn_conv_b", "w_ffn_out", "norm_final_g"]


def run(inputs, ncores=8, nb=4, seq=2048, phases="LRMBC"):
    prog = get_prog(nb, seq, phases)
    shapes = {k: tuple(v.shape) for k, v in prog.inp.items()}
    shared = {}
    for k in _W_NAMES:
        a = np.ascontiguousarray(np.asarray(inputs[k], dtype=np.float32))
        shared[k] = a.reshape(shapes[k])
    x = np.asarray(inputs["x"], dtype=np.float32)
    mem = np.asarray(inputs["mem"], dtype=np.float32)
    in_maps = []
    for c in range(ncores):
        m = dict(shared)
        m["x"] = np.ascontiguousarray(x[c * nb:(c + 1) * nb, :seq]).reshape(nb * seq, D)
        m["mem"] = np.ascontiguousarray(mem[c * nb:(c + 1) * nb]).reshape(nb * NMEM, D)
        in_maps.append(m)
    res = run_bass_kernel_spmd(prog.nc, in_maps, core_ids=list(range(ncores)))
    outs = [np.asarray(r["out"]).reshape(nb, seq, D) for r in res.results]
    return np.concatenate(outs, axis=0)


def kernel(**inputs):
    return run(inputs).astype(np.float32)
```

```python
import numpy as np
from contextlib import ExitStack
import concourse.bass as bass
import concourse.mybir as mybir
from concourse.bass_utils import run_bass_kernel_spmd

F32 = mybir.dt.float32
BF16 = mybir.dt.bfloat16
AF = mybir.ActivationFunctionType
ALU = mybir.AluOpType
AX = mybir.AxisListType

D = 1024
NMEM = 256
AW = 512
RWKV_COLS = 1792
P_IN = 4864
DFF = 2816
NORM_EPS = 1e-6
LNX_EPS = 64e-5
SAME_ENGINE_SYNC = True
import os as _os
RSTOP = int(_os.environ.get("RSTOP", "0"))
if _os.environ.get("NOSES"):
    SAME_ENGINE_SYNC = False
LVAR = int(_os.environ.get("LVAR", "2"))


_ALL_TILES = []


class T:
    def __init__(self, ap, name=""):
        self.ap = ap if isinstance(ap, bass.AP) else ap[:]
        self.w = None
        self.r = []
        self.name = name
        self.dsem = None
        _ALL_TILES.append(self)

    def __getitem__(self, k):
        return V([self], self.ap[k])

    @property
    def v(self):
        return V([self], self.ap)


class V:
    def __init__(self, ts, ap):
        self.ts = ts
        self.ap = ap

    def __getitem__(self, k):
        return V(self.ts, self.ap[k])

    def re(self, pat, **kw):
        return V(self.ts, self.ap.rearrange(pat, **kw))

    def bc(self, shape):
        return V(self.ts, self.ap.to_broadcast(shape))

    def un(self, axis):
        return V(self.ts, self.ap.unsqueeze(axis))

    def bitcast(self, dt):
        return V(self.ts, self.ap.bitcast(dt))


def _ap(x):
    return x.ap if isinstance(x, V) else x


class Sync:
    def __init__(self, nc, es, n_dma_sems=64):
        self.nc = nc
        self.es = es
        self.engs = {'pe': nc.tensor, 'act': nc.scalar, 'dve': nc.vector, 'pool': nc.gpsimd, 'sp': nc.sync}
        self.sem = {}
        self.cnt = {}
        self.seen = {}
        for e in self.engs:
            self.sem[e] = es.enter_context(nc.semaphore("c_" + e))
            self.cnt[e] = 0
            self.seen[e] = {}
        self.dpool = [{'sem': es.enter_context(nc.semaphore("d%d" % i)), 'cnt': 0} for i in range(n_dma_sems)]
        self.dfree = list(range(n_dma_sems))
        self.n_inst = 0
        self.bar1 = es.enter_context(nc.semaphore("bar1"))
        self.bar2 = es.enter_context(nc.semaphore("bar2"))
        self.bar_k = 0

    def _wait(self, e, tok):
        if tok is None:
            return
        sem, val, src = tok
        if src == e and (e == 'pe' or not SAME_ENGINE_SYNC):
            return
        key = sem.name
        if self.seen[e].get(key, 0) >= val:
            return
        self.seen[e][key] = val
        self.engs[e].wait_ge(sem, val)

    def deps(self, e, reads, writes):
        for v in reads:
            if not isinstance(v, V):
                continue
            for t in v.ts:
                self._wait(e, t.w)
        for v in writes:
            for t in v.ts:
                self._wait(e, t.w)
                for tok in t.r:
                    self._wait(e, tok)

    def done(self, tok, reads, writes):
        for v in reads:
            if not isinstance(v, V):
                continue
            for t in v.ts:
                t.r.append(tok)
                if len(t.r) > 24:
                    t.r = t.r[-24:] if False else t.r
        for v in writes:
            for t in v.ts:
                t.w = tok
                t.r = []

    def op(self, e, fn, reads, writes, signal=True):
        self.deps(e, reads, writes)
        inst = fn(self.engs[e])
        self.n_inst += 1
        if signal:
            self.cnt[e] += 1
            inst.then_inc(self.sem[e], 1)
            tok = (self.sem[e], self.cnt[e], e)
        else:
            tok = (self.sem[e], self.cnt[e] + 1, e)
        self.done(tok, reads, writes)
        return inst

    def _dsem(self, t):
        if t.dsem is None:
            if not self.dfree:
                raise RuntimeError("out of dma semaphores")
            t.dsem = self.dpool[self.dfree.pop(0)]
        return t.dsem

    def dma(self, q, out, in_, **kw):
        reads = [in_] if isinstance(in_, V) else []
        writes = [out] if isinstance(out, V) else []
        self.deps(q, reads, writes)
        sbv = out if isinstance(out, V) else in_
        ds = self._dsem(sbv.ts[0])
        inst = self.engs[q].dma_start(out=_ap(out), in_=_ap(in_), **kw)
        self.n_inst += 1
        ds['cnt'] += 16
        inst.then_inc(ds['sem'], 16)
        tok = (ds['sem'], ds['cnt'], 'dma')
        self.done(tok, reads, writes)
        return tok

    def barrier(self):
        toks = [(self.sem[f], self.cnt[f], f) for f in self.engs if self.cnt[f] > 0]
        toks += [(d['sem'], d['cnt'], 'dma') for d in self.dpool if d['cnt'] > 0]
        for e in self.engs:
            for tok in toks:
                if tok[2] == e:
                    continue
                self._wait(e, tok)

    def release_dma_sems(self):
        self.dfree = list(range(len(self.dpool)))
        for t in _ALL_TILES:
            t.dsem = None

    def need_reset(self, limit=2600):
        return max(self.cnt.values()) > limit or max(d['cnt'] for d in self.dpool) > limit

    def hard_barrier(self):
        self.barrier()
        self.bar_k += 1
        k = self.bar_k
        for e in self.engs:
            self.engs[e].sem_inc(self.bar1, 1)
        sp = self.engs['sp']
        sp.wait_ge(self.bar1, len(self.engs) * k)
        for e in self.engs:
            if self.cnt[e] > 0:
                sp.sem_clear(self.sem[e])
        for d in self.dpool:
            if d['cnt'] > 0:
                sp.sem_clear(d['sem'])
        sp.sem_inc(self.bar2, 1)
        for e in self.engs:
            if e != 'sp':
                self.engs[e].wait_ge(self.bar2, k)
        for e in self.engs:
            self.cnt[e] = 0
            self.seen[e] = {}
        for d in self.dpool:
            d['cnt'] = 0
        for t in _ALL_TILES:
            t.w = None
            t.r = []


class Prog:
    def __init__(self, nb=4, seq=2048, phases="LRMBC", dbg=False):
        self.nb, self.seq, self.phases, self.dbg = nb, seq, phases, dbg
        self.ntok = nb * seq
        nc = self.nc = bass.Bass("TRN2", target_bir_lowering=False)
        self.inp = {}
        del _ALL_TILES[:]

        def din(name, shape):
            self.inp[name] = nc.dram_tensor(name, list(shape), F32, kind="ExternalInput").ap()
            return self.inp[name]

        ntok = self.ntok
        din("x", [ntok, D])
        din("mem", [nb * NMEM, D])
        din("norm_mix_g", [D]); din("w_in", [D, P_IN]); din("b_in", [P_IN]); din("mu_shift", [RWKV_COLS])
        din("w0", [AW]); din("w_lora_up", [64, AW]); din("a0", [AW]); din("a_lora_up", [64, AW])
        din("g_lora_up", [128, AW]); din("k_k", [AW]); din("k_a", [AW]); din("r_k", [AW])
        din("lnx_g", [AW]); din("lnx_b", [AW]); din("w_branch_a", [AW, D])
        din("conv_b_w", [4, AW]); din("conv_b_b", [AW]); din("w_rg_a", [8, 64, 64]); din("b_rg_a", [AW])
        din("w_rg_x", [8, 64, 64]); din("b_rg_x", [AW]); din("lru_lambda", [AW]); din("w_branch_b", [AW, D])
        din("w_mix_out", [D, D]); din("norm_x_g", [D]); din("norm_mem_g", [D]); din("w_cq", [D, D])
        din("w_ckv", [D, 2 * D]); din("w_co", [D, D]); din("norm_ffn_g", [D]); din("w_ffn_in", [D, 2 * DFF])
        din("ffn_conv_w", [3, DFF]); din("ffn_conv_b", [DFF]); din("w_ffn_out", [DFF, D]); din("norm_final_g", [D])
        self.out = nc.dram_tensor("out", [ntok, D], F32, kind="ExternalOutput").ap()
        self.x1 = nc.dram_tensor("x1s", [ntok, D], F32).ap()
        self.x2 = nc.dram_tensor("x2s", [ntok, D], F32).ap()
        self.dbg_out = {}

        with ExitStack() as es:
            self.es = es
            self.S = Sync(nc, es)
            self.ps = [T(es.enter_context(nc.psum_tensor("ps%d" % i, [128, 512], F32)), "ps%d" % i) for i in range(8)]
            self.ps_i = 0
            self.ident = self.sb(es, "ident", [128, 128], BF16)
            self.identf = self.sb(es, "identf", [128, 128], F32)
            for idt in (self.ident, self.identf):
                self.S.op('pool', lambda e: e.memset(idt.ap[:], 0.0), [], [idt.v])
                self.S.op('pool', lambda e: e.affine_select(idt.ap[:], idt.ap[:], pattern=[[-1, 128]],
                                                           compare_op=ALU.not_equal, fill=1.0, base=0,
                                                           channel_multiplier=1), [idt.v], [idt.v])
            self.mhalf = self.sb(es, "mhalf", [128, 512], F32)
            self.memset('pool', self.mhalf.v, -0.5)
            self.phalf = self.sb(es, "phalf", [128, 512], F32)
            self.memset('pool', self.phalf.v, 0.5)
            self.yas = nc.dram_tensor("yas", [AW, ntok], BF16).ap()
            self.ybs = nc.dram_tensor("ybs", [AW, ntok], BF16).ap()
            src = {'L': self.inp["x"], 'R': self.inp["x"], 'M': self.inp["x"], 'B': self.x1, 'C': self.x2}
            dst = {'L': None, 'R': None, 'M': self.x1, 'B': self.x2, 'C': self.out}
            order = [p for p in "LRMBC" if p in phases]
            chain = [p for p in order if p in "MBC"]
            for i, p in enumerate(order):
                s_ap, d_ap = src[p], dst[p]
                if p in chain:
                    if chain.index(p) == 0:
                        s_ap = self.inp["x"]
                    if chain.index(p) == len(chain) - 1:
                        d_ap = self.out
                with ExitStack() as pes:
                    getattr(self, "phase" + p)(pes, s_ap, d_ap, final=(p == 'C'))
                    self.S.hard_barrier()
                self.S.release_dma_sems()
            self.S.barrier()

    def sb(self, es, name, shape, dt):
        return T(es.enter_context(self.nc.sbuf_tensor(name, list(shape), dt)), name)

    ps_allowed = None

    def next_ps(self):
        if self.ps_allowed is not None:
            self._psa_i = getattr(self, "_psa_i", 0) + 1
            return self.ps[self.ps_allowed[self._psa_i % len(self.ps_allowed)]]
        t = self.ps[self.ps_i]
        self.ps_i = (self.ps_i + 1) % 8
        return t

    def act(self, out, in_, func, bias=0.0, scale=1.0, accum=None):
        rd = [in_] + [a for a in (bias, scale) if isinstance(a, V)]
        wr = [out] + ([accum] if accum is not None else [])
        kw = {}
        if accum is not None:
            kw['accum_out'] = accum.ap
        return self.S.op('act', lambda e: e.activation(out=out.ap, in_=in_.ap, func=func, bias=_ap(bias),
                                                       scale=_ap(scale), **kw), rd, wr)

    def tt(self, eng, out, in0, in1, op):
        return self.S.op(eng, lambda e: e.tensor_tensor(out=out.ap, in0=in0.ap, in1=in1.ap, op=op), [in0, in1], [out])

    def ts(self, eng, out, in0, s1, s2, op0, op1=None):
        rd = [in0] + [a for a in (s1, s2) if isinstance(a, V)]
        if op1 is None:
            return self.S.op(eng, lambda e: e.tensor_scalar(out=out.ap, in0=in0.ap, scalar1=_ap(s1), scalar2=None,
                                                            op0=op0), rd, [out])
        return self.S.op(eng, lambda e: e.tensor_scalar(out=out.ap, in0=in0.ap, scalar1=_ap(s1), scalar2=_ap(s2),
                                                        op0=op0, op1=op1), rd, [out])

    def stt(self, eng, out, in0, scalar, in1, op0, op1):
        rd = [in0, in1] + ([scalar] if isinstance(scalar, V) else [])
        return self.S.op(eng, lambda e: e.scalar_tensor_tensor(out=out.ap, in0=in0.ap, scalar=_ap(scalar), in1=in1.ap,
                                                               op0=op0, op1=op1), rd, [out])

    def cp(self, eng, out, in_):
        if eng == 'act':
            return self.act(out, in_, AF.Copy)
        return self.S.op(eng, lambda e: e.tensor_copy(out=out.ap, in_=in_.ap), [in_], [out])

    def rsqrt(self, out, in_):
        shp = list(in_.ap.shape)
        if shp[-1] > 8:
            self.act(out, in_, AF.Ln)
            return self.act(out, out, AF.Exp, scale=-0.5)
        mh = self.mhalf[0:shp[0], 0:shp[-1]]
        if len(shp) == 3:
            mh = mh.un(1).bc(shp)
        return self.tt('pool', out, in_, mh, ALU.pow)

    def memset(self, eng, out, val):
        return self.S.op(eng, lambda e: e.memset(out.ap, val), [], [out])

    def mm(self, out, lhsT, rhs, start, stop, signal=None):
        if signal is None:
            signal = stop
        return self.S.op('pe', lambda e: e.matmul(out.ap, lhsT=lhsT.ap, rhs=rhs.ap, start=start, stop=stop),
                         [lhsT, rhs], [out], signal=signal)

    def tr(self, out, in_, signal=True, f32=False):
        idt = self.identf if f32 else self.ident
        n = in_.ap.shape[0]
        return self.S.op('pe', lambda e: e.transpose(out.ap, in_.ap, idt.ap[0:n, 0:n]), [in_, idt.v], [out],
                         signal=signal)

    def dma(self, q, out, in_, **kw):
        return self.S.dma(q, out, in_, **kw)

    def load_weight(self, stg, dst, src_rows, ncols, scale=None, piece=2048):
        rows = src_rows.shape[0]
        c0 = 0
        while c0 < ncols:
            n = min(piece, ncols - c0)
            st = stg[self._stg_i % len(stg)]
            eng = ('dve', 'act')[self._stg_i % 2]
            self._stg_i += 1
            self.dma('sp', st[0:rows, 0:n], src_rows[:, c0:c0 + n])
            o = dst[:, c0:c0 + n]
            i = st[0:rows, 0:n]
            if scale is None:
                self.cp(eng, o, i)
            elif eng == 'act':
                if isinstance(scale, V):
                    self.act(o, i, AF.Copy, scale=scale)
                else:
                    self.act(o, i, AF.Copy, scale=float(scale))
            else:
                self.ts(eng, o, i, scale, None, ALU.mult)
            c0 += n

    def load_cols(self, es, name, specs):
        cols = {}
        n = 0
        for k, v in specs:
            cols[k] = n
            n += (v.shape[0] + 127) // 128
        res = self.sb(es, name, [128, n], F32)
        with ExitStack() as tes:
            ngrp = (n + 127) // 128
            stage = [self.sb(tes, name + "_st%d" % g, [128, 128], F32) for g in range(ngrp)]
            for st in stage:
                self.memset('pool', st.v, 0.0)
            for k, v in specs:
                L = v.shape[0]
                m = (L + 127) // 128
                c = cols[k]
                r = 0
                while r < m:
                    g, rr = divmod(c + r, 128)
                    cnt = min(m - r, 128 - rr)
                    if L >= 128:
                        self.dma('sp', stage[g][rr:rr + cnt, :], v[r * 128:(r + cnt) * 128].rearrange("(m p) -> m p", p=128))
                    else:
                        self.dma('sp', stage[g][rr:rr + 1, 0:L], v.rearrange("(m p) -> m p", m=1))
                    r += cnt
            for g in range(ngrp):
                w = min(128, n - g * 128)
                ps = self.next_ps()
                self.tr(ps[:, 0:w], stage[g][0:w, :], f32=True)
                self.cp('dve', res[:, g * 128:g * 128 + w], ps[:, 0:w])
            self.S.barrier()
        return res, cols

    def rms_h(self, xt, nsub, ss, rstd, junk, h):
        for s in range(nsub):
            self.act(junk.v, xt[:, s, :], AF.Square, accum=ss[:, s:s + 1])
        self.ts('dve', rstd[:, 0:nsub], ss[:, 0:nsub], 1.0 / D, NORM_EPS, ALU.mult, ALU.add)
        self.rsqrt(rstd[:, 0:nsub], rstd[:, 0:nsub])
        for s in range(nsub):
            self.ts('dve', h[:, s, :], xt[:, s, :], rstd[:, s:s + 1], None, ALU.mult)

    def transpose_h(self, h, nsub, hT, evac=('act', 'dve')):
        tw = nsub * 128
        per_bank = 1024 // tw
        c = 0
        i = 0
        while c < 8:
            ps = self.next_ps()
            pb = ps.v.bitcast(BF16)
            nchunk = min(per_bank, 8 - c)
            for cc in range(nchunk):
                for s in range(nsub):
                    last = (cc == nchunk - 1 and s == nsub - 1)
                    self.tr(pb[:, cc * tw + s * 128: cc * tw + (s + 1) * 128], h[:, s, (c + cc) * 128:(c + cc + 1) * 128],
                            signal=last)
            self.cp(evac[i % len(evac)], hT[:, c:c + nchunk, :], pb[:, 0:nchunk * tw].re("p (c t) -> p c t", c=nchunk))
            c += nchunk
            i += 1

    def norm_from_dram(self, src, tok0, NS, xbufs, sst, rst, junk, h):
        for s in range(NS):
            xb = xbufs[self._xb_i % len(xbufs)]
            self._xb_i += 1
            self.dma('sp', xb.v, src[tok0 + s * 128: tok0 + (s + 1) * 128, :])
            self.act(junk.v, xb.v, AF.Square, accum=sst[s].v)
            self.ts('dve', rst[s].v, sst[s].v, 1.0 / D, NORM_EPS, ALU.mult, ALU.add)
            self.rsqrt(rst[s].v, rst[s].v)
            self.ts('dve', h[:, s, :], xb.v, rst[s].v, None, ALU.mult)

    def norm_bufs(self, es, pfx, NS):
        xbufs = [self.sb(es, pfx + "_xb%d" % i, [128, D], F32) for i in range(2)]
        sst = [self.sb(es, pfx + "_ss%d" % i, [128, 1], F32) for i in range(NS)]
        rst = [self.sb(es, pfx + "_rs%d" % i, [128, 1], F32) for i in range(NS)]
        junk = self.sb(es, pfx + "_junk", [128, D], BF16)
        self._xb_i = 0
        return xbufs, sst, rst, junk

    def phaseL(self, es, src, dst, final=False):
        S = self.S
        inp = self.inp
        TT = 512
        NS = 4
        self._stg_i = 0
        self._ev_i = 0
        cols, ci = self.load_cols(es, "l_cols", [
            ("g", inp["norm_mix_g"]), ("bin", inp["b_in"][1792:2816]),
            ("cw0", inp["conv_b_w"][0]), ("cw1", inp["conv_b_w"][1]), ("cw2", inp["conv_b_w"][2]), ("cw3", inp["conv_b_w"][3]),
            ("cbb", inp["conv_b_b"]), ("bra", inp["b_rg_a"]), ("brx", inp["b_rg_x"]), ("lam", inp["lru_lambda"])])
        hb = self.sb(es, "l_hb", [128, 8], F32)
        self.ts('dve', hb[:, 0:4], cols[:, ci["bra"]:ci["bra"] + 4], 0.5, None, ALU.mult)
        self.ts('dve', hb[:, 4:8], cols[:, ci["brx"]:ci["brx"] + 4], 0.5, None, ALU.mult)
        cA = self.sb(es, "l_cA", [128, 8], F32)
        lt = self.sb(es, "l_lt", [128, 4], F32)
        self.act(lt.v, cols[:, ci["lam"]:ci["lam"] + 4], AF.Exp, scale=-1.0)
        self.ts('dve', lt.v, lt.v, 1.0, None, ALU.add)
        self.act(lt.v, lt.v, AF.Ln)
        self.ts('dve', cA[:, 0:4], lt.v, -4.0, None, ALU.mult)
        self.ts('dve', cA[:, 4:8], lt.v, -8.0, None, ALU.mult)
        win = self.sb(es, "l_win", [128, 8, 1024], BF16)
        wg = self.sb(es, "l_wg", [128, 2, 4, 128], BF16)
        with ExitStack() as tes:
            stg = [self.sb(tes, "l_stg%d" % i, [128, 1024], F32) for i in range(3)]
            for k in range(8):
                self.load_weight(stg, win[:, k, :], inp["w_in"][k * 128:(k + 1) * 128, 1792:2816], 1024,
                                 scale=cols[:, ci["g"] + k: ci["g"] + k + 1], piece=1024)
            for gi, nm in enumerate(("w_rg_a", "w_rg_x")):
                st = stg[gi]
                self.memset('pool', st.v, 0.0)
                for blk in range(8):
                    c, hl = divmod(blk, 2)
                    self.dma('sp', st[hl * 64:(hl + 1) * 64, c * 128 + hl * 64: c * 128 + (hl + 1) * 64], inp[nm][blk])
                self.cp('dve', wg[:, gi, :, :], st[:, 0:512].re("p (c m) -> p c m", c=4))
            S.barrier()
        xbufs, sst, rst, junk = self.norm_bufs(es, "l", NS)
        h = self.sb(es, "l_h", [128, NS, D], BF16)
        hT = self.sb(es, "l_hT", [128, 8, TT], BF16)
        PX = [self.sb(es, "l_px%d" % c, [128, 3 + TT], F32) for c in range(4)]
        GY = [self.sb(es, "l_gy%d" % c, [128, TT], F32) for c in range(4)]
        YB = [self.sb(es, "l_yb%d" % c, [128, TT], BF16) for c in range(4)]
        carry = [self.sb(es, "l_cy%d" % c, [128, 1], F32) for c in range(4)]
        NW = 2
        def wk(nm, dt=F32, n=NW):
            return [self.sb(es, "l_%s%d" % (nm, i), [128, TT], dt) for i in range(n)]
        acc = wk("acc", n=4); xbb = wk("xbb", BF16, n=4); ta = wk("ta", n=4); tx = wk("tx", n=4)
        av = wk("av"); a2 = wk("a2"); u = wk("u"); hl_ = wk("hl")
        ntile = self.ntok // TT
        tiles_per_b = self.seq // TT
        hs = [h, self.sb(es, "l_h1", [128, NS, D], BF16)]
        hTs = [hT, self.sb(es, "l_hT1", [128, 8, TT], BF16)]

        def front(i):
            self.norm_from_dram(src, i * TT, NS, xbufs, sst, rst, junk, hs[i % 2])
            self.transpose_h(hs[i % 2], NS, hTs[i % 2])

        def midA(i):
            hT_ = hTs[i % 2]
            first = (i % tiles_per_b == 0)
            if first:
                for c in range(4):
                    self.memset('pool', PX[c][:, 0:3], 0.0)
                    self.memset('pool', carry[c].v, 0.0)
            for c in range(4):
                ps = self.next_ps()
                for k in range(8):
                    self.mm(ps.v, win[:, k, c * 128:(c + 1) * 128], hT_[:, k, :], start=(k == 0), stop=(k == 7))
                self.act(PX[c][:, 3:3 + TT], ps.v, AF.Identity, bias=cols[:, ci["bin"] + c: ci["bin"] + c + 1])
            for c in range(4):
                ps = self.next_ps()
                for k in range(8):
                    self.mm(ps.v, win[:, k, 512 + c * 128: 512 + (c + 1) * 128], hT_[:, k, :], start=(k == 0), stop=(k == 7))
                self.act(GY[c].v, ps.v, AF.Gelu_apprx_tanh, bias=cols[:, ci["bin"] + 4 + c: ci["bin"] + 5 + c])
            for c in range(4):
                A = acc[c]; XB = xbb[c]; TA = ta[c]; TX = tx[c]
                cw = [cols[:, ci["cw%d" % j] + c: ci["cw%d" % j] + c + 1] for j in range(4)]
                self.ts('dve', A.v, PX[c][:, 0:TT], cw[0], cols[:, ci["cbb"] + c: ci["cbb"] + c + 1], ALU.mult, ALU.add)
                for j in range(1, 4):
                    self.stt('dve', A.v, PX[c][:, j:j + TT], cw[j], A.v, ALU.mult, ALU.add)
                self.cp('pool', PX[c][:, 0:3], PX[c][:, TT:TT + 3])
                self.cp('dve', XB.v, A.v)
                psa = self.next_ps()
                self.mm(psa.v, wg[:, 0, c, :], XB.v, start=True, stop=True)
                psx = self.next_ps()
                self.mm(psx.v, wg[:, 1, c, :], XB.v, start=True, stop=True)
                self.act(TA.v, psa.v, AF.Tanh, scale=0.5, bias=hb[:, c:c + 1])
                self.act(TX.v, psx.v, AF.Tanh, scale=0.5, bias=hb[:, 4 + c:5 + c])

        def midB(i):
            tok0 = i * TT
            first = (i % tiles_per_b == 0)
            for c in range(4):
                A = acc[c]; TA = ta[c]; TX = tx[c]; AV = av[c % NW]; A2 = a2[c % NW]; U = u[c % NW]; HL = hl_[c % NW]
                self.act(AV.v, TA.v, AF.Exp, scale=cA[:, c:c + 1], bias=cA[:, c:c + 1])
                self.act(A2.v, TA.v, AF.Exp, scale=cA[:, 4 + c:5 + c], bias=cA[:, 4 + c:5 + c])
                self.ts('dve', A2.v, A2.v, -1.0, 1.0, ALU.mult, ALU.add)
                self.ts('dve', A2.v, A2.v, 1e-30, None, ALU.max)
                self.act(A2.v, A2.v, AF.Ln)
                self.act(A2.v, A2.v, AF.Exp, scale=0.5)
                if first:
                    self.memset('pool', A2[:, 0:1], 1.0)
                self.stt('dve', U.v, TX.v, 1.0, A.v, ALU.add, ALU.mult)
                self.tt('dve', U.v, U.v, A2.v, ALU.mult)
                self.S.op('dve', lambda e: e.tensor_tensor_scan(HL.ap, AV.ap, U.ap, carry[c].ap, ALU.mult, ALU.add),
                          [AV.v, U.v, carry[c].v], [HL.v])
                self.cp('pool', carry[c].v, HL[:, TT - 1:TT])
                self.tt('dve', YB[c].v, HL.v, GY[c].v, ALU.mult)
                self.dma('sp', self.ybs[c * 128:(c + 1) * 128, tok0:tok0 + TT], YB[c].v)

        front(0)
        for i in range(ntile):
            if S.need_reset():
                S.hard_barrier()
            midA(i)
            if i + 1 < ntile:
                front(i + 1)
            midB(i)

    def phaseR(self, es, src, dst, final=False):
        S = self.S
        inp = self.inp
        TT = 256
        NS = 2
        NQ = 4
        CH = 64
        self._stg_i = 0
        self._ev_i = 0
        cols, ci = self.load_cols(es, "r_cols", [
            ("g", inp["norm_mix_g"]), ("bin", inp["b_in"][0:RWKV_COLS]), ("mu", inp["mu_shift"]),
            ("w0", inp["w0"]), ("a0", inp["a0"]), ("kk", inp["k_k"]), ("ka", inp["k_a"]), ("rk", inp["r_k"])])

        def col(key, j):
            return cols[:, ci[key] + j: ci[key] + j + 1]

        der = self.sb(es, "r_der", [128, 32], F32)
        self.ts('dve', der[:, 0:14], cols[:, ci["mu"]:ci["mu"] + 14], -1.0, 1.0, ALU.mult, ALU.add)
        self.ts('dve', der[:, 14:18], cols[:, ci["w0"]:ci["w0"] + 4], 0.5, None, ALU.mult)
        self.ts('dve', der[:, 18:22], cols[:, ci["a0"]:ci["a0"] + 4], 0.5, None, ALU.mult)
        self.ts('dve', der[:, 22:26], cols[:, ci["ka"]:ci["ka"] + 4], 0.5, None, ALU.mult)
        self.ts('dve', der[:, 26:30], cols[:, ci["ka"]:ci["ka"] + 4], -0.5, 1.0, ALU.mult, ALU.add)
        rkb = self.sb(es, "r_rkb", [128, 4], BF16)
        self.cp('dve', rkb.v, cols[:, ci["rk"]:ci["rk"] + 4])
        m64 = [self.sb(es, "r_m64_%d" % i, [64, 64], BF16) for i in range(3)]
        masks = [self.sb(es, "r_mask%d" % i, [128, 128], BF16) for i in range(3)]
        specs = [([[1, 64]], -1, ALU.is_gt), ([[-1, 64]], 1, ALU.is_gt), ([[1, 64]], -1, ALU.is_ge)]
        for mi in range(3):
            pat, cm, op = specs[mi]
            self.memset('pool', m64[mi].v, 1.0)
            self.S.op('pool', lambda e: e.affine_select(m64[mi].ap, m64[mi].ap, pattern=pat, compare_op=op, fill=0.0,
                                                        base=0, channel_multiplier=cm), [m64[mi].v], [m64[mi].v])
            for a in range(2):
                for b_ in range(2):
                    self.dma('sp', masks[mi][a * 64:(a + 1) * 64, b_ * 64:(b_ + 1) * 64], m64[mi].v)
        mask_su, mask_sl, mask_u = masks
        bones = self.sb(es, "r_bones", [128, 128], BF16)
        self.memset('pool', bones.v, 0.0)
        self.memset('pool', bones[0:64, 0:64], 1.0)
        self.memset('pool', bones[64:128, 64:128], 1.0)
        m01 = self.sb(es, "r_m01", [128, TT], F32)
        self.memset('pool', m01.v, 1.0)
        self.memset('pool', m01.v.re("p (q s) -> p q s", s=CH)[:, :, 0:1], 0.0)
        lnxg = self.sb(es, "r_lnxg", [128, 4, 64], F32)
        lnxb = self.sb(es, "r_lnxb", [128, 4, 64], F32)
        for c in range(4):
            for hl in range(2):
                hh = 2 * c + hl
                self.dma('sp', lnxg[hl * 64:(hl + 1) * 64, c, :], inp["lnx_g"][hh * 64:(hh + 1) * 64].partition_broadcast(64))
                self.dma('sp', lnxb[hl * 64:(hl + 1) * 64, c, :], inp["lnx_b"][hh * 64:(hh + 1) * 64].partition_broadcast(64))
        win = self.sb(es, "r_win", [128, 8, RWKV_COLS], BF16)
        wlora = self.sb(es, "r_wlora", [128, 2, AW], BF16)
        gup = self.sb(es, "r_gup", [128, AW], BF16)
        with ExitStack() as tes:
            stg = [self.sb(tes, "r_stg%d" % i, [128, RWKV_COLS], F32) for i in range(3)]
            for k in range(8):
                self.load_weight(stg, win[:, k, :], inp["w_in"][k * 128:(k + 1) * 128, 0:RWKV_COLS], RWKV_COLS,
                                 scale=cols[:, ci["g"] + k: ci["g"] + k + 1], piece=RWKV_COLS)
            st = stg[0]
            self.memset('pool', st[:, 0:2 * AW], 0.0)
            self.dma('sp', st[0:64, 0:AW], inp["w_lora_up"])
            self.dma('sp', st[64:128, AW:2 * AW], inp["a_lora_up"])
            self.cp('dve', wlora.v, st[:, 0:2 * AW].re("p (a m) -> p a m", a=2))
            st = stg[1]
            self.dma('sp', st[:, 0:AW], inp["g_lora_up"])
            self.cp('dve', gup.v, st[:, 0:AW])
            S.barrier()
        xbufs, sst, rst, junk = self.norm_bufs(es, "r", NS)
        h = self.sb(es, "r_h", [128, NS, D], BF16)
        hT = self.sb(es, "r_hT", [128, 8, TT], BF16)
        pcarry = [self.sb(es, "r_pc%d" % m, [128, 1], F32) for m in range(14)]
        Sx = [self.sb(es, "r_s%d" % m, [128, TT], F32) for m in range(14)]
        ltmp = [self.sb(es, "r_lt%d" % i, [128, TT], F32) for i in range(2)]
        LB = self.sb(es, "r_lb", [128, TT], BF16)
        SGd = self.sb(es, "r_sgd", [128, NQ, 128], BF16)
        NW = 2

        def wk(nm, dt=F32):
            return [self.sb(es, "r_%s%d" % (nm, i), [128, TT], dt) for i in range(NW)]

        logw = [self.sb(es, "r_logw%d" % i, [128, TT], F32) for i in range(4)]
        ta = [self.sb(es, "r_ta%d" % i, [128, TT], F32) for i in range(4)]
        cum = [self.sb(es, "r_cum%d" % i, [128, TT], F32) for i in range(4)]
        egm1 = wk("egm1"); eg = wk("eg"); eig = wk("eig"); egc = wk("egc")
        kk = wk("kk"); kk2 = wk("kk2", BF16); rn = wk("rn"); kkn = wk("kkn"); kmod = wk("kmod"); bv = wk("bv")
        names = ["AT", "BT", "KT", "RT", "BGT", "KGT", "VB", "RK"]
        EXP = {n: [self.sb(es, "r_%s%d" % (n, c), [128, NQ, 128], BF16) for c in range(4)] for n in names}
        for n in names:
            for c in range(4):
                self.memset('pool', EXP[n][c].v, 0.0)
        GC = self.sb(es, "r_gc", [128, 4, NQ], F32)
        BKG = [self.sb(es, "r_bkg%d" % q, [128, 2, 4, 128], BF16) for q in range(NQ)]
        Vst = [self.sb(es, "r_vst%d" % q, [128, 4, 64], BF16) for q in range(NQ)]
        Nb = [[self.sb(es, "r_nb%d_%d" % (q, i), [128, 4, 128], BF16) for i in range(2)] for q in range(NQ)]
        Lb = [[self.sb(es, "r_lb%d_%d" % (q, i), [128, 4, 128], BF16) for i in range(2)] for q in range(NQ)]
        Pb = [self.sb(es, "r_pb%d" % q, [128, 4, 128], BF16) for q in range(NQ)]
        LakT = [self.sb(es, "r_lak%d" % q, [128, 4, 128], BF16) for q in range(NQ)]
        MrbT = [self.sb(es, "r_mrb%d" % q, [128, 4, 128], BF16) for q in range(NQ)]
        MrkT = [self.sb(es, "r_mrk%d" % q, [128, 4, 128], BF16) for q in range(NQ)]
        H = self.sb(es, "r_H", [128, 4, 64], F32)
        Hb = self.sb(es, "r_Hb", [128, 4, 64], BF16)
        Xb = [self.sb(es, "r_Xb%d" % i, [128, 4, 64], BF16) for i in range(2)]
        Ub = [self.sb(es, "r_Ub%d" % i, [128, 4, 64], BF16) for i in range(2)]

        def yt(nm, shape=(128, 4, 64), dt=F32):
            return [self.sb(es, "r_%s%d" % (nm, i), list(shape), dt) for i in range(2)]

        Yv = yt("Yv"); Ysq = yt("Ysq"); Yn = yt("Yn"); Bn = yt("Bn"); Yf = yt("Yf", dt=BF16)
        st1 = yt("st1", (128, 4)); st2 = yt("st2", (128, 4)); mean = yt("mean", (128, 4)); var = yt("var", (128, 4))
        YT = [self.sb(es, "r_YT%d" % i, [64, 4, 2, TT], BF16) for i in range(2)]
        ident = self.ident
        ntile = self.ntok // TT
        tiles_per_b = self.seq // TT
        yas_v = self.yas.rearrange("(c hl i) t -> i c hl t", hl=2, i=64)

        PWraw = [es.enter_context(self.nc.sbuf_tensor("r_pwx%d" % i, [128, 1 + TT], F32)) for i in range(3)]
        PWh = [T(r[:, 0:1], "r_pwh") for r in PWraw]
        PWb = [T(r[:, 1:1 + TT], "r_pwb") for r in PWraw]

        def s12_pieces(it):
            tok0 = it * TT

            def p0():
                if it % tiles_per_b == 0:
                    for m in range(14):
                        self.memset('pool', pcarry[m].v, 0.0)
                self.norm_from_dram(src, tok0, NS, xbufs, sst, rst, junk, h)
                self.transpose_h(h, NS, hT)

            def proj(ms):
                def f():
                    ps = None
                    for idx, m in enumerate(ms):
                        if idx % 2 == 0:
                            ps = self.next_ps()
                        o = (idx % 2) * TT
                        for k in range(8):
                            self.mm(ps[:, o:o + TT], win[:, k, m * 128:(m + 1) * 128], hT[:, k, :], start=(k == 0), stop=(k == 7))
                        pi = m % 3
                        self.cp('dve', PWh[pi].v, pcarry[m].v)
                        self.act(PWb[pi].v, ps[:, o:o + TT], AF.Identity, bias=col("bin", m))
                        self.cp('dve', pcarry[m].v, PWb[pi][:, TT - 1:TT])
                        tmp = ltmp[m % 2]
                        self.act(tmp.v, V([PWh[pi], PWb[pi]], PWraw[pi][:, 0:TT]), AF.Copy, scale=col("mu", m))
                        self.stt('dve', Sx[m].v, PWb[pi].v, der[:, m:m + 1], tmp.v, ALU.mult, ALU.add)
                return f

            return [p0] + [proj([m]) for m in range(14)]

        def stage3(it):
            if it % tiles_per_b == 0:
                self.memset('pool', H.v, 0.0)
                self.memset('pool', Hb.v, 0.0)
            for _ in range(1):
                if RSTOP == 1:
                    return
                self.act(LB[0:64, :], Sx[12][0:64, :], AF.Tanh)
                self.cp('act', LB[64:128, :], Sx[12][64:128, :])
                tmp = ltmp[0]
                self.act(tmp.v, Sx[13].v, AF.Tanh, scale=0.5)
                for hl in range(2):
                    self.ts('dve', SGd[:, :, hl * 64:(hl + 1) * 64], tmp.v.re("p (q s) -> p q s", s=CH), 0.5, 0.5, ALU.mult, ALU.add)
                if RSTOP == 21:
                    return
                for c in range(4):
                    LW = logw[c]; TA = ta[c]; CU = cum[c]
                    ps = self.next_ps()
                    self.mm(ps[:, 0:TT], wlora[:, 0, c * 128:(c + 1) * 128], LB.v, start=True, stop=True)
                    self.mm(ps[:, TT:2 * TT], wlora[:, 1, c * 128:(c + 1) * 128], LB.v, start=True, stop=True)
                    self.act(LW.v, ps[:, 0:TT], AF.Tanh, scale=0.5, bias=der[:, 14 + c:15 + c])
                    self.act(TA.v, ps[:, TT:2 * TT], AF.Tanh, scale=0.5, bias=der[:, 18 + c:19 + c])
                    self.ts('dve', LW.v, LW.v, 1.0, -0.30326532985631671, ALU.add, ALU.mult)
                    self.S.op('dve', lambda e: e.tensor_tensor_scan(CU.ap, m01.ap, LW.ap, 0.0, ALU.mult, ALU.add),
                              [m01.v, LW.v], [CU.v])
                def stageB(c):
                    w = it * 4 + c
                    r_, k_, v_ = Sx[c], Sx[4 + c], Sx[8 + c]
                    LW = logw[c]; TA = ta[c]; CU = cum[c]
                    E1 = egm1[w % NW]; EG = eg[w % NW]; EI = eig[w % NW]
                    EC = egc[w % NW]; KK = kk[w % NW]; K2 = kk2[w % NW]; RN = rn[w % NW]; KN = kkn[w % NW]; KM = kmod[w % NW]
                    BV = bv[w % NW]
                    cu3 = CU.v.re("p (q s) -> p q s", s=CH)
                    cuC = cu3[:, :, CH - 1:CH]
                    self.act(KK.v, k_.v, AF.Copy, scale=col("kk", c))
                    self.tt('pool', E1.v, CU.v, LW.v, ALU.subtract)
                    yield
                    self.act(K2.v, KK.v, AF.Square)
                    self.tt('pool', EC.v.re("p (q s) -> p q s", s=CH), cuC.bc([128, NQ, CH]), cu3, ALU.subtract)
                    yield
                    psn = self.next_ps()
                    self.mm(psn[:, 0:TT], bones.v, K2.v, start=True, stop=True)
                    self.act(E1.v, E1.v, AF.Exp)
                    yield
                    self.act(RN.v, psn[:, 0:TT], AF.Ln)
                    yield
                    self.act(RN.v, RN.v, AF.Exp, scale=-0.5)
                    yield
                    self.tt('pool', KN.v, KK.v, RN.v, ALU.mult)
                    self.act(EI.v, CU.v, AF.Exp, scale=-1.0)
                    yield
                    self.act(EC.v, EC.v, AF.Exp)
                    self.act(KM.v, TA.v, AF.Identity, scale=der[:, 22 + c:23 + c], bias=der[:, 26 + c:27 + c])
                    yield
                    self.stt('dve', BV.v, TA.v, 1.0, KN.v, ALU.add, ALU.mult)
                    self.tt('dve', KM.v, KM.v, k_.v, ALU.mult)
                    self.act(EG.v, CU.v, AF.Exp)
                    self.act(GC[:, c, :], cuC.re("p q o -> p (q o)"), AF.Exp)
                    yield
                    for hl in range(2):
                        P_ = slice(hl * 64, (hl + 1) * 64)

                        def hv(t):
                            return t[P_, :].re("p (q s) -> p q s", s=CH)

                        def ov(n):
                            return EXP[n][c][P_, :, hl * 64:(hl + 1) * 64]

                        self.stt('dve', ov("AT"), hv(KN), -1.0, hv(E1), ALU.mult, ALU.mult)
                        self.tt('pool', ov("KT"), hv(KM), hv(EI), ALU.mult)
                        self.cp('act', ov("VB"), hv(v_))
                        yield
                        self.stt('dve', ov("BT"), hv(BV), 0.5, hv(EI), ALU.mult, ALU.mult)
                        self.tt('pool', ov("KGT"), hv(KM), hv(EC), ALU.mult)
                        yield
                        self.stt('dve', ov("BGT"), hv(BV), 0.5, hv(EC), ALU.mult, ALU.mult)
                        self.tt('pool', ov("RT"), hv(r_), hv(EG), ALU.mult)
                        yield
                        self.tt('pool', ov("RK"), hv(r_), hv(KM), ALU.mult)
                        yield

                for pair in ((0, 1), (2, 3)):
                    gens = [stageB(c) for c in pair]
                    alive = True
                    while alive:
                        alive = False
                        for g in gens:
                            try:
                                next(g)
                                alive = True
                            except StopIteration:
                                pass

        def stage45(it):
            for _ in range(1):
                if RSTOP == 2:
                    return
                for q in range(NQ):
                    psA = self.next_ps()
                    pbA = psA.v.bitcast(BF16)
                    for gi, n in enumerate(("BGT", "KGT")):
                        for c in range(4):
                            self.tr(pbA[:, (gi * 4 + c) * 128:(gi * 4 + c + 1) * 128], EXP[n][c][:, q, :], signal=(gi == 1 and c == 3))
                    self.evac(BKG[q].v, pbA.re("p (g c m) -> p g c m", g=2, c=4))
                    psB = self.next_ps()
                    pbB = psB.v.bitcast(BF16)
                    for c in range(4):
                        self.tr(pbB[:, c * 128:(c + 1) * 128], EXP["VB"][c][:, q, :], signal=(c == 3))
                    for hl in range(2):
                        self.evac(Vst[q][hl * 64:(hl + 1) * 64, :, :],
                                  pbB[hl * 64:(hl + 1) * 64, 0:512].re("p (c m) -> p c m", c=4)[:, :, hl * 64:(hl + 1) * 64])
                    combos = [("BT", "AT", mask_su, Nb[q][0]), ("AT", "BT", mask_sl, Lb[q][0]), ("KT", "AT", mask_su, LakT[q]),
                              ("BT", "RT", mask_u, MrbT[q]), ("KT", "RT", mask_u, MrkT[q])]
                    for (ln, rn_, mk, dstt) in combos:
                        ps = self.next_ps()
                        for c in range(4):
                            self.mm(ps[:, c * 128:(c + 1) * 128], EXP[ln][c][:, q, :], EXP[rn_][c][:, q, :], start=True, stop=True,
                                    signal=(c == 3))
                        self.tt('dve', dstt.v, ps.v.re("p (c m) -> p c m", c=4), mk.v.un(1).bc([128, 4, 128]), ALU.mult)
                    self.tt('pool', Pb[q].v, Nb[q][0].v, ident.v.un(1).bc([128, 4, 128]), ALU.add)
                if RSTOP == 3:
                    return
                cur = 0
                for lvl in range(1, 6):
                    nxt = 1 - cur
                    for q in range(NQ):
                        if lvl < 5:
                            ps = self.next_ps()
                            for c in range(4):
                                self.mm(ps[:, c * 128:(c + 1) * 128], Lb[q][cur][:, c, :], Nb[q][cur][:, c, :], start=True, stop=True,
                                        signal=(c == 3))
                            self.cp('act', Nb[q][nxt].v, ps.v.re("p (c m) -> p c m", c=4))
                        ps = self.next_ps()
                        for c in range(4):
                            self.mm(ps[:, c * 128:(c + 1) * 128], Nb[q][cur][:, c, :], Lb[q][cur][:, c, :], start=True, stop=True,
                                    signal=(c == 3))
                        self.cp('act', Lb[q][nxt].v, ps.v.re("p (c m) -> p c m", c=4))
                    for q in range(NQ):
                        ps = self.next_ps()
                        for c in range(4):
                            self.mm(ps[:, c * 128:(c + 1) * 128], Lb[q][nxt][:, c, :], Pb[q][:, c, :], start=True, stop=True,
                                    signal=(c == 3))
                        self.tt('dve', Pb[q].v, Pb[q].v, ps.v.re("p (c m) -> p c m", c=4), ALU.add)
                    cur = nxt

        PS = self.ps

        def crit(it, q):
            w = it * NQ + q
            xb_, ub_ = Xb[w % 2], Ub[w % 2]
            psx, psu, psh, psy = PS[0], PS[1], PS[2], PS[3 + (q % 2)]
            for c in range(4):
                self.mm(psx[:, c * 64:(c + 1) * 64], EXP["AT"][c][:, q, :], Hb[:, c, :], start=True, stop=False, signal=False)
                self.mm(psx[:, c * 64:(c + 1) * 64], LakT[q][:, c, :], Vst[q][:, c, :], start=False, stop=True, signal=(c == 3))
            self.cp('act', xb_.v, psx[:, 0:256].re("p (c m) -> p c m", c=4))
            yield
            for c in range(4):
                self.mm(psu[:, c * 64:(c + 1) * 64], Pb[q][:, c, :], xb_[:, c, :], start=True, stop=True, signal=(c == 3))
            self.cp('dve', ub_.v, psu[:, 0:256].re("p (c m) -> p c m", c=4))
            yield
            for c in range(4):
                o = psh[:, c * 64:(c + 1) * 64]
                self.mm(o, BKG[q][:, 0, c, :], ub_[:, c, :], start=True, stop=False, signal=False)
                self.mm(o, BKG[q][:, 1, c, :], Vst[q][:, c, :], start=False, stop=True, signal=(c == 3))
            for c in range(4):
                o = psy[:, c * 64:(c + 1) * 64]
                self.mm(o, EXP["RT"][c][:, q, :], Hb[:, c, :], start=True, stop=False, signal=False)
                self.mm(o, MrbT[q][:, c, :], ub_[:, c, :], start=False, stop=False, signal=False)
                self.mm(o, MrkT[q][:, c, :], Vst[q][:, c, :], start=False, stop=True, signal=False)
            for c in range(4):
                self.mm(psy[:, 256 + c:257 + c], EXP["RK"][c][:, q, :], rkb[:, c:c + 1], start=True, stop=True, signal=(c == 3))
            yield
            self.tt('dve', H.v, H.v, GC[:, :, q:q + 1].bc([128, 4, 64]), ALU.mult)
            self.tt('dve', H.v, H.v, psh[:, 0:256].re("p (c m) -> p c m", c=4), ALU.add)
            self.cp('act', Hb.v, H.v)

        def post_a(it, q):
            w = it * NQ + q
            psy = PS[3 + (q % 2)]
            psg = PS[5]
            self.mm(psg.v, SGd[:, q, :], gup.v, start=True, stop=True)
            Y = Yv[w % 2]; Y2 = Ysq[w % 2]; YN = Yn[w % 2]; BN = Bn[w % 2]; YF = Yf[w % 2]
            s1 = st1[w % 2]; s2 = st2[w % 2]; mn = mean[w % 2]; vr = var[w % 2]
            py3 = psy[:, 0:256].re("p (c m) -> p c m", c=4)
            self.cp('act', Y.v, py3)
            self.act(Y2.v, py3, AF.Square)
            self.tt('dve', BN.v, Vst[q].v, psy[:, 256:260].un(2).bc([128, 4, 64]), ALU.mult)
            self.S.op('dve', lambda e: e.reduce_sum(out=s1.ap, in_=Y.ap, axis=AX.X), [Y.v], [s1.v])
            self.S.op('dve', lambda e: e.reduce_sum(out=s2.ap, in_=Y2.ap, axis=AX.X), [Y2.v], [s2.v])
            self.ts('dve', mn.v, s1.v, 1.0 / 64.0, None, ALU.mult)
            self.tt('dve', vr.v, mn.v, mn.v, ALU.mult)
            self.stt('dve', vr.v, s2.v, 1.0 / 64.0, vr.v, ALU.mult, ALU.subtract)
            self.ts('dve', vr.v, vr.v, LNX_EPS, None, ALU.add)
            self.act(vr.v, vr.v, AF.Ln)
            self.act(vr.v, vr.v, AF.Exp, scale=-0.5)
            self.tt('dve', YN.v, Y.v, mn.v.un(2).bc([128, 4, 64]), ALU.subtract)
            self.tt('dve', YN.v, YN.v, vr.v.un(2).bc([128, 4, 64]), ALU.mult)
            self.tt('pool', YN.v, YN.v, lnxg.v, ALU.mult)
            self.tt('pool', YN.v, YN.v, lnxb.v, ALU.add)
            self.tt('pool', YN.v, YN.v, BN.v, ALU.add)
            for hl in range(2):
                P_ = slice(hl * 64, (hl + 1) * 64)
                gv = psg[P_, :].re("p (c h m) -> p c h m", c=4, h=2)[:, :, hl, :]
                self.tt('dve', YF[P_, :, :], YN[P_, :, :], gv, ALU.mult)

        def post_b(it, q, YTt):
            w = it * NQ + q
            YF = Yf[w % 2]
            pst = self.next_ps()
            pbt = pst.v.bitcast(BF16)
            for c in range(4):
                self.tr(pbt[0:64, c * 128:(c + 1) * 128], YF[:, c, :], signal=(c == 3))
            self.evac(YTt[:, :, :, q * CH:(q + 1) * CH], pbt[0:64, 0:512].re("p (c h t) -> p c h t", c=4, h=2))

        for p in s12_pieces(0):
            p()
        for it in range(ntile):
            tok0 = it * TT
            if S.need_reset():
                S.hard_barrier()
            stage3(it)
            stage45(it)
            pieces = s12_pieces(it + 1) if it + 1 < ntile else []
            YTt = YT[it % 2]
            self.ps_allowed = [6, 7]

            def filler():
                if pieces:
                    pieces.pop(0)()

            for q in range(NQ + 2):
                if q < NQ:
                    for _ in crit(it, q):
                        filler()
                if q == NQ:
                    while pieces:
                        pieces.pop(0)()
                if 1 <= q <= NQ:
                    post_a(it, q - 1)
                if q >= 2:
                    post_b(it, q - 2, YTt)
            self.ps_allowed = None
            self.dma('sp', yas_v[:, :, :, tok0:tok0 + TT], YTt.v)
        self._sbuf_left = self.nc.sbuf_bytes_remaining

    def phaseM(self, es, src, dst, final=False):
        S = self.S
        inp = self.inp
        TT = 512
        NS = 4
        self._stg_i = 0
        self._ev_i = 0
        use_a = 'R' in self.phases
        cols, ci = self.load_cols(es, "m_cols", [("g", inp["norm_mix_g"]), ("bin", inp["b_in"][2816:4864])])
        hb = self.sb(es, "m_hb", [128, 16], F32)
        self.ts('dve', hb.v, cols[:, ci["bin"]:ci["bin"] + 16], 0.5, None, ALU.mult)
        win = self.sb(es, "m_win", [128, 8, 2048], BF16)
        wa = self.sb(es, "m_wa", [128, 4, D], BF16)
        wb = self.sb(es, "m_wb", [128, 4, D], BF16)
        wmo = self.sb(es, "m_wmo", [128, 8, D], BF16)
        with ExitStack() as tes:
            stg = [self.sb(tes, "m_stg%d" % i, [128, 2048], F32) for i in range(3)]
            for k in range(8):
                self.load_weight(stg, win[:, k, :], inp["w_in"][k * 128:(k + 1) * 128, 2816:4864], 2048,
                                 scale=cols[:, ci["g"] + k: ci["g"] + k + 1])
                self.load_weight(stg, wmo[:, k, :], inp["w_mix_out"][k * 128:(k + 1) * 128, :], D)
            for c in range(4):
                self.load_weight(stg, wa[:, c, :], inp["w_branch_a"][c * 128:(c + 1) * 128, :], D, scale=0.5)
                self.load_weight(stg, wb[:, c, :], inp["w_branch_b"][c * 128:(c + 1) * 128, :], D, scale=0.25)
            S.barrier()
        xbufs, sst, rst, junk = self.norm_bufs(es, "m", NS)
        xrb = [self.sb(es, "m_xr%d" % i, [128, D], F32) for i in range(NS)]
        hs = [self.sb(es, "m_h%d" % i, [128, NS, D], BF16) for i in range(2)]
        hTs = [self.sb(es, "m_hT%d" % i, [128, 8, TT], BF16) for i in range(2)]
        TG = [self.sb(es, "m_tg%d" % m, [128, TT], BF16) for m in range(16)]
        YA = [self.sb(es, "m_ya%d" % i, [128, 4, TT], BF16) for i in range(2)]
        YB = [self.sb(es, "m_yb%d" % i, [128, 4, TT], BF16) for i in range(2)]
        MA = [self.sb(es, "m_ma%d" % i, [128, TT], F32) for i in range(3)]
        MG = [self.sb(es, "m_mg%d" % m, [128, TT], BF16) for m in range(8)]
        self._mb = [self.sb(es, "m_mb%d" % i, [128, TT], F32) for i in range(2)]
        ntile = self.ntok // TT

        def load_y(i):
            if i >= ntile:
                return
            tok0 = i * TT
            if use_a:
                self.dma('sp', YA[i % 2].v, self.yas[:, tok0:tok0 + TT].rearrange("(c p) t -> p c t", p=128))
            self.dma('sp', YB[i % 2].v, self.ybs[:, tok0:tok0 + TT].rearrange("(c p) t -> p c t", p=128))

        def front(i):
            self.norm_from_dram(src, i * TT, NS, xbufs, sst, rst, junk, hs[i % 2])
            self.transpose_h(hs[i % 2], NS, hTs[i % 2])

        def mid(i):
            hT = hTs[i % 2]
            ya, yb = YA[i % 2], YB[i % 2]
            for m in range(16):
                if m == 6 and i + 1 < ntile:
                    self.norm_from_dram(src, (i + 1) * TT, NS, xbufs, sst, rst, junk, hs[(i + 1) % 2])
                ps = self.next_ps()
                for k in range(8):
                    self.mm(ps.v, win[:, k, m * 128:(m + 1) * 128], hT[:, k, :], start=(k == 0), stop=(k == 7))
                self.act(TG[m].v, ps.v, AF.Tanh, scale=0.5, bias=hb[:, m:m + 1])
            for m in range(8):
                ma = MA[m % 3]
                if use_a:
                    ps = self.next_ps()
                    for c in range(4):
                        self.mm(ps.v, wa[:, c, m * 128:(m + 1) * 128], ya[:, c, :], start=(c == 0), stop=(c == 3))
                    self.stt('dve', ma.v, TG[m].v, 1.0, ps.v, ALU.add, ALU.mult)
                ps2 = self.next_ps()
                for c in range(4):
                    self.mm(ps2.v, wb[:, c, m * 128:(m + 1) * 128], yb[:, c, :], start=(c == 0), stop=(c == 3))
                if use_a:
                    mb = self._mb[m % 2]
                    self.stt('dve', mb.v, TG[8 + m].v, 1.0, ps2.v, ALU.add, ALU.mult)
                    self.tt('pool', MG[m].v, mb.v, ma.v, ALU.add)
                else:
                    self.stt('dve', MG[m].v, TG[8 + m].v, 1.0, ps2.v, ALU.add, ALU.mult)
            if i + 1 < ntile:
                self.transpose_h(hs[(i + 1) % 2], NS, hTs[(i + 1) % 2])

        def back(i):
            tok0 = i * TT
            for s in range(NS):
                self.dma('sp', xrb[s].v, src[tok0 + s * 128: tok0 + (s + 1) * 128, :])
            for s in range(NS):
                pss = [self.next_ps(), self.next_ps()]
                for half in range(2):
                    for m in range(8):
                        self.mm(pss[half].v, MG[m][:, s * 128:(s + 1) * 128], wmo[:, m, half * 512:(half + 1) * 512],
                                start=(m == 0), stop=(m == 7))
                xb = xrb[s]
                for half in range(2):
                    self.tt('dve', xb[:, half * 512:(half + 1) * 512], xb[:, half * 512:(half + 1) * 512], pss[half].v, ALU.add)
                self.dma('sp', dst[tok0 + s * 128: tok0 + (s + 1) * 128, :], xb.v)

        load_y(0)
        front(0)
        for i in range(ntile):
            if S.need_reset():
                S.hard_barrier()
            load_y(i + 1)
            mid(i)
            back(i)

    def evac(self, out, in_, i=None):
        if i is None:
            i = self._ev_i
            self._ev_i += 1
        return self.cp(('act', 'dve')[i % 2], out, in_)

    def phaseB(self, es, src, dst, final=False):
        S = self.S
        inp = self.inp
        TT = 512
        NS = 4
        self._stg_i = 0
        self._ev_i = 0
        cols, ci = self.load_cols(es, "b_cols", [("gx", inp["norm_x_g"]), ("gm", inp["norm_mem_g"])])
        gq = self.sb(es, "b_gq", [128, 8], F32)
        self.ts('dve', gq.v, cols[:, ci["gx"]:ci["gx"] + 8], 1.0 / 16.0, None, ALU.mult)
        wq = self.sb(es, "b_wq", [128, 8, D], BF16)
        wo = self.sb(es, "b_wo", [128, 8, D], BF16)
        kT = [self.sb(es, "b_kT%d" % b, [128, 8, NMEM], BF16) for b in range(self.nb)]
        Vt = [self.sb(es, "b_V%d" % b, [128, 2, D], BF16) for b in range(self.nb)]
        xts = [self.sb(es, "b_xt%d" % i, [128, NS, D], F32) for i in range(2)]
        h = self.sb(es, "b_h", [128, NS, D], BF16)
        hT = self.sb(es, "b_hT", [128, 8, TT], BF16)
        junk = self.sb(es, "b_junk", [128, D], BF16)
        ss = self.sb(es, "b_ss", [128, 4], F32)
        rstd = self.sb(es, "b_rstd", [128, 4], F32)
        with ExitStack() as tes:
            stg = [self.sb(tes, "b_stg%d" % i, [128, 2048], F32) for i in range(3)]
            wkv = self.sb(tes, "b_wkv", [128, 8, 2 * D], BF16)
            for k in range(8):
                self.load_weight(stg, wkv[:, k, :], inp["w_ckv"][k * 128:(k + 1) * 128, :], 2 * D,
                                 scale=cols[:, ci["gm"] + k: ci["gm"] + k + 1])
            for k in range(8):
                self.load_weight(stg, wq[:, k, :], inp["w_cq"][k * 128:(k + 1) * 128, :], D, scale=gq[:, k:k + 1])
                self.load_weight(stg, wo[:, k, :], inp["w_co"][k * 128:(k + 1) * 128, :], D)
            for b in range(self.nb):
                mt = xts[b % 2]
                self.dma('sp', mt[:, 0:2, :], inp["mem"][b * NMEM:(b + 1) * NMEM, :].rearrange("(s p) d -> p s d", p=128))
                self.rms_h(mt, 2, ss, rstd, junk, h)
                self.transpose_h(h, 2, hT[:, :, 0:NMEM])
                for m in range(8):
                    if m % 2 == 0:
                        ps = self.next_ps()
                    o = (m % 2) * NMEM
                    for k in range(8):
                        self.mm(ps[:, o:o + NMEM], wkv[:, k, m * 128:(m + 1) * 128], hT[:, k, 0:NMEM], start=(k == 0), stop=(k == 7))
                    if m % 2 == 1:
                        self.evac(kT[b][:, m - 1:m + 1, :], ps.v.re("p (a t) -> p a t", a=2))
                for mc in range(2):
                    for half in range(2):
                        ps = self.next_ps()
                        for k in range(8):
                            self.mm(ps.v, hT[:, k, mc * 128:(mc + 1) * 128], wkv[:, k, D + half * 512: D + (half + 1) * 512],
                                    start=(k == 0), stop=(k == 7))
                        self.evac(Vt[b][:, mc, half * 512:(half + 1) * 512], ps.v)
            S.barrier()
        qT = self.sb(es, "b_qT", [128, 8, TT], BF16)
        oT = self.sb(es, "b_oT", [128, 8, TT], BF16)
        prT = self.sb(es, "b_prT", [128, 2, 4, TT], BF16)
        prs = [self.sb(es, "b_pr%d" % i, [128, 4, NMEM], BF16) for i in range(2)]
        prn = [self.sb(es, "b_prn%d" % i, [128, 4, NMEM], BF16) for i in range(2)]
        mx = [self.sb(es, "b_mx%d" % i, [128, 4], F32) for i in range(2)]
        sm = [self.sb(es, "b_sm%d" % i, [128, 4], F32) for i in range(2)]
        hs = [h, self.sb(es, "b_h1", [128, NS, D], BF16)]
        hTs = [hT, self.sb(es, "b_hT1", [128, 8, TT], BF16)]
        sss = [ss, self.sb(es, "b_ss1", [128, 4], F32)]
        rstds = [rstd, self.sb(es, "b_rstd1", [128, 4], F32)]
        ntile = self.ntok // TT
        tiles_per_b = self.seq // TT

        def load(i):
            if i < ntile:
                self.dma('sp', xts[i % 2].v, src[i * TT:(i + 1) * TT, :].rearrange("(s p) d -> p s d", p=128))

        def front(i):
            self.rms_h(xts[i % 2], NS, sss[i % 2], rstds[i % 2], junk, hs[i % 2])
            self.transpose_h(hs[i % 2], NS, hTs[i % 2])

        def mid(i):
            b = i // tiles_per_b
            hT_ = hTs[i % 2]
            for m in range(8):
                ps = self.next_ps()
                for k in range(8):
                    self.mm(ps.v, wq[:, k, m * 128:(m + 1) * 128], hT_[:, k, :], start=(k == 0), stop=(k == 7))
                self.evac(qT[:, m, :], ps.v)
            def scores(s):
                banks = [self.next_ps(), self.next_ps()]
                for hh in range(4):
                    ps = banks[hh // 2]
                    o = (hh % 2) * NMEM
                    for c in range(2):
                        self.mm(ps[:, o:o + NMEM], qT[:, 2 * hh + c, s * 128:(s + 1) * 128], kT[b][:, 2 * hh + c, :],
                                start=(c == 0), stop=(c == 1))
                return banks

            nxt = scores(0)
            for s in range(NS):
                pr = prs[s % 2]; pn = prn[s % 2]; mxs = mx[s % 2]; sms = sm[s % 2]
                banks = nxt
                if s + 1 < NS:
                    nxt = scores(s + 1)
                for g in range(2):
                    self.S.op('dve', lambda e: e.reduce_max(out=mxs.ap[:, 2 * g:2 * g + 2],
                                                            in_=banks[g].ap.rearrange("p (a t) -> p a t", a=2), axis=AX.X),
                              [banks[g].v], [mxs.v])
                self.ts('dve', mxs.v, mxs.v, -1.0, None, ALU.mult)
                for hh in range(4):
                    ps = banks[hh // 2]
                    o = (hh % 2) * NMEM
                    self.act(pr[:, hh, :], ps[:, o:o + NMEM], AF.Exp, bias=mxs[:, hh:hh + 1], accum=sms[:, hh:hh + 1])
                self.S.op('dve', lambda e: e.reciprocal(out=sms.ap, in_=sms.ap), [sms.v], [sms.v])
                self.tt('dve', pn.v, pr.v, sms.v.un(2).bc([128, 4, NMEM]), ALU.mult)
                ps = self.next_ps()
                pb = ps.v.bitcast(BF16)
                for hh in range(4):
                    for mc in range(2):
                        self.tr(pb[:, (hh * 2 + mc) * 128:(hh * 2 + mc + 1) * 128], pn[:, hh, mc * 128:(mc + 1) * 128],
                                signal=(hh == 3 and mc == 1))
                self.evac(prT[:, :, :, s * 128:(s + 1) * 128], pb.re("p (h m t) -> p m h t", h=4, m=2))
                if s == 1 and i + 1 < ntile:
                    self.rms_h(xts[(i + 1) % 2], NS, sss[(i + 1) % 2], rstds[(i + 1) % 2], junk, hs[(i + 1) % 2])
            for hh in range(4):
                for c in range(2):
                    ps = self.next_ps()
                    for mc in range(2):
                        self.mm(ps.v, Vt[b][:, mc, hh * 256 + c * 128: hh * 256 + (c + 1) * 128], prT[:, mc, hh, :],
                                start=(mc == 0), stop=(mc == 1))
                    self.evac(oT[:, 2 * hh + c, :], ps.v)
            if i + 1 < ntile:
                self.transpose_h(hs[(i + 1) % 2], NS, hTs[(i + 1) % 2])

        def back(i):
            xt = xts[i % 2]
            for s in range(NS):
                pss = [self.next_ps(), self.next_ps()]
                for half in range(2):
                    for k in range(8):
                        self.mm(pss[half].v, oT[:, k, s * 128:(s + 1) * 128], wo[:, k, half * 512:(half + 1) * 512],
                                start=(k == 0), stop=(k == 7))
                for half in range(2):
                    self.tt('dve', xt[:, s, half * 512:(half + 1) * 512], xt[:, s, half * 512:(half + 1) * 512], pss[half].v, ALU.add)
                self.dma('sp', dst[i * TT + s * 128: i * TT + (s + 1) * 128, :], xt[:, s, :])

        load(0)
        load(1)
        front(0)
        for i in range(ntile):
            if S.need_reset():
                S.hard_barrier()
            mid(i)
            back(i)
            load(i + 2)

    def phaseC(self, es, src, dst, final=True):
        nc, S = self.nc, self.S
        TT = 256
        NS = TT // 128
        NJ = DFF // 128
        inp = self.inp
        self._stg_i = 0
        cols, ci = self.load_cols(es, "c_cols", [("g", inp["norm_ffn_g"]), ("cw0", inp["ffn_conv_w"][0]),
                                                 ("cw1", inp["ffn_conv_w"][1]), ("cw2", inp["ffn_conv_w"][2]),
                                                 ("cb", inp["ffn_conv_b"])])
        wi = self.sb(es, "c_wi", [128, 8, 2 * DFF], BF16)
        wo = self.sb(es, "c_wo", [128, NJ, D], BF16)
        gfin = self.sb(es, "c_gfin", [128, D], F32)
        self.dma('sp', gfin.v, inp["norm_final_g"].partition_broadcast(128))
        with ExitStack() as tes:
            stg = [self.sb(tes, "c_stg%d" % i, [128, 2048], F32) for i in range(3)]
            for k in range(8):
                self.load_weight(stg, wi[:, k, :], inp["w_ffn_in"][k * 128:(k + 1) * 128, :], 2 * DFF,
                                 scale=cols[:, ci["g"] + k: ci["g"] + k + 1])
            for j in range(NJ):
                self.load_weight(stg, wo[:, j, :], inp["w_ffn_out"][j * 128:(j + 1) * 128, :], D)
            S.barrier()
        xts = [self.sb(es, "c_xt%d" % i, [128, NS, D], F32) for i in range(2)]
        hs = [self.sb(es, "c_h%d" % i, [128, NS, D], BF16) for i in range(2)]
        hTs = [self.sb(es, "c_hT%d" % i, [128, 8, TT], BF16) for i in range(2)]
        actT = [self.sb(es, "c_actT%d" % j, [128, TT], BF16) for j in range(NJ)]
        junk = self.sb(es, "c_junk", [128, D], BF16)
        sss = [self.sb(es, "c_ss%d" % i, [128, 4], F32) for i in range(3)]
        rstds = [self.sb(es, "c_rstd%d" % i, [128, 4], F32) for i in range(3)]
        halos = [self.sb(es, "c_halo%d" % j, [128, 2], F32) for j in range(NJ)]
        NW = 4
        uraw = [es.enter_context(self.nc.sbuf_tensor("c_uw%d" % i, [128, 2 + TT], F32)) for i in range(NW)]
        uh = [T(r[:, 0:2], "c_uh") for r in uraw]
        ub = [T(r[:, 2:2 + TT], "c_ub") for r in uraw]

        def uview(jj, a, b_):
            return V([uh[jj], ub[jj]], uraw[jj][:, a:b_])

        acc = [self.sb(es, "c_acc%d" % i, [128, TT], F32) for i in range(NW)]
        th = [self.sb(es, "c_th%d" % i, [128, TT], F32) for i in range(NW)]
        ntile = self.ntok // TT
        tiles_per_b = self.seq // TT

        def load(i):
            if i < ntile:
                self.dma('sp', xts[i % 2].v, src[i * TT:(i + 1) * TT, :].rearrange("(s p) d -> p s d", p=128))

        def front(i):
            self.rms_h(xts[i % 2], NS, sss[i % 2], rstds[i % 2], junk, hs[i % 2])
            self.transpose_h(hs[i % 2], NS, hTs[i % 2])

        def mid(i):
            hT = hTs[i % 2]
            st = {}
            for j in range(NJ + 2):
                if i + 1 < ntile:
                    if j == 8:
                        self.rms_h(xts[(i + 1) % 2], NS, sss[(i + 1) % 2], rstds[(i + 1) % 2], junk, hs[(i + 1) % 2])
                    if j == NJ:
                        self.transpose_h(hs[(i + 1) % 2], NS, hTs[(i + 1) % 2])
                if j < NJ:
                    ps = self.next_ps()
                    for half in range(2):
                        c0 = half * DFF + j * 128
                        for k in range(8):
                            self.mm(ps[:, half * TT:(half + 1) * TT], wi[:, k, c0:c0 + 128], hT[:, k, :], start=(k == 0), stop=(k == 7))
                    jj = j % NW
                    a = acc[jj]
                    w0 = cols[:, ci["cw0"] + j: ci["cw0"] + j + 1]
                    w1 = cols[:, ci["cw1"] + j: ci["cw1"] + j + 1]
                    w2 = cols[:, ci["cw2"] + j: ci["cw2"] + j + 1]
                    cb = cols[:, ci["cb"] + j: ci["cb"] + j + 1]
                    self.cp('dve', uh[jj].v, halos[j].v)
                    self.cp('act', ub[jj].v, ps[:, 0:TT])
                    self.act(a.v, ps[:, 0:TT], AF.Identity, scale=w2, bias=cb)
                    self.stt('dve', a.v, uview(jj, 1, 1 + TT), w1, a.v, ALU.mult, ALU.add)
                    self.stt('dve', a.v, uview(jj, 0, TT), w0, a.v, ALU.mult, ALU.add)
                    self.cp('dve', halos[j].v, ub[jj][:, TT - 2:TT])
                    st[j] = ps
                if 0 <= j - 1 < NJ:
                    jj = (j - 1) % NW
                    self.act(th[jj].v, acc[jj].v, AF.Gelu_apprx_tanh)
                if 0 <= j - 2 < NJ:
                    jj = (j - 2) % NW
                    self.tt('dve', actT[j - 2].v, th[jj].v, st[j - 2][:, TT:2 * TT], ALU.mult)

        def back(i):
            xt = xts[i % 2]
            ss, rstd = sss[2], rstds[2]
            for s in range(NS):
                pss = [self.next_ps(), self.next_ps()]
                for half in range(2):
                    for j in range(NJ):
                        self.mm(pss[half].v, actT[j][:, s * 128:(s + 1) * 128], wo[:, j, half * 512:(half + 1) * 512],
                                start=(j == 0), stop=(j == NJ - 1))
                for half in range(2):
                    self.tt('dve', xt[:, s, half * 512:(half + 1) * 512], xt[:, s, half * 512:(half + 1) * 512], pss[half].v, ALU.add)
                if final:
                    self.act(junk.v, xt[:, s, :], AF.Square, accum=ss[:, s:s + 1])
                    self.ts('dve', rstd[:, s:s + 1], ss[:, s:s + 1], 1.0 / D, NORM_EPS, ALU.mult, ALU.add)
                    self.rsqrt(rstd[:, s:s + 1], rstd[:, s:s + 1])
                    self.stt('dve', xt[:, s, :], xt[:, s, :], rstd[:, s:s + 1], gfin.v, ALU.mult, ALU.mult)
                self.dma('sp', dst[i * TT + s * 128: i * TT + (s + 1) * 128, :], xt[:, s, :])

        load(0)
        load(1)
        front(0)
        for i in range(ntile):
            if S.need_reset():
                S.hard_barrier()
            if i % tiles_per_b == 0:
                for j in range(NJ):
                    self.memset('pool', halos[j].v, 0.0)
            mid(i)
            back(i)
            load(i + 2)


_PROG_CACHE = {}


def get_prog(nb=4, seq=2048, phases="LRMBC"):
    key = (nb, seq, phases)
    if key not in _PROG_CACHE:
        _PROG_CACHE[key] = Prog(nb, seq, phases)
    return _PROG_CACHE[key]


_W_NAMES = ["norm_mix_g", "w_in", "b_in", "mu_shift", "w0", "w_lora_up", "a0", "a_lora_up", "g_lora_up", "k_k", "k_a",
            "r_k", "lnx_g", "lnx_b", "w_branch_a", "conv_b_w", "conv_b_b", "w_rg_a", "b_rg_a", "w_rg_x", "b_rg_x",
            "lru_lambda", "w_branch_b", "w_mix_out", "norm_x_g", "norm_mem_g", "w_cq", "w_ckv", "w_co", "norm_ffn_g",
            "w_ffn_in", "ffn_conv_w", "ffn_conv_b", "w_ffn_out", "norm_final_g"]


def run(inputs, ncores=8, nb=4, seq=2048, phases="LRMBC"):
    prog = get_prog(nb, seq, phases)
    shapes = {k: tuple(v.shape) for k, v in prog.inp.items()}
    shared = {}
    for k in _W_NAMES:
        a = np.ascontiguousarray(np.asarray(inputs[k], dtype=np.float32))
        shared[k] = a.reshape(shapes[k])
    x = np.asarray(inputs["x"], dtype=np.float32)
    mem = np.asarray(inputs["mem"], dtype=np.float32)
    in_maps = []
    for c in range(ncores):
        m = dict(shared)
        m["x"] = np.ascontiguousarray(x[c * nb:(c + 1) * nb, :seq]).reshape(nb * seq, D)
        m["mem"] = np.ascontiguousarray(mem[c * nb:(c + 1) * nb]).reshape(nb * NMEM, D)
        in_maps.append(m)
    res = run_bass_kernel_spmd(prog.nc, in_maps, core_ids=list(range(ncores)))
    outs = [np.asarray(r["out"]).reshape(nb, seq, D) for r in res.results]
    return np.concatenate(outs, axis=0)


def kernel(**inputs):
    return run(inputs).astype(np.float32)
```

```python
import numpy as np
from contextlib import ExitStack
import concourse.bass as bass
import concourse.mybir as mybir
from concourse.bass_utils import run_bass_kernel_spmd

F32 = mybir.dt.float32
BF16 = mybir.dt.bfloat16
AF = mybir.ActivationFunctionType
ALU = mybir.AluOpType
AX = mybir.AxisListType

D = 1024
NMEM = 256
AW = 512
RWKV_COLS = 1792
P_IN = 4864
DFF = 2816
NORM_EPS = 1e-6
LNX_EPS = 64e-5
SAME_ENGINE_SYNC = True
import os as _os
RSTOP = int(_os.environ.get("RSTOP", "0"))
if _os.environ.get("NOSES"):
    SAME_ENGINE_SYNC = False
LVAR = int(_os.environ.get("LVAR", "2"))


_ALL_TILES = []


class T:
    def __init__(self, ap, name=""):
        self.ap = ap if isinstance(ap, bass.AP) else ap[:]
        self.w = None
        self.r = []
        self.name = name
        self.dsem = None
        _ALL_TILES.append(self)

    def __getitem__(self, k):
        return V([self], self.ap[k])

    @property
    def v(self):
        return V([self], self.ap)


class V:
    def __init__(self, ts, ap):
        self.ts = ts
        self.ap = ap

    def __getitem__(self, k):
        return V(self.ts, self.ap[k])

    def re(self, pat, **kw):
        return V(self.ts, self.ap.rearrange(pat, **kw))

    def bc(self, shape):
        return V(self.ts, self.ap.to_broadcast(shape))

    def un(self, axis):
        return V(self.ts, self.ap.unsqueeze(axis))

    def bitcast(self, dt):
        return V(self.ts, self.ap.bitcast(dt))


def _ap(x):
    return x.ap if isinstance(x, V) else x


class Sync:
    def __init__(self, nc, es, n_dma_sems=64):
        self.nc = nc
        self.es = es
        self.engs = {'pe': nc.tensor, 'act': nc.scalar, 'dve': nc.vector, 'pool': nc.gpsimd, 'sp': nc.sync}
        self.sem = {}
        self.cnt = {}
        self.seen = {}
        for e in self.engs:
            self.sem[e] = es.enter_context(nc.semaphore("c_" + e))
            self.cnt[e] = 0
            self.seen[e] = {}
        self.dpool = [{'sem': es.enter_context(nc.semaphore("d%d" % i)), 'cnt': 0} for i in range(n_dma_sems)]
        self.dfree = list(range(n_dma_sems))
        self.n_inst = 0
        self.bar1 = es.enter_context(nc.semaphore("bar1"))
        self.bar2 = es.enter_context(nc.semaphore("bar2"))
        self.bar_k = 0

    def _wait(self, e, tok):
        if tok is None:
            return
        sem, val, src = tok
        if src == e and (e == 'pe' or not SAME_ENGINE_SYNC):
            return
        key = sem.name
        if self.seen[e].get(key, 0) >= val:
            return
        self.seen[e][key] = val
        self.engs[e].wait_ge(sem, val)

    def deps(self, e, reads, writes):
        for v in reads:
            if not isinstance(v, V):
                continue
            for t in v.ts:
                self._wait(e, t.w)
        for v in writes:
            for t in v.ts:
                self._wait(e, t.w)
                for tok in t.r:
                    self._wait(e, tok)

    def done(self, tok, reads, writes):
        for v in reads:
            if not isinstance(v, V):
                continue
            for t in v.ts:
                t.r.append(tok)
                if len(t.r) > 24:
                    t.r = t.r[-24:] if False else t.r
        for v in writes:
            for t in v.ts:
                t.w = tok
                t.r = []

    def op(self, e, fn, reads, writes, signal=True):
        self.deps(e, reads, writes)
        inst = fn(self.engs[e])
        self.n_inst += 1
        if signal:
            self.cnt[e] += 1
            inst.then_inc(self.sem[e], 1)
            tok = (self.sem[e], self.cnt[e], e)
        else:
            tok = (self.sem[e], self.cnt[e] + 1, e)
        self.done(tok, reads, writes)
        return inst

    def _dsem(self, t):
        if t.dsem is None:
            if not self.dfree:
                raise RuntimeError("out of dma semaphores")
            t.dsem = self.dpool[self.dfree.pop(0)]
        return t.dsem

    def dma(self, q, out, in_, **kw):
        reads = [in_] if isinstance(in_, V) else []
        writes = [out] if isinstance(out, V) else []
        self.deps(q, reads, writes)
        sbv = out if isinstance(out, V) else in_
        ds = self._dsem(sbv.ts[0])
        inst = self.engs[q].dma_start(out=_ap(out), in_=_ap(in_), **kw)
        self.n_inst += 1
        ds['cnt'] += 16
        inst.then_inc(ds['sem'], 16)
        tok = (ds['sem'], ds['cnt'], 'dma')
        self.done(tok, reads, writes)
        return tok

    def barrier(self):
        toks = [(self.sem[f], self.cnt[f], f) for f in self.engs if self.cnt[f] > 0]
        toks += [(d['sem'], d['cnt'], 'dma') for d in self.dpool if d['cnt'] > 0]
        for e in self.engs:
            for tok in toks:
                if tok[2] == e:
                    continue
                self._wait(e, tok)

    def release_dma_sems(self):
        self.dfree = list(range(len(self.dpool)))
        for t in _ALL_TILES:
            t.dsem = None

    def need_reset(self, limit=2600):
        return max(self.cnt.values()) > limit or max(d['cnt'] for d in self.dpool) > limit

    def hard_barrier(self):
        self.barrier()
        self.bar_k += 1
        k = self.bar_k
        for e in self.engs:
            self.engs[e].sem_inc(self.bar1, 1)
        sp = self.engs['sp']
        sp.wait_ge(self.bar1, len(self.engs) * k)
        for e in self.engs:
            if self.cnt[e] > 0:
                sp.sem_clear(self.sem[e])
        for d in self.dpool:
            if d['cnt'] > 0:
                sp.sem_clear(d['sem'])
        sp.sem_inc(self.bar2, 1)
        for e in self.engs:
            if e != 'sp':
                self.engs[e].wait_ge(self.bar2, k)
        for e in self.engs:
            self.cnt[e] = 0
            self.seen[e] = {}
        for d in self.dpool:
            d['cnt'] = 0
        for t in _ALL_TILES:
            t.w = None
            t.r = []


class Prog:
    def __init__(self, nb=4, seq=2048, phases="LRMBC", dbg=False):
        self.nb, self.seq, self.phases, self.dbg = nb, seq, phases, dbg
        self.ntok = nb * seq
        nc = self.nc = bass.Bass("TRN2", target_bir_lowering=False)
        self.inp = {}
        del _ALL_TILES[:]

        def din(name, shape):
            self.inp[name] = nc.dram_tensor(name, list(shape), F32, kind="ExternalInput").ap()
            return self.inp[name]

        ntok = self.ntok
        din("x", [ntok, D])
        din("mem", [nb * NMEM, D])
        din("norm_mix_g", [D]); din("w_in", [D, P_IN]); din("b_in", [P_IN]); din("mu_shift", [RWKV_COLS])
        din("w0", [AW]); din("w_lora_up", [64, AW]); din("a0", [AW]); din("a_lora_up", [64, AW])
        din("g_lora_up", [128, AW]); din("k_k", [AW]); din("k_a", [AW]); din("r_k", [AW])
        din("lnx_g", [AW]); din("lnx_b", [AW]); din("w_branch_a", [AW, D])
        din("conv_b_w", [4, AW]); din("conv_b_b", [AW]); din("w_rg_a", [8, 64, 64]); din("b_rg_a", [AW])
        din("w_rg_x", [8, 64, 64]); din("b_rg_x", [AW]); din("lru_lambda", [AW]); din("w_branch_b", [AW, D])
        din("w_mix_out", [D, D]); din("norm_x_g", [D]); din("norm_mem_g", [D]); din("w_cq", [D, D])
        din("w_ckv", [D, 2 * D]); din("w_co", [D, D]); din("norm_ffn_g", [D]); din("w_ffn_in", [D, 2 * DFF])
        din("ffn_conv_w", [3, DFF]); din("ffn_conv_b", [DFF]); din("w_ffn_out", [DFF, D]); din("norm_final_g", [D])
        self.out = nc.dram_tensor("out", [ntok, D], F32, kind="ExternalOutput").ap()
        self.x1 = nc.dram_tensor("x1s", [ntok, D], F32).ap()
        self.x2 = nc.dram_tensor("x2s", [ntok, D], F32).ap()
        self.dbg_out = {}

        with ExitStack() as es:
            self.es = es
            self.S = Sync(nc, es)
            self.ps = [T(es.enter_context(nc.psum_tensor("ps%d" % i, [128, 512], F32)), "ps%d" % i) for i in range(8)]
            self.ps_i = 0
            self.ident = self.sb(es, "ident", [128, 128], BF16)
            self.identf = self.sb(es, "identf", [128, 128], F32)
            for idt in (self.ident, self.identf):
                self.S.op('pool', lambda e: e.memset(idt.ap[:], 0.0), [], [idt.v])
                self.S.op('pool', lambda e: e.affine_select(idt.ap[:], idt.ap[:], pattern=[[-1, 128]],
                                                           compare_op=ALU.not_equal, fill=1.0, base=0,
                                                           channel_multiplier=1), [idt.v], [idt.v])
            self.mhalf = self.sb(es, "mhalf", [128, 512], F32)
            self.memset('pool', self.mhalf.v, -0.5)
            self.phalf = self.sb(es, "phalf", [128, 512], F32)
            self.memset('pool', self.phalf.v, 0.5)
            self.yas = nc.dram_tensor("yas", [AW, ntok], BF16).ap()
            self.ybs = nc.dram_tensor("ybs", [AW, ntok], BF16).ap()
            src = {'L': self.inp["x"], 'R': self.inp["x"], 'M': self.inp["x"], 'B': self.x1, 'C': self.x2}
            dst = {'L': None, 'R': None, 'M': self.x1, 'B': self.x2, 'C': self.out}
            order = [p for p in "LRMBC" if p in phases]
            chain = [p for p in order if p in "MBC"]
            for i, p in enumerate(order):
                s_ap, d_ap = src[p], dst[p]
                if p in chain:
                    if chain.index(p) == 0:
                        s_ap = self.inp["x"]
                    if chain.index(p) == len(chain) - 1:
                        d_ap = self.out
                with ExitStack() as pes:
                    getattr(self, "phase" + p)(pes, s_ap, d_ap, final=(p == 'C'))
                    self.S.hard_barrier()
                self.S.release_dma_sems()
            self.S.barrier()

    def sb(self, es, name, shape, dt):
        return T(es.enter_context(self.nc.sbuf_tensor(name, list(shape), dt)), name)

    ps_allowed = None

    def next_ps(self):
        if self.ps_allowed is not None:
            self._psa_i = getattr(self, "_psa_i", 0) + 1
            return self.ps[self.ps_allowed[self._psa_i % len(self.ps_allowed)]]
        t = self.ps[self.ps_i]
        self.ps_i = (self.ps_i + 1) % 8
        return t

    def act(self, out, in_, func, bias=0.0, scale=1.0, accum=None):
        rd = [in_] + [a for a in (bias, scale) if isinstance(a, V)]
        wr = [out] + ([accum] if accum is not None else [])
        kw = {}
        if accum is not None:
            kw['accum_out'] = accum.ap
        return self.S.op('act', lambda e: e.activation(out=out.ap, in_=in_.ap, func=func, bias=_ap(bias),
                                                       scale=_ap(scale), **kw), rd, wr)

    def tt(self, eng, out, in0, in1, op):
        return self.S.op(eng, lambda e: e.tensor_tensor(out=out.ap, in0=in0.ap, in1=in1.ap, op=op), [in0, in1], [out])

    def ts(self, eng, out, in0, s1, s2, op0, op1=None):
        rd = [in0] + [a for a in (s1, s2) if isinstance(a, V)]
        if op1 is None:
            return self.S.op(eng, lambda e: e.tensor_scalar(out=out.ap, in0=in0.ap, scalar1=_ap(s1), scalar2=None,
                                                            op0=op0), rd, [out])
        return self.S.op(eng, lambda e: e.tensor_scalar(out=out.ap, in0=in0.ap, scalar1=_ap(s1), scalar2=_ap(s2),
                                                        op0=op0, op1=op1), rd, [out])

    def stt(self, eng, out, in0, scalar, in1, op0, op1):
        rd = [in0, in1] + ([scalar] if isinstance(scalar, V) else [])
        return self.S.op(eng, lambda e: e.scalar_tensor_tensor(out=out.ap, in0=in0.ap, scalar=_ap(scalar), in1=in1.ap,
                                                               op0=op0, op1=op1), rd, [out])

    def cp(self, eng, out, in_):
        if eng == 'act':
            return self.act(out, in_, AF.Copy)
        return self.S.op(eng, lambda e: e.tensor_copy(out=out.ap, in_=in_.ap), [in_], [out])

    def rsqrt(self, out, in_):
        shp = list(in_.ap.shape)
        if shp[-1] > 8:
            self.act(out, in_, AF.Ln)
            return self.act(out, out, AF.Exp, scale=-0.5)
        mh = self.mhalf[0:shp[0], 0:shp[-1]]
        if len(shp) == 3:
            mh = mh.un(1).bc(shp)
        return self.tt('pool', out, in_, mh, ALU.pow)

    def memset(self, eng, out, val):
        return self.S.op(eng, lambda e: e.memset(out.ap, val), [], [out])

    def mm(self, out, lhsT, rhs, start, stop, signal=None):
        if signal is None:
            signal = stop
        return self.S.op('pe', lambda e: e.matmul(out.ap, lhsT=lhsT.ap, rhs=rhs.ap, start=start, stop=stop),
                         [lhsT, rhs], [out], signal=signal)

    def tr(self, out, in_, signal=True, f32=False):
        idt = self.identf if f32 else self.ident
        n = in_.ap.shape[0]
        return self.S.op('pe', lambda e: e.transpose(out.ap, in_.ap, idt.ap[0:n, 0:n]), [in_, idt.v], [out],
                         signal=signal)

    def dma(self, q, out, in_, **kw):
        return self.S.dma(q, out, in_, **kw)

    def load_weight(self, stg, dst, src_rows, ncols, scale=None, piece=2048):
        rows = src_rows.shape[0]
        c0 = 0
        while c0 < ncols:
            n = min(piece, ncols - c0)
            st = stg[self._stg_i % len(stg)]
            eng = ('dve', 'act')[self._stg_i % 2]
            self._stg_i += 1
            self.dma('sp', st[0:rows, 0:n], src_rows[:, c0:c0 + n])
            o = dst[:, c0:c0 + n]
            i = st[0:rows, 0:n]
            if scale is None:
                self.cp(eng, o, i)
            elif eng == 'act':
                if isinstance(scale, V):
                    self.act(o, i, AF.Copy, scale=scale)
                else:
                    self.act(o, i, AF.Copy, scale=float(scale))
            else:
                self.ts(eng, o, i, scale, None, ALU.mult)
            c0 += n

    def load_cols(self, es, name, specs):
        cols = {}
        n = 0
        for k, v in specs:
            cols[k] = n
            n += (v.shape[0] + 127) // 128
        res = self.sb(es, name, [128, n], F32)
        with ExitStack() as tes:
            ngrp = (n + 127) // 128
            stage = [self.sb(tes, name + "_st%d" % g, [128, 128], F32) for g in range(ngrp)]
            for st in stage:
                self.memset('pool', st.v, 0.0)
            for k, v in specs:
                L = v.shape[0]
                m = (L + 127) // 128
                c = cols[k]
                r = 0
                while r < m:
                    g, rr = divmod(c + r, 128)
                    cnt = min(m - r, 128 - rr)
                    if L >= 128:
                        self.dma('sp', stage[g][rr:rr + cnt, :], v[r * 128:(r + cnt) * 128].rearrange("(m p) -> m p", p=128))
                    else:
                        self.dma('sp', stage[g][rr:rr + 1, 0:L], v.rearrange("(m p) -> m p", m=1))
                    r += cnt
            for g in range(ngrp):
                w = min(128, n - g * 128)
                ps = self.next_ps()
                self.tr(ps[:, 0:w], stage[g][0:w, :], f32=True)
                self.cp('dve', res[:, g * 128:g * 128 + w], ps[:, 0:w])
            self.S.barrier()
        return res, cols

    def rms_h(self, xt, nsub, ss, rstd, junk, h):
        for s in range(nsub):
            self.act(junk.v, xt[:, s, :], AF.Square, accum=ss[:, s:s + 1])
        self.ts('dve', rstd[:, 0:nsub], ss[:, 0:nsub], 1.0 / D, NORM_EPS, ALU.mult, ALU.add)
        self.rsqrt(rstd[:, 0:nsub], rstd[:, 0:nsub])
        for s in range(nsub):
            self.ts('dve', h[:, s, :], xt[:, s, :], rstd[:, s:s + 1], None, ALU.mult)

    def transpose_h(self, h, nsub, hT, evac=('act', 'dve')):
        tw = nsub * 128
        per_bank = 1024 // tw
        c = 0
        i = 0
        while c < 8:
            ps = self.next_ps()
            pb = ps.v.bitcast(BF16)
            nchunk = min(per_bank, 8 - c)
            for cc in range(nchunk):
                for s in range(nsub):
                    last = (cc == nchunk - 1 and s == nsub - 1)
                    self.tr(pb[:, cc * tw + s * 128: cc * tw + (s + 1) * 128], h[:, s, (c + cc) * 128:(c + cc + 1) * 128],
                            signal=last)
            self.cp(evac[i % len(evac)], hT[:, c:c + nchunk, :], pb[:, 0:nchunk * tw].re("p (c t) -> p c t", c=nchunk))
            c += nchunk
            i += 1

    def norm_from_dram(self, src, tok0, NS, xbufs, sst, rst, junk, h):
        for s in range(NS):
            xb = xbufs[self._xb_i % len(xbufs)]
            self._xb_i += 1
            self.dma('sp', xb.v, src[tok0 + s * 128: tok0 + (s + 1) * 128, :])
            self.act(junk.v, xb.v, AF.Square, accum=sst[s].v)
            self.ts('dve', rst[s].v, sst[s].v, 1.0 / D, NORM_EPS, ALU.mult, ALU.add)
            self.rsqrt(rst[s].v, rst[s].v)
            self.ts('dve', h[:, s, :], xb.v, rst[s].v, None, ALU.mult)

    def norm_bufs(self, es, pfx, NS):
        xbufs = [self.sb(es, pfx + "_xb%d" % i, [128, D], F32) for i in range(2)]
        sst = [self.sb(es, pfx + "_ss%d" % i, [128, 1], F32) for i in range(NS)]
        rst = [self.sb(es, pfx + "_rs%d" % i, [128, 1], F32) for i in range(NS)]
        junk = self.sb(es, pfx + "_junk", [128, D], BF16)
        self._xb_i = 0
        return xbufs, sst, rst, junk

    def phaseL(self, es, src, dst, final=False):
        S = self.S
        inp = self.inp
        TT = 512
        NS = 4
        self._stg_i = 0
        self._ev_i = 0
        cols, ci = self.load_cols(es, "l_cols", [
            ("g", inp["norm_mix_g"]), ("bin", inp["b_in"][1792:2816]),
            ("cw0", inp["conv_b_w"][0]), ("cw1", inp["conv_b_w"][1]), ("cw2", inp["conv_b_w"][2]), ("cw3", inp["conv_b_w"][3]),
            ("cbb", inp["conv_b_b"]), ("bra", inp["b_rg_a"]), ("brx", inp["b_rg_x"]), ("lam", inp["lru_lambda"])])
        hb = self.sb(es, "l_hb", [128, 8], F32)
        self.ts('dve', hb[:, 0:4], cols[:, ci["bra"]:ci["bra"] + 4], 0.5, None, ALU.mult)
        self.ts('dve', hb[:, 4:8], cols[:, ci["brx"]:ci["brx"] + 4], 0.5, None, ALU.mult)
        cA = self.sb(es, "l_cA", [128, 8], F32)
        lt = self.sb(es, "l_lt", [128, 4], F32)
        self.act(lt.v, cols[:, ci["lam"]:ci["lam"] + 4], AF.Exp, scale=-1.0)
        self.ts('dve', lt.v, lt.v, 1.0, None, ALU.add)
        self.act(lt.v, lt.v, AF.Ln)
        self.ts('dve', cA[:, 0:4], lt.v, -4.0, None, ALU.mult)
        self.ts('dve', cA[:, 4:8], lt.v, -8.0, None, ALU.mult)
        win = self.sb(es, "l_win", [128, 8, 1024], BF16)
        wg = self.sb(es, "l_wg", [128, 2, 4, 128], BF16)
        with ExitStack() as tes:
            stg = [self.sb(tes, "l_stg%d" % i, [128, 1024], F32) for i in range(3)]
            for k in range(8):
                self.load_weight(stg, win[:, k, :], inp["w_in"][k * 128:(k + 1) * 128, 1792:2816], 1024,
                                 scale=cols[:, ci["g"] + k: ci["g"] + k + 1], piece=1024)
            for gi, nm in enumerate(("w_rg_a", "w_rg_x")):
                st = stg[gi]
                self.memset('pool', st.v, 0.0)
                for blk in range(8):
                    c, hl = divmod(blk, 2)
                    self.dma('sp', st[hl * 64:(hl + 1) * 64, c * 128 + hl * 64: c * 128 + (hl + 1) * 64], inp[nm][blk])
                self.cp('dve', wg[:, gi, :, :], st[:, 0:512].re("p (c m) -> p c m", c=4))
            S.barrier()
        xbufs, sst, rst, junk = self.norm_bufs(es, "l", NS)
        h = self.sb(es, "l_h", [128, NS, D], BF16)
        hT = self.sb(es, "l_hT", [128, 8, TT], BF16)
        PX = [self.sb(es, "l_px%d" % c, [128, 3 + TT], F32) for c in range(4)]
        GY = [self.sb(es, "l_gy%d" % c, [128, TT], F32) for c in range(4)]
        YB = [self.sb(es, "l_yb%d" % c, [128, TT], BF16) for c in range(4)]
        carry = [self.sb(es, "l_cy%d" % c, [128, 1], F32) for c in range(4)]
        NW = 2
        def wk(nm, dt=F32, n=NW):
            return [self.sb(es, "l_%s%d" % (nm, i), [128, TT], dt) for i in range(n)]
        acc = wk("acc", n=4); xbb = wk("xbb", BF16, n=4); ta = wk("ta", n=4); tx = wk("tx", n=4)
        av = wk("av"); a2 = wk("a2"); u = wk("u"); hl_ = wk("hl")
        ntile = self.ntok // TT
        tiles_per_b = self.seq // TT
        hs = [h, self.sb(es, "l_h1", [128, NS, D], BF16)]
        hTs = [hT, self.sb(es, "l_hT1", [128, 8, TT], BF16)]

        def front(i):
            self.norm_from_dram(src, i * TT, NS, xbufs, sst, rst, junk, hs[i % 2])
            self.transpose_h(hs[i % 2], NS, hTs[i % 2])

        def midA(i):
            hT_ = hTs[i % 2]
            first = (i % tiles_per_b == 0)
            if first:
                for c in range(4):
                    self.memset('pool', PX[c][:, 0:3], 0.0)
                    self.memset('pool', carry[c].v, 0.0)
            for c in range(4):
                ps = self.next_ps()
                for k in range(8):
                    self.mm(ps.v, win[:, k, c * 128:(c + 1) * 128], hT_[:, k, :], start=(k == 0), stop=(k == 7))
                self.act(PX[c][:, 3:3 + TT], ps.v, AF.Identity, bias=cols[:, ci["bin"] + c: ci["bin"] + c + 1])
            for c in range(4):
                ps = self.next_ps()
                for k in range(8):
                    self.mm(ps.v, win[:, k, 512 + c * 128: 512 + (c + 1) * 128], hT_[:, k, :], start=(k == 0), stop=(k == 7))
                self.act(GY[c].v, ps.v, AF.Gelu_apprx_tanh, bias=cols[:, ci["bin"] + 4 + c: ci["bin"] + 5 + c])
            for c in range(4):
                A = acc[c]; XB = xbb[c]; TA = ta[c]; TX = tx[c]
                cw = [cols[:, ci["cw%d" % j] + c: ci["cw%d" % j] + c + 1] for j in range(4)]
                self.ts('dve', A.v, PX[c][:, 0:TT], cw[0], cols[:, ci["cbb"] + c: ci["cbb"] + c + 1], ALU.mult, ALU.add)
                for j in range(1, 4):
                    self.stt('dve', A.v, PX[c][:, j:j + TT], cw[j], A.v, ALU.mult, ALU.add)
                self.cp('pool', PX[c][:, 0:3], PX[c][:, TT:TT + 3])
                self.cp('dve', XB.v, A.v)
                psa = self.next_ps()
                self.mm(psa.v, wg[:, 0, c, :], XB.v, start=True, stop=True)
                psx = self.next_ps()
                self.mm(psx.v, wg[:, 1, c, :], XB.v, start=True, stop=True)
                self.act(TA.v, psa.v, AF.Tanh, scale=0.5, bias=hb[:, c:c + 1])
                self.act(TX.v, psx.v, AF.Tanh, scale=0.5, bias=hb[:, 4 + c:5 + c])

        def midB(i):
            tok0 = i * TT
            first = (i % tiles_per_b == 0)
            for c in range(4):
                A = acc[c]; TA = ta[c]; TX = tx[c]; AV = av[c % NW]; A2 = a2[c % NW]; U = u[c % NW]; HL = hl_[c % NW]
                self.act(AV.v, TA.v, AF.Exp, scale=cA[:, c:c + 1], bias=cA[:, c:c + 1])
                self.act(A2.v, TA.v, AF.Exp, scale=cA[:, 4 + c:5 + c], bias=cA[:, 4 + c:5 + c])
                self.ts('dve', A2.v, A2.v, -1.0, 1.0, ALU.mult, ALU.add)
                self.ts('dve', A2.v, A2.v, 1e-30, None, ALU.max)
                self.act(A2.v, A2.v, AF.Ln)
                self.act(A2.v, A2.v, AF.Exp, scale=0.5)
                if first:
                    self.memset('pool', A2[:, 0:1], 1.0)
                self.stt('dve', U.v, TX.v, 1.0, A.v, ALU.add, ALU.mult)
                self.tt('dve', U.v, U.v, A2.v, ALU.mult)
                self.S.op('dve', lambda e: e.tensor_tensor_scan(HL.ap, AV.ap, U.ap, carry[c].ap, ALU.mult, ALU.add),
                          [AV.v, U.v, carry[c].v], [HL.v])
                self.cp('pool', carry[c].v, HL[:, TT - 1:TT])
                self.tt('dve', YB[c].v, HL.v, GY[c].v, ALU.mult)
                self.dma('sp', self.ybs[c * 128:(c + 1) * 128, tok0:tok0 + TT], YB[c].v)

        front(0)
        for i in range(ntile):
            if S.need_reset():
                S.hard_barrier()
            midA(i)
            if i + 1 < ntile:
                front(i + 1)
            midB(i)

    def phaseR(self, es, src, dst, final=False):
        S = self.S
        inp = self.inp
        TT = 256
        NS = 2
        NQ = 4
        CH = 64
        self._stg_i = 0
        self._ev_i = 0
        cols, ci = self.load_cols(es, "r_cols", [
            ("g", inp["norm_mix_g"]), ("bin", inp["b_in"][0:RWKV_COLS]), ("mu", inp["mu_shift"]),
            ("w0", inp["w0"]), ("a0", inp["a0"]), ("kk", inp["k_k"]), ("ka", inp["k_a"]), ("rk", inp["r_k"])])

        def col(key, j):
            return cols[:, ci[key] + j: ci[key] + j + 1]

        der = self.sb(es, "r_der", [128, 32], F32)
        self.ts('dve', der[:, 0:14], cols[:, ci["mu"]:ci["mu"] + 14], -1.0, 1.0, ALU.mult, ALU.add)
        self.ts('dve', der[:, 14:18], cols[:, ci["w0"]:ci["w0"] + 4], 0.5, None, ALU.mult)
        self.ts('dve', der[:, 18:22], cols[:, ci["a0"]:ci["a0"] + 4], 0.5, None, ALU.mult)
        self.ts('dve', der[:, 22:26], cols[:, ci["ka"]:ci["ka"] + 4], 0.5, None, ALU.mult)
        self.ts('dve', der[:, 26:30], cols[:, ci["ka"]:ci["ka"] + 4], -0.5, 1.0, ALU.mult, ALU.add)
        rkb = self.sb(es, "r_rkb", [128, 4], BF16)
        self.cp('dve', rkb.v, cols[:, ci["rk"]:ci["rk"] + 4])
        m64 = [self.sb(es, "r_m64_%d" % i, [64, 64], BF16) for i in range(3)]
        masks = [self.sb(es, "r_mask%d" % i, [128, 128], BF16) for i in range(3)]
        specs = [([[1, 64]], -1, ALU.is_gt), ([[-1, 64]], 1, ALU.is_gt), ([[1, 64]], -1, ALU.is_ge)]
        for mi in range(3):
            pat, cm, op = specs[mi]
            self.memset('pool', m64[mi].v, 1.0)
            self.S.op('pool', lambda e: e.affine_select(m64[mi].ap, m64[mi].ap, pattern=pat, compare_op=op, fill=0.0,
                                                        base=0, channel_multiplier=cm), [m64[mi].v], [m64[mi].v])
            for a in range(2):
                for b_ in range(2):
                    self.dma('sp', masks[mi][a * 64:(a + 1) * 64, b_ * 64:(b_ + 1) * 64], m64[mi].v)
        mask_su, mask_sl, mask_u = masks
        bones = self.sb(es, "r_bones", [128, 128], BF16)
        self.memset('pool', bones.v, 0.0)
        self.memset('pool', bones[0:64, 0:64], 1.0)
        self.memset('pool', bones[64:128, 64:128], 1.0)
        m01 = self.sb(es, "r_m01", [128, TT], F32)
        self.memset('pool', m01.v, 1.0)
        self.memset('pool', m01.v.re("p (q s) -> p q s", s=CH)[:, :, 0:1], 0.0)
        lnxg = self.sb(es, "r_lnxg", [128, 4, 64], F32)
        lnxb = self.sb(es, "r_lnxb", [128, 4, 64], F32)
        for c in range(4):
            for hl in range(2):
                hh = 2 * c + hl
                self.dma('sp', lnxg[hl * 64:(hl + 1) * 64, c, :], inp["lnx_g"][hh * 64:(hh + 1) * 64].partition_broadcast(64))
                self.dma('sp', lnxb[hl * 64:(hl + 1) * 64, c, :], inp["lnx_b"][hh * 64:(hh + 1) * 64].partition_broadcast(64))
        win = self.sb(es, "r_win", [128, 8, RWKV_COLS], BF16)
        wlora = self.sb(es, "r_wlora", [128, 2, AW], BF16)
        gup = self.sb(es, "r_gup", [128, AW], BF16)
        with ExitStack() as tes:
            stg = [self.sb(tes, "r_stg%d" % i, [128, RWKV_COLS], F32) for i in range(3)]
            for k in range(8):
                self.load_weight(stg, win[:, k, :], inp["w_in"][k * 128:(k + 1) * 128, 0:RWKV_COLS], RWKV_COLS,
                                 scale=cols[:, ci["g"] + k: ci["g"] + k + 1], piece=RWKV_COLS)
            st = stg[0]
            self.memset('pool', st[:, 0:2 * AW], 0.0)
            self.dma('sp', st[0:64, 0:AW], inp["w_lora_up"])
            self.dma('sp', st[64:128, AW:2 * AW], inp["a_lora_up"])
            self.cp('dve', wlora.v, st[:, 0:2 * AW].re("p (a m) -> p a m", a=2))
            st = stg[1]
            self.dma('sp', st[:, 0:AW], inp["g_lora_up"])
            self.cp('dve', gup.v, st[:, 0:AW])
            S.barrier()
        xbufs, sst, rst, junk = self.norm_bufs(es, "r", NS)
        h = self.sb(es, "r_h", [128, NS, D], BF16)
        hT = self.sb(es, "r_hT", [128, 8, TT], BF16)
        pcarry = [self.sb(es, "r_pc%d" % m, [128, 1], F32) for m in range(14)]
        Sx = [self.sb(es, "r_s%d" % m, [128, TT], F32) for m in range(14)]
        ltmp = [self.sb(es, "r_lt%d" % i, [128, TT], F32) for i in range(2)]
        LB = self.sb(es, "r_lb", [128, TT], BF16)
        SGd = self.sb(es, "r_sgd", [128, NQ, 128], BF16)
        NW = 2

        def wk(nm, dt=F32):
            return [self.sb(es, "r_%s%d" % (nm, i), [128, TT], dt) for i in range(NW)]

        logw = [self.sb(es, "r_logw%d" % i, [128, TT], F32) for i in range(4)]
        ta = [self.sb(es, "r_ta%d" % i, [128, TT], F32) for i in range(4)]
        cum = [self.sb(es, "r_cum%d" % i, [128, TT], F32) for i in range(4)]
        egm1 = wk("egm1"); eg = wk("eg"); eig = wk("eig"); egc = wk("egc")
        kk = wk("kk"); kk2 = wk("kk2", BF16); rn = wk("rn"); kkn = wk("kkn"); kmod = wk("kmod"); bv = wk("bv")
        names = ["AT", "BT", "KT", "RT", "BGT", "KGT", "VB", "RK"]
        EXP = {n: [self.sb(es, "r_%s%d" % (n, c), [128, NQ, 128], BF16) for c in range(4)] for n in names}
        for n in names:
            for c in range(4):
                self.memset('pool', EXP[n][c].v, 0.0)
        GC = self.sb(es, "r_gc", [128, 4, NQ], F32)
        BKG = [self.sb(es, "r_bkg%d" % c, [128, 2, NQ, 128], BF16) for c in range(4)]
        VstAll = self.sb(es, "r_vst", [128, NQ, 4, 64], BF16)
        Nb = [[self.sb(es, "r_nb%d_%d" % (q, i), [128, 4, 128], BF16) for i in range(2)] for q in range(NQ)]
        Lb = [[self.sb(es, "r_lb%d_%d" % (q, i), [128, 4, 128], BF16) for i in range(2)] for q in range(NQ)]
        Pb = [self.sb(es, "r_pb%d" % q, [128, 4, 128], BF16) for q in range(NQ)]
        LakT = [self.sb(es, "r_lak%d" % q, [128, 4, 128], BF16) for q in range(NQ)]
        MrbT = [self.sb(es, "r_mrb%d" % q, [128, 4, 128], BF16) for q in range(NQ)]
        MrkT = [self.sb(es, "r_mrk%d" % q, [128, 4, 128], BF16) for q in range(NQ)]
        H = self.sb(es, "r_H", [128, 4, 64], F32)
        Hb = self.sb(es, "r_Hb", [128, 4, 64], BF16)
        Xb = [self.sb(es, "r_Xb%d" % i, [128, 4, 64], BF16) for i in range(2)]
        Ub = [self.sb(es, "r_Ub%d" % i, [128, 4, 64], BF16) for i in range(2)]

        def yt(nm, shape=(128, 4, 64), dt=F32):
            return [self.sb(es, "r_%s%d" % (nm, i), list(shape), dt) for i in range(2)]

        Yv = yt("Yv"); Ysq = yt("Ysq"); Yn = yt("Yn"); Bn = yt("Bn"); Yf = yt("Yf", dt=BF16)
        st1 = yt("st1", (128, 4)); st2 = yt("st2", (128, 4)); mean = yt("mean", (128, 4)); var = yt("var", (128, 4))
        YT = [self.sb(es, "r_YT%d" % i, [64, 4, 2, TT], BF16) for i in range(2)]
        ident = self.ident
        ntile = self.ntok // TT
        tiles_per_b = self.seq // TT
        yas_v = self.yas.rearrange("(c hl i) t -> i c hl t", hl=2, i=64)

        PWraw = [es.enter_context(self.nc.sbuf_tensor("r_pwx%d" % i, [128, 1 + TT], F32)) for i in range(3)]
        PWh = [T(r[:, 0:1], "r_pwh") for r in PWraw]
        PWb = [T(r[:, 1:1 + TT], "r_pwb") for r in PWraw]

        def s12_pieces(it):
            tok0 = it * TT

            def p0():
                if it % tiles_per_b == 0:
                    for m in range(14):
                        self.memset('pool', pcarry[m].v, 0.0)
                self.norm_from_dram(src, tok0, NS, xbufs, sst, rst, junk, h)
                self.transpose_h(h, NS, hT)

            def projA(m):
                def f():
                    ps = self.next_ps()
                    for k in range(8):
                        self.mm(ps[:, 0:TT], win[:, k, m * 128:(m + 1) * 128], hT[:, k, :], start=(k == 0), stop=(k == 7))
                    pi = m % 3
                    self.cp('dve', PWh[pi].v, pcarry[m].v)
                    self.act(PWb[pi].v, ps[:, 0:TT], AF.Identity, bias=col("bin", m))
                    tmp = ltmp[m % 2]
                    self.act(tmp.v, V([PWh[pi], PWb[pi]], PWraw[pi][:, 0:TT]), AF.Copy, scale=col("mu", m))
                return f

            def projB(m):
                def f():
                    pi = m % 3
                    tmp = ltmp[m % 2]
                    self.cp('dve', pcarry[m].v, PWb[pi][:, TT - 1:TT])
                    self.stt('dve', Sx[m].v, PWb[pi].v, der[:, m:m + 1], tmp.v, ALU.mult, ALU.add)
                return f

            def both(fa, fb):
                def f():
                    if fb is not None:
                        fb()
                    if fa is not None:
                        fa()
                return f

            chain_ = [both(projA(0), None)] + [both(projA(m), projB(m - 1)) for m in range(1, 14)] + [both(None, projB(13))]
            return [p0] + chain_

        def stage3(it):
            if it % tiles_per_b == 0:
                self.memset('pool', H.v, 0.0)
                self.memset('pool', Hb.v, 0.0)
            for _ in range(1):
                if RSTOP == 1:
                    return
                self.act(LB[0:64, :], Sx[12][0:64, :], AF.Tanh)
                self.cp('act', LB[64:128, :], Sx[12][64:128, :])
                tmp = ltmp[0]
                self.act(tmp.v, Sx[13].v, AF.Tanh, scale=0.5)
                for hl in range(2):
                    self.ts('dve', SGd[:, :, hl * 64:(hl + 1) * 64], tmp.v.re("p (q s) -> p q s", s=CH), 0.5, 0.5, ALU.mult, ALU.add)
                if RSTOP == 21:
                    return
                for c in range(4):
                    LW = logw[c]; TA = ta[c]; CU = cum[c]
                    ps = self.next_ps()
                    self.mm(ps[:, 0:TT], wlora[:, 0, c * 128:(c + 1) * 128], LB.v, start=True, stop=True)
                    self.mm(ps[:, TT:2 * TT], wlora[:, 1, c * 128:(c + 1) * 128], LB.v, start=True, stop=True)
                    self.act(LW.v, ps[:, 0:TT], AF.Tanh, scale=0.5, bias=der[:, 14 + c:15 + c])
                    self.act(TA.v, ps[:, TT:2 * TT], AF.Tanh, scale=0.5, bias=der[:, 18 + c:19 + c])
                    self.ts('dve', LW.v, LW.v, 1.0, -0.30326532985631671, ALU.add, ALU.mult)
                    self.S.op('dve', lambda e: e.tensor_tensor_scan(CU.ap, m01.ap, LW.ap, 0.0, ALU.mult, ALU.add),
                              [m01.v, LW.v], [CU.v])
                def stageB(c):
                    w = it * 4 + c
                    r_, k_, v_ = Sx[c], Sx[4 + c], Sx[8 + c]
                    LW = logw[c]; TA = ta[c]; CU = cum[c]
                    E1 = egm1[w % NW]; EG = eg[w % NW]; EI = eig[w % NW]
                    EC = egc[w % NW]; KK = kk[w % NW]; K2 = kk2[w % NW]; RN = rn[w % NW]; KN = kkn[w % NW]; KM = kmod[w % NW]
                    BV = bv[w % NW]
                    cu3 = CU.v.re("p (q s) -> p q s", s=CH)
                    cuC = cu3[:, :, CH - 1:CH]
                    self.act(KK.v, k_.v, AF.Copy, scale=col("kk", c))
                    self.tt('pool', E1.v, CU.v, LW.v, ALU.subtract)
                    yield
                    self.act(K2.v, KK.v, AF.Square)
                    self.tt('pool', EC.v.re("p (q s) -> p q s", s=CH), cuC.bc([128, NQ, CH]), cu3, ALU.subtract)
                    yield
                    psn = self.next_ps()
                    self.mm(psn[:, 0:TT], bones.v, K2.v, start=True, stop=True)
                    self.act(E1.v, E1.v, AF.Exp)
                    yield
                    self.act(RN.v, psn[:, 0:TT], AF.Ln)
                    yield
                    self.act(RN.v, RN.v, AF.Exp, scale=-0.5)
                    yield
                    self.tt('pool', KN.v, KK.v, RN.v, ALU.mult)
                    self.act(EI.v, CU.v, AF.Exp, scale=-1.0)
                    yield
                    self.act(EC.v, EC.v, AF.Exp)
                    self.act(KM.v, TA.v, AF.Identity, scale=der[:, 22 + c:23 + c], bias=der[:, 26 + c:27 + c])
                    yield
                    self.stt('dve', BV.v, TA.v, 1.0, KN.v, ALU.add, ALU.mult)
                    self.tt('dve', KM.v, KM.v, k_.v, ALU.mult)
                    self.act(EG.v, CU.v, AF.Exp)
                    self.act(GC[:, c, :], cuC.re("p q o -> p (q o)"), AF.Exp)
                    yield
                    for hl in range(2):
                        P_ = slice(hl * 64, (hl + 1) * 64)

                        def hv(t):
                            return t[P_, :].re("p (q s) -> p q s", s=CH)

                        def ov(n):
                            return EXP[n][c][P_, :, hl * 64:(hl + 1) * 64]

                        self.stt('dve', ov("AT"), hv(KN), -1.0, hv(E1), ALU.mult, ALU.mult)
                        self.tt('pool', ov("KT"), hv(KM), hv(EI), ALU.mult)
                        self.cp('act', ov("VB"), hv(v_))
                        yield
                        self.stt('dve', ov("BT"), hv(BV), 0.5, hv(EI), ALU.mult, ALU.mult)
                        self.tt('pool', ov("KGT"), hv(KM), hv(EC), ALU.mult)
                        yield
                        self.stt('dve', ov("BGT"), hv(BV), 0.5, hv(EC), ALU.mult, ALU.mult)
                        self.tt('pool', ov("RT"), hv(r_), hv(EG), ALU.mult)
                        yield
                        self.tt('pool', ov("RK"), hv(r_), hv(KM), ALU.mult)
                        yield

                lockstep([stageB(0), stageB(1)])
                lockstep([stageB(2), stageB(3), gen45((0, 1))])
                lockstep([gen45((2,)), gen45((3,))])

        def gen45(cs):
            for c in cs:
                psA = self.next_ps()
                pbA = psA.v.bitcast(BF16)
                for gi, n in enumerate(("BGT", "KGT")):
                    for q in range(NQ):
                        self.tr(pbA[:, (gi * NQ + q) * 128:(gi * NQ + q + 1) * 128], EXP[n][c][:, q, :], signal=(gi == 1 and q == NQ - 1))
                self.evac(BKG[c].v, pbA.re("p (g q m) -> p g q m", g=2, q=NQ))
                psB = self.next_ps()
                pbB = psB.v.bitcast(BF16)
                for q in range(NQ):
                    self.tr(pbB[:, q * 128:(q + 1) * 128], EXP["VB"][c][:, q, :], signal=(q == NQ - 1))
                for hl in range(2):
                    self.evac(VstAll[hl * 64:(hl + 1) * 64, :, c, :],
                              pbB[hl * 64:(hl + 1) * 64, 0:NQ * 128].re("p (q m) -> p q m", q=NQ)[:, :, hl * 64:(hl + 1) * 64])
                yield
            for c in cs:
                combos = [("BT", "AT", mask_su, Nb[c][0]), ("AT", "BT", mask_sl, Lb[c][0]), ("KT", "AT", mask_su, LakT[c]),
                          ("BT", "RT", mask_u, MrbT[c]), ("KT", "RT", mask_u, MrkT[c])]
                for (ln, rn_, mk, dstt) in combos:
                    ps = self.next_ps()
                    for q in range(NQ):
                        self.mm(ps[:, q * 128:(q + 1) * 128], EXP[ln][c][:, q, :], EXP[rn_][c][:, q, :], start=True, stop=True,
                                signal=(q == NQ - 1))
                    self.tt('dve', dstt.v, ps.v.re("p (q m) -> p q m", q=NQ), mk.v.un(1).bc([128, NQ, 128]), ALU.mult)
                    yield
                self.tt('pool', Pb[c].v, Nb[c][0].v, ident.v.un(1).bc([128, NQ, 128]), ALU.add)
            cur = 0
            for lvl in range(1, 6):
                nxt = 1 - cur
                for c in cs:
                    if lvl < 5:
                        ps = self.next_ps()
                        for q in range(NQ):
                            self.mm(ps[:, q * 128:(q + 1) * 128], Lb[c][cur][:, q, :], Nb[c][cur][:, q, :], start=True, stop=True,
                                    signal=(q == NQ - 1))
                        self.cp('act', Nb[c][nxt].v, ps.v.re("p (q m) -> p q m", q=NQ))
                    ps = self.next_ps()
                    for q in range(NQ):
                        self.mm(ps[:, q * 128:(q + 1) * 128], Nb[c][cur][:, q, :], Lb[c][cur][:, q, :], start=True, stop=True,
                                signal=(q == NQ - 1))
                    self.cp('act', Lb[c][nxt].v, ps.v.re("p (q m) -> p q m", q=NQ))
                    yield
                for c in cs:
                    ps = self.next_ps()
                    for q in range(NQ):
                        self.mm(ps[:, q * 128:(q + 1) * 128], Lb[c][nxt][:, q, :], Pb[c][:, q, :], start=True, stop=True,
                                signal=(q == NQ - 1))
                    self.tt('dve', Pb[c].v, Pb[c].v, ps.v.re("p (q m) -> p q m", q=NQ), ALU.add)
                    yield
                cur = nxt

        def lockstep(gens):
            alive = True
            while alive:
                alive = False
                for g in gens:
                    try:
                        next(g)
                        alive = True
                    except StopIteration:
                        pass

        PS = self.ps

        def crit(it, q):
            w = it * NQ + q
            xb_, ub_ = Xb[w % 2], Ub[w % 2]
            psx, psu, psh, psy = PS[0], PS[1], PS[2], PS[3 + (q % 2)]
            for c in range(4):
                self.mm(psx[:, c * 64:(c + 1) * 64], EXP["AT"][c][:, q, :], Hb[:, c, :], start=True, stop=False, signal=False)
                self.mm(psx[:, c * 64:(c + 1) * 64], LakT[c][:, q, :], VstAll[:, q, c, :], start=False, stop=True, signal=(c == 3))
            self.cp('act', xb_.v, psx[:, 0:256].re("p (c m) -> p c m", c=4))
            yield
            for c in range(4):
                self.mm(psu[:, c * 64:(c + 1) * 64], Pb[c][:, q, :], xb_[:, c, :], start=True, stop=True, signal=(c == 3))
            self.cp('dve', ub_.v, psu[:, 0:256].re("p (c m) -> p c m", c=4))
            yield
            for c in range(4):
                o = psh[:, c * 64:(c + 1) * 64]
                self.mm(o, BKG[c][:, 0, q, :], ub_[:, c, :], start=True, stop=False, signal=False)
                self.mm(o, BKG[c][:, 1, q, :], VstAll[:, q, c, :], start=False, stop=True, signal=(c == 3))
            for c in range(4):
                o = psy[:, c * 64:(c + 1) * 64]
                self.mm(o, EXP["RT"][c][:, q, :], Hb[:, c, :], start=True, stop=False, signal=False)
                self.mm(o, MrbT[c][:, q, :], ub_[:, c, :], start=False, stop=False, signal=False)
                self.mm(o, MrkT[c][:, q, :], VstAll[:, q, c, :], start=False, stop=True, signal=False)
            for c in range(4):
                self.mm(psy[:, 256 + c:257 + c], EXP["RK"][c][:, q, :], rkb[:, c:c + 1], start=True, stop=True, signal=(c == 3))
            self.tt('dve', H.v, H.v, GC[:, :, q:q + 1].bc([128, 4, 64]), ALU.mult)
            self.tt('dve', H.v, H.v, psh[:, 0:256].re("p (c m) -> p c m", c=4), ALU.add)
            self.cp('act', Hb.v, H.v)
            yield

        def post_a(it, q):
            w = it * NQ + q
            psy = PS[3 + (q % 2)]
            psg = PS[5]
            self.mm(psg.v, SGd[:, q, :], gup.v, start=True, stop=True)
            Y = Yv[w % 2]; Y2 = Ysq[w % 2]; YN = Yn[w % 2]; BN = Bn[w % 2]; YF = Yf[w % 2]
            s1 = st1[w % 2]; s2 = st2[w % 2]; mn = mean[w % 2]; vr = var[w % 2]
            py3 = psy[:, 0:256].re("p (c m) -> p c m", c=4)
            self.cp('act', Y.v, py3)
            self.act(Y2.v, py3, AF.Square)
            self.tt('dve', BN.v, VstAll[:, q, :, :], psy[:, 256:260].un(2).bc([128, 4, 64]), ALU.mult)
            self.S.op('dve', lambda e: e.reduce_sum(out=s1.ap, in_=Y.ap, axis=AX.X), [Y.v], [s1.v])
            self.S.op('dve', lambda e: e.reduce_sum(out=s2.ap, in_=Y2.ap, axis=AX.X), [Y2.v], [s2.v])
            self.ts('dve', mn.v, s1.v, 1.0 / 64.0, None, ALU.mult)
            self.tt('dve', vr.v, mn.v, mn.v, ALU.mult)
            self.stt('dve', vr.v, s2.v, 1.0 / 64.0, vr.v, ALU.mult, ALU.subtract)
            self.ts('dve', vr.v, vr.v, LNX_EPS, None, ALU.add)
            self.act(vr.v, vr.v, AF.Ln)
            self.act(vr.v, vr.v, AF.Exp, scale=-0.5)
            self.tt('dve', YN.v, Y.v, mn.v.un(2).bc([128, 4, 64]), ALU.subtract)
            self.tt('dve', YN.v, YN.v, vr.v.un(2).bc([128, 4, 64]), ALU.mult)
            self.tt('pool', YN.v, YN.v, lnxg.v, ALU.mult)
            self.tt('pool', YN.v, YN.v, lnxb.v, ALU.add)
            self.tt('pool', YN.v, YN.v, BN.v, ALU.add)
            for hl in range(2):
                P_ = slice(hl * 64, (hl + 1) * 64)
                gv = psg[P_, :].re("p (c h m) -> p c h m", c=4, h=2)[:, :, hl, :]
                self.tt('dve', YF[P_, :, :], YN[P_, :, :], gv, ALU.mult)

        def post_b(it, q, YTt):
            w = it * NQ + q
            YF = Yf[w % 2]
            pst = self.next_ps()
            pbt = pst.v.bitcast(BF16)
            for c in range(4):
                self.tr(pbt[0:64, c * 128:(c + 1) * 128], YF[:, c, :], signal=(c == 3))
            self.evac(YTt[:, :, :, q * CH:(q + 1) * CH], pbt[0:64, 0:512].re("p (c h t) -> p c h t", c=4, h=2))

        for p in s12_pieces(0):
            p()
        for it in range(ntile):
            tok0 = it * TT
            if S.need_reset():
                S.hard_barrier()
            stage3(it)
            pieces = s12_pieces(it + 1) if it + 1 < ntile else []
            YTt = YT[it % 2]
            self.ps_allowed = [6, 7]

            def filler():
                if pieces:
                    pieces.pop(0)()

            for q in range(NQ + 2):
                if q < NQ:
                    for _ in crit(it, q):
                        filler()
                if q == NQ:
                    while pieces:
                        pieces.pop(0)()
                if 1 <= q <= NQ:
                    post_a(it, q - 1)
                if q >= 2:
                    post_b(it, q - 2, YTt)
            self.ps_allowed = None
            self.dma('sp', yas_v[:, :, :, tok0:tok0 + TT], YTt.v)
        self._sbuf_left = self.nc.sbuf_bytes_remaining

    def phaseM(self, es, src, dst, final=False):
        S = self.S
        inp = self.inp
        TT = 512
        NS = 4
        self._stg_i = 0
        self._ev_i = 0
        use_a = 'R' in self.phases
        cols, ci = self.load_cols(es, "m_cols", [("g", inp["norm_mix_g"]), ("bin", inp["b_in"][2816:4864])])
        hb = self.sb(es, "m_hb", [128, 16], F32)
        self.ts('dve', hb.v, cols[:, ci["bin"]:ci["bin"] + 16], 0.5, None, ALU.mult)
        win = self.sb(es, "m_win", [128, 8, 2048], BF16)
        wa = self.sb(es, "m_wa", [128, 4, D], BF16)
        wb = self.sb(es, "m_wb", [128, 4, D], BF16)
        wmo = self.sb(es, "m_wmo", [128, 8, D], BF16)
        with ExitStack() as tes:
            stg = [self.sb(tes, "m_stg%d" % i, [128, 2048], F32) for i in range(3)]
            for k in range(8):
                self.load_weight(stg, win[:, k, :], inp["w_in"][k * 128:(k + 1) * 128, 2816:4864], 2048,
                                 scale=cols[:, ci["g"] + k: ci["g"] + k + 1])
                self.load_weight(stg, wmo[:, k, :], inp["w_mix_out"][k * 128:(k + 1) * 128, :], D)
            for c in range(4):
                self.load_weight(stg, wa[:, c, :], inp["w_branch_a"][c * 128:(c + 1) * 128, :], D, scale=0.5)
                self.load_weight(stg, wb[:, c, :], inp["w_branch_b"][c * 128:(c + 1) * 128, :], D, scale=0.25)
            S.barrier()
        xbufs, sst, rst, junk = self.norm_bufs(es, "m", NS)
        xrb = [self.sb(es, "m_xr%d" % i, [128, D], F32) for i in range(NS)]
        hs = [self.sb(es, "m_h%d" % i, [128, NS, D], BF16) for i in range(2)]
        hTs = [self.sb(es, "m_hT%d" % i, [128, 8, TT], BF16) for i in range(2)]
        TG = [self.sb(es, "m_tg%d" % m, [128, TT], BF16) for m in range(16)]
        YA = [self.sb(es, "m_ya%d" % i, [128, 4, TT], BF16) for i in range(2)]
        YB = [self.sb(es, "m_yb%d" % i, [128, 4, TT], BF16) for i in range(2)]
        MA = [self.sb(es, "m_ma%d" % i, [128, TT], F32) for i in range(3)]
        MG = [self.sb(es, "m_mg%d" % m, [128, TT], BF16) for m in range(8)]
        self._mb = [self.sb(es, "m_mb%d" % i, [128, TT], F32) for i in range(2)]
        ntile = self.ntok // TT

        def load_y(i):
            if i >= ntile:
                return
            tok0 = i * TT
            if use_a:
                self.dma('sp', YA[i % 2].v, self.yas[:, tok0:tok0 + TT].rearrange("(c p) t -> p c t", p=128))
            self.dma('sp', YB[i % 2].v, self.ybs[:, tok0:tok0 + TT].rearrange("(c p) t -> p c t", p=128))

        def front(i):
            self.norm_from_dram(src, i * TT, NS, xbufs, sst, rst, junk, hs[i % 2])
            self.transpose_h(hs[i % 2], NS, hTs[i % 2])

        def mid(i):
            hT = hTs[i % 2]
            ya, yb = YA[i % 2], YB[i % 2]
            for m in range(16):
                if m == 6 and i + 1 < ntile:
                    self.norm_from_dram(src, (i + 1) * TT, NS, xbufs, sst, rst, junk, hs[(i + 1) % 2])
                ps = self.next_ps()
                for k in range(8):
                    self.mm(ps.v, win[:, k, m * 128:(m + 1) * 128], hT[:, k, :], start=(k == 0), stop=(k == 7))
                self.act(TG[m].v, ps.v, AF.Tanh, scale=0.5, bias=hb[:, m:m + 1])
            for m in range(8):
                ma = MA[m % 3]
                if use_a:
                    ps = self.next_ps()
                    for c in range(4):
                        self.mm(ps.v, wa[:, c, m * 128:(m + 1) * 128], ya[:, c, :], start=(c == 0), stop=(c == 3))
                    self.stt('dve', ma.v, TG[m].v, 1.0, ps.v, ALU.add, ALU.mult)
                ps2 = self.next_ps()
                for c in range(4):
                    self.mm(ps2.v, wb[:, c, m * 128:(m + 1) * 128], yb[:, c, :], start=(c == 0), stop=(c == 3))
                if use_a:
                    mb = self._mb[m % 2]
                    self.stt('dve', mb.v, TG[8 + m].v, 1.0, ps2.v, ALU.add, ALU.mult)
                    self.tt('pool', MG[m].v, mb.v, ma.v, ALU.add)
                else:
                    self.stt('dve', MG[m].v, TG[8 + m].v, 1.0, ps2.v, ALU.add, ALU.mult)
            if i + 1 < ntile:
                self.transpose_h(hs[(i + 1) % 2], NS, hTs[(i + 1) % 2])

        def back(i):
            tok0 = i * TT
            for s in range(NS):
                self.dma('sp', xrb[s].v, src[tok0 + s * 128: tok0 + (s + 1) * 128, :])
            for s in range(NS):
                pss = [self.next_ps(), self.next_ps()]
                for half in range(2):
                    for m in range(8):
                        self.mm(pss[half].v, MG[m][:, s * 128:(s + 1) * 128], wmo[:, m, half * 512:(half + 1) * 512],
                                start=(m == 0), stop=(m == 7))
                xb = xrb[s]
                for half in range(2):
                    self.tt('dve', xb[:, half * 512:(half + 1) * 512], xb[:, half * 512:(half + 1) * 512], pss[half].v, ALU.add)
                self.dma('sp', dst[tok0 + s * 128: tok0 + (s + 1) * 128, :], xb.v)

        load_y(0)
        front(0)
        for i in range(ntile):
            if S.need_reset():
                S.hard_barrier()
            load_y(i + 1)
            mid(i)
            back(i)

    def evac(self, out, in_, i=None):
        if i is None:
            i = self._ev_i
            self._ev_i += 1
        return self.cp(('act', 'dve')[i % 2], out, in_)

    def phaseB(self, es, src, dst, final=False):
        S = self.S
        inp = self.inp
        TT = 512
        NS = 4
        self._stg_i = 0
        self._ev_i = 0
        cols, ci = self.load_cols(es, "b_cols", [("gx", inp["norm_x_g"]), ("gm", inp["norm_mem_g"])])
        gq = self.sb(es, "b_gq", [128, 8], F32)
        self.ts('dve', gq.v, cols[:, ci["gx"]:ci["gx"] + 8], 1.0 / 16.0, None, ALU.mult)
        wq = self.sb(es, "b_wq", [128, 8, D], BF16)
        wo = self.sb(es, "b_wo", [128, 8, D], BF16)
        kT = [self.sb(es, "b_kT%d" % b, [128, 8, NMEM], BF16) for b in range(self.nb)]
        Vt = [self.sb(es, "b_V%d" % b, [128, 2, D], BF16) for b in range(self.nb)]
        xts = [self.sb(es, "b_xt%d" % i, [128, NS, D], F32) for i in range(2)]
        h = self.sb(es, "b_h", [128, NS, D], BF16)
        hT = self.sb(es, "b_hT", [128, 8, TT], BF16)
        junk = self.sb(es, "b_junk", [128, D], BF16)
        ss = self.sb(es, "b_ss", [128, 4], F32)
        rstd = self.sb(es, "b_rstd", [128, 4], F32)
        with ExitStack() as tes:
            stg = [self.sb(tes, "b_stg%d" % i, [128, 2048], F32) for i in range(3)]
            wkv = self.sb(tes, "b_wkv", [128, 8, 2 * D], BF16)
            for k in range(8):
                self.load_weight(stg, wkv[:, k, :], inp["w_ckv"][k * 128:(k + 1) * 128, :], 2 * D,
                                 scale=cols[:, ci["gm"] + k: ci["gm"] + k + 1])
            for k in range(8):
                self.load_weight(stg, wq[:, k, :], inp["w_cq"][k * 128:(k + 1) * 128, :], D, scale=gq[:, k:k + 1])
                self.load_weight(stg, wo[:, k, :], inp["w_co"][k * 128:(k + 1) * 128, :], D)
            for b in range(self.nb):
                mt = xts[b % 2]
                self.dma('sp', mt[:, 0:2, :], inp["mem"][b * NMEM:(b + 1) * NMEM, :].rearrange("(s p) d -> p s d", p=128))
                self.rms_h(mt, 2, ss, rstd, junk, h)
                self.transpose_h(h, 2, hT[:, :, 0:NMEM])
                for m in range(8):
                    if m % 2 == 0:
                        ps = self.next_ps()
                    o = (m % 2) * NMEM
                    for k in range(8):
                        self.mm(ps[:, o:o + NMEM], wkv[:, k, m * 128:(m + 1) * 128], hT[:, k, 0:NMEM], start=(k == 0), stop=(k == 7))
                    if m % 2 == 1:
                        self.evac(kT[b][:, m - 1:m + 1, :], ps.v.re("p (a t) -> p a t", a=2))
                for mc in range(2):
                    for half in range(2):
                        ps = self.next_ps()
                        for k in range(8):
                            self.mm(ps.v, hT[:, k, mc * 128:(mc + 1) * 128], wkv[:, k, D + half * 512: D + (half + 1) * 512],
                                    start=(k == 0), stop=(k == 7))
                        self.evac(Vt[b][:, mc, half * 512:(half + 1) * 512], ps.v)
            S.barrier()
        qT = self.sb(es, "b_qT", [128, 8, TT], BF16)
        oT = self.sb(es, "b_oT", [128, 8, TT], BF16)
        prT = self.sb(es, "b_prT", [128, 2, 4, TT], BF16)
        prs = [self.sb(es, "b_pr%d" % i, [128, 4, NMEM], BF16) for i in range(2)]
        prn = [self.sb(es, "b_prn%d" % i, [128, 4, NMEM], BF16) for i in range(2)]
        mx = [self.sb(es, "b_mx%d" % i, [128, 4], F32) for i in range(2)]
        sm = [self.sb(es, "b_sm%d" % i, [128, 4], F32) for i in range(2)]
        hs = [h, self.sb(es, "b_h1", [128, NS, D], BF16)]
        hTs = [hT, self.sb(es, "b_hT1", [128, 8, TT], BF16)]
        sss = [ss, self.sb(es, "b_ss1", [128, 4], F32)]
        rstds = [rstd, self.sb(es, "b_rstd1", [128, 4], F32)]
        ntile = self.ntok // TT
        tiles_per_b = self.seq // TT

        def load(i):
            if i < ntile:
                self.dma('sp', xts[i % 2].v, src[i * TT:(i + 1) * TT, :].rearrange("(s p) d -> p s d", p=128))

        def front(i):
            self.rms_h(xts[i % 2], NS, sss[i % 2], rstds[i % 2], junk, hs[i % 2])
            self.transpose_h(hs[i % 2], NS, hTs[i % 2])

        def mid(i):
            b = i // tiles_per_b
            hT_ = hTs[i % 2]
            for m in range(8):
                ps = self.next_ps()
                for k in range(8):
                    self.mm(ps.v, wq[:, k, m * 128:(m + 1) * 128], hT_[:, k, :], start=(k == 0), stop=(k == 7))
                self.evac(qT[:, m, :], ps.v)
            def scores(s):
                banks = [self.next_ps(), self.next_ps()]
                for hh in range(4):
                    ps = banks[hh // 2]
                    o = (hh % 2) * NMEM
                    for c in range(2):
                        self.mm(ps[:, o:o + NMEM], qT[:, 2 * hh + c, s * 128:(s + 1) * 128], kT[b][:, 2 * hh + c, :],
                                start=(c == 0), stop=(c == 1))
                return banks

            nxt = scores(0)
            for s in range(NS):
                pr = prs[s % 2]; pn = prn[s % 2]; mxs = mx[s % 2]; sms = sm[s % 2]
                banks = nxt
                if s + 1 < NS:
                    nxt = scores(s + 1)
                for g in range(2):
                    self.S.op('dve', lambda e: e.reduce_max(out=mxs.ap[:, 2 * g:2 * g + 2],
                                                            in_=banks[g].ap.rearrange("p (a t) -> p a t", a=2), axis=AX.X),
                              [banks[g].v], [mxs.v])
                self.ts('dve', mxs.v, mxs.v, -1.0, None, ALU.mult)
                for hh in range(4):
                    ps = banks[hh // 2]
                    o = (hh % 2) * NMEM
                    self.act(pr[:, hh, :], ps[:, o:o + NMEM], AF.Exp, bias=mxs[:, hh:hh + 1], accum=sms[:, hh:hh + 1])
                self.S.op('dve', lambda e: e.reciprocal(out=sms.ap, in_=sms.ap), [sms.v], [sms.v])
                self.tt('dve', pn.v, pr.v, sms.v.un(2).bc([128, 4, NMEM]), ALU.mult)
                ps = self.next_ps()
                pb = ps.v.bitcast(BF16)
                for hh in range(4):
                    for mc in range(2):
                        self.tr(pb[:, (hh * 2 + mc) * 128:(hh * 2 + mc + 1) * 128], pn[:, hh, mc * 128:(mc + 1) * 128],
                                signal=(hh == 3 and mc == 1))
                self.evac(prT[:, :, :, s * 128:(s + 1) * 128], pb.re("p (h m t) -> p m h t", h=4, m=2))
                if s == 1 and i + 1 < ntile:
                    self.rms_h(xts[(i + 1) % 2], NS, sss[(i + 1) % 2], rstds[(i + 1) % 2], junk, hs[(i + 1) % 2])
            for hh in range(4):
                for c in range(2):
                    ps = self.next_ps()
                    for mc in range(2):
                        self.mm(ps.v, Vt[b][:, mc, hh * 256 + c * 128: hh * 256 + (c + 1) * 128], prT[:, mc, hh, :],
                                start=(mc == 0), stop=(mc == 1))
                    self.evac(oT[:, 2 * hh + c, :], ps.v)
            if i + 1 < ntile:
                self.transpose_h(hs[(i + 1) % 2], NS, hTs[(i + 1) % 2])

        def back(i):
            xt = xts[i % 2]
            for s in range(NS):
                pss = [self.next_ps(), self.next_ps()]
                for half in range(2):
                    for k in range(8):
                        self.mm(pss[half].v, oT[:, k, s * 128:(s + 1) * 128], wo[:, k, half * 512:(half + 1) * 512],
                                start=(k == 0), stop=(k == 7))
                for half in range(2):
                    self.tt('dve', xt[:, s, half * 512:(half + 1) * 512], xt[:, s, half * 512:(half + 1) * 512], pss[half].v, ALU.add)
                self.dma('sp', dst[i * TT + s * 128: i * TT + (s + 1) * 128, :], xt[:, s, :])

        load(0)
        load(1)
        front(0)
        for i in range(ntile):
            if S.need_reset():
                S.hard_barrier()
            mid(i)
            back(i)
            load(i + 2)

    def phaseC(self, es, src, dst, final=True):
        nc, S = self.nc, self.S
        TT = 256
        NS = TT // 128
        NJ = DFF // 128
        inp = self.inp
        self._stg_i = 0
        cols, ci = self.load_cols(es, "c_cols", [("g", inp["norm_ffn_g"]), ("cw0", inp["ffn_conv_w"][0]),
                                                 ("cw1", inp["ffn_conv_w"][1]), ("cw2", inp["ffn_conv_w"][2]),
                                                 ("cb", inp["ffn_conv_b"])])
        wi = self.sb(es, "c_wi", [128, 8, 2 * DFF], BF16)
        wo = self.sb(es, "c_wo", [128, NJ, D], BF16)
        gfin = self.sb(es, "c_gfin", [128, D], F32)
        self.dma('sp', gfin.v, inp["norm_final_g"].partition_broadcast(128))
        with ExitStack() as tes:
            stg = [self.sb(tes, "c_stg%d" % i, [128, 2048], F32) for i in range(3)]
            for k in range(8):
                self.load_weight(stg, wi[:, k, :], inp["w_ffn_in"][k * 128:(k + 1) * 128, :], 2 * DFF,
                                 scale=cols[:, ci["g"] + k: ci["g"] + k + 1])
            for j in range(NJ):
                self.load_weight(stg, wo[:, j, :], inp["w_ffn_out"][j * 128:(j + 1) * 128, :], D)
            S.barrier()
        xts = [self.sb(es, "c_xt%d" % i, [128, NS, D], F32) for i in range(2)]
        hs = [self.sb(es, "c_h%d" % i, [128, NS, D], BF16) for i in range(2)]
        hTs = [self.sb(es, "c_hT%d" % i, [128, 8, TT], BF16) for i in range(2)]
        actT = [self.sb(es, "c_actT%d" % j, [128, TT], BF16) for j in range(NJ)]
        junk = self.sb(es, "c_junk", [128, D], BF16)
        sss = [self.sb(es, "c_ss%d" % i, [128, 4], F32) for i in range(3)]
        rstds = [self.sb(es, "c_rstd%d" % i, [128, 4], F32) for i in range(3)]
        halos = [self.sb(es, "c_halo%d" % j, [128, 2], F32) for j in range(NJ)]
        NW = 4
        uraw = [es.enter_context(self.nc.sbuf_tensor("c_uw%d" % i, [128, 2 + TT], F32)) for i in range(NW)]
        uh = [T(r[:, 0:2], "c_uh") for r in uraw]
        ub = [T(r[:, 2:2 + TT], "c_ub") for r in uraw]

        def uview(jj, a, b_):
            return V([uh[jj], ub[jj]], uraw[jj][:, a:b_])

        acc = [self.sb(es, "c_acc%d" % i, [128, TT], F32) for i in range(NW)]
        th = [self.sb(es, "c_th%d" % i, [128, TT], F32) for i in range(NW)]
        ntile = self.ntok // TT
        tiles_per_b = self.seq // TT

        def load(i):
            if i < ntile:
                self.dma('sp', xts[i % 2].v, src[i * TT:(i + 1) * TT, :].rearrange("(s p) d -> p s d", p=128))

        def front(i):
            self.rms_h(xts[i % 2], NS, sss[i % 2], rstds[i % 2], junk, hs[i % 2])
            self.transpose_h(hs[i % 2], NS, hTs[i % 2])

        def mid(i):
            hT = hTs[i % 2]
            st = {}
            for j in range(NJ + 2):
                if i + 1 < ntile:
                    if j == 8:
                        self.rms_h(xts[(i + 1) % 2], NS, sss[(i + 1) % 2], rstds[(i + 1) % 2], junk, hs[(i + 1) % 2])
                    if j == NJ:
                        self.transpose_h(hs[(i + 1) % 2], NS, hTs[(i + 1) % 2])
                if j < NJ:
                    ps = self.next_ps()
                    for half in range(2):
                        c0 = half * DFF + j * 128
                        for k in range(8):
                            self.mm(ps[:, half * TT:(half + 1) * TT], wi[:, k, c0:c0 + 128], hT[:, k, :], start=(k == 0), stop=(k == 7))
                    jj = j % NW
                    a = acc[jj]
                    w0 = cols[:, ci["cw0"] + j: ci["cw0"] + j + 1]
                    w1 = cols[:, ci["cw1"] + j: ci["cw1"] + j + 1]
                    w2 = cols[:, ci["cw2"] + j: ci["cw2"] + j + 1]
                    cb = cols[:, ci["cb"] + j: ci["cb"] + j + 1]
                    self.cp('dve', uh[jj].v, halos[j].v)
                    self.cp('act', ub[jj].v, ps[:, 0:TT])
                    self.act(a.v, ps[:, 0:TT], AF.Identity, scale=w2, bias=cb)
                    self.stt('dve', a.v, uview(jj, 1, 1 + TT), w1, a.v, ALU.mult, ALU.add)
                    self.stt('dve', a.v, uview(jj, 0, TT), w0, a.v, ALU.mult, ALU.add)
                    self.cp('dve', halos[j].v, ub[jj][:, TT - 2:TT])
                    st[j] = ps
                if 0 <= j - 1 < NJ:
                    jj = (j - 1) % NW
                    self.act(th[jj].v, acc[jj].v, AF.Gelu_apprx_tanh)
                if 0 <= j - 2 < NJ:
                    jj = (j - 2) % NW
                    self.tt('dve', actT[j - 2].v, th[jj].v, st[j - 2][:, TT:2 * TT], ALU.mult)

        def back(i):
            xt = xts[i % 2]
            ss, rstd = sss[2], rstds[2]
            for s in range(NS):
                pss = [self.next_ps(), self.next_ps()]
                for half in range(2):
                    for j in range(NJ):
                        self.mm(pss[half].v, actT[j][:, s * 128:(s + 1) * 128], wo[:, j, half * 512:(half + 1) * 512],
                                start=(j == 0), stop=(j == NJ - 1))
                for half in range(2):
                    self.tt('dve', xt[:, s, half * 512:(half + 1) * 512], xt[:, s, half * 512:(half + 1) * 512], pss[half].v, ALU.add)
                if final:
                    self.act(junk.v, xt[:, s, :], AF.Square, accum=ss[:, s:s + 1])
                    self.ts('dve', rstd[:, s:s + 1], ss[:, s:s + 1], 1.0 / D, NORM_EPS, ALU.mult, ALU.add)
                    self.rsqrt(rstd[:, s:s + 1], rstd[:, s:s + 1])
                    self.stt('dve', xt[:, s, :], xt[:, s, :], rstd[:, s:s + 1], gfin.v, ALU.mult, ALU.mult)
                self.dma('sp', dst[i * TT + s * 128: i * TT + (s + 1) * 128, :], xt[:, s, :])

        load(0)
        load(1)
        front(0)
        for i in range(ntile):
            if S.need_reset():
                S.hard_barrier()
            if i % tiles_per_b == 0:
                for j in range(NJ):
                    self.memset('pool', halos[j].v, 0.0)
            mid(i)
            back(i)
            load(i + 2)


_PROG_CACHE = {}


def get_prog(nb=4, seq=2048, phases="LRMBC"):
    key = (nb, seq, phases)
    if key not in _PROG_CACHE:
        _PROG_CACHE[key] = Prog(nb, seq, phases)
    return _PROG_CACHE[key]


_W_NAMES = ["norm_mix_g", "w_in", "b_in", "mu_shift", "w0", "w_lora_up", "a0", "a_lora_up", "g_lora_up", "k_k", "k_a",
            "r_k", "lnx_g", "lnx_b", "w_branch_a", "conv_b_w", "conv_b_b", "w_rg_a", "b_rg_a", "w_rg_x", "b_rg_x",
            "lru_lambda", "w_branch_b", "w_mix_out", "norm_x_g", "norm_mem_g", "w_cq", "w_ckv", "w_co", "norm_ffn_g",
            "w_ffn_in", "ffn_conv_w", "ffn_conv_b", "w_ffn_out", "norm_final_g"]


def run(inputs, ncores=8, nb=4, seq=2048, phases="LRMBC"):
    prog = get_prog(nb, seq, phases)
    shapes = {k: tuple(v.shape) for k, v in prog.inp.items()}
    shared = {}
    for k in _W_NAMES:
        a = np.ascontiguousarray(np.asarray(inputs[k], dtype=np.float32))
        shared[k] = a.reshape(shapes[k])
    x = np.asarray(inputs["x"], dtype=np.float32)
    mem = np.asarray(inputs["mem"], dtype=np.float32)
    in_maps = []
    for c in range(ncores):
        m = dict(shared)
        m["x"] = np.ascontiguousarray(x[c * nb:(c + 1) * nb, :seq]).reshape(nb * seq, D)
        m["mem"] = np.ascontiguousarray(mem[c * nb:(c + 1) * nb]).reshape(nb * NMEM, D)
        in_maps.append(m)
    res = run_bass_kernel_spmd(prog.nc, in_maps, core_ids=list(range(ncores)))
    outs = [np.asarray(r["out"]).reshape(nb, seq, D) for r in res.results]
    return np.concatenate(outs, axis=0)


def kernel(**inputs):
    return run(inputs).astype(np.float32)
```

```python
import numpy as np
from contextlib import ExitStack
import concourse.bass as bass
import concourse.mybir as mybir
from concourse.bass_utils import run_bass_kernel_spmd

F32 = mybir.dt.float32
BF16 = mybir.dt.bfloat16
AF = mybir.ActivationFunctionType
ALU = mybir.AluOpType
AX = mybir.AxisListType

D = 1024
NMEM = 256
AW = 512
RWKV_COLS = 1792
P_IN = 4864
DFF = 2816
NORM_EPS = 1e-6
LNX_EPS = 64e-5
SAME_ENGINE_SYNC = True
import os as _os
RSTOP = int(_os.environ.get("RSTOP", "0"))
if _os.environ.get("NOSES"):
    SAME_ENGINE_SYNC = False
LVAR = int(_os.environ.get("LVAR", "2"))


_ALL_TILES = []


class T:
    def __init__(self, ap, name=""):
        self.ap = ap if isinstance(ap, bass.AP) else ap[:]
        self.w = None
        self.r = []
        self.name = name
        self.dsem = None
        _ALL_TILES.append(self)

    def __getitem__(self, k):
        return V([self], self.ap[k])

    @property
    def v(self):
        return V([self], self.ap)


class V:
    def __init__(self, ts, ap):
        self.ts = ts
        self.ap = ap

    def __getitem__(self, k):
        return V(self.ts, self.ap[k])

    def re(self, pat, **kw):
        return V(self.ts, self.ap.rearrange(pat, **kw))

    def bc(self, shape):
        return V(self.ts, self.ap.to_broadcast(shape))

    def un(self, axis):
        return V(self.ts, self.ap.unsqueeze(axis))

    def bitcast(self, dt):
        return V(self.ts, self.ap.bitcast(dt))


def _ap(x):
    return x.ap if isinstance(x, V) else x


class Sync:
    def __init__(self, nc, es, n_dma_sems=64):
        self.nc = nc
        self.es = es
        self.engs = {'pe': nc.tensor, 'act': nc.scalar, 'dve': nc.vector, 'pool': nc.gpsimd, 'sp': nc.sync}
        self.sem = {}
        self.cnt = {}
        self.seen = {}
        for e in self.engs:
            self.sem[e] = es.enter_context(nc.semaphore("c_" + e))
            self.cnt[e] = 0
            self.seen[e] = {}
        self.dpool = [{'sem': es.enter_context(nc.semaphore("d%d" % i)), 'cnt': 0} for i in range(n_dma_sems)]
        self.dfree = list(range(n_dma_sems))
        self.n_inst = 0
        self.bar1 = es.enter_context(nc.semaphore("bar1"))
        self.bar2 = es.enter_context(nc.semaphore("bar2"))
        self.bar_k = 0

    def _wait(self, e, tok):
        if tok is None:
            return
        sem, val, src = tok
        if src == e and (e == 'pe' or not SAME_ENGINE_SYNC):
            return
        key = sem.name
        if self.seen[e].get(key, 0) >= val:
            return
        self.seen[e][key] = val
        self.engs[e].wait_ge(sem, val)

    def deps(self, e, reads, writes):
        for v in reads:
            if not isinstance(v, V):
                continue
            for t in v.ts:
                self._wait(e, t.w)
        for v in writes:
            for t in v.ts:
                self._wait(e, t.w)
                for tok in t.r:
                    self._wait(e, tok)

    def done(self, tok, reads, writes):
        for v in reads:
            if not isinstance(v, V):
                continue
            for t in v.ts:
                t.r.append(tok)
                if len(t.r) > 24:
                    t.r = t.r[-24:] if False else t.r
        for v in writes:
            for t in v.ts:
                t.w = tok
                t.r = []

    def op(self, e, fn, reads, writes, signal=True):
        self.deps(e, reads, writes)
        inst = fn(self.engs[e])
        self.n_inst += 1
        if signal:
            self.cnt[e] += 1
            inst.then_inc(self.sem[e], 1)
            tok = (self.sem[e], self.cnt[e], e)
        else:
            tok = (self.sem[e], self.cnt[e] + 1, e)
        self.done(tok, reads, writes)
        return inst

    def _dsem(self, t):
        if t.dsem is None:
            if not self.dfree:
                raise RuntimeError("out of dma semaphores")
            t.dsem = self.dpool[self.dfree.pop(0)]
        return t.dsem

    def dma(self, q, out, in_, **kw):
        reads = [in_] if isinstance(in_, V) else []
        writes = [out] if isinstance(out, V) else []
        self.deps(q, reads, writes)
        sbv = out if isinstance(out, V) else in_
        ds = self._dsem(sbv.ts[0])
        inst = self.engs[q].dma_start(out=_ap(out), in_=_ap(in_), **kw)
        self.n_inst += 1
        ds['cnt'] += 16
        inst.then_inc(ds['sem'], 16)
        tok = (ds['sem'], ds['cnt'], 'dma')
        self.done(tok, reads, writes)
        return tok

    def barrier(self):
        toks = [(self.sem[f], self.cnt[f], f) for f in self.engs if self.cnt[f] > 0]
        toks += [(d['sem'], d['cnt'], 'dma') for d in self.dpool if d['cnt'] > 0]
        for e in self.engs:
            for tok in toks:
                if tok[2] == e:
                    continue
                self._wait(e, tok)

    def release_dma_sems(self):
        self.dfree = list(range(len(self.dpool)))
        for t in _ALL_TILES:
            t.dsem = None

    def need_reset(self, limit=2600):
        return max(self.cnt.values()) > limit or max(d['cnt'] for d in self.dpool) > limit

    def hard_barrier(self):
        self.barrier()
        self.bar_k += 1
        k = self.bar_k
        for e in self.engs:
            self.engs[e].sem_inc(self.bar1, 1)
        sp = self.engs['sp']
        sp.wait_ge(self.bar1, len(self.engs) * k)
        for e in self.engs:
            if self.cnt[e] > 0:
                sp.sem_clear(self.sem[e])
        for d in self.dpool:
            if d['cnt'] > 0:
                sp.sem_clear(d['sem'])
        sp.sem_inc(self.bar2, 1)
        for e in self.engs:
            if e != 'sp':
                self.engs[e].wait_ge(self.bar2, k)
        for e in self.engs:
            self.cnt[e] = 0
            self.seen[e] = {}
        for d in self.dpool:
            d['cnt'] = 0
        for t in _ALL_TILES:
            t.w = None
            t.r = []


class Prog:
    def __init__(self, nb=4, seq=2048, phases="LRMBC", dbg=False):
        self.nb, self.seq, self.phases, self.dbg = nb, seq, phases, dbg
        self.ntok = nb * seq
        nc = self.nc = bass.Bass("TRN2", target_bir_lowering=False)
        self.inp = {}
        del _ALL_TILES[:]

        def din(name, shape):
            self.inp[name] = nc.dram_tensor(name, list(shape), F32, kind="ExternalInput").ap()
            return self.inp[name]

        ntok = self.ntok
        din("x", [ntok, D])
        din("mem", [nb * NMEM, D])
        din("norm_mix_g", [D]); din("w_in", [D, P_IN]); din("b_in", [P_IN]); din("mu_shift", [RWKV_COLS])
        din("w0", [AW]); din("w_lora_up", [64, AW]); din("a0", [AW]); din("a_lora_up", [64, AW])
        din("g_lora_up", [128, AW]); din("k_k", [AW]); din("k_a", [AW]); din("r_k", [AW])
        din("lnx_g", [AW]); din("lnx_b", [AW]); din("w_branch_a", [AW, D])
        din("conv_b_w", [4, AW]); din("conv_b_b", [AW]); din("w_rg_a", [8, 64, 64]); din("b_rg_a", [AW])
        din("w_rg_x", [8, 64, 64]); din("b_rg_x", [AW]); din("lru_lambda", [AW]); din("w_branch_b", [AW, D])
        din("w_mix_out", [D, D]); din("norm_x_g", [D]); din("norm_mem_g", [D]); din("w_cq", [D, D])
        din("w_ckv", [D, 2 * D]); din("w_co", [D, D]); din("norm_ffn_g", [D]); din("w_ffn_in", [D, 2 * DFF])
        din("ffn_conv_w", [3, DFF]); din("ffn_conv_b", [DFF]); din("w_ffn_out", [DFF, D]); din("norm_final_g", [D])
        self.out = nc.dram_tensor("out", [ntok, D], F32, kind="ExternalOutput").ap()
        self.x1 = nc.dram_tensor("x1s", [ntok, D], F32).ap()
        self.x2 = nc.dram_tensor("x2s", [ntok, D], F32).ap()
        self.dbg_out = {}

        with ExitStack() as es:
            self.es = es
            self.S = Sync(nc, es)
            self.ps = [T(es.enter_context(nc.psum_tensor("ps%d" % i, [128, 512], F32)), "ps%d" % i) for i in range(8)]
            self.ps_i = 0
            self.ident = self.sb(es, "ident", [128, 128], BF16)
            self.identf = self.sb(es, "identf", [128, 128], F32)
            for idt in (self.ident, self.identf):
                self.S.op('pool', lambda e: e.memset(idt.ap[:], 0.0), [], [idt.v])
                self.S.op('pool', lambda e: e.affine_select(idt.ap[:], idt.ap[:], pattern=[[-1, 128]],
                                                           compare_op=ALU.not_equal, fill=1.0, base=0,
                                                           channel_multiplier=1), [idt.v], [idt.v])
            self.mhalf = self.sb(es, "mhalf", [128, 512], F32)
            self.memset('pool', self.mhalf.v, -0.5)
            self.phalf = self.sb(es, "phalf", [128, 512], F32)
            self.memset('pool', self.phalf.v, 0.5)
            self.yas = nc.dram_tensor("yas", [AW, ntok], BF16).ap()
            self.ybs = nc.dram_tensor("ybs", [AW, ntok], BF16).ap()
            src = {'L': self.inp["x"], 'R': self.inp["x"], 'M': self.inp["x"], 'B': self.x1, 'C': self.x2}
            dst = {'L': None, 'R': None, 'M': self.x1, 'B': self.x2, 'C': self.out}
            order = [p for p in "LRMBC" if p in phases]
            chain = [p for p in order if p in "MBC"]
            for i, p in enumerate(order):
                s_ap, d_ap = src[p], dst[p]
                if p in chain:
                    if chain.index(p) == 0:
                        s_ap = self.inp["x"]
                    if chain.index(p) == len(chain) - 1:
                        d_ap = self.out
                with ExitStack() as pes:
                    getattr(self, "phase" + p)(pes, s_ap, d_ap, final=(p == 'C'))
                    self.S.hard_barrier()
                self.S.release_dma_sems()
            self.S.barrier()

    def sb(self, es, name, shape, dt):
        return T(es.enter_context(self.nc.sbuf_tensor(name, list(shape), dt)), name)

    ps_allowed = None

    def next_ps(self):
        if self.ps_allowed is not None:
            self._psa_i = getattr(self, "_psa_i", 0) + 1
            return self.ps[self.ps_allowed[self._psa_i % len(self.ps_allowed)]]
        t = self.ps[self.ps_i]
        self.ps_i = (self.ps_i + 1) % 8
        return t

    def act(self, out, in_, func, bias=0.0, scale=1.0, accum=None):
        rd = [in_] + [a for a in (bias, scale) if isinstance(a, V)]
        wr = [out] + ([accum] if accum is not None else [])
        kw = {}
        if accum is not None:
            kw['accum_out'] = accum.ap
        return self.S.op('act', lambda e: e.activation(out=out.ap, in_=in_.ap, func=func, bias=_ap(bias),
                                                       scale=_ap(scale), **kw), rd, wr)

    def tt(self, eng, out, in0, in1, op):
        return self.S.op(eng, lambda e: e.tensor_tensor(out=out.ap, in0=in0.ap, in1=in1.ap, op=op), [in0, in1], [out])

    def ts(self, eng, out, in0, s1, s2, op0, op1=None):
        rd = [in0] + [a for a in (s1, s2) if isinstance(a, V)]
        if op1 is None:
            return self.S.op(eng, lambda e: e.tensor_scalar(out=out.ap, in0=in0.ap, scalar1=_ap(s1), scalar2=None,
                                                            op0=op0), rd, [out])
        return self.S.op(eng, lambda e: e.tensor_scalar(out=out.ap, in0=in0.ap, scalar1=_ap(s1), scalar2=_ap(s2),
                                                        op0=op0, op1=op1), rd, [out])

    def stt(self, eng, out, in0, scalar, in1, op0, op1):
        rd = [in0, in1] + ([scalar] if isinstance(scalar, V) else [])
        return self.S.op(eng, lambda e: e.scalar_tensor_tensor(out=out.ap, in0=in0.ap, scalar=_ap(scalar), in1=in1.ap,
                                                               op0=op0, op1=op1), rd, [out])

    def cp(self, eng, out, in_):
        if eng == 'act':
            return self.act(out, in_, AF.Copy)
        return self.S.op(eng, lambda e: e.tensor_copy(out=out.ap, in_=in_.ap), [in_], [out])

    def rsqrt(self, out, in_):
        shp = list(in_.ap.shape)
        if shp[-1] > 8:
            self.act(out, in_, AF.Ln)
            return self.act(out, out, AF.Exp, scale=-0.5)
        mh = self.mhalf[0:shp[0], 0:shp[-1]]
        if len(shp) == 3:
            mh = mh.un(1).bc(shp)
        return self.tt('pool', out, in_, mh, ALU.pow)

    def memset(self, eng, out, val):
        return self.S.op(eng, lambda e: e.memset(out.ap, val), [], [out])

    def mm(self, out, lhsT, rhs, start, stop, signal=None):
        if signal is None:
            signal = stop
        return self.S.op('pe', lambda e: e.matmul(out.ap, lhsT=lhsT.ap, rhs=rhs.ap, start=start, stop=stop),
                         [lhsT, rhs], [out], signal=signal)

    def tr(self, out, in_, signal=True, f32=False):
        idt = self.identf if f32 else self.ident
        n = in_.ap.shape[0]
        return self.S.op('pe', lambda e: e.transpose(out.ap, in_.ap, idt.ap[0:n, 0:n]), [in_, idt.v], [out],
                         signal=signal)

    def dma(self, q, out, in_, **kw):
        return self.S.dma(q, out, in_, **kw)

    def load_weight(self, stg, dst, src_rows, ncols, scale=None, piece=2048):
        rows = src_rows.shape[0]
        c0 = 0
        while c0 < ncols:
            n = min(piece, ncols - c0)
            st = stg[self._stg_i % len(stg)]
            eng = ('dve', 'act')[self._stg_i % 2]
            self._stg_i += 1
            self.dma('sp', st[0:rows, 0:n], src_rows[:, c0:c0 + n])
            o = dst[:, c0:c0 + n]
            i = st[0:rows, 0:n]
            if scale is None:
                self.cp(eng, o, i)
            elif eng == 'act':
                if isinstance(scale, V):
                    self.act(o, i, AF.Copy, scale=scale)
                else:
                    self.act(o, i, AF.Copy, scale=float(scale))
            else:
                self.ts(eng, o, i, scale, None, ALU.mult)
            c0 += n

    def load_cols(self, es, name, specs):
        cols = {}
        n = 0
        for k, v in specs:
            cols[k] = n
            n += (v.shape[0] + 127) // 128
        res = self.sb(es, name, [128, n], F32)
        with ExitStack() as tes:
            ngrp = (n + 127) // 128
            stage = [self.sb(tes, name + "_st%d" % g, [128, 128], F32) for g in range(ngrp)]
            for st in stage:
                self.memset('pool', st.v, 0.0)
            for k, v in specs:
                L = v.shape[0]
                m = (L + 127) // 128
                c = cols[k]
                r = 0
                while r < m:
                    g, rr = divmod(c + r, 128)
                    cnt = min(m - r, 128 - rr)
                    if L >= 128:
                        self.dma('sp', stage[g][rr:rr + cnt, :], v[r * 128:(r + cnt) * 128].rearrange("(m p) -> m p", p=128))
                    else:
                        self.dma('sp', stage[g][rr:rr + 1, 0:L], v.rearrange("(m p) -> m p", m=1))
                    r += cnt
            for g in range(ngrp):
                w = min(128, n - g * 128)
                ps = self.next_ps()
                self.tr(ps[:, 0:w], stage[g][0:w, :], f32=True)
                self.cp('dve', res[:, g * 128:g * 128 + w], ps[:, 0:w])
            self.S.barrier()
        return res, cols

    def rms_h(self, xt, nsub, ss, rstd, junk, h):
        for s in range(nsub):
            self.act(junk.v, xt[:, s, :], AF.Square, accum=ss[:, s:s + 1])
        self.ts('dve', rstd[:, 0:nsub], ss[:, 0:nsub], 1.0 / D, NORM_EPS, ALU.mult, ALU.add)
        self.rsqrt(rstd[:, 0:nsub], rstd[:, 0:nsub])
        for s in range(nsub):
            self.ts('dve', h[:, s, :], xt[:, s, :], rstd[:, s:s + 1], None, ALU.mult)

    def transpose_h(self, h, nsub, hT, evac=('act', 'dve')):
        tw = nsub * 128
        per_bank = 1024 // tw
        c = 0
        i = 0
        while c < 8:
            ps = self.next_ps()
            pb = ps.v.bitcast(BF16)
            nchunk = min(per_bank, 8 - c)
            for cc in range(nchunk):
                for s in range(nsub):
                    last = (cc == nchunk - 1 and s == nsub - 1)
                    self.tr(pb[:, cc * tw + s * 128: cc * tw + (s + 1) * 128], h[:, s, (c + cc) * 128:(c + cc + 1) * 128],
                            signal=last)
            self.cp(evac[i % len(evac)], hT[:, c:c + nchunk, :], pb[:, 0:nchunk * tw].re("p (c t) -> p c t", c=nchunk))
            c += nchunk
            i += 1

    def norm_from_dram(self, src, tok0, NS, xbufs, sst, rst, junk, h):
        for s in range(NS):
            xb = xbufs[self._xb_i % len(xbufs)]
            self._xb_i += 1
            self.dma('sp', xb.v, src[tok0 + s * 128: tok0 + (s + 1) * 128, :])
            self.act(junk.v, xb.v, AF.Square, accum=sst[s].v)
            self.ts('dve', rst[s].v, sst[s].v, 1.0 / D, NORM_EPS, ALU.mult, ALU.add)
            self.rsqrt(rst[s].v, rst[s].v)
            self.ts('dve', h[:, s, :], xb.v, rst[s].v, None, ALU.mult)

    def norm_bufs(self, es, pfx, NS):
        xbufs = [self.sb(es, pfx + "_xb%d" % i, [128, D], F32) for i in range(2)]
        sst = [self.sb(es, pfx + "_ss%d" % i, [128, 1], F32) for i in range(NS)]
        rst = [self.sb(es, pfx + "_rs%d" % i, [128, 1], F32) for i in range(NS)]
        junk = self.sb(es, pfx + "_junk", [128, D], BF16)
        self._xb_i = 0
        return xbufs, sst, rst, junk

    def phaseL(self, es, src, dst, final=False):
        S = self.S
        inp = self.inp
        TT = 512
        NS = 4
        self._stg_i = 0
        self._ev_i = 0
        cols, ci = self.load_cols(es, "l_cols", [
            ("g", inp["norm_mix_g"]), ("bin", inp["b_in"][1792:2816]),
            ("cw0", inp["conv_b_w"][0]), ("cw1", inp["conv_b_w"][1]), ("cw2", inp["conv_b_w"][2]), ("cw3", inp["conv_b_w"][3]),
            ("cbb", inp["conv_b_b"]), ("bra", inp["b_rg_a"]), ("brx", inp["b_rg_x"]), ("lam", inp["lru_lambda"])])
        hb = self.sb(es, "l_hb", [128, 8], F32)
        self.ts('dve', hb[:, 0:4], cols[:, ci["bra"]:ci["bra"] + 4], 0.5, None, ALU.mult)
        self.ts('dve', hb[:, 4:8], cols[:, ci["brx"]:ci["brx"] + 4], 0.5, None, ALU.mult)
        cA = self.sb(es, "l_cA", [128, 8], F32)
        lt = self.sb(es, "l_lt", [128, 4], F32)
        self.act(lt.v, cols[:, ci["lam"]:ci["lam"] + 4], AF.Exp, scale=-1.0)
        self.ts('dve', lt.v, lt.v, 1.0, None, ALU.add)
        self.act(lt.v, lt.v, AF.Ln)
        self.ts('dve', cA[:, 0:4], lt.v, -4.0, None, ALU.mult)
        self.ts('dve', cA[:, 4:8], lt.v, -8.0, None, ALU.mult)
        win = self.sb(es, "l_win", [128, 8, 1024], BF16)
        wg = self.sb(es, "l_wg", [128, 2, 4, 128], BF16)
        with ExitStack() as tes:
            stg = [self.sb(tes, "l_stg%d" % i, [128, 1024], F32) for i in range(3)]
            for k in range(8):
                self.load_weight(stg, win[:, k, :], inp["w_in"][k * 128:(k + 1) * 128, 1792:2816], 1024,
                                 scale=cols[:, ci["g"] + k: ci["g"] + k + 1], piece=1024)
            for gi, nm in enumerate(("w_rg_a", "w_rg_x")):
                st = stg[gi]
                self.memset('pool', st.v, 0.0)
                for blk in range(8):
                    c, hl = divmod(blk, 2)
                    self.dma('sp', st[hl * 64:(hl + 1) * 64, c * 128 + hl * 64: c * 128 + (hl + 1) * 64], inp[nm][blk])
                self.cp('dve', wg[:, gi, :, :], st[:, 0:512].re("p (c m) -> p c m", c=4))
            S.barrier()
        xbufs, sst, rst, junk = self.norm_bufs(es, "l", NS)
        h = self.sb(es, "l_h", [128, NS, D], BF16)
        hT = self.sb(es, "l_hT", [128, 8, TT], BF16)
        PX = [self.sb(es, "l_px%d" % c, [128, 3 + TT], F32) for c in range(4)]
        GY = [self.sb(es, "l_gy%d" % c, [128, TT], F32) for c in range(4)]
        YB = [self.sb(es, "l_yb%d" % c, [128, TT], BF16) for c in range(4)]
        carry = [self.sb(es, "l_cy%d" % c, [128, 1], F32) for c in range(4)]
        NW = 2
        def wk(nm, dt=F32, n=NW):
            return [self.sb(es, "l_%s%d" % (nm, i), [128, TT], dt) for i in range(n)]
        acc = wk("acc", n=4); xbb = wk("xbb", BF16, n=4); ta = wk("ta", n=4); tx = wk("tx", n=4)
        av = wk("av", n=4); a2 = wk("a2", n=4); u = wk("u", n=4); hl_ = wk("hl", n=4)
        ntile = self.ntok // TT
        tiles_per_b = self.seq // TT
        hs = [h, self.sb(es, "l_h1", [128, NS, D], BF16)]
        hTs = [hT, self.sb(es, "l_hT1", [128, 8, TT], BF16)]

        def front(i):
            self.norm_from_dram(src, i * TT, NS, xbufs, sst, rst, junk, hs[i % 2])
            self.transpose_h(hs[i % 2], NS, hTs[i % 2])

        def midA(i):
            hT_ = hTs[i % 2]
            first = (i % tiles_per_b == 0)
            if first:
                for c in range(4):
                    self.memset('pool', PX[c][:, 0:3], 0.0)
                    self.memset('pool', carry[c].v, 0.0)
            for c in range(4):
                ps = self.next_ps()
                for k in range(8):
                    self.mm(ps.v, win[:, k, c * 128:(c + 1) * 128], hT_[:, k, :], start=(k == 0), stop=(k == 7))
                self.act(PX[c][:, 3:3 + TT], ps.v, AF.Identity, bias=cols[:, ci["bin"] + c: ci["bin"] + c + 1])
            for c in range(4):
                ps = self.next_ps()
                for k in range(8):
                    self.mm(ps.v, win[:, k, 512 + c * 128: 512 + (c + 1) * 128], hT_[:, k, :], start=(k == 0), stop=(k == 7))
                self.act(GY[c].v, ps.v, AF.Gelu_apprx_tanh, bias=cols[:, ci["bin"] + 4 + c: ci["bin"] + 5 + c])
            def genA(c):
                A = acc[c]; XB = xbb[c]; TA = ta[c]; TX = tx[c]
                cw = [cols[:, ci["cw%d" % j] + c: ci["cw%d" % j] + c + 1] for j in range(4)]
                self.ts('dve', A.v, PX[c][:, 0:TT], cw[0], cols[:, ci["cbb"] + c: ci["cbb"] + c + 1], ALU.mult, ALU.add)
                yield
                for j in range(1, 4):
                    self.stt('dve', A.v, PX[c][:, j:j + TT], cw[j], A.v, ALU.mult, ALU.add)
                    yield
                self.cp('pool', PX[c][:, 0:3], PX[c][:, TT:TT + 3])
                self.cp('dve', XB.v, A.v)
                yield
                psa = self.next_ps()
                self.mm(psa.v, wg[:, 0, c, :], XB.v, start=True, stop=True)
                self.act(TA.v, psa.v, AF.Tanh, scale=0.5, bias=hb[:, c:c + 1])
                psx = self.next_ps()
                self.mm(psx.v, wg[:, 1, c, :], XB.v, start=True, stop=True)
                self.act(TX.v, psx.v, AF.Tanh, scale=0.5, bias=hb[:, 4 + c:5 + c])
                yield

            gens = [genA(c) for c in range(4)]
            alive = True
            while alive:
                alive = False
                for g in gens:
                    try:
                        next(g)
                        alive = True
                    except StopIteration:
                        pass

        def midB(i):
            tok0 = i * TT
            first = (i % tiles_per_b == 0)

            def gen(c):
                A = acc[c]; TA = ta[c]; TX = tx[c]; AV = av[c]; A2 = a2[c]; U = u[c]; HL = hl_[c]
                self.act(AV.v, TA.v, AF.Exp, scale=cA[:, c:c + 1], bias=cA[:, c:c + 1])
                self.act(A2.v, TA.v, AF.Exp, scale=cA[:, 4 + c:5 + c], bias=cA[:, 4 + c:5 + c])
                self.stt('dve', U.v, TX.v, 1.0, A.v, ALU.add, ALU.mult)
                yield
                self.ts('dve', A2.v, A2.v, -1.0, 1.0, ALU.mult, ALU.add)
                self.ts('dve', A2.v, A2.v, 1e-30, None, ALU.max)
                yield
                self.act(A2.v, A2.v, AF.Ln)
                yield
                self.act(A2.v, A2.v, AF.Exp, scale=0.5)
                if first:
                    self.memset('pool', A2[:, 0:1], 1.0)
                yield
                self.tt('dve', U.v, U.v, A2.v, ALU.mult)
                yield
                self.S.op('dve', lambda e: e.tensor_tensor_scan(HL.ap, AV.ap, U.ap, carry[c].ap, ALU.mult, ALU.add),
                          [AV.v, U.v, carry[c].v], [HL.v])
                yield
                self.cp('pool', carry[c].v, HL[:, TT - 1:TT])
                self.tt('dve', YB[c].v, HL.v, GY[c].v, ALU.mult)
                self.dma('sp', self.ybs[c * 128:(c + 1) * 128, tok0:tok0 + TT], YB[c].v)
                yield

            gens = [gen(c) for c in range(4)]
            alive = True
            while alive:
                alive = False
                for g in gens:
                    try:
                        next(g)
                        alive = True
                    except StopIteration:
                        pass

        front(0)
        for i in range(ntile):
            if S.need_reset():
                S.hard_barrier()
            midA(i)
            if i + 1 < ntile:
                front(i + 1)
            midB(i)

    def phaseR(self, es, src, dst, final=False):
        S = self.S
        inp = self.inp
        TT = 256
        NS = 2
        NQ = 4
        CH = 64
        self._stg_i = 0
        self._ev_i = 0
        cols, ci = self.load_cols(es, "r_cols", [
            ("g", inp["norm_mix_g"]), ("bin", inp["b_in"][0:RWKV_COLS]), ("mu", inp["mu_shift"]),
            ("w0", inp["w0"]), ("a0", inp["a0"]), ("kk", inp["k_k"]), ("ka", inp["k_a"]), ("rk", inp["r_k"])])

        def col(key, j):
            return cols[:, ci[key] + j: ci[key] + j + 1]

        der = self.sb(es, "r_der", [128, 32], F32)
        self.ts('dve', der[:, 0:14], cols[:, ci["mu"]:ci["mu"] + 14], -1.0, 1.0, ALU.mult, ALU.add)
        self.ts('dve', der[:, 14:18], cols[:, ci["w0"]:ci["w0"] + 4], 0.5, None, ALU.mult)
        self.ts('dve', der[:, 18:22], cols[:, ci["a0"]:ci["a0"] + 4], 0.5, None, ALU.mult)
        self.ts('dve', der[:, 22:26], cols[:, ci["ka"]:ci["ka"] + 4], 0.5, None, ALU.mult)
        self.ts('dve', der[:, 26:30], cols[:, ci["ka"]:ci["ka"] + 4], -0.5, 1.0, ALU.mult, ALU.add)
        rkb = self.sb(es, "r_rkb", [128, 4], BF16)
        self.cp('dve', rkb.v, cols[:, ci["rk"]:ci["rk"] + 4])
        m64 = [self.sb(es, "r_m64_%d" % i, [64, 64], BF16) for i in range(3)]
        masks = [self.sb(es, "r_mask%d" % i, [128, 128], BF16) for i in range(3)]
        specs = [([[1, 64]], -1, ALU.is_gt), ([[-1, 64]], 1, ALU.is_gt), ([[1, 64]], -1, ALU.is_ge)]
        for mi in range(3):
            pat, cm, op = specs[mi]
            self.memset('pool', m64[mi].v, 1.0)
            self.S.op('pool', lambda e: e.affine_select(m64[mi].ap, m64[mi].ap, pattern=pat, compare_op=op, fill=0.0,
                                                        base=0, channel_multiplier=cm), [m64[mi].v], [m64[mi].v])
            for a in range(2):
                for b_ in range(2):
                    self.dma('sp', masks[mi][a * 64:(a + 1) * 64, b_ * 64:(b_ + 1) * 64], m64[mi].v)
        mask_su, mask_sl, mask_u = masks
        bones = self.sb(es, "r_bones", [128, 128], BF16)
        self.memset('pool', bones.v, 0.0)
        self.memset('pool', bones[0:64, 0:64], 1.0)
        self.memset('pool', bones[64:128, 64:128], 1.0)
        m01 = self.sb(es, "r_m01", [128, TT], F32)
        self.memset('pool', m01.v, 1.0)
        self.memset('pool', m01.v.re("p (q s) -> p q s", s=CH)[:, :, 0:1], 0.0)
        lnxg = self.sb(es, "r_lnxg", [128, 4, 64], F32)
        lnxb = self.sb(es, "r_lnxb", [128, 4, 64], F32)
        for c in range(4):
            for hl in range(2):
                hh = 2 * c + hl
                self.dma('sp', lnxg[hl * 64:(hl + 1) * 64, c, :], inp["lnx_g"][hh * 64:(hh + 1) * 64].partition_broadcast(64))
                self.dma('sp', lnxb[hl * 64:(hl + 1) * 64, c, :], inp["lnx_b"][hh * 64:(hh + 1) * 64].partition_broadcast(64))
        win = self.sb(es, "r_win", [128, 8, RWKV_COLS], BF16)
        wlora = self.sb(es, "r_wlora", [128, 2, AW], BF16)
        gup = self.sb(es, "r_gup", [128, AW], BF16)
        with ExitStack() as tes:
            stg = [self.sb(tes, "r_stg%d" % i, [128, RWKV_COLS], F32) for i in range(3)]
            for k in range(8):
                self.load_weight(stg, win[:, k, :], inp["w_in"][k * 128:(k + 1) * 128, 0:RWKV_COLS], RWKV_COLS,
                                 scale=cols[:, ci["g"] + k: ci["g"] + k + 1], piece=RWKV_COLS)
            st = stg[0]
            self.memset('pool', st[:, 0:2 * AW], 0.0)
            self.dma('sp', st[0:64, 0:AW], inp["w_lora_up"])
            self.dma('sp', st[64:128, AW:2 * AW], inp["a_lora_up"])
            self.cp('dve', wlora.v, st[:, 0:2 * AW].re("p (a m) -> p a m", a=2))
            st = stg[1]
            self.dma('sp', st[:, 0:AW], inp["g_lora_up"])
            self.cp('dve', gup.v, st[:, 0:AW])
            S.barrier()
        xbufs, sst, rst, junk = self.norm_bufs(es, "r", NS)
        h = self.sb(es, "r_h", [128, NS, D], BF16)
        hT = self.sb(es, "r_hT", [128, 8, TT], BF16)
        pcarry = [self.sb(es, "r_pc%d" % m, [128, 1], F32) for m in range(14)]
        Sx = [self.sb(es, "r_s%d" % m, [128, TT], F32) for m in range(14)]
        ltmp = [self.sb(es, "r_lt%d" % i, [128, TT], F32) for i in range(2)]
        LB = self.sb(es, "r_lb", [128, TT], BF16)
        SGd = self.sb(es, "r_sgd", [128, NQ, 128], BF16)
        NW = 2

        def wk(nm, dt=F32):
            return [self.sb(es, "r_%s%d" % (nm, i), [128, TT], dt) for i in range(NW)]

        logw = [self.sb(es, "r_logw%d" % i, [128, TT], F32) for i in range(4)]
        ta = [self.sb(es, "r_ta%d" % i, [128, TT], F32) for i in range(4)]
        cum = [self.sb(es, "r_cum%d" % i, [128, TT], F32) for i in range(4)]
        egm1 = wk("egm1"); eg = wk("eg"); eig = wk("eig"); egc = wk("egc")
        kk = wk("kk"); kk2 = wk("kk2", BF16); rn = wk("rn"); kkn = wk("kkn"); kmod = wk("kmod"); bv = wk("bv")
        names = ["AT", "BT", "KT", "RT", "BGT", "KGT", "VB", "RK"]
        EXP = {n: [self.sb(es, "r_%s%d" % (n, c), [128, NQ, 128], BF16) for c in range(4)] for n in names}
        for n in names:
            for c in range(4):
                self.memset('pool', EXP[n][c].v, 0.0)
        GC = self.sb(es, "r_gc", [128, 4, NQ], F32)
        BKG = [self.sb(es, "r_bkg%d" % c, [128, 2, NQ, 128], BF16) for c in range(4)]
        VstAll = self.sb(es, "r_vst", [128, NQ, 4, 64], BF16)
        Nb = [[self.sb(es, "r_nb%d_%d" % (q, i), [128, 4, 128], BF16) for i in range(2)] for q in range(NQ)]
        Lb = [[self.sb(es, "r_lb%d_%d" % (q, i), [128, 4, 128], BF16) for i in range(2)] for q in range(NQ)]
        Pb = [self.sb(es, "r_pb%d" % q, [128, 4, 128], BF16) for q in range(NQ)]
        LakT = [self.sb(es, "r_lak%d" % q, [128, 4, 128], BF16) for q in range(NQ)]
        MrbT = [self.sb(es, "r_mrb%d" % q, [128, 4, 128], BF16) for q in range(NQ)]
        MrkT = [self.sb(es, "r_mrk%d" % q, [128, 4, 128], BF16) for q in range(NQ)]
        H = self.sb(es, "r_H", [128, 4, 64], F32)
        Hb = self.sb(es, "r_Hb", [128, 4, 64], BF16)
        Xb = [self.sb(es, "r_Xb%d" % i, [128, 4, 64], BF16) for i in range(2)]
        Ub = [self.sb(es, "r_Ub%d" % i, [128, 4, 64], BF16) for i in range(2)]

        def yt(nm, shape=(128, 4, 64), dt=F32):
            return [self.sb(es, "r_%s%d" % (nm, i), list(shape), dt) for i in range(2)]

        Yv = yt("Yv"); Ysq = yt("Ysq"); Yn = yt("Yn"); Bn = yt("Bn"); Yf = yt("Yf", dt=BF16)
        st1 = yt("st1", (128, 4)); st2 = yt("st2", (128, 4)); mean = yt("mean", (128, 4)); var = yt("var", (128, 4))
        YT = [self.sb(es, "r_YT%d" % i, [64, 4, 2, TT], BF16) for i in range(2)]
        ident = self.ident
        ntile = self.ntok // TT
        tiles_per_b = self.seq // TT
        yas_v = self.yas.rearrange("(c hl i) t -> i c hl t", hl=2, i=64)

        PWraw = [es.enter_context(self.nc.sbuf_tensor("r_pwx%d" % i, [128, 1 + TT], F32)) for i in range(3)]
        PWh = [T(r[:, 0:1], "r_pwh") for r in PWraw]
        PWb = [T(r[:, 1:1 + TT], "r_pwb") for r in PWraw]

        def s12_pieces(it):
            tok0 = it * TT

            def p0():
                if it % tiles_per_b == 0:
                    for m in range(14):
                        self.memset('pool', pcarry[m].v, 0.0)
                self.norm_from_dram(src, tok0, NS, xbufs, sst, rst, junk, h)
                self.transpose_h(h, NS, hT)

            def projA(m):
                def f():
                    ps = self.next_ps()
                    for k in range(8):
                        self.mm(ps[:, 0:TT], win[:, k, m * 128:(m + 1) * 128], hT[:, k, :], start=(k == 0), stop=(k == 7))
                    pi = m % 3
                    self.cp('dve', PWh[pi].v, pcarry[m].v)
                    self.act(PWb[pi].v, ps[:, 0:TT], AF.Identity, bias=col("bin", m))
                    tmp = ltmp[m % 2]
                    self.act(tmp.v, V([PWh[pi], PWb[pi]], PWraw[pi][:, 0:TT]), AF.Copy, scale=col("mu", m))
                return f

            def projB(m):
                def f():
                    pi = m % 3
                    tmp = ltmp[m % 2]
                    self.cp('dve', pcarry[m].v, PWb[pi][:, TT - 1:TT])
                    self.stt('dve', Sx[m].v, PWb[pi].v, der[:, m:m + 1], tmp.v, ALU.mult, ALU.add)
                return f

            def both(fa, fb):
                def f():
                    if fb is not None:
                        fb()
                    if fa is not None:
                        fa()
                return f

            chain_ = [both(projA(0), None)] + [both(projA(m), projB(m - 1)) for m in range(1, 14)] + [both(None, projB(13))]
            return [p0] + chain_

        def stage3(it):
            if it % tiles_per_b == 0:
                self.memset('pool', H.v, 0.0)
                self.memset('pool', Hb.v, 0.0)
            for _ in range(1):
                if RSTOP == 1:
                    return
                self.act(LB[0:64, :], Sx[12][0:64, :], AF.Tanh)
                self.cp('act', LB[64:128, :], Sx[12][64:128, :])
                tmp = ltmp[0]
                self.act(tmp.v, Sx[13].v, AF.Tanh, scale=0.5)
                for hl in range(2):
                    self.ts('dve', SGd[:, :, hl * 64:(hl + 1) * 64], tmp.v.re("p (q s) -> p q s", s=CH), 0.5, 0.5, ALU.mult, ALU.add)
                if RSTOP == 21:
                    return
                for c in range(4):
                    LW = logw[c]; TA = ta[c]; CU = cum[c]
                    ps = self.next_ps()
                    self.mm(ps[:, 0:TT], wlora[:, 0, c * 128:(c + 1) * 128], LB.v, start=True, stop=True)
                    self.mm(ps[:, TT:2 * TT], wlora[:, 1, c * 128:(c + 1) * 128], LB.v, start=True, stop=True)
                    self.act(LW.v, ps[:, 0:TT], AF.Tanh, scale=0.5, bias=der[:, 14 + c:15 + c])
                    self.act(TA.v, ps[:, TT:2 * TT], AF.Tanh, scale=0.5, bias=der[:, 18 + c:19 + c])
                    self.ts('dve', LW.v, LW.v, 1.0, -0.30326532985631671, ALU.add, ALU.mult)
                    self.S.op('dve', lambda e: e.tensor_tensor_scan(CU.ap, m01.ap, LW.ap, 0.0, ALU.mult, ALU.add),
                              [m01.v, LW.v], [CU.v])
                def stageB(c):
                    w = it * 4 + c
                    r_, k_, v_ = Sx[c], Sx[4 + c], Sx[8 + c]
                    LW = logw[c]; TA = ta[c]; CU = cum[c]
                    E1 = egm1[w % NW]; EG = eg[w % NW]; EI = eig[w % NW]
                    EC = egc[w % NW]; KK = kk[w % NW]; K2 = kk2[w % NW]; RN = rn[w % NW]; KN = kkn[w % NW]; KM = kmod[w % NW]
                    BV = bv[w % NW]
                    cu3 = CU.v.re("p (q s) -> p q s", s=CH)
                    cuC = cu3[:, :, CH - 1:CH]
                    self.act(KK.v, k_.v, AF.Copy, scale=col("kk", c))
                    self.tt('pool', E1.v, CU.v, LW.v, ALU.subtract)
                    yield
                    self.act(K2.v, KK.v, AF.Square)
                    self.tt('pool', EC.v.re("p (q s) -> p q s", s=CH), cuC.bc([128, NQ, CH]), cu3, ALU.subtract)
                    yield
                    psn = self.next_ps()
                    self.mm(psn[:, 0:TT], bones.v, K2.v, start=True, stop=True)
                    self.act(E1.v, E1.v, AF.Exp)
                    yield
                    self.act(RN.v, psn[:, 0:TT], AF.Ln)
                    yield
                    self.act(RN.v, RN.v, AF.Exp, scale=-0.5)
                    yield
                    self.tt('pool', KN.v, KK.v, RN.v, ALU.mult)
                    self.act(EI.v, CU.v, AF.Exp, scale=-1.0)
                    yield
                    self.act(EC.v, EC.v, AF.Exp)
                    self.act(KM.v, TA.v, AF.Identity, scale=der[:, 22 + c:23 + c], bias=der[:, 26 + c:27 + c])
                    yield
                    self.stt('dve', BV.v, TA.v, 1.0, KN.v, ALU.add, ALU.mult)
                    self.tt('dve', KM.v, KM.v, k_.v, ALU.mult)
                    self.act(EG.v, CU.v, AF.Exp)
                    self.act(GC[:, c, :], cuC.re("p q o -> p (q o)"), AF.Exp)
                    yield
                    for hl in range(2):
                        P_ = slice(hl * 64, (hl + 1) * 64)

                        def hv(t):
                            return t[P_, :].re("p (q s) -> p q s", s=CH)

                        def ov(n):
                            return EXP[n][c][P_, :, hl * 64:(hl + 1) * 64]

                        self.stt('dve', ov("AT"), hv(KN), -1.0, hv(E1), ALU.mult, ALU.mult)
                        self.tt('pool', ov("KT"), hv(KM), hv(EI), ALU.mult)
                        self.cp('act', ov("VB"), hv(v_))
                        yield
                        self.stt('dve', ov("BT"), hv(BV), 0.5, hv(EI), ALU.mult, ALU.mult)
                        self.tt('pool', ov("KGT"), hv(KM), hv(EC), ALU.mult)
                        yield
                        self.stt('dve', ov("BGT"), hv(BV), 0.5, hv(EC), ALU.mult, ALU.mult)
                        self.tt('pool', ov("RT"), hv(r_), hv(EG), ALU.mult)
                        yield
                        self.tt('pool', ov("RK"), hv(r_), hv(KM), ALU.mult)
                        yield

                lockstep([stageB(0), stageB(1)])
                lockstep([stageB(2), stageB(3), gen45((0, 1))])
                lockstep([gen45((2,)), gen45((3,))])

        def gen45(cs):
            for c in cs:
                psA = self.next_ps()
                pbA = psA.v.bitcast(BF16)
                for gi, n in enumerate(("BGT", "KGT")):
                    for q in range(NQ):
                        self.tr(pbA[:, (gi * NQ + q) * 128:(gi * NQ + q + 1) * 128], EXP[n][c][:, q, :], signal=(gi == 1 and q == NQ - 1))
                self.evac(BKG[c].v, pbA.re("p (g q m) -> p g q m", g=2, q=NQ))
                psB = self.next_ps()
                pbB = psB.v.bitcast(BF16)
                for q in range(NQ):
                    self.tr(pbB[:, q * 128:(q + 1) * 128], EXP["VB"][c][:, q, :], signal=(q == NQ - 1))
                for hl in range(2):
                    self.evac(VstAll[hl * 64:(hl + 1) * 64, :, c, :],
                              pbB[hl * 64:(hl + 1) * 64, 0:NQ * 128].re("p (q m) -> p q m", q=NQ)[:, :, hl * 64:(hl + 1) * 64])
                yield
            for c in cs:
                combos = [("BT", "AT", mask_su, Nb[c][0]), ("AT", "BT", mask_sl, Lb[c][0]), ("KT", "AT", mask_su, LakT[c]),
                          ("BT", "RT", mask_u, MrbT[c]), ("KT", "RT", mask_u, MrkT[c])]
                for (ln, rn_, mk, dstt) in combos:
                    ps = self.next_ps()
                    for q in range(NQ):
                        self.mm(ps[:, q * 128:(q + 1) * 128], EXP[ln][c][:, q, :], EXP[rn_][c][:, q, :], start=True, stop=True,
                                signal=(q == NQ - 1))
                    self.tt('dve', dstt.v, ps.v.re("p (q m) -> p q m", q=NQ), mk.v.un(1).bc([128, NQ, 128]), ALU.mult)
                    yield
                self.tt('pool', Pb[c].v, Nb[c][0].v, ident.v.un(1).bc([128, NQ, 128]), ALU.add)
            cur = 0
            for lvl in range(1, 6):
                nxt = 1 - cur
                for c in cs:
                    if lvl < 5:
                        ps = self.next_ps()
                        for q in range(NQ):
                            self.mm(ps[:, q * 128:(q + 1) * 128], Lb[c][cur][:, q, :], Nb[c][cur][:, q, :], start=True, stop=True,
                                    signal=(q == NQ - 1))
                        self.cp('act', Nb[c][nxt].v, ps.v.re("p (q m) -> p q m", q=NQ))
                    ps = self.next_ps()
                    for q in range(NQ):
                        self.mm(ps[:, q * 128:(q + 1) * 128], Nb[c][cur][:, q, :], Lb[c][cur][:, q, :], start=True, stop=True,
                                signal=(q == NQ - 1))
                    self.cp('act', Lb[c][nxt].v, ps.v.re("p (q m) -> p q m", q=NQ))
                    yield
                for c in cs:
                    ps = self.next_ps()
                    for q in range(NQ):
                        self.mm(ps[:, q * 128:(q + 1) * 128], Lb[c][nxt][:, q, :], Pb[c][:, q, :], start=True, stop=True,
                                signal=(q == NQ - 1))
                    self.tt('dve', Pb[c].v, Pb[c].v, ps.v.re("p (q m) -> p q m", q=NQ), ALU.add)
                    yield
                cur = nxt

        def lockstep(gens):
            alive = True
            while alive:
                alive = False
                for g in gens:
                    try:
                        next(g)
                        alive = True
                    except StopIteration:
                        pass

        PS = self.ps

        def crit(it, q):
            w = it * NQ + q
            xb_, ub_ = Xb[w % 2], Ub[w % 2]
            psx, psu, psh, psy = PS[0], PS[1], PS[2], PS[3 + (q % 2)]
            for c in range(4):
                self.mm(psx[:, c * 64:(c + 1) * 64], EXP["AT"][c][:, q, :], Hb[:, c, :], start=True, stop=False, signal=False)
                self.mm(psx[:, c * 64:(c + 1) * 64], LakT[c][:, q, :], VstAll[:, q, c, :], start=False, stop=True, signal=(c == 3))
            self.cp('act', xb_.v, psx[:, 0:256].re("p (c m) -> p c m", c=4))
            yield
            for c in range(4):
                self.mm(psu[:, c * 64:(c + 1) * 64], Pb[c][:, q, :], xb_[:, c, :], start=True, stop=True, signal=(c == 3))
            self.cp('dve', ub_.v, psu[:, 0:256].re("p (c m) -> p c m", c=4))
            yield
            for c in range(4):
                o = psh[:, c * 64:(c + 1) * 64]
                self.mm(o, BKG[c][:, 0, q, :], ub_[:, c, :], start=True, stop=False, signal=False)
                self.mm(o, BKG[c][:, 1, q, :], VstAll[:, q, c, :], start=False, stop=True, signal=(c == 3))
            for c in range(4):
                o = psy[:, c * 64:(c + 1) * 64]
                self.mm(o, EXP["RT"][c][:, q, :], Hb[:, c, :], start=True, stop=False, signal=False)
                self.mm(o, MrbT[c][:, q, :], ub_[:, c, :], start=False, stop=False, signal=False)
                self.mm(o, MrkT[c][:, q, :], VstAll[:, q, c, :], start=False, stop=True, signal=False)
            for c in range(4):
                self.mm(psy[:, 256 + c:257 + c], EXP["RK"][c][:, q, :], rkb[:, c:c + 1], start=True, stop=True, signal=(c == 3))
            self.tt('dve', H.v, H.v, GC[:, :, q:q + 1].bc([128, 4, 64]), ALU.mult)
            self.tt('dve', H.v, H.v, psh[:, 0:256].re("p (c m) -> p c m", c=4), ALU.add)
            self.cp('act', Hb.v, H.v)
            yield

        def post_a(it, q):
            w = it * NQ + q
            psy = PS[3 + (q % 2)]
            psg = PS[5]
            self.mm(psg.v, SGd[:, q, :], gup.v, start=True, stop=True)
            Y = Yv[w % 2]; Y2 = Ysq[w % 2]; YN = Yn[w % 2]; BN = Bn[w % 2]; YF = Yf[w % 2]
            s1 = st1[w % 2]; s2 = st2[w % 2]; mn = mean[w % 2]; vr = var[w % 2]
            py3 = psy[:, 0:256].re("p (c m) -> p c m", c=4)
            self.cp('act', Y.v, py3)
            self.act(Y2.v, py3, AF.Square)
            self.tt('dve', BN.v, VstAll[:, q, :, :], psy[:, 256:260].un(2).bc([128, 4, 64]), ALU.mult)
            self.S.op('dve', lambda e: e.reduce_sum(out=s1.ap, in_=Y.ap, axis=AX.X), [Y.v], [s1.v])
            self.S.op('dve', lambda e: e.reduce_sum(out=s2.ap, in_=Y2.ap, axis=AX.X), [Y2.v], [s2.v])
            self.ts('dve', mn.v, s1.v, 1.0 / 64.0, None, ALU.mult)
            self.tt('dve', vr.v, mn.v, mn.v, ALU.mult)
            self.stt('dve', vr.v, s2.v, 1.0 / 64.0, vr.v, ALU.mult, ALU.subtract)
            self.ts('dve', vr.v, vr.v, LNX_EPS, None, ALU.add)
            self.act(vr.v, vr.v, AF.Ln)
            self.act(vr.v, vr.v, AF.Exp, scale=-0.5)
            self.tt('dve', YN.v, Y.v, mn.v.un(2).bc([128, 4, 64]), ALU.subtract)
            self.tt('dve', YN.v, YN.v, vr.v.un(2).bc([128, 4, 64]), ALU.mult)
            self.tt('pool', YN.v, YN.v, lnxg.v, ALU.mult)
            self.tt('pool', YN.v, YN.v, lnxb.v, ALU.add)
            self.tt('pool', YN.v, YN.v, BN.v, ALU.add)
            for hl in range(2):
                P_ = slice(hl * 64, (hl + 1) * 64)
                gv = psg[P_, :].re("p (c h m) -> p c h m", c=4, h=2)[:, :, hl, :]
                self.tt('dve', YF[P_, :, :], YN[P_, :, :], gv, ALU.mult)

        def post_b(it, q, YTt):
            w = it * NQ + q
            YF = Yf[w % 2]
            pst = self.next_ps()
            pbt = pst.v.bitcast(BF16)
            for c in range(4):
                self.tr(pbt[0:64, c * 128:(c + 1) * 128], YF[:, c, :], signal=(c == 3))
            self.evac(YTt[:, :, :, q * CH:(q + 1) * CH], pbt[0:64, 0:512].re("p (c h t) -> p c h t", c=4, h=2))

        for p in s12_pieces(0):
            p()
        for it in range(ntile):
            tok0 = it * TT
            if S.need_reset():
                S.hard_barrier()
            stage3(it)
            pieces = s12_pieces(it + 1) if it + 1 < ntile else []
            YTt = YT[it % 2]
            self.ps_allowed = [6, 7]

            def filler():
                if pieces:
                    pieces.pop(0)()

            for q in range(NQ + 2):
                if q < NQ:
                    for _ in crit(it, q):
                        filler()
                if q == NQ:
                    while pieces:
                        pieces.pop(0)()
                if 1 <= q <= NQ:
                    post_a(it, q - 1)
                if q >= 2:
                    post_b(it, q - 2, YTt)
            self.ps_allowed = None
            self.dma('sp', yas_v[:, :, :, tok0:tok0 + TT], YTt.v)
        self._sbuf_left = self.nc.sbuf_bytes_remaining

    def phaseM(self, es, src, dst, final=False):
        S = self.S
        inp = self.inp
        TT = 512
        NS = 4
        self._stg_i = 0
        self._ev_i = 0
        use_a = 'R' in self.phases
        cols, ci = self.load_cols(es, "m_cols", [("g", inp["norm_mix_g"]), ("bin", inp["b_in"][2816:4864])])
        hb = self.sb(es, "m_hb", [128, 16], F32)
        self.ts('dve', hb.v, cols[:, ci["bin"]:ci["bin"] + 16], 0.5, None, ALU.mult)
        win = self.sb(es, "m_win", [128, 8, 2048], BF16)
        wa = self.sb(es, "m_wa", [128, 4, D], BF16)
        wb = self.sb(es, "m_wb", [128, 4, D], BF16)
        wmo = self.sb(es, "m_wmo", [128, 8, D], BF16)
        with ExitStack() as tes:
            stg = [self.sb(tes, "m_stg%d" % i, [128, 2048], F32) for i in range(3)]
            for k in range(8):
                self.load_weight(stg, win[:, k, :], inp["w_in"][k * 128:(k + 1) * 128, 2816:4864], 2048,
                                 scale=cols[:, ci["g"] + k: ci["g"] + k + 1])
                self.load_weight(stg, wmo[:, k, :], inp["w_mix_out"][k * 128:(k + 1) * 128, :], D)
            for c in range(4):
                self.load_weight(stg, wa[:, c, :], inp["w_branch_a"][c * 128:(c + 1) * 128, :], D, scale=0.5)
                self.load_weight(stg, wb[:, c, :], inp["w_branch_b"][c * 128:(c + 1) * 128, :], D, scale=0.25)
            S.barrier()
        xbufs, sst, rst, junk = self.norm_bufs(es, "m", NS)
        xrb = [self.sb(es, "m_xr%d" % i, [128, D], F32) for i in range(NS)]
        hs = [self.sb(es, "m_h%d" % i, [128, NS, D], BF16) for i in range(2)]
        hTs = [self.sb(es, "m_hT%d" % i, [128, 8, TT], BF16) for i in range(2)]
        TG = [self.sb(es, "m_tg%d" % m, [128, TT], BF16) for m in range(16)]
        YA = [self.sb(es, "m_ya%d" % i, [128, 4, TT], BF16) for i in range(2)]
        YB = [self.sb(es, "m_yb%d" % i, [128, 4, TT], BF16) for i in range(2)]
        MA = [self.sb(es, "m_ma%d" % i, [128, TT], F32) for i in range(3)]
        MG = [self.sb(es, "m_mg%d" % m, [128, TT], BF16) for m in range(8)]
        self._mb = [self.sb(es, "m_mb%d" % i, [128, TT], F32) for i in range(2)]
        ntile = self.ntok // TT

        def load_y(i):
            if i >= ntile:
                return
            tok0 = i * TT
            if use_a:
                self.dma('sp', YA[i % 2].v, self.yas[:, tok0:tok0 + TT].rearrange("(c p) t -> p c t", p=128))
            self.dma('sp', YB[i % 2].v, self.ybs[:, tok0:tok0 + TT].rearrange("(c p) t -> p c t", p=128))

        def front(i):
            self.norm_from_dram(src, i * TT, NS, xbufs, sst, rst, junk, hs[i % 2])
            self.transpose_h(hs[i % 2], NS, hTs[i % 2])

        def mid(i):
            hT = hTs[i % 2]
            ya, yb = YA[i % 2], YB[i % 2]
            for m in range(16):
                if m == 6 and i + 1 < ntile:
                    self.norm_from_dram(src, (i + 1) * TT, NS, xbufs, sst, rst, junk, hs[(i + 1) % 2])
                ps = self.next_ps()
                for k in range(8):
                    self.mm(ps.v, win[:, k, m * 128:(m + 1) * 128], hT[:, k, :], start=(k == 0), stop=(k == 7))
                self.act(TG[m].v, ps.v, AF.Tanh, scale=0.5, bias=hb[:, m:m + 1])
            for m in range(8):
                ma = MA[m % 3]
                if use_a:
                    ps = self.next_ps()
                    for c in range(4):
                        self.mm(ps.v, wa[:, c, m * 128:(m + 1) * 128], ya[:, c, :], start=(c == 0), stop=(c == 3))
                    self.stt('dve', ma.v, TG[m].v, 1.0, ps.v, ALU.add, ALU.mult)
                ps2 = self.next_ps()
                for c in range(4):
                    self.mm(ps2.v, wb[:, c, m * 128:(m + 1) * 128], yb[:, c, :], start=(c == 0), stop=(c == 3))
                if use_a:
                    mb = self._mb[m % 2]
                    self.stt('dve', mb.v, TG[8 + m].v, 1.0, ps2.v, ALU.add, ALU.mult)
                    self.tt('pool', MG[m].v, mb.v, ma.v, ALU.add)
                else:
                    self.stt('dve', MG[m].v, TG[8 + m].v, 1.0, ps2.v, ALU.add, ALU.mult)
            if i + 1 < ntile:
                self.transpose_h(hs[(i + 1) % 2], NS, hTs[(i + 1) % 2])

        def back(i):
            tok0 = i * TT
            for s in range(NS):
                self.dma('sp', xrb[s].v, src[tok0 + s * 128: tok0 + (s + 1) * 128, :])
            for s in range(NS):
                pss = [self.next_ps(), self.next_ps()]
                for half in range(2):
                    for m in range(8):
                        self.mm(pss[half].v, MG[m][:, s * 128:(s + 1) * 128], wmo[:, m, half * 512:(half + 1) * 512],
                                start=(m == 0), stop=(m == 7))
                xb = xrb[s]
                for half in range(2):
                    self.tt('dve', xb[:, half * 512:(half + 1) * 512], xb[:, half * 512:(half + 1) * 512], pss[half].v, ALU.add)
                self.dma('sp', dst[tok0 + s * 128: tok0 + (s + 1) * 128, :], xb.v)

        load_y(0)
        front(0)
        for i in range(ntile):
            if S.need_reset():
                S.hard_barrier()
            load_y(i + 1)
            mid(i)
            back(i)

    def evac(self, out, in_, i=None):
        if i is None:
            i = self._ev_i
            self._ev_i += 1
        return self.cp(('act', 'dve')[i % 2], out, in_)

    def phaseB(self, es, src, dst, final=False):
        S = self.S
        inp = self.inp
        TT = 512
        NS = 4
        self._stg_i = 0
        self._ev_i = 0
        cols, ci = self.load_cols(es, "b_cols", [("gx", inp["norm_x_g"]), ("gm", inp["norm_mem_g"])])
        gq = self.sb(es, "b_gq", [128, 8], F32)
        self.ts('dve', gq.v, cols[:, ci["gx"]:ci["gx"] + 8], 1.0 / 16.0, None, ALU.mult)
        wq = self.sb(es, "b_wq", [128, 8, D], BF16)
        wo = self.sb(es, "b_wo", [128, 8, D], BF16)
        kT = [self.sb(es, "b_kT%d" % b, [128, 8, NMEM], BF16) for b in range(self.nb)]
        Vt = [self.sb(es, "b_V%d" % b, [128, 2, D], BF16) for b in range(self.nb)]
        xts = [self.sb(es, "b_xt%d" % i, [128, NS, D], F32) for i in range(2)]
        h = self.sb(es, "b_h", [128, NS, D], BF16)
        hT = self.sb(es, "b_hT", [128, 8, TT], BF16)
        junk = self.sb(es, "b_junk", [128, D], BF16)
        ss = self.sb(es, "b_ss", [128, 4], F32)
        rstd = self.sb(es, "b_rstd", [128, 4], F32)
        with ExitStack() as tes:
            stg = [self.sb(tes, "b_stg%d" % i, [128, 2048], F32) for i in range(3)]
            wkv = self.sb(tes, "b_wkv", [128, 8, 2 * D], BF16)
            for k in range(8):
                self.load_weight(stg, wkv[:, k, :], inp["w_ckv"][k * 128:(k + 1) * 128, :], 2 * D,
                                 scale=cols[:, ci["gm"] + k: ci["gm"] + k + 1])
            for k in range(8):
                self.load_weight(stg, wq[:, k, :], inp["w_cq"][k * 128:(k + 1) * 128, :], D, scale=gq[:, k:k + 1])
                self.load_weight(stg, wo[:, k, :], inp["w_co"][k * 128:(k + 1) * 128, :], D)
            for b in range(self.nb):
                mt = xts[b % 2]
                self.dma('sp', mt[:, 0:2, :], inp["mem"][b * NMEM:(b + 1) * NMEM, :].rearrange("(s p) d -> p s d", p=128))
                self.rms_h(mt, 2, ss, rstd, junk, h)
                self.transpose_h(h, 2, hT[:, :, 0:NMEM])
                for m in range(8):
                    if m % 2 == 0:
                        ps = self.next_ps()
                    o = (m % 2) * NMEM
                    for k in range(8):
                        self.mm(ps[:, o:o + NMEM], wkv[:, k, m * 128:(m + 1) * 128], hT[:, k, 0:NMEM], start=(k == 0), stop=(k == 7))
                    if m % 2 == 1:
                        self.evac(kT[b][:, m - 1:m + 1, :], ps.v.re("p (a t) -> p a t", a=2))
                for mc in range(2):
                    for half in range(2):
                        ps = self.next_ps()
                        for k in range(8):
                            self.mm(ps.v, hT[:, k, mc * 128:(mc + 1) * 128], wkv[:, k, D + half * 512: D + (half + 1) * 512],
                                    start=(k == 0), stop=(k == 7))
                        self.evac(Vt[b][:, mc, half * 512:(half + 1) * 512], ps.v)
            S.barrier()
        qT = self.sb(es, "b_qT", [128, 8, TT], BF16)
        oT = self.sb(es, "b_oT", [128, 8, TT], BF16)
        prT = self.sb(es, "b_prT", [128, 2, 4, TT], BF16)
        prs = [self.sb(es, "b_pr%d" % i, [128, 4, NMEM], BF16) for i in range(2)]
        prn = [self.sb(es, "b_prn%d" % i, [128, 4, NMEM], BF16) for i in range(2)]
        mx = [self.sb(es, "b_mx%d" % i, [128, 4], F32) for i in range(2)]
        sm = [self.sb(es, "b_sm%d" % i, [128, 4], F32) for i in range(2)]
        hs = [h, self.sb(es, "b_h1", [128, NS, D], BF16)]
        hTs = [hT, self.sb(es, "b_hT1", [128, 8, TT], BF16)]
        sss = [ss, self.sb(es, "b_ss1", [128, 4], F32)]
        rstds = [rstd, self.sb(es, "b_rstd1", [128, 4], F32)]
        ntile = self.ntok // TT
        tiles_per_b = self.seq // TT

        def load(i):
            if i < ntile:
                self.dma('sp', xts[i % 2].v, src[i * TT:(i + 1) * TT, :].rearrange("(s p) d -> p s d", p=128))

        def front(i):
            self.rms_h(xts[i % 2], NS, sss[i % 2], rstds[i % 2], junk, hs[i % 2])
            self.transpose_h(hs[i % 2], NS, hTs[i % 2])

        def mid(i):
            b = i // tiles_per_b
            hT_ = hTs[i % 2]
            for m in range(8):
                ps = self.next_ps()
                for k in range(8):
                    self.mm(ps.v, wq[:, k, m * 128:(m + 1) * 128], hT_[:, k, :], start=(k == 0), stop=(k == 7))
                self.evac(qT[:, m, :], ps.v)
            def scores(s):
                banks = [self.ps[2 * (s % 2)], self.ps[2 * (s % 2) + 1]]
                for hh in range(4):
                    ps = banks[hh // 2]
                    o = (hh % 2) * NMEM
                    for c in range(2):
                        self.mm(ps[:, o:o + NMEM], qT[:, 2 * hh + c, s * 128:(s + 1) * 128], kT[b][:, 2 * hh + c, :],
                                start=(c == 0), stop=(c == 1))
                return banks

            def sm_gen(s, banks):
                pr = prs[s % 2]; pn = prn[s % 2]; mxs = mx[s % 2]; sms = sm[s % 2]
                for g in range(2):
                    self.S.op('dve', lambda e: e.reduce_max(out=mxs.ap[:, 2 * g:2 * g + 2],
                                                            in_=banks[g].ap.rearrange("p (a t) -> p a t", a=2), axis=AX.X),
                              [banks[g].v], [mxs.v])
                self.ts('dve', mxs.v, mxs.v, -1.0, None, ALU.mult)
                yield
                for hh in range(4):
                    ps = banks[hh // 2]
                    o = (hh % 2) * NMEM
                    self.act(pr[:, hh, :], ps[:, o:o + NMEM], AF.Exp, bias=mxs[:, hh:hh + 1], accum=sms[:, hh:hh + 1])
                yield
                self.S.op('dve', lambda e: e.reciprocal(out=sms.ap, in_=sms.ap), [sms.v], [sms.v])
                self.tt('dve', pn.v, pr.v, sms.v.un(2).bc([128, 4, NMEM]), ALU.mult)
                yield
                ps = self.next_ps()
                pb = ps.v.bitcast(BF16)
                for hh in range(4):
                    for mc in range(2):
                        self.tr(pb[:, (hh * 2 + mc) * 128:(hh * 2 + mc + 1) * 128], pn[:, hh, mc * 128:(mc + 1) * 128],
                                signal=(hh == 3 and mc == 1))
                self.evac(prT[:, :, :, s * 128:(s + 1) * 128], pb.re("p (h m t) -> p m h t", h=4, m=2))
                yield

            self.ps_allowed = [4, 5, 6, 7]
            for pair in ((0, 1), (2, 3)):
                gens = [sm_gen(s, scores(s)) for s in pair]
                alive = True
                while alive:
                    alive = False
                    for g in gens:
                        try:
                            next(g)
                            alive = True
                        except StopIteration:
                            pass
                if pair == (0, 1) and i + 1 < ntile:
                    self.rms_h(xts[(i + 1) % 2], NS, sss[(i + 1) % 2], rstds[(i + 1) % 2], junk, hs[(i + 1) % 2])
            self.ps_allowed = None
            for hh in range(4):
                for c in range(2):
                    ps = self.next_ps()
                    for mc in range(2):
                        self.mm(ps.v, Vt[b][:, mc, hh * 256 + c * 128: hh * 256 + (c + 1) * 128], prT[:, mc, hh, :],
                                start=(mc == 0), stop=(mc == 1))
                    self.evac(oT[:, 2 * hh + c, :], ps.v)
            if i + 1 < ntile:
                self.transpose_h(hs[(i + 1) % 2], NS, hTs[(i + 1) % 2])

        def back(i):
            xt = xts[i % 2]
            for s in range(NS):
                pss = [self.next_ps(), self.next_ps()]
                for half in range(2):
                    for k in range(8):
                        self.mm(pss[half].v, oT[:, k, s * 128:(s + 1) * 128], wo[:, k, half * 512:(half + 1) * 512],
                                start=(k == 0), stop=(k == 7))
                for half in range(2):
                    self.tt('dve', xt[:, s, half * 512:(half + 1) * 512], xt[:, s, half * 512:(half + 1) * 512], pss[half].v, ALU.add)
                self.dma('sp', dst[i * TT + s * 128: i * TT + (s + 1) * 128, :], xt[:, s, :])

        load(0)
        load(1)
        front(0)
        for i in range(ntile):
            if S.need_reset():
                S.hard_barrier()
            mid(i)
            back(i)
            load(i + 2)

    def phaseC(self, es, src, dst, final=True):
        nc, S = self.nc, self.S
        TT = 256
        NS = TT // 128
        NJ = DFF // 128
        inp = self.inp
        self._stg_i = 0
        cols, ci = self.load_cols(es, "c_cols", [("g", inp["norm_ffn_g"]), ("cw0", inp["ffn_conv_w"][0]),
                                                 ("cw1", inp["ffn_conv_w"][1]), ("cw2", inp["ffn_conv_w"][2]),
                                                 ("cb", inp["ffn_conv_b"])])
        wi = self.sb(es, "c_wi", [128, 8, 2 * DFF], BF16)
        wo = self.sb(es, "c_wo", [128, NJ, D], BF16)
        gfin = self.sb(es, "c_gfin", [128, D], F32)
        self.dma('sp', gfin.v, inp["norm_final_g"].partition_broadcast(128))
        with ExitStack() as tes:
            stg = [self.sb(tes, "c_stg%d" % i, [128, 2048], F32) for i in range(3)]
            for k in range(8):
                self.load_weight(stg, wi[:, k, :], inp["w_ffn_in"][k * 128:(k + 1) * 128, :], 2 * DFF,
                                 scale=cols[:, ci["g"] + k: ci["g"] + k + 1])
            for j in range(NJ):
                self.load_weight(stg, wo[:, j, :], inp["w_ffn_out"][j * 128:(j + 1) * 128, :], D)
            S.barrier()
        xts = [self.sb(es, "c_xt%d" % i, [128, NS, D], F32) for i in range(2)]
        hs = [self.sb(es, "c_h%d" % i, [128, NS, D], BF16) for i in range(2)]
        hTs = [self.sb(es, "c_hT%d" % i, [128, 8, TT], BF16) for i in range(2)]
        actT = [self.sb(es, "c_actT%d" % j, [128, TT], BF16) for j in range(NJ)]
        junk = self.sb(es, "c_junk", [128, D], BF16)
        sss = [self.sb(es, "c_ss%d" % i, [128, 4], F32) for i in range(3)]
        rstds = [self.sb(es, "c_rstd%d" % i, [128, 4], F32) for i in range(3)]
        halos = [self.sb(es, "c_halo%d" % j, [128, 2], F32) for j in range(NJ)]
        NW = 4
        uraw = [es.enter_context(self.nc.sbuf_tensor("c_uw%d" % i, [128, 2 + TT], F32)) for i in range(NW)]
        uh = [T(r[:, 0:2], "c_uh") for r in uraw]
        ub = [T(r[:, 2:2 + TT], "c_ub") for r in uraw]

        def uview(jj, a, b_):
            return V([uh[jj], ub[jj]], uraw[jj][:, a:b_])

        acc = [self.sb(es, "c_acc%d" % i, [128, TT], F32) for i in range(NW)]
        th = [self.sb(es, "c_th%d" % i, [128, TT], F32) for i in range(NW)]
        ntile = self.ntok // TT
        tiles_per_b = self.seq // TT

        def load(i):
            if i < ntile:
                self.dma('sp', xts[i % 2].v, src[i * TT:(i + 1) * TT, :].rearrange("(s p) d -> p s d", p=128))

        def front(i):
            self.rms_h(xts[i % 2], NS, sss[i % 2], rstds[i % 2], junk, hs[i % 2])
            self.transpose_h(hs[i % 2], NS, hTs[i % 2])

        def mid(i):
            hT = hTs[i % 2]
            st = {}
            for j in range(NJ + 2):
                if i + 1 < ntile:
                    if j == 8:
                        self.rms_h(xts[(i + 1) % 2], NS, sss[(i + 1) % 2], rstds[(i + 1) % 2], junk, hs[(i + 1) % 2])
                    if j == NJ:
                        self.transpose_h(hs[(i + 1) % 2], NS, hTs[(i + 1) % 2])
                if j < NJ:
                    ps = self.next_ps()
                    for half in range(2):
                        c0 = half * DFF + j * 128
                        for k in range(8):
                            self.mm(ps[:, half * TT:(half + 1) * TT], wi[:, k, c0:c0 + 128], hT[:, k, :], start=(k == 0), stop=(k == 7))
                    jj = j % NW
                    a = acc[jj]
                    w0 = cols[:, ci["cw0"] + j: ci["cw0"] + j + 1]
                    w1 = cols[:, ci["cw1"] + j: ci["cw1"] + j + 1]
                    w2 = cols[:, ci["cw2"] + j: ci["cw2"] + j + 1]
                    cb = cols[:, ci["cb"] + j: ci["cb"] + j + 1]
                    self.cp('dve', uh[jj].v, halos[j].v)
                    self.cp('act', ub[jj].v, ps[:, 0:TT])
                    self.act(a.v, ps[:, 0:TT], AF.Identity, scale=w2, bias=cb)
                    self.stt('dve', a.v, uview(jj, 1, 1 + TT), w1, a.v, ALU.mult, ALU.add)
                    self.stt('dve', a.v, uview(jj, 0, TT), w0, a.v, ALU.mult, ALU.add)
                    self.cp('dve', halos[j].v, ub[jj][:, TT - 2:TT])
                    st[j] = ps
                if 0 <= j - 1 < NJ:
                    jj = (j - 1) % NW
                    self.act(th[jj].v, acc[jj].v, AF.Gelu_apprx_tanh)
                if 0 <= j - 2 < NJ:
                    jj = (j - 2) % NW
                    self.tt('dve', actT[j - 2].v, th[jj].v, st[j - 2][:, TT:2 * TT], ALU.mult)

        def back(i):
            xt = xts[i % 2]
            ss, rstd = sss[2], rstds[2]
            for s in range(NS):
                pss = [self.next_ps(), self.next_ps()]
                for half in range(2):
                    for j in range(NJ):
                        self.mm(pss[half].v, actT[j][:, s * 128:(s + 1) * 128], wo[:, j, half * 512:(half + 1) * 512],
                                start=(j == 0), stop=(j == NJ - 1))
                for half in range(2):
                    self.tt('dve', xt[:, s, half * 512:(half + 1) * 512], xt[:, s, half * 512:(half + 1) * 512], pss[half].v, ALU.add)
                if final:
                    self.act(junk.v, xt[:, s, :], AF.Square, accum=ss[:, s:s + 1])
                    self.ts('dve', rstd[:, s:s + 1], ss[:, s:s + 1], 1.0 / D, NORM_EPS, ALU.mult, ALU.add)
                    self.rsqrt(rstd[:, s:s + 1], rstd[:, s:s + 1])
                    self.stt('dve', xt[:, s, :], xt[:, s, :], rstd[:, s:s + 1], gfin.v, ALU.mult, ALU.mult)
                self.dma('sp', dst[i * TT + s * 128: i * TT + (s + 1) * 128, :], xt[:, s, :])

        load(0)
        load(1)
        front(0)
        for i in range(ntile):
            if S.need_reset():
                S.hard_barrier()
            if i % tiles_per_b == 0:
                for j in range(NJ):
                    self.memset('pool', halos[j].v, 0.0)
            mid(i)
            back(i)
            load(i + 2)


_PROG_CACHE = {}


def get_prog(nb=4, seq=2048, phases="LRMBC"):
    key = (nb, seq, phases)
    if key not in _PROG_CACHE:
        _PROG_CACHE[key] = Prog(nb, seq, phases)
    return _PROG_CACHE[key]


_W_NAMES = ["norm_mix_g", "w_in", "b_in", "mu_shift", "w0", "w_lora_up", "a0", "a_lora_up", "g_lora_up", "k_k", "k_a",
            "r_k", "lnx_g", "lnx_b", "w_branch_a", "conv_b_w", "conv_b_b", "w_rg_a", "b_rg_a", "w_rg_x", "b_rg_x",
            "lru_lambda", "w_branch_b", "w_mix_out", "norm_x_g", "norm_mem_g", "w_cq", "w_ckv", "w_co", "norm_ffn_g",
            "w_ffn_in", "ffn_conv_w", "ffn_conv_b", "w_ffn_out", "norm_final_g"]


def run(inputs, ncores=8, nb=4, seq=2048, phases="LRMBC"):
    prog = get_prog(nb, seq, phases)
    shapes = {k: tuple(v.shape) for k, v in prog.inp.items()}
    shared = {}
    for k in _W_NAMES:
        a = np.ascontiguousarray(np.asarray(inputs[k], dtype=np.float32))
        shared[k] = a.reshape(shapes[k])
    x = np.asarray(inputs["x"], dtype=np.float32)
    mem = np.asarray(inputs["mem"], dtype=np.float32)
    in_maps = []
    for c in range(ncores):
        m = dict(shared)
        m["x"] = np.ascontiguousarray(x[c * nb:(c + 1) * nb, :seq]).reshape(nb * seq, D)
        m["mem"] = np.ascontiguousarray(mem[c * nb:(c + 1) * nb]).reshape(nb * NMEM, D)
        in_maps.append(m)
    res = run_bass_kernel_spmd(prog.nc, in_maps, core_ids=list(range(ncores)))
    outs = [np.asarray(r["out"]).reshape(nb, seq, D) for r in res.results]
    return np.concatenate(outs, axis=0)


def kernel(**inputs):
    return run(inputs).astype(np.float32)
```

```python
import numpy as np
from contextlib import ExitStack
import concourse.bass as bass
import concourse.mybir as mybir
from concourse.bass_utils import run_bass_kernel_spmd

F32 = mybir.dt.float32
BF16 = mybir.dt.bfloat16
AF = mybir.ActivationFunctionType
ALU = mybir.AluOpType
AX = mybir.AxisListType

D = 1024
NMEM = 256
AW = 512
RWKV_COLS = 1792
P_IN = 4864
DFF = 2816
NORM_EPS = 1e-6
LNX_EPS = 64e-5
SAME_ENGINE_SYNC = True
import os as _os
RSTOP = int(_os.environ.get("RSTOP", "0"))
if _os.environ.get("NOSES"):
    SAME_ENGINE_SYNC = False
LVAR = int(_os.environ.get("LVAR", "2"))


_ALL_TILES = []


class T:
    def __init__(self, ap, name=""):
        self.ap = ap if isinstance(ap, bass.AP) else ap[:]
        self.w = None
        self.r = []
        self.name = name
        self.dsem = None
        _ALL_TILES.append(self)

    def __getitem__(self, k):
        return V([self], self.ap[k])

    @property
    def v(self):
        return V([self], self.ap)


class V:
    def __init__(self, ts, ap):
        self.ts = ts
        self.ap = ap

    def __getitem__(self, k):
        return V(self.ts, self.ap[k])

    def re(self, pat, **kw):
        return V(self.ts, self.ap.rearrange(pat, **kw))

    def bc(self, shape):
        return V(self.ts, self.ap.to_broadcast(shape))

    def un(self, axis):
        return V(self.ts, self.ap.unsqueeze(axis))

    def bitcast(self, dt):
        return V(self.ts, self.ap.bitcast(dt))


def _ap(x):
    return x.ap if isinstance(x, V) else x


class Sync:
    def __init__(self, nc, es, n_dma_sems=64):
        self.nc = nc
        self.es = es
        self.engs = {'pe': nc.tensor, 'act': nc.scalar, 'dve': nc.vector, 'pool': nc.gpsimd, 'sp': nc.sync}
        self.sem = {}
        self.cnt = {}
        self.seen = {}
        for e in self.engs:
            self.sem[e] = es.enter_context(nc.semaphore("c_" + e))
            self.cnt[e] = 0
            self.seen[e] = {}
        self.dpool = [{'sem': es.enter_context(nc.semaphore("d%d" % i)), 'cnt': 0} for i in range(n_dma_sems)]
        self.dfree = list(range(n_dma_sems))
        self.n_inst = 0
        self.bar1 = es.enter_context(nc.semaphore("bar1"))
        self.bar2 = es.enter_context(nc.semaphore("bar2"))
        self.bar_k = 0

    def _wait(self, e, tok):
        if tok is None:
            return
        sem, val, src = tok
        if src == e and (e == 'pe' or not SAME_ENGINE_SYNC):
            return
        key = sem.name
        if self.seen[e].get(key, 0) >= val:
            return
        self.seen[e][key] = val
        self.engs[e].wait_ge(sem, val)

    def deps(self, e, reads, writes):
        for v in reads:
            if not isinstance(v, V):
                continue
            for t in v.ts:
                self._wait(e, t.w)
        for v in writes:
            for t in v.ts:
                self._wait(e, t.w)
                for tok in t.r:
                    self._wait(e, tok)

    def done(self, tok, reads, writes):
        for v in reads:
            if not isinstance(v, V):
                continue
            for t in v.ts:
                t.r.append(tok)
                if len(t.r) > 24:
                    t.r = t.r[-24:] if False else t.r
        for v in writes:
            for t in v.ts:
                t.w = tok
                t.r = []

    def op(self, e, fn, reads, writes, signal=True):
        self.deps(e, reads, writes)
        inst = fn(self.engs[e])
        self.n_inst += 1
        if signal:
            self.cnt[e] += 1
            inst.then_inc(self.sem[e], 1)
            tok = (self.sem[e], self.cnt[e], e)
        else:
            tok = (self.sem[e], self.cnt[e] + 1, e)
        self.done(tok, reads, writes)
        return inst

    def _dsem(self, t):
        if t.dsem is None:
            if not self.dfree:
                raise RuntimeError("out of dma semaphores")
            t.dsem = self.dpool[self.dfree.pop(0)]
        return t.dsem

    def dma(self, q, out, in_, **kw):
        reads = [in_] if isinstance(in_, V) else []
        writes = [out] if isinstance(out, V) else []
        self.deps(q, reads, writes)
        sbv = out if isinstance(out, V) else in_
        ds = self._dsem(sbv.ts[0])
        inst = self.engs[q].dma_start(out=_ap(out), in_=_ap(in_), **kw)
        self.n_inst += 1
        ds['cnt'] += 16
        inst.then_inc(ds['sem'], 16)
        tok = (ds['sem'], ds['cnt'], 'dma')
        self.done(tok, reads, writes)
        return tok

    def barrier(self):
        toks = [(self.sem[f], self.cnt[f], f) for f in self.engs if self.cnt[f] > 0]
        toks += [(d['sem'], d['cnt'], 'dma') for d in self.dpool if d['cnt'] > 0]
        for e in self.engs:
            for tok in toks:
                if tok[2] == e:
                    continue
                self._wait(e, tok)

    def release_dma_sems(self):
        self.dfree = list(range(len(self.dpool)))
        for t in _ALL_TILES:
            t.dsem = None

    def need_reset(self, limit=2600):
        return max(self.cnt.values()) > limit or max(d['cnt'] for d in self.dpool) > limit

    def hard_barrier(self):
        self.barrier()
        self.bar_k += 1
        k = self.bar_k
        for e in self.engs:
            self.engs[e].sem_inc(self.bar1, 1)
        sp = self.engs['sp']
        sp.wait_ge(self.bar1, len(self.engs) * k)
        for e in self.engs:
            if self.cnt[e] > 0:
                sp.sem_clear(self.sem[e])
        for d in self.dpool:
            if d['cnt'] > 0:
                sp.sem_clear(d['sem'])
        sp.sem_inc(self.bar2, 1)
        for e in self.engs:
            if e != 'sp':
                self.engs[e].wait_ge(self.bar2, k)
        for e in self.engs:
            self.cnt[e] = 0
            self.seen[e] = {}
        for d in self.dpool:
            d['cnt'] = 0
        for t in _ALL_TILES:
            t.w = None
            t.r = []


class Prog:
    def __init__(self, nb=4, seq=2048, phases="LRMBC", dbg=False):
        self.nb, self.seq, self.phases, self.dbg = nb, seq, phases, dbg
        self.ntok = nb * seq
        nc = self.nc = bass.Bass("TRN2", target_bir_lowering=False)
        self.inp = {}
        del _ALL_TILES[:]

        def din(name, shape):
            self.inp[name] = nc.dram_tensor(name, list(shape), F32, kind="ExternalInput").ap()
            return self.inp[name]

        ntok = self.ntok
        din("x", [ntok, D])
        din("mem", [nb * NMEM, D])
        din("norm_mix_g", [D]); din("w_in", [D, P_IN]); din("b_in", [P_IN]); din("mu_shift", [RWKV_COLS])
        din("w0", [AW]); din("w_lora_up", [64, AW]); din("a0", [AW]); din("a_lora_up", [64, AW])
        din("g_lora_up", [128, AW]); din("k_k", [AW]); din("k_a", [AW]); din("r_k", [AW])
        din("lnx_g", [AW]); din("lnx_b", [AW]); din("w_branch_a", [AW, D])
        din("conv_b_w", [4, AW]); din("conv_b_b", [AW]); din("w_rg_a", [8, 64, 64]); din("b_rg_a", [AW])
        din("w_rg_x", [8, 64, 64]); din("b_rg_x", [AW]); din("lru_lambda", [AW]); din("w_branch_b", [AW, D])
        din("w_mix_out", [D, D]); din("norm_x_g", [D]); din("norm_mem_g", [D]); din("w_cq", [D, D])
        din("w_ckv", [D, 2 * D]); din("w_co", [D, D]); din("norm_ffn_g", [D]); din("w_ffn_in", [D, 2 * DFF])
        din("ffn_conv_w", [3, DFF]); din("ffn_conv_b", [DFF]); din("w_ffn_out", [DFF, D]); din("norm_final_g", [D])
        self.out = nc.dram_tensor("out", [ntok, D], F32, kind="ExternalOutput").ap()
        self.x1 = nc.dram_tensor("x1s", [ntok, D], F32).ap()
        self.x2 = nc.dram_tensor("x2s", [ntok, D], F32).ap()
        self.dbg_out = {}

        with ExitStack() as es:
            self.es = es
            self.S = Sync(nc, es)
            self.ps = [T(es.enter_context(nc.psum_tensor("ps%d" % i, [128, 512], F32)), "ps%d" % i) for i in range(8)]
            self.ps_i = 0
            self.ident = self.sb(es, "ident", [128, 128], BF16)
            self.identf = self.sb(es, "identf", [128, 128], F32)
            for idt in (self.ident, self.identf):
                self.S.op('pool', lambda e: e.memset(idt.ap[:], 0.0), [], [idt.v])
                self.S.op('pool', lambda e: e.affine_select(idt.ap[:], idt.ap[:], pattern=[[-1, 128]],
                                                           compare_op=ALU.not_equal, fill=1.0, base=0,
                                                           channel_multiplier=1), [idt.v], [idt.v])
            self.mhalf = self.sb(es, "mhalf", [128, 512], F32)
            self.memset('pool', self.mhalf.v, -0.5)
            self.phalf = self.sb(es, "phalf", [128, 512], F32)
            self.memset('pool', self.phalf.v, 0.5)
            self.yas = nc.dram_tensor("yas", [AW, ntok], BF16).ap()
            self.ybs = nc.dram_tensor("ybs", [AW, ntok], BF16).ap()
            src = {'L': self.inp["x"], 'R': self.inp["x"], 'M': self.inp["x"], 'B': self.x1, 'C': self.x2}
            dst = {'L': None, 'R': None, 'M': self.x1, 'B': self.x2, 'C': self.out}
            order = [p for p in "LRMBC" if p in phases]
            chain = [p for p in order if p in "MBC"]
            for i, p in enumerate(order):
                s_ap, d_ap = src[p], dst[p]
                if p in chain:
                    if chain.index(p) == 0:
                        s_ap = self.inp["x"]
                    if chain.index(p) == len(chain) - 1:
                        d_ap = self.out
                with ExitStack() as pes:
                    getattr(self, "phase" + p)(pes, s_ap, d_ap, final=(p == 'C'))
                    self.S.hard_barrier()
                self.S.release_dma_sems()
            self.S.barrier()

    def sb(self, es, name, shape, dt):
        return T(es.enter_context(self.nc.sbuf_tensor(name, list(shape), dt)), name)

    ps_allowed = None

    def next_ps(self):
        if self.ps_allowed is not None:
            self._psa_i = getattr(self, "_psa_i", 0) + 1
            return self.ps[self.ps_allowed[self._psa_i % len(self.ps_allowed)]]
        t = self.ps[self.ps_i]
        self.ps_i = (self.ps_i + 1) % 8
        return t

    def act(self, out, in_, func, bias=0.0, scale=1.0, accum=None):
        rd = [in_] + [a for a in (bias, scale) if isinstance(a, V)]
        wr = [out] + ([accum] if accum is not None else [])
        kw = {}
        if accum is not None:
            kw['accum_out'] = accum.ap
        return self.S.op('act', lambda e: e.activation(out=out.ap, in_=in_.ap, func=func, bias=_ap(bias),
                                                       scale=_ap(scale), **kw), rd, wr)

    def tt(self, eng, out, in0, in1, op):
        return self.S.op(eng, lambda e: e.tensor_tensor(out=out.ap, in0=in0.ap, in1=in1.ap, op=op), [in0, in1], [out])

    def ts(self, eng, out, in0, s1, s2, op0, op1=None):
        rd = [in0] + [a for a in (s1, s2) if isinstance(a, V)]
        if op1 is None:
            return self.S.op(eng, lambda e: e.tensor_scalar(out=out.ap, in0=in0.ap, scalar1=_ap(s1), scalar2=None,
                                                            op0=op0), rd, [out])
        return self.S.op(eng, lambda e: e.tensor_scalar(out=out.ap, in0=in0.ap, scalar1=_ap(s1), scalar2=_ap(s2),
                                                        op0=op0, op1=op1), rd, [out])

    def stt(self, eng, out, in0, scalar, in1, op0, op1):
        rd = [in0, in1] + ([scalar] if isinstance(scalar, V) else [])
        return self.S.op(eng, lambda e: e.scalar_tensor_tensor(out=out.ap, in0=in0.ap, scalar=_ap(scalar), in1=in1.ap,
                                                               op0=op0, op1=op1), rd, [out])

    def cp(self, eng, out, in_):
        if eng == 'act':
            return self.act(out, in_, AF.Copy)
        return self.S.op(eng, lambda e: e.tensor_copy(out=out.ap, in_=in_.ap), [in_], [out])

    def rsqrt(self, out, in_):
        shp = list(in_.ap.shape)
        if shp[-1] > 8:
            self.act(out, in_, AF.Ln)
            return self.act(out, out, AF.Exp, scale=-0.5)
        mh = self.mhalf[0:shp[0], 0:shp[-1]]
        if len(shp) == 3:
            mh = mh.un(1).bc(shp)
        return self.tt('pool', out, in_, mh, ALU.pow)

    def memset(self, eng, out, val):
        return self.S.op(eng, lambda e: e.memset(out.ap, val), [], [out])

    def mm(self, out, lhsT, rhs, start, stop, signal=None):
        if signal is None:
            signal = stop
        return self.S.op('pe', lambda e: e.matmul(out.ap, lhsT=lhsT.ap, rhs=rhs.ap, start=start, stop=stop),
                         [lhsT, rhs], [out], signal=signal)

    def tr(self, out, in_, signal=True, f32=False):
        idt = self.identf if f32 else self.ident
        n = in_.ap.shape[0]
        return self.S.op('pe', lambda e: e.transpose(out.ap, in_.ap, idt.ap[0:n, 0:n]), [in_, idt.v], [out],
                         signal=signal)

    def dma(self, q, out, in_, **kw):
        return self.S.dma(q, out, in_, **kw)

    def load_weight(self, stg, dst, src_rows, ncols, scale=None, piece=2048):
        rows = src_rows.shape[0]
        c0 = 0
        while c0 < ncols:
            n = min(piece, ncols - c0)
            st = stg[self._stg_i % len(stg)]
            eng = ('dve', 'act')[self._stg_i % 2]
            self._stg_i += 1
            self.dma(('sp', 'pool')[self._stg_i % 2], st[0:rows, 0:n], src_rows[:, c0:c0 + n])
            o = dst[:, c0:c0 + n]
            i = st[0:rows, 0:n]
            if scale is None:
                self.cp(eng, o, i)
            elif eng == 'act':
                if isinstance(scale, V):
                    self.act(o, i, AF.Copy, scale=scale)
                else:
                    self.act(o, i, AF.Copy, scale=float(scale))
            else:
                self.ts(eng, o, i, scale, None, ALU.mult)
            c0 += n

    def load_cols(self, es, name, specs):
        cols = {}
        n = 0
        for k, v in specs:
            cols[k] = n
            n += (v.shape[0] + 127) // 128
        res = self.sb(es, name, [128, n], F32)
        with ExitStack() as tes:
            ngrp = (n + 127) // 128
            stage = [self.sb(tes, name + "_st%d" % g, [128, 128], F32) for g in range(ngrp)]
            for st in stage:
                self.memset('pool', st.v, 0.0)
            for k, v in specs:
                L = v.shape[0]
                m = (L + 127) // 128
                c = cols[k]
                r = 0
                while r < m:
                    g, rr = divmod(c + r, 128)
                    cnt = min(m - r, 128 - rr)
                    if L >= 128:
                        self.dma('sp', stage[g][rr:rr + cnt, :], v[r * 128:(r + cnt) * 128].rearrange("(m p) -> m p", p=128))
                    else:
                        self.dma('sp', stage[g][rr:rr + 1, 0:L], v.rearrange("(m p) -> m p", m=1))
                    r += cnt
            for g in range(ngrp):
                w = min(128, n - g * 128)
                ps = self.next_ps()
                self.tr(ps[:, 0:w], stage[g][0:w, :], f32=True)
                self.cp('dve', res[:, g * 128:g * 128 + w], ps[:, 0:w])
            self.S.barrier()
        return res, cols

    def rms_h(self, xt, nsub, ss, rstd, junk, h):
        for s in range(nsub):
            self.act(junk.v, xt[:, s, :], AF.Square, accum=ss[:, s:s + 1])
        self.ts('dve', rstd[:, 0:nsub], ss[:, 0:nsub], 1.0 / D, NORM_EPS, ALU.mult, ALU.add)
        self.rsqrt(rstd[:, 0:nsub], rstd[:, 0:nsub])
        for s in range(nsub):
            self.ts('dve', h[:, s, :], xt[:, s, :], rstd[:, s:s + 1], None, ALU.mult)

    def transpose_h(self, h, nsub, hT, evac=('act', 'dve')):
        tw = nsub * 128
        per_bank = 1024 // tw
        c = 0
        i = 0
        while c < 8:
            ps = self.next_ps()
            pb = ps.v.bitcast(BF16)
            nchunk = min(per_bank, 8 - c)
            for cc in range(nchunk):
                for s in range(nsub):
                    last = (cc == nchunk - 1 and s == nsub - 1)
                    self.tr(pb[:, cc * tw + s * 128: cc * tw + (s + 1) * 128], h[:, s, (c + cc) * 128:(c + cc + 1) * 128],
                            signal=last)
            self.cp(evac[i % len(evac)], hT[:, c:c + nchunk, :], pb[:, 0:nchunk * tw].re("p (c t) -> p c t", c=nchunk))
            c += nchunk
            i += 1

    def norm_from_dram(self, src, tok0, NS, xbufs, sst, rst, junk, h):
        for s in range(NS):
            xb = xbufs[self._xb_i % len(xbufs)]
            self._xb_i += 1
            self.dma('sp', xb.v, src[tok0 + s * 128: tok0 + (s + 1) * 128, :])
            self.act(junk.v, xb.v, AF.Square, accum=sst[s].v)
            self.ts('dve', rst[s].v, sst[s].v, 1.0 / D, NORM_EPS, ALU.mult, ALU.add)
            self.rsqrt(rst[s].v, rst[s].v)
            self.ts('dve', h[:, s, :], xb.v, rst[s].v, None, ALU.mult)

    def norm_bufs(self, es, pfx, NS):
        xbufs = [self.sb(es, pfx + "_xb%d" % i, [128, D], F32) for i in range(2)]
        sst = [self.sb(es, pfx + "_ss%d" % i, [128, 1], F32) for i in range(NS)]
        rst = [self.sb(es, pfx + "_rs%d" % i, [128, 1], F32) for i in range(NS)]
        junk = self.sb(es, pfx + "_junk", [128, D], BF16)
        self._xb_i = 0
        return xbufs, sst, rst, junk

    def phaseL(self, es, src, dst, final=False):
        S = self.S
        inp = self.inp
        TT = 512
        NS = 4
        self._stg_i = 0
        self._ev_i = 0
        cols, ci = self.load_cols(es, "l_cols", [
            ("g", inp["norm_mix_g"]), ("bin", inp["b_in"][1792:2816]),
            ("cw0", inp["conv_b_w"][0]), ("cw1", inp["conv_b_w"][1]), ("cw2", inp["conv_b_w"][2]), ("cw3", inp["conv_b_w"][3]),
            ("cbb", inp["conv_b_b"]), ("bra", inp["b_rg_a"]), ("brx", inp["b_rg_x"]), ("lam", inp["lru_lambda"])])
        hb = self.sb(es, "l_hb", [128, 8], F32)
        self.ts('dve', hb[:, 0:4], cols[:, ci["bra"]:ci["bra"] + 4], 0.5, None, ALU.mult)
        self.ts('dve', hb[:, 4:8], cols[:, ci["brx"]:ci["brx"] + 4], 0.5, None, ALU.mult)
        cA = self.sb(es, "l_cA", [128, 8], F32)
        lt = self.sb(es, "l_lt", [128, 4], F32)
        self.act(lt.v, cols[:, ci["lam"]:ci["lam"] + 4], AF.Exp, scale=-1.0)
        self.ts('dve', lt.v, lt.v, 1.0, None, ALU.add)
        self.act(lt.v, lt.v, AF.Ln)
        self.ts('dve', cA[:, 0:4], lt.v, -4.0, None, ALU.mult)
        self.ts('dve', cA[:, 4:8], lt.v, -8.0, None, ALU.mult)
        win = self.sb(es, "l_win", [128, 8, 1024], BF16)
        wg = self.sb(es, "l_wg", [128, 2, 4, 128], BF16)
        with ExitStack() as tes:
            stg = [self.sb(tes, "l_stg%d" % i, [128, 1024], F32) for i in range(6)]
            for k in range(8):
                self.load_weight(stg, win[:, k, :], inp["w_in"][k * 128:(k + 1) * 128, 1792:2816], 1024,
                                 scale=cols[:, ci["g"] + k: ci["g"] + k + 1], piece=1024)
            for gi, nm in enumerate(("w_rg_a", "w_rg_x")):
                st = stg[gi]
                self.memset('pool', st.v, 0.0)
                for blk in range(8):
                    c, hl = divmod(blk, 2)
                    self.dma('sp', st[hl * 64:(hl + 1) * 64, c * 128 + hl * 64: c * 128 + (hl + 1) * 64], inp[nm][blk])
                self.cp('dve', wg[:, gi, :, :], st[:, 0:512].re("p (c m) -> p c m", c=4))
            S.barrier()
        xbufs, sst, rst, junk = self.norm_bufs(es, "l", NS)
        h = self.sb(es, "l_h", [128, NS, D], BF16)
        hT = self.sb(es, "l_hT", [128, 8, TT], BF16)
        PX = [self.sb(es, "l_px%d" % c, [128, 3 + TT], F32) for c in range(4)]
        GY = [self.sb(es, "l_gy%d" % c, [128, TT], F32) for c in range(4)]
        YB = [self.sb(es, "l_yb%d" % c, [128, TT], BF16) for c in range(4)]
        carry = [self.sb(es, "l_cy%d" % c, [128, 1], F32) for c in range(4)]
        NW = 2
        def wk(nm, dt=F32, n=NW):
            return [self.sb(es, "l_%s%d" % (nm, i), [128, TT], dt) for i in range(n)]
        acc = wk("acc", n=4); xbb = wk("xbb", BF16, n=4); ta = wk("ta", n=4); tx = wk("tx", n=4)
        av = wk("av", n=4); a2 = wk("a2", n=4); u = wk("u", n=4); hl_ = wk("hl", n=4)
        ntile = self.ntok // TT
        tiles_per_b = self.seq // TT
        hs = [h, self.sb(es, "l_h1", [128, NS, D], BF16)]
        hTs = [hT, self.sb(es, "l_hT1", [128, 8, TT], BF16)]

        def front(i):
            self.norm_from_dram(src, i * TT, NS, xbufs, sst, rst, junk, hs[i % 2])
            self.transpose_h(hs[i % 2], NS, hTs[i % 2])

        def midA(i):
            hT_ = hTs[i % 2]
            first = (i % tiles_per_b == 0)
            if first:
                for c in range(4):
                    self.memset('pool', PX[c][:, 0:3], 0.0)
                    self.memset('pool', carry[c].v, 0.0)
            for c in range(4):
                ps = self.next_ps()
                for k in range(8):
                    self.mm(ps.v, win[:, k, c * 128:(c + 1) * 128], hT_[:, k, :], start=(k == 0), stop=(k == 7))
                self.act(PX[c][:, 3:3 + TT], ps.v, AF.Identity, bias=cols[:, ci["bin"] + c: ci["bin"] + c + 1])
            for c in range(4):
                ps = self.next_ps()
                for k in range(8):
                    self.mm(ps.v, win[:, k, 512 + c * 128: 512 + (c + 1) * 128], hT_[:, k, :], start=(k == 0), stop=(k == 7))
                self.act(GY[c].v, ps.v, AF.Gelu_apprx_tanh, bias=cols[:, ci["bin"] + 4 + c: ci["bin"] + 5 + c])
            def genA(c):
                A = acc[c]; XB = xbb[c]; TA = ta[c]; TX = tx[c]
                cw = [cols[:, ci["cw%d" % j] + c: ci["cw%d" % j] + c + 1] for j in range(4)]
                self.ts('dve', A.v, PX[c][:, 0:TT], cw[0], cols[:, ci["cbb"] + c: ci["cbb"] + c + 1], ALU.mult, ALU.add)
                yield
                for j in range(1, 4):
                    self.stt('dve', A.v, PX[c][:, j:j + TT], cw[j], A.v, ALU.mult, ALU.add)
                    yield
                self.cp('pool', PX[c][:, 0:3], PX[c][:, TT:TT + 3])
                self.cp('dve', XB.v, A.v)
                yield
                psa = self.next_ps()
                self.mm(psa.v, wg[:, 0, c, :], XB.v, start=True, stop=True)
                self.act(TA.v, psa.v, AF.Tanh, scale=0.5, bias=hb[:, c:c + 1])
                psx = self.next_ps()
                self.mm(psx.v, wg[:, 1, c, :], XB.v, start=True, stop=True)
                self.act(TX.v, psx.v, AF.Tanh, scale=0.5, bias=hb[:, 4 + c:5 + c])
                yield

            gens = [genA(c) for c in range(4)]
            alive = True
            while alive:
                alive = False
                for g in gens:
                    try:
                        next(g)
                        alive = True
                    except StopIteration:
                        pass

        def midB(i):
            tok0 = i * TT
            first = (i % tiles_per_b == 0)

            def gen(c):
                A = acc[c]; TA = ta[c]; TX = tx[c]; AV = av[c]; A2 = a2[c]; U = u[c]; HL = hl_[c]
                self.act(AV.v, TA.v, AF.Exp, scale=cA[:, c:c + 1], bias=cA[:, c:c + 1])
                self.act(A2.v, TA.v, AF.Exp, scale=cA[:, 4 + c:5 + c], bias=cA[:, 4 + c:5 + c])
                self.stt('dve', U.v, TX.v, 1.0, A.v, ALU.add, ALU.mult)
                yield
                self.ts('dve', A2.v, A2.v, -1.0, 1.0, ALU.mult, ALU.add)
                self.ts('dve', A2.v, A2.v, 1e-30, None, ALU.max)
                yield
                self.act(A2.v, A2.v, AF.Ln)
                yield
                self.act(A2.v, A2.v, AF.Exp, scale=0.5)
                if first:
                    self.memset('pool', A2[:, 0:1], 1.0)
                yield
                self.tt('dve', U.v, U.v, A2.v, ALU.mult)
                yield
                self.S.op('dve', lambda e: e.tensor_tensor_scan(HL.ap, AV.ap, U.ap, carry[c].ap, ALU.mult, ALU.add),
                          [AV.v, U.v, carry[c].v], [HL.v])
                yield
                self.cp('pool', carry[c].v, HL[:, TT - 1:TT])
                self.tt('dve', YB[c].v, HL.v, GY[c].v, ALU.mult)
                self.dma('sp', self.ybs[c * 128:(c + 1) * 128, tok0:tok0 + TT], YB[c].v)
                yield

            gens = [gen(c) for c in range(4)]
            alive = True
            while alive:
                alive = False
                for g in gens:
                    try:
                        next(g)
                        alive = True
                    except StopIteration:
                        pass

        front(0)
        for i in range(ntile):
            if S.need_reset():
                S.hard_barrier()
            midA(i)
            if i + 1 < ntile:
                front(i + 1)
            midB(i)

    def phaseR(self, es, src, dst, final=False):
        S = self.S
        inp = self.inp
        TT = 256
        NS = 2
        NQ = 4
        CH = 64
        self._stg_i = 0
        self._ev_i = 0
        cols, ci = self.load_cols(es, "r_cols", [
            ("g", inp["norm_mix_g"]), ("bin", inp["b_in"][0:RWKV_COLS]), ("mu", inp["mu_shift"]),
            ("w0", inp["w0"]), ("a0", inp["a0"]), ("kk", inp["k_k"]), ("ka", inp["k_a"]), ("rk", inp["r_k"])])

        def col(key, j):
            return cols[:, ci[key] + j: ci[key] + j + 1]

        der = self.sb(es, "r_der", [128, 32], F32)
        self.ts('dve', der[:, 0:14], cols[:, ci["mu"]:ci["mu"] + 14], -1.0, 1.0, ALU.mult, ALU.add)
        self.ts('dve', der[:, 14:18], cols[:, ci["w0"]:ci["w0"] + 4], 0.5, None, ALU.mult)
        self.ts('dve', der[:, 18:22], cols[:, ci["a0"]:ci["a0"] + 4], 0.5, None, ALU.mult)
        self.ts('dve', der[:, 22:26], cols[:, ci["ka"]:ci["ka"] + 4], 0.5, None, ALU.mult)
        self.ts('dve', der[:, 26:30], cols[:, ci["ka"]:ci["ka"] + 4], -0.5, 1.0, ALU.mult, ALU.add)
        rkb = self.sb(es, "r_rkb", [128, 4], BF16)
        self.cp('dve', rkb.v, cols[:, ci["rk"]:ci["rk"] + 4])
        m64 = [self.sb(es, "r_m64_%d" % i, [64, 64], BF16) for i in range(3)]
        masks = [self.sb(es, "r_mask%d" % i, [128, 128], BF16) for i in range(3)]
        specs = [([[1, 64]], -1, ALU.is_gt), ([[-1, 64]], 1, ALU.is_gt), ([[1, 64]], -1, ALU.is_ge)]
        for mi in range(3):
            pat, cm, op = specs[mi]
            self.memset('pool', m64[mi].v, 1.0)
            self.S.op('pool', lambda e: e.affine_select(m64[mi].ap, m64[mi].ap, pattern=pat, compare_op=op, fill=0.0,
                                                        base=0, channel_multiplier=cm), [m64[mi].v], [m64[mi].v])
            for a in range(2):
                for b_ in range(2):
                    self.dma('sp', masks[mi][a * 64:(a + 1) * 64, b_ * 64:(b_ + 1) * 64], m64[mi].v)
        mask_su, mask_sl, mask_u = masks
        bones = self.sb(es, "r_bones", [128, 128], BF16)
        self.memset('pool', bones.v, 0.0)
        self.memset('pool', bones[0:64, 0:64], 1.0)
        self.memset('pool', bones[64:128, 64:128], 1.0)
        m01 = self.sb(es, "r_m01", [128, TT], F32)
        self.memset('pool', m01.v, 1.0)
        self.memset('pool', m01.v.re("p (q s) -> p q s", s=CH)[:, :, 0:1], 0.0)
        lnxg = self.sb(es, "r_lnxg", [128, 4, 64], F32)
        lnxb = self.sb(es, "r_lnxb", [128, 4, 64], F32)
        for c in range(4):
            for hl in range(2):
                hh = 2 * c + hl
                self.dma('sp', lnxg[hl * 64:(hl + 1) * 64, c, :], inp["lnx_g"][hh * 64:(hh + 1) * 64].partition_broadcast(64))
                self.dma('sp', lnxb[hl * 64:(hl + 1) * 64, c, :], inp["lnx_b"][hh * 64:(hh + 1) * 64].partition_broadcast(64))
        win = self.sb(es, "r_win", [128, 8, RWKV_COLS], BF16)
        wlora = self.sb(es, "r_wlora", [128, 2, AW], BF16)
        gup = self.sb(es, "r_gup", [128, AW], BF16)
        with ExitStack() as tes:
            stg = [self.sb(tes, "r_stg%d" % i, [128, RWKV_COLS], F32) for i in range(6)]
            for k in range(8):
                self.load_weight(stg, win[:, k, :], inp["w_in"][k * 128:(k + 1) * 128, 0:RWKV_COLS], RWKV_COLS,
                                 scale=cols[:, ci["g"] + k: ci["g"] + k + 1], piece=RWKV_COLS)
            st = stg[0]
            self.memset('pool', st[:, 0:2 * AW], 0.0)
            self.dma('sp', st[0:64, 0:AW], inp["w_lora_up"])
            self.dma('sp', st[64:128, AW:2 * AW], inp["a_lora_up"])
            self.cp('dve', wlora.v, st[:, 0:2 * AW].re("p (a m) -> p a m", a=2))
            st = stg[1]
            self.dma('sp', st[:, 0:AW], inp["g_lora_up"])
            self.cp('dve', gup.v, st[:, 0:AW])
            S.barrier()
        xbufs, sst, rst, junk = self.norm_bufs(es, "r", NS)
        h = self.sb(es, "r_h", [128, NS, D], BF16)
        hT = self.sb(es, "r_hT", [128, 8, TT], BF16)
        pcarry = [self.sb(es, "r_pc%d" % m, [128, 1], F32) for m in range(14)]
        Sx = [self.sb(es, "r_s%d" % m, [128, TT], F32) for m in range(14)]
        ltmp = [self.sb(es, "r_lt%d" % i, [128, TT], F32) for i in range(2)]
        LB = self.sb(es, "r_lb", [128, TT], BF16)
        SGd = self.sb(es, "r_sgd", [128, NQ, 128], BF16)
        NW = 2

        def wk(nm, dt=F32):
            return [self.sb(es, "r_%s%d" % (nm, i), [128, TT], dt) for i in range(NW)]

        logw = [self.sb(es, "r_logw%d" % i, [128, TT], F32) for i in range(4)]
        ta = [self.sb(es, "r_ta%d" % i, [128, TT], F32) for i in range(4)]
        cum = [self.sb(es, "r_cum%d" % i, [128, TT], F32) for i in range(4)]
        egm1 = wk("egm1"); eg = wk("eg"); eig = wk("eig"); egc = wk("egc")
        kk = wk("kk"); kk2 = wk("kk2", BF16); rn = wk("rn"); kkn = wk("kkn"); kmod = wk("kmod"); bv = wk("bv")
        names = ["AT", "BT", "KT", "RT", "BGT", "KGT", "VB", "RK"]
        EXP = {n: [self.sb(es, "r_%s%d" % (n, c), [128, NQ, 128], BF16) for c in range(4)] for n in names}
        for n in names:
            for c in range(4):
                self.memset('pool', EXP[n][c].v, 0.0)
        GC = self.sb(es, "r_gc", [128, 4, NQ], F32)
        BKG = [self.sb(es, "r_bkg%d" % c, [128, 2, NQ, 128], BF16) for c in range(4)]
        VstAll = self.sb(es, "r_vst", [128, NQ, 4, 64], BF16)
        Nb = [[self.sb(es, "r_nb%d_%d" % (q, i), [128, 4, 128], BF16) for i in range(2)] for q in range(NQ)]
        Lb = [[self.sb(es, "r_lb%d_%d" % (q, i), [128, 4, 128], BF16) for i in range(2)] for q in range(NQ)]
        Pb = [self.sb(es, "r_pb%d" % q, [128, 4, 128], BF16) for q in range(NQ)]
        LakT = [self.sb(es, "r_lak%d" % q, [128, 4, 128], BF16) for q in range(NQ)]
        MrbT = [self.sb(es, "r_mrb%d" % q, [128, 4, 128], BF16) for q in range(NQ)]
        MrkT = [self.sb(es, "r_mrk%d" % q, [128, 4, 128], BF16) for q in range(NQ)]
        H = self.sb(es, "r_H", [128, 4, 64], F32)
        Hb = self.sb(es, "r_Hb", [128, 4, 64], BF16)
        Xb = [self.sb(es, "r_Xb%d" % i, [128, 4, 64], BF16) for i in range(2)]
        Ub = [self.sb(es, "r_Ub%d" % i, [128, 4, 64], BF16) for i in range(2)]

        def yt(nm, shape=(128, 4, 64), dt=F32):
            return [self.sb(es, "r_%s%d" % (nm, i), list(shape), dt) for i in range(2)]

        Yv = yt("Yv"); Ysq = yt("Ysq"); Yn = yt("Yn"); Bn = yt("Bn"); Yf = yt("Yf", dt=BF16)
        st1 = yt("st1", (128, 4)); st2 = yt("st2", (128, 4)); mean = yt("mean", (128, 4)); var = yt("var", (128, 4))
        YT = [self.sb(es, "r_YT%d" % i, [64, 4, 2, TT], BF16) for i in range(2)]
        ident = self.ident
        ntile = self.ntok // TT
        tiles_per_b = self.seq // TT
        yas_v = self.yas.rearrange("(c hl i) t -> i c hl t", hl=2, i=64)

        PWraw = [es.enter_context(self.nc.sbuf_tensor("r_pwx%d" % i, [128, 1 + TT], F32)) for i in range(3)]
        PWh = [T(r[:, 0:1], "r_pwh") for r in PWraw]
        PWb = [T(r[:, 1:1 + TT], "r_pwb") for r in PWraw]

        def s12_pieces(it):
            tok0 = it * TT

            def p0():
                if it % tiles_per_b == 0:
                    for m in range(14):
                        self.memset('pool', pcarry[m].v, 0.0)
                self.norm_from_dram(src, tok0, NS, xbufs, sst, rst, junk, h)
                self.transpose_h(h, NS, hT)

            def projA(m):
                def f():
                    ps = self.next_ps()
                    for k in range(8):
                        self.mm(ps[:, 0:TT], win[:, k, m * 128:(m + 1) * 128], hT[:, k, :], start=(k == 0), stop=(k == 7))
                    pi = m % 3
                    self.cp('dve', PWh[pi].v, pcarry[m].v)
                    self.act(PWb[pi].v, ps[:, 0:TT], AF.Identity, bias=col("bin", m))
                    tmp = ltmp[m % 2]
                    self.act(tmp.v, V([PWh[pi], PWb[pi]], PWraw[pi][:, 0:TT]), AF.Copy, scale=col("mu", m))
                return f

            def projB(m):
                def f():
                    pi = m % 3
                    tmp = ltmp[m % 2]
                    self.cp('dve', pcarry[m].v, PWb[pi][:, TT - 1:TT])
                    self.stt('dve', Sx[m].v, PWb[pi].v, der[:, m:m + 1], tmp.v, ALU.mult, ALU.add)
                return f

            def both(fa, fb):
                def f():
                    if fb is not None:
                        fb()
                    if fa is not None:
                        fa()
                return f

            chain_ = [both(projA(0), None)] + [both(projA(m), projB(m - 1)) for m in range(1, 14)] + [both(None, projB(13))]
            return [p0] + chain_

        def stage3(it):
            if it % tiles_per_b == 0:
                self.memset('pool', H.v, 0.0)
                self.memset('pool', Hb.v, 0.0)
            for _ in range(1):
                if RSTOP == 1:
                    return
                self.act(LB[0:64, :], Sx[12][0:64, :], AF.Tanh)
                self.cp('act', LB[64:128, :], Sx[12][64:128, :])
                tmp = ltmp[0]
                self.act(tmp.v, Sx[13].v, AF.Tanh, scale=0.5)
                for hl in range(2):
                    self.ts('dve', SGd[:, :, hl * 64:(hl + 1) * 64], tmp.v.re("p (q s) -> p q s", s=CH), 0.5, 0.5, ALU.mult, ALU.add)
                if RSTOP == 21:
                    return
                for c in range(4):
                    LW = logw[c]; TA = ta[c]; CU = cum[c]
                    ps = self.next_ps()
                    self.mm(ps[:, 0:TT], wlora[:, 0, c * 128:(c + 1) * 128], LB.v, start=True, stop=True)
                    self.mm(ps[:, TT:2 * TT], wlora[:, 1, c * 128:(c + 1) * 128], LB.v, start=True, stop=True)
                    self.act(LW.v, ps[:, 0:TT], AF.Tanh, scale=0.5, bias=der[:, 14 + c:15 + c])
                    self.act(TA.v, ps[:, TT:2 * TT], AF.Tanh, scale=0.5, bias=der[:, 18 + c:19 + c])
                    self.ts('dve', LW.v, LW.v, 1.0, -0.30326532985631671, ALU.add, ALU.mult)
                    self.S.op('dve', lambda e: e.tensor_tensor_scan(CU.ap, m01.ap, LW.ap, 0.0, ALU.mult, ALU.add),
                              [m01.v, LW.v], [CU.v])
                while deferred:
                    deferred.pop(0)()
                def stageB(c):
                    w = it * 4 + c
                    r_, k_, v_ = Sx[c], Sx[4 + c], Sx[8 + c]
                    LW = logw[c]; TA = ta[c]; CU = cum[c]
                    E1 = egm1[w % NW]; EG = eg[w % NW]; EI = eig[w % NW]
                    EC = egc[w % NW]; KK = kk[w % NW]; K2 = kk2[w % NW]; RN = rn[w % NW]; KN = kkn[w % NW]; KM = kmod[w % NW]
                    BV = bv[w % NW]
                    cu3 = CU.v.re("p (q s) -> p q s", s=CH)
                    cuC = cu3[:, :, CH - 1:CH]
                    self.act(KK.v, k_.v, AF.Copy, scale=col("kk", c))
                    self.tt('pool', E1.v, CU.v, LW.v, ALU.subtract)
                    yield
                    self.act(K2.v, KK.v, AF.Square)
                    self.tt('pool', EC.v.re("p (q s) -> p q s", s=CH), cuC.bc([128, NQ, CH]), cu3, ALU.subtract)
                    yield
                    psn = self.next_ps()
                    self.mm(psn[:, 0:TT], bones.v, K2.v, start=True, stop=True)
                    self.act(E1.v, E1.v, AF.Exp)
                    yield
                    self.act(RN.v, psn[:, 0:TT], AF.Ln)
                    yield
                    self.act(RN.v, RN.v, AF.Exp, scale=-0.5)
                    yield
                    self.tt('pool', KN.v, KK.v, RN.v, ALU.mult)
                    self.act(EI.v, CU.v, AF.Exp, scale=-1.0)
                    yield
                    self.act(EC.v, EC.v, AF.Exp)
                    self.act(KM.v, TA.v, AF.Identity, scale=der[:, 22 + c:23 + c], bias=der[:, 26 + c:27 + c])
                    yield
                    self.stt('dve', BV.v, TA.v, 1.0, KN.v, ALU.add, ALU.mult)
                    self.tt('dve', KM.v, KM.v, k_.v, ALU.mult)
                    self.act(EG.v, CU.v, AF.Exp)
                    self.act(GC[:, c, :], cuC.re("p q o -> p (q o)"), AF.Exp)
                    yield
                    for hl in range(2):
                        P_ = slice(hl * 64, (hl + 1) * 64)

                        def hv(t):
                            return t[P_, :].re("p (q s) -> p q s", s=CH)

                        def ov(n):
                            return EXP[n][c][P_, :, hl * 64:(hl + 1) * 64]

                        self.stt('dve', ov("AT"), hv(KN), -1.0, hv(E1), ALU.mult, ALU.mult)
                        self.tt('pool', ov("KT"), hv(KM), hv(EI), ALU.mult)
                        self.cp('act', ov("VB"), hv(v_))
                        yield
                        self.stt('dve', ov("BT"), hv(BV), 0.5, hv(EI), ALU.mult, ALU.mult)
                        self.tt('pool', ov("KGT"), hv(KM), hv(EC), ALU.mult)
                        yield
                        self.stt('dve', ov("BGT"), hv(BV), 0.5, hv(EC), ALU.mult, ALU.mult)
                        self.tt('pool', ov("RT"), hv(r_), hv(EG), ALU.mult)
                        yield
                        self.tt('pool', ov("RK"), hv(r_), hv(KM), ALU.mult)
                        yield

                lockstep([stageB(0), stageB(1)])
                lockstep([stageB(2), stageB(3), gen45((0, 1))])
                lockstep([gen45((2,)), gen45((3,))])

        def gen45(cs):
            for c in cs:
                psA = self.next_ps()
                pbA = psA.v.bitcast(BF16)
                for gi, n in enumerate(("BGT", "KGT")):
                    for q in range(NQ):
                        self.tr(pbA[:, (gi * NQ + q) * 128:(gi * NQ + q + 1) * 128], EXP[n][c][:, q, :], signal=(gi == 1 and q == NQ - 1))
                self.evac(BKG[c].v, pbA.re("p (g q m) -> p g q m", g=2, q=NQ))
                psB = self.next_ps()
                pbB = psB.v.bitcast(BF16)
                for q in range(NQ):
                    self.tr(pbB[:, q * 128:(q + 1) * 128], EXP["VB"][c][:, q, :], signal=(q == NQ - 1))
                for hl in range(2):
                    self.evac(VstAll[hl * 64:(hl + 1) * 64, :, c, :],
                              pbB[hl * 64:(hl + 1) * 64, 0:NQ * 128].re("p (q m) -> p q m", q=NQ)[:, :, hl * 64:(hl + 1) * 64])
                yield
            for c in cs:
                combos = [("BT", "AT", mask_su, Nb[c][0]), ("AT", "BT", mask_sl, Lb[c][0]), ("KT", "AT", mask_su, LakT[c]),
                          ("BT", "RT", mask_u, MrbT[c]), ("KT", "RT", mask_u, MrkT[c])]
                for (ln, rn_, mk, dstt) in combos:
                    ps = self.next_ps()
                    for q in range(NQ):
                        self.mm(ps[:, q * 128:(q + 1) * 128], EXP[ln][c][:, q, :], EXP[rn_][c][:, q, :], start=True, stop=True,
                                signal=(q == NQ - 1))
                    self.tt('dve', dstt.v, ps.v.re("p (q m) -> p q m", q=NQ), mk.v.un(1).bc([128, NQ, 128]), ALU.mult)
                    yield
                self.tt('pool', Pb[c].v, Nb[c][0].v, ident.v.un(1).bc([128, NQ, 128]), ALU.add)
            cur = 0
            for lvl in range(1, 6):
                nxt = 1 - cur
                for c in cs:
                    if lvl < 5:
                        ps = self.next_ps()
                        for q in range(NQ):
                            self.mm(ps[:, q * 128:(q + 1) * 128], Lb[c][cur][:, q, :], Nb[c][cur][:, q, :], start=True, stop=True,
                                    signal=(q == NQ - 1))
                        self.cp('act', Nb[c][nxt].v, ps.v.re("p (q m) -> p q m", q=NQ))
                    ps = self.next_ps()
                    for q in range(NQ):
                        self.mm(ps[:, q * 128:(q + 1) * 128], Nb[c][cur][:, q, :], Lb[c][cur][:, q, :], start=True, stop=True,
                                signal=(q == NQ - 1))
                    self.cp('act', Lb[c][nxt].v, ps.v.re("p (q m) -> p q m", q=NQ))
                    yield
                for c in cs:
                    ps = self.next_ps()
                    for q in range(NQ):
                        self.mm(ps[:, q * 128:(q + 1) * 128], Lb[c][nxt][:, q, :], Pb[c][:, q, :], start=True, stop=True,
                                signal=(q == NQ - 1))
                    self.tt('dve', Pb[c].v, Pb[c].v, ps.v.re("p (q m) -> p q m", q=NQ), ALU.add)
                    yield
                cur = nxt

        def lockstep(gens):
            alive = True
            while alive:
                alive = False
                for g in gens:
                    try:
                        next(g)
                        alive = True
                    except StopIteration:
                        pass

        PS = self.ps

        def crit(it, q):
            w = it * NQ + q
            xb_, ub_ = Xb[w % 2], Ub[w % 2]
            psx, psu, psh, psy = PS[0], PS[1], PS[2], PS[3 + (q % 2)]
            for c in range(4):
                self.mm(psx[:, c * 64:(c + 1) * 64], EXP["AT"][c][:, q, :], Hb[:, c, :], start=True, stop=False, signal=False)
                self.mm(psx[:, c * 64:(c + 1) * 64], LakT[c][:, q, :], VstAll[:, q, c, :], start=False, stop=True, signal=(c == 3))
            self.cp('act', xb_.v, psx[:, 0:256].re("p (c m) -> p c m", c=4))
            yield
            for c in range(4):
                self.mm(psu[:, c * 64:(c + 1) * 64], Pb[c][:, q, :], xb_[:, c, :], start=True, stop=True, signal=(c == 3))
            self.cp('dve', ub_.v, psu[:, 0:256].re("p (c m) -> p c m", c=4))
            yield
            for c in range(4):
                o = psh[:, c * 64:(c + 1) * 64]
                self.mm(o, BKG[c][:, 0, q, :], ub_[:, c, :], start=True, stop=False, signal=False)
                self.mm(o, BKG[c][:, 1, q, :], VstAll[:, q, c, :], start=False, stop=True, signal=(c == 3))
            for c in range(4):
                o = psy[:, c * 64:(c + 1) * 64]
                self.mm(o, EXP["RT"][c][:, q, :], Hb[:, c, :], start=True, stop=False, signal=False)
                self.mm(o, MrbT[c][:, q, :], ub_[:, c, :], start=False, stop=False, signal=False)
                self.mm(o, MrkT[c][:, q, :], VstAll[:, q, c, :], start=False, stop=True, signal=False)
            for c in range(4):
                self.mm(psy[:, 256 + c:257 + c], EXP["RK"][c][:, q, :], rkb[:, c:c + 1], start=True, stop=True, signal=(c == 3))
            self.tt('dve', H.v, H.v, GC[:, :, q:q + 1].bc([128, 4, 64]), ALU.mult)
            self.tt('dve', H.v, H.v, psh[:, 0:256].re("p (c m) -> p c m", c=4), ALU.add)
            self.cp('act', Hb.v, H.v)
            yield

        def post_a(it, q):
            w = it * NQ + q
            psy = PS[3 + (q % 2)]
            psg = PS[5]
            self.mm(psg.v, SGd[:, q, :], gup.v, start=True, stop=True)
            Y = Yv[w % 2]; Y2 = Ysq[w % 2]; YN = Yn[w % 2]; BN = Bn[w % 2]; YF = Yf[w % 2]
            s1 = st1[w % 2]; s2 = st2[w % 2]; mn = mean[w % 2]; vr = var[w % 2]
            py3 = psy[:, 0:256].re("p (c m) -> p c m", c=4)
            self.cp('act', Y.v, py3)
            self.act(Y2.v, py3, AF.Square)
            self.tt('dve', BN.v, VstAll[:, q, :, :], psy[:, 256:260].un(2).bc([128, 4, 64]), ALU.mult)
            self.S.op('dve', lambda e: e.reduce_sum(out=s1.ap, in_=Y.ap, axis=AX.X), [Y.v], [s1.v])
            self.S.op('dve', lambda e: e.reduce_sum(out=s2.ap, in_=Y2.ap, axis=AX.X), [Y2.v], [s2.v])
            self.ts('dve', mn.v, s1.v, 1.0 / 64.0, None, ALU.mult)
            self.tt('dve', vr.v, mn.v, mn.v, ALU.mult)
            self.stt('dve', vr.v, s2.v, 1.0 / 64.0, vr.v, ALU.mult, ALU.subtract)
            self.ts('dve', vr.v, vr.v, LNX_EPS, None, ALU.add)
            self.act(vr.v, vr.v, AF.Ln)
            self.act(vr.v, vr.v, AF.Exp, scale=-0.5)
            self.tt('dve', YN.v, Y.v, mn.v.un(2).bc([128, 4, 64]), ALU.subtract)
            self.tt('dve', YN.v, YN.v, vr.v.un(2).bc([128, 4, 64]), ALU.mult)
            self.tt('pool', YN.v, YN.v, lnxg.v, ALU.mult)
            self.tt('pool', YN.v, YN.v, lnxb.v, ALU.add)
            self.tt('pool', YN.v, YN.v, BN.v, ALU.add)
            for hl in range(2):
                P_ = slice(hl * 64, (hl + 1) * 64)
                gv = psg[P_, :].re("p (c h m) -> p c h m", c=4, h=2)[:, :, hl, :]
                self.tt('dve', YF[P_, :, :], YN[P_, :, :], gv, ALU.mult)

        def post_b(it, q, YTt):
            w = it * NQ + q
            YF = Yf[w % 2]
            pst = self.next_ps()
            pbt = pst.v.bitcast(BF16)
            for c in range(4):
                self.tr(pbt[0:64, c * 128:(c + 1) * 128], YF[:, c, :], signal=(c == 3))
            self.evac(YTt[:, :, :, q * CH:(q + 1) * CH], pbt[0:64, 0:512].re("p (c h t) -> p c h t", c=4, h=2))

        deferred = []
        for p in s12_pieces(0):
            p()
        for it in range(ntile):
            tok0 = it * TT
            if S.need_reset():
                S.hard_barrier()
            stage3(it)
            pieces = s12_pieces(it + 1) if it + 1 < ntile else []
            YTt = YT[it % 2]
            self.ps_allowed = [6, 7]

            def filler():
                if pieces:
                    pieces.pop(0)()

            for q in range(NQ + 2):
                if q < NQ:
                    for _ in crit(it, q):
                        filler()
                if q == NQ:
                    while pieces:
                        pieces.pop(0)()
                if 1 <= q <= NQ:
                    post_a(it, q - 1)
                if 2 <= q <= NQ - 1:
                    post_b(it, q - 2, YTt)
            self.ps_allowed = None
            deferred.append(lambda it=it, YTt=YTt: post_b(it, NQ - 2, YTt))
            deferred.append(lambda it=it, YTt=YTt: post_b(it, NQ - 1, YTt))
            deferred.append(lambda tok0=tok0, YTt=YTt: self.dma('sp', yas_v[:, :, :, tok0:tok0 + TT], YTt.v))
        while deferred:
            deferred.pop(0)()
        self._sbuf_left = self.nc.sbuf_bytes_remaining

    def phaseM(self, es, src, dst, final=False):
        S = self.S
        inp = self.inp
        TT = 512
        NS = 4
        self._stg_i = 0
        self._ev_i = 0
        use_a = 'R' in self.phases
        cols, ci = self.load_cols(es, "m_cols", [("g", inp["norm_mix_g"]), ("bin", inp["b_in"][2816:4864])])
        hb = self.sb(es, "m_hb", [128, 16], F32)
        self.ts('dve', hb.v, cols[:, ci["bin"]:ci["bin"] + 16], 0.5, None, ALU.mult)
        win = self.sb(es, "m_win", [128, 8, 2048], BF16)
        wa = self.sb(es, "m_wa", [128, 4, D], BF16)
        wb = self.sb(es, "m_wb", [128, 4, D], BF16)
        wmo = self.sb(es, "m_wmo", [128, 8, D], BF16)
        with ExitStack() as tes:
            stg = [self.sb(tes, "m_stg%d" % i, [128, 2048], F32) for i in range(6)]
            for k in range(8):
                self.load_weight(stg, win[:, k, :], inp["w_in"][k * 128:(k + 1) * 128, 2816:4864], 2048,
                                 scale=cols[:, ci["g"] + k: ci["g"] + k + 1])
                self.load_weight(stg, wmo[:, k, :], inp["w_mix_out"][k * 128:(k + 1) * 128, :], D)
            for c in range(4):
                self.load_weight(stg, wa[:, c, :], inp["w_branch_a"][c * 128:(c + 1) * 128, :], D, scale=0.5)
                self.load_weight(stg, wb[:, c, :], inp["w_branch_b"][c * 128:(c + 1) * 128, :], D, scale=0.25)
            S.barrier()
        xbufs, sst, rst, junk = self.norm_bufs(es, "m", NS)
        xrb = [self.sb(es, "m_xr%d" % i, [128, D], F32) for i in range(NS)]
        hs = [self.sb(es, "m_h%d" % i, [128, NS, D], BF16) for i in range(2)]
        hTs = [self.sb(es, "m_hT%d" % i, [128, 8, TT], BF16) for i in range(2)]
        TG = [self.sb(es, "m_tg%d" % m, [128, TT], BF16) for m in range(16)]
        YA = [self.sb(es, "m_ya%d" % i, [128, 4, TT], BF16) for i in range(2)]
        YB = [self.sb(es, "m_yb%d" % i, [128, 4, TT], BF16) for i in range(2)]
        MA = [self.sb(es, "m_ma%d" % i, [128, TT], F32) for i in range(3)]
        MG = [self.sb(es, "m_mg%d" % m, [128, TT], BF16) for m in range(8)]
        self._mb = [self.sb(es, "m_mb%d" % i, [128, TT], F32) for i in range(2)]
        ntile = self.ntok // TT

        def load_y(i):
            if i >= ntile:
                return
            tok0 = i * TT
            if use_a:
                self.dma('sp', YA[i % 2].v, self.yas[:, tok0:tok0 + TT].rearrange("(c p) t -> p c t", p=128))
            self.dma('sp', YB[i % 2].v, self.ybs[:, tok0:tok0 + TT].rearrange("(c p) t -> p c t", p=128))

        def front(i):
            self.norm_from_dram(src, i * TT, NS, xbufs, sst, rst, junk, hs[i % 2])
            self.transpose_h(hs[i % 2], NS, hTs[i % 2])

        def mid(i):
            hT = hTs[i % 2]
            ya, yb = YA[i % 2], YB[i % 2]
            for m in range(16):
                if m == 6 and i + 1 < ntile:
                    self.norm_from_dram(src, (i + 1) * TT, NS, xbufs, sst, rst, junk, hs[(i + 1) % 2])
                ps = self.next_ps()
                for k in range(8):
                    self.mm(ps.v, win[:, k, m * 128:(m + 1) * 128], hT[:, k, :], start=(k == 0), stop=(k == 7))
                self.act(TG[m].v, ps.v, AF.Tanh, scale=0.5, bias=hb[:, m:m + 1])
            for m in range(8):
                ma = MA[m % 3]
                if use_a:
                    ps = self.next_ps()
                    for c in range(4):
                        self.mm(ps.v, wa[:, c, m * 128:(m + 1) * 128], ya[:, c, :], start=(c == 0), stop=(c == 3))
                    self.stt('dve', ma.v, TG[m].v, 1.0, ps.v, ALU.add, ALU.mult)
                ps2 = self.next_ps()
                for c in range(4):
                    self.mm(ps2.v, wb[:, c, m * 128:(m + 1) * 128], yb[:, c, :], start=(c == 0), stop=(c == 3))
                if use_a:
                    mb = self._mb[m % 2]
                    self.stt('dve', mb.v, TG[8 + m].v, 1.0, ps2.v, ALU.add, ALU.mult)
                    self.tt('pool', MG[m].v, mb.v, ma.v, ALU.add)
                else:
                    self.stt('dve', MG[m].v, TG[8 + m].v, 1.0, ps2.v, ALU.add, ALU.mult)
            if i + 1 < ntile:
                self.transpose_h(hs[(i + 1) % 2], NS, hTs[(i + 1) % 2])

        def back(i):
            tok0 = i * TT
            for s in range(NS):
                self.dma('sp', xrb[s].v, src[tok0 + s * 128: tok0 + (s + 1) * 128, :])
            for s in range(NS):
                pss = [self.next_ps(), self.next_ps()]
                for half in range(2):
                    for m in range(8):
                        self.mm(pss[half].v, MG[m][:, s * 128:(s + 1) * 128], wmo[:, m, half * 512:(half + 1) * 512],
                                start=(m == 0), stop=(m == 7))
                xb = xrb[s]
                for half in range(2):
                    self.tt('dve', xb[:, half * 512:(half + 1) * 512], xb[:, half * 512:(half + 1) * 512], pss[half].v, ALU.add)
                self.dma('sp', dst[tok0 + s * 128: tok0 + (s + 1) * 128, :], xb.v)

        load_y(0)
        front(0)
        for i in range(ntile):
            if S.need_reset():
                S.hard_barrier()
            load_y(i + 1)
            mid(i)
            back(i)

    def evac(self, out, in_, i=None):
        if i is None:
            i = self._ev_i
            self._ev_i += 1
        return self.cp(('act', 'dve')[i % 2], out, in_)

    def phaseB(self, es, src, dst, final=False):
        S = self.S
        inp = self.inp
        TT = 512
        NS = 4
        self._stg_i = 0
        self._ev_i = 0
        cols, ci = self.load_cols(es, "b_cols", [("gx", inp["norm_x_g"]), ("gm", inp["norm_mem_g"])])
        gq = self.sb(es, "b_gq", [128, 8], F32)
        self.ts('dve', gq.v, cols[:, ci["gx"]:ci["gx"] + 8], 1.0 / 16.0, None, ALU.mult)
        wq = self.sb(es, "b_wq", [128, 8, D], BF16)
        wo = self.sb(es, "b_wo", [128, 8, D], BF16)
        kT = [self.sb(es, "b_kT%d" % b, [128, 8, NMEM], BF16) for b in range(self.nb)]
        Vt = [self.sb(es, "b_V%d" % b, [128, 2, D], BF16) for b in range(self.nb)]
        xts = [self.sb(es, "b_xt%d" % i, [128, NS, D], F32) for i in range(2)]
        h = self.sb(es, "b_h", [128, NS, D], BF16)
        hT = self.sb(es, "b_hT", [128, 8, TT], BF16)
        junk = self.sb(es, "b_junk", [128, D], BF16)
        ss = self.sb(es, "b_ss", [128, 4], F32)
        rstd = self.sb(es, "b_rstd", [128, 4], F32)
        with ExitStack() as tes:
            stg = [self.sb(tes, "b_stg%d" % i, [128, 2048], F32) for i in range(3)]
            wkv = self.sb(tes, "b_wkv", [128, 8, 2 * D], BF16)
            for k in range(8):
                self.load_weight(stg, wkv[:, k, :], inp["w_ckv"][k * 128:(k + 1) * 128, :], 2 * D,
                                 scale=cols[:, ci["gm"] + k: ci["gm"] + k + 1])
            for k in range(8):
                self.load_weight(stg, wq[:, k, :], inp["w_cq"][k * 128:(k + 1) * 128, :], D, scale=gq[:, k:k + 1])
                self.load_weight(stg, wo[:, k, :], inp["w_co"][k * 128:(k + 1) * 128, :], D)
            for b in range(self.nb):
                mt = xts[b % 2]
                self.dma('sp', mt[:, 0:2, :], inp["mem"][b * NMEM:(b + 1) * NMEM, :].rearrange("(s p) d -> p s d", p=128))
                self.rms_h(mt, 2, ss, rstd, junk, h)
                self.transpose_h(h, 2, hT[:, :, 0:NMEM])
                for m in range(8):
                    if m % 2 == 0:
                        ps = self.next_ps()
                    o = (m % 2) * NMEM
                    for k in range(8):
                        self.mm(ps[:, o:o + NMEM], wkv[:, k, m * 128:(m + 1) * 128], hT[:, k, 0:NMEM], start=(k == 0), stop=(k == 7))
                    if m % 2 == 1:
                        self.evac(kT[b][:, m - 1:m + 1, :], ps.v.re("p (a t) -> p a t", a=2))
                for mc in range(2):
                    for half in range(2):
                        ps = self.next_ps()
                        for k in range(8):
                            self.mm(ps.v, hT[:, k, mc * 128:(mc + 1) * 128], wkv[:, k, D + half * 512: D + (half + 1) * 512],
                                    start=(k == 0), stop=(k == 7))
                        self.evac(Vt[b][:, mc, half * 512:(half + 1) * 512], ps.v)
            S.barrier()
        qT = self.sb(es, "b_qT", [128, 8, TT], BF16)
        oT = self.sb(es, "b_oT", [128, 8, TT], BF16)
        prT = self.sb(es, "b_prT", [128, 2, 4, TT], BF16)
        prs = [self.sb(es, "b_pr%d" % i, [128, 4, NMEM], BF16) for i in range(2)]
        prn = [self.sb(es, "b_prn%d" % i, [128, 4, NMEM], BF16) for i in range(2)]
        mx = [self.sb(es, "b_mx%d" % i, [128, 4], F32) for i in range(2)]
        sm = [self.sb(es, "b_sm%d" % i, [128, 4], F32) for i in range(2)]
        hs = [h, self.sb(es, "b_h1", [128, NS, D], BF16)]
        hTs = [hT, self.sb(es, "b_hT1", [128, 8, TT], BF16)]
        sss = [ss, self.sb(es, "b_ss1", [128, 4], F32)]
        rstds = [rstd, self.sb(es, "b_rstd1", [128, 4], F32)]
        ntile = self.ntok // TT
        tiles_per_b = self.seq // TT

        def load(i):
            if i < ntile:
                self.dma('sp', xts[i % 2].v, src[i * TT:(i + 1) * TT, :].rearrange("(s p) d -> p s d", p=128))

        def front(i):
            self.rms_h(xts[i % 2], NS, sss[i % 2], rstds[i % 2], junk, hs[i % 2])
            self.transpose_h(hs[i % 2], NS, hTs[i % 2])

        def mid(i):
            b = i // tiles_per_b
            hT_ = hTs[i % 2]
            for m in range(8):
                ps = self.next_ps()
                for k in range(8):
                    self.mm(ps.v, wq[:, k, m * 128:(m + 1) * 128], hT_[:, k, :], start=(k == 0), stop=(k == 7))
                self.evac(qT[:, m, :], ps.v)
            def scores(s):
                banks = [self.ps[2 * (s % 2)], self.ps[2 * (s % 2) + 1]]
                for hh in range(4):
                    ps = banks[hh // 2]
                    o = (hh % 2) * NMEM
                    for c in range(2):
                        self.mm(ps[:, o:o + NMEM], qT[:, 2 * hh + c, s * 128:(s + 1) * 128], kT[b][:, 2 * hh + c, :],
                                start=(c == 0), stop=(c == 1))
                return banks

            def sm_gen(s, banks):
                pr = prs[s % 2]; pn = prn[s % 2]; mxs = mx[s % 2]; sms = sm[s % 2]
                for g in range(2):
                    self.S.op('dve', lambda e: e.reduce_max(out=mxs.ap[:, 2 * g:2 * g + 2],
                                                            in_=banks[g].ap.rearrange("p (a t) -> p a t", a=2), axis=AX.X),
                              [banks[g].v], [mxs.v])
                self.ts('dve', mxs.v, mxs.v, -1.0, None, ALU.mult)
                yield
                for hh in range(4):
                    ps = banks[hh // 2]
                    o = (hh % 2) * NMEM
                    self.act(pr[:, hh, :], ps[:, o:o + NMEM], AF.Exp, bias=mxs[:, hh:hh + 1], accum=sms[:, hh:hh + 1])
                yield
                self.S.op('dve', lambda e: e.reciprocal(out=sms.ap, in_=sms.ap), [sms.v], [sms.v])
                self.tt('dve', pn.v, pr.v, sms.v.un(2).bc([128, 4, NMEM]), ALU.mult)
                yield
                ps = self.next_ps()
                pb = ps.v.bitcast(BF16)
                for hh in range(4):
                    for mc in range(2):
                        self.tr(pb[:, (hh * 2 + mc) * 128:(hh * 2 + mc + 1) * 128], pn[:, hh, mc * 128:(mc + 1) * 128],
                                signal=(hh == 3 and mc == 1))
                self.evac(prT[:, :, :, s * 128:(s + 1) * 128], pb.re("p (h m t) -> p m h t", h=4, m=2))
                yield

            self.ps_allowed = [4, 5, 6, 7]
            for pair in ((0, 1), (2, 3)):
                gens = [sm_gen(s, scores(s)) for s in pair]
                alive = True
                while alive:
                    alive = False
                    for g in gens:
                        try:
                            next(g)
                            alive = True
                        except StopIteration:
                            pass
                if pair == (0, 1) and i + 1 < ntile:
                    self.rms_h(xts[(i + 1) % 2], NS, sss[(i + 1) % 2], rstds[(i + 1) % 2], junk, hs[(i + 1) % 2])
            self.ps_allowed = None
            for hh in range(4):
                for c in range(2):
                    ps = self.next_ps()
                    for mc in range(2):
                        self.mm(ps.v, Vt[b][:, mc, hh * 256 + c * 128: hh * 256 + (c + 1) * 128], prT[:, mc, hh, :],
                                start=(mc == 0), stop=(mc == 1))
                    self.evac(oT[:, 2 * hh + c, :], ps.v)
            if i + 1 < ntile:
                self.transpose_h(hs[(i + 1) % 2], NS, hTs[(i + 1) % 2])

        def back(i):
            xt = xts[i % 2]
            for s in range(NS):
                pss = [self.next_ps(), self.next_ps()]
                for half in range(2):
                    for k in range(8):
                        self.mm(pss[half].v, oT[:, k, s * 128:(s + 1) * 128], wo[:, k, half * 512:(half + 1) * 512],
                                start=(k == 0), stop=(k == 7))
                for half in range(2):
                    self.tt('dve', xt[:, s, half * 512:(half + 1) * 512], xt[:, s, half * 512:(half + 1) * 512], pss[half].v, ALU.add)
                self.dma('sp', dst[i * TT + s * 128: i * TT + (s + 1) * 128, :], xt[:, s, :])

        load(0)
        load(1)
        front(0)
        for i in range(ntile):
            if S.need_reset():
                S.hard_barrier()
            mid(i)
            back(i)
            load(i + 2)

    def phaseC(self, es, src, dst, final=True):
        nc, S = self.nc, self.S
        TT = 256
        NS = TT // 128
        NJ = DFF // 128
        inp = self.inp
        self._stg_i = 0
        cols, ci = self.load_cols(es, "c_cols", [("g", inp["norm_ffn_g"]), ("cw0", inp["ffn_conv_w"][0]),
                                                 ("cw1", inp["ffn_conv_w"][1]), ("cw2", inp["ffn_conv_w"][2]),
                                                 ("cb", inp["ffn_conv_b"])])
        wi = self.sb(es, "c_wi", [128, 8, 2 * DFF], BF16)
        wo = self.sb(es, "c_wo", [128, NJ, D], BF16)
        gfin = self.sb(es, "c_gfin", [128, D], F32)
        self.dma('sp', gfin.v, inp["norm_final_g"].partition_broadcast(128))
        with ExitStack() as tes:
            stg = [self.sb(tes, "c_stg%d" % i, [128, 2048], F32) for i in range(6)]
            for k in range(8):
                self.load_weight(stg, wi[:, k, :], inp["w_ffn_in"][k * 128:(k + 1) * 128, :], 2 * DFF,
                                 scale=cols[:, ci["g"] + k: ci["g"] + k + 1])
            for j in range(NJ):
                self.load_weight(stg, wo[:, j, :], inp["w_ffn_out"][j * 128:(j + 1) * 128, :], D)
            S.barrier()
        xts = [self.sb(es, "c_xt%d" % i, [128, NS, D], F32) for i in range(2)]
        hs = [self.sb(es, "c_h%d" % i, [128, NS, D], BF16) for i in range(2)]
        hTs = [self.sb(es, "c_hT%d" % i, [128, 8, TT], BF16) for i in range(2)]
        actT = [self.sb(es, "c_actT%d" % j, [128, TT], BF16) for j in range(NJ)]
        junk = self.sb(es, "c_junk", [128, D], BF16)
        sss = [self.sb(es, "c_ss%d" % i, [128, 4], F32) for i in range(3)]
        rstds = [self.sb(es, "c_rstd%d" % i, [128, 4], F32) for i in range(3)]
        halos = [self.sb(es, "c_halo%d" % j, [128, 2], F32) for j in range(NJ)]
        NW = 4
        uraw = [es.enter_context(self.nc.sbuf_tensor("c_uw%d" % i, [128, 2 + TT], F32)) for i in range(NW)]
        uh = [T(r[:, 0:2], "c_uh") for r in uraw]
        ub = [T(r[:, 2:2 + TT], "c_ub") for r in uraw]

        def uview(jj, a, b_):
            return V([uh[jj], ub[jj]], uraw[jj][:, a:b_])

        acc = [self.sb(es, "c_acc%d" % i, [128, TT], F32) for i in range(NW)]
        th = [self.sb(es, "c_th%d" % i, [128, TT], F32) for i in range(NW)]
        ntile = self.ntok // TT
        tiles_per_b = self.seq // TT

        def load(i):
            if i < ntile:
                self.dma('sp', xts[i % 2].v, src[i * TT:(i + 1) * TT, :].rearrange("(s p) d -> p s d", p=128))

        def front(i):
            self.rms_h(xts[i % 2], NS, sss[i % 2], rstds[i % 2], junk, hs[i % 2])
            self.transpose_h(hs[i % 2], NS, hTs[i % 2])

        def mid(i):
            hT = hTs[i % 2]
            st = {}
            for j in range(NJ + 2):
                if i + 1 < ntile:
                    if j == 8:
                        self.rms_h(xts[(i + 1) % 2], NS, sss[(i + 1) % 2], rstds[(i + 1) % 2], junk, hs[(i + 1) % 2])
                    if j == NJ:
                        self.transpose_h(hs[(i + 1) % 2], NS, hTs[(i + 1) % 2])
                if j < NJ:
                    ps = self.next_ps()
                    for half in range(2):
                        c0 = half * DFF + j * 128
                        for k in range(8):
                            self.mm(ps[:, half * TT:(half + 1) * TT], wi[:, k, c0:c0 + 128], hT[:, k, :], start=(k == 0), stop=(k == 7))
                    jj = j % NW
                    a = acc[jj]
                    w0 = cols[:, ci["cw0"] + j: ci["cw0"] + j + 1]
                    w1 = cols[:, ci["cw1"] + j: ci["cw1"] + j + 1]
                    w2 = cols[:, ci["cw2"] + j: ci["cw2"] + j + 1]
                    cb = cols[:, ci["cb"] + j: ci["cb"] + j + 1]
                    self.cp('dve', uh[jj].v, halos[j].v)
                    self.cp('act', ub[jj].v, ps[:, 0:TT])
                    self.act(a.v, ps[:, 0:TT], AF.Identity, scale=w2, bias=cb)
                    self.stt('dve', a.v, uview(jj, 1, 1 + TT), w1, a.v, ALU.mult, ALU.add)
                    self.stt('dve', a.v, uview(jj, 0, TT), w0, a.v, ALU.mult, ALU.add)
                    self.cp('dve', halos[j].v, ub[jj][:, TT - 2:TT])
                    st[j] = ps
                if 0 <= j - 1 < NJ:
                    jj = (j - 1) % NW
                    self.act(th[jj].v, acc[jj].v, AF.Gelu_apprx_tanh)
                if 0 <= j - 2 < NJ:
                    jj = (j - 2) % NW
                    self.tt('dve', actT[j - 2].v, th[jj].v, st[j - 2][:, TT:2 * TT], ALU.mult)

        def back(i):
            xt = xts[i % 2]
            ss, rstd = sss[2], rstds[2]
            for s in range(NS):
                pss = [self.next_ps(), self.next_ps()]
                for half in range(2):
                    for j in range(NJ):
                        self.mm(pss[half].v, actT[j][:, s * 128:(s + 1) * 128], wo[:, j, half * 512:(half + 1) * 512],
                                start=(j == 0), stop=(j == NJ - 1))
                for half in range(2):
                    self.tt('dve', xt[:, s, half * 512:(half + 1) * 512], xt[:, s, half * 512:(half + 1) * 512], pss[half].v, ALU.add)
                if final:
                    self.act(junk.v, xt[:, s, :], AF.Square, accum=ss[:, s:s + 1])
                    self.ts('dve', rstd[:, s:s + 1], ss[:, s:s + 1], 1.0 / D, NORM_EPS, ALU.mult, ALU.add)
                    self.rsqrt(rstd[:, s:s + 1], rstd[:, s:s + 1])
                    self.stt('dve', xt[:, s, :], xt[:, s, :], rstd[:, s:s + 1], gfin.v, ALU.mult, ALU.mult)
                self.dma('sp', dst[i * TT + s * 128: i * TT + (s + 1) * 128, :], xt[:, s, :])

        load(0)
        load(1)
        front(0)
        for i in range(ntile):
            if S.need_reset():
                S.hard_barrier()
            if i % tiles_per_b == 0:
                for j in range(NJ):
                    self.memset('pool', halos[j].v, 0.0)
            mid(i)
            back(i)
            load(i + 2)


_PROG_CACHE = {}


def get_prog(nb=4, seq=2048, phases="LRMBC"):
    key = (nb, seq, phases)
    if key not in _PROG_CACHE:
        _PROG_CACHE[key] = Prog(nb, seq, phases)
    return _PROG_CACHE[key]


_W_NAMES = ["norm_mix_g", "w_in", "b_in", "mu_shift", "w0", "w_lora_up", "a0", "a_lora_up", "g_lora_up", "k_k", "k_a",
            "r_k", "lnx_g", "lnx_b", "w_branch_a", "conv_b_w", "conv_b_b", "w_rg_a", "b_rg_a", "w_rg_x", "b_rg_x",
            "lru_lambda", "w_branch_b", "w_mix_out", "norm_x_g", "norm_mem_g", "w_cq", "w_ckv", "w_co", "norm_ffn_g",
            "w_ffn_in", "ffn_conv_w", "ffn_conv_b", "w_ffn_out", "norm_final_g"]


def run(inputs, ncores=8, nb=4, seq=2048, phases="LRMBC"):
    prog = get_prog(nb, seq, phases)
    shapes = {k: tuple(v.shape) for k, v in prog.inp.items()}
    shared = {}
    for k in _W_NAMES:
        a = np.ascontiguousarray(np.asarray(inputs[k], dtype=np.float32))
        shared[k] = a.reshape(shapes[k])
    x = np.asarray(inputs["x"], dtype=np.float32)
    mem = np.asarray(inputs["mem"], dtype=np.float32)
    in_maps = []
    for c in range(ncores):
        m = dict(shared)
        m["x"] = np.ascontiguousarray(x[c * nb:(c + 1) * nb, :seq]).reshape(nb * seq, D)
        m["mem"] = np.ascontiguousarray(mem[c * nb:(c + 1) * nb]).reshape(nb * NMEM, D)
        in_maps.append(m)
    res = run_bass_kernel_spmd(prog.nc, in_maps, core_ids=list(range(ncores)))
    outs = [np.asarray(r["out"]).reshape(nb, seq, D) for r in res.results]
    return np.concatenate(outs, axis=0)


def kernel(**inputs):
    return run(inputs).astype(np.float32)
```
